# Optimizing a Trainium2 kernel written in Bass

```python
import math
import jax
import jax.numpy as jnp
from jax import lax
import numpy as np

D_MODEL = 2048
BATCH = 2
SEQ = 16384
DEPTH = 4

N_MIXERS = 3
N_MEM = 256
NORM_EPS = 1e-6

SSD_EXPAND = 2
SSD_INNER = SSD_EXPAND * D_MODEL
SSD_HEADDIM = 64
SSD_HEADS = SSD_INNER // SSD_HEADDIM
SSD_STATE = 128
SSD_GROUPS = 8
SSD_CONV = 4
SSD_CHUNK = 128
SSD_CONV_DIM = SSD_INNER + 2 * SSD_GROUPS * SSD_STATE
SSD_IN_DIM = SSD_INNER + SSD_CONV_DIM + 2 * SSD_HEADS
DT_MIN = 1e-3
DT_MAX = 1e-1

MLSTM_HEADS = 8
MLSTM_QK_DIM = D_MODEL // 2
MLSTM_V_DIM = D_MODEL
MLSTM_QK_HEAD = MLSTM_QK_DIM // MLSTM_HEADS
MLSTM_V_HEAD = MLSTM_V_DIM // MLSTM_HEADS
MLSTM_CHUNK = 128
MLSTM_IN_DIM = 2 * MLSTM_QK_DIM + 2 * MLSTM_V_DIM + 4 * MLSTM_HEADS

RG_WIDTH = D_MODEL
RG_BLOCKS = 8
RG_BLOCK_DIM = RG_WIDTH // RG_BLOCKS
RG_CONV = 4
RG_C = 8.0
RG_A_MIN = 0.9
RG_A_MAX = 0.999

XA_HEADS = 4
XA_HEAD_DIM = D_MODEL // XA_HEADS

FFN_DIM = 5632
FFN_CONV = 3

kernel_name = "bidir_hybrid_ssd_mlstm_rglru_trunk"


def rmsnorm(x, g):
    xf = x.astype(jnp.float32)
    y = xf * lax.rsqrt(jnp.mean(xf * xf, axis=-1, keepdims=True) + NORM_EPS)
    return y.astype(x.dtype) * g


def dwconv(x, w, b):
    width = w.shape[0]
    left = width // 2
    out = lax.conv_general_dilated(
        x, w[:, None, :].astype(x.dtype), window_strides=(1,),
        padding=[(left, width - 1 - left)],
        dimension_numbers=("NWC", "WIO", "NWC"),
        feature_group_count=x.shape[-1])
    return out + b


def flip(t):
    return jnp.flip(t, axis=1)


def ssd_scan(x, dt, A, B, C):
    b, s, h, p = x.shape
    g, n = B.shape[2], B.shape[3]
    k = h // g
    L = SSD_CHUNK
    nc = s // L
    xd = (x * dt[..., None]).reshape(b, nc, L, g, k, p)
    Bc = B.reshape(b, nc, L, g, n)
    Cc = C.reshape(b, nc, L, g, n)
    la = jnp.moveaxis((dt * A).reshape(b, nc, L, g, k), 2, -1)
    cum = jnp.cumsum(la, axis=-1)
    tril = jnp.tril(jnp.ones((L, L), dtype=bool))
    seg = jnp.exp(jnp.where(tril, cum[..., :, None] - cum[..., None, :], -jnp.inf))
    cb = jnp.einsum("bclgn,bcmgn->bcglm", Cc, Bc)
    y_diag = jnp.einsum("bcglm,bcgklm,bcmgkp->bclgkp", cb, seg, xd)
    decay_to_end = jnp.exp(cum[..., -1:] - cum)
    states = jnp.einsum("bclgn,bcgkl,bclgkp->bcgkpn", Bc, decay_to_end, xd)
    chunk_decay = jnp.exp(cum[..., -1])

    def step(carry, inp):
        st, dec = inp
        return carry * dec[..., None, None] + st, carry

    init = jnp.zeros_like(states[:, 0])
    _, prev = lax.scan(step, init, (jnp.moveaxis(states, 1, 0), jnp.moveaxis(chunk_decay, 1, 0)))
    prev = jnp.moveaxis(prev, 0, 1)
    y_off = jnp.einsum("bclgn,bcgkpn,bcgkl->bclgkp", Cc, prev, jnp.exp(cum))
    return (y_diag + y_off).reshape(b, s, h, p)


def ssd_mixer(u, w_in, conv_w, conv_b, dt_bias, a_log, d_skip, norm_g, w_out):
    b, s, _ = u.shape
    proj = u @ w_in
    z, xbc, dt_raw = jnp.split(proj, [SSD_INNER, SSD_INNER + SSD_CONV_DIM], axis=-1)
    xbc = jax.nn.silu(dwconv(xbc, conv_w, conv_b))
    xs, Bm, Cm = jnp.split(xbc, [SSD_INNER, SSD_INNER + SSD_GROUPS * SSD_STATE], axis=-1)
    xs = xs.reshape(b, s, SSD_HEADS, SSD_HEADDIM)
    Bm = Bm.reshape(b, s, SSD_GROUPS, SSD_STATE)
    Cm = Cm.reshape(b, s, SSD_GROUPS, SSD_STATE)
    dt = jax.nn.softplus(dt_raw.astype(jnp.float32).reshape(b, s, 2, SSD_HEADS) + dt_bias)
    A = -jnp.exp(a_log.astype(jnp.float32))
    y_f = ssd_scan(xs, dt[:, :, 0], A[0], Bm, Cm)
    y_b = flip(ssd_scan(flip(xs), flip(dt[:, :, 1]), A[1], flip(Bm), flip(Cm)))
    y = y_f + y_b + xs * d_skip[:, None]
    y = rmsnorm(y.reshape(b, s, SSD_INNER) * jax.nn.silu(z), norm_g)
    return y.astype(u.dtype) @ w_out


def mlstm_chunked(q, k, v, ig, fg):
    b, s, h, dk = q.shape
    dv = v.shape[-1]
    L = MLSTM_CHUNK
    nc = s // L
    qc = q.reshape(b, nc, L, h, dk)
    kc = k.reshape(b, nc, L, h, dk)
    vc = v.reshape(b, nc, L, h, dv)
    li = jnp.moveaxis(ig.reshape(b, nc, L, h), 2, -1)
    lf = jnp.moveaxis(jax.nn.log_sigmoid(fg).reshape(b, nc, L, h), 2, -1)
    F = jnp.cumsum(lf, axis=-1)
    G = F[..., -1]
    a = G[..., None] - F + li
    a_max = jnp.max(a, axis=-1)
    w = jnp.exp(a - a_max[..., None])
    kv_loc = jnp.einsum("bchl,bclhk,bclhv->bchkv", w, kc, vc)
    n_loc = jnp.einsum("bchl,bclhk->bchk", w, kc)

    def step(carry, inp):
        Cs, ns, m = carry
        g_c, amax_c, kv_c, n_c = inp
        m_new = jnp.maximum(g_c + m, amax_c)
        s_old = jnp.exp(g_c + m - m_new)
        s_new = jnp.exp(amax_c - m_new)
        C_new = s_old[..., None, None] * Cs + s_new[..., None, None] * kv_c
        n_new = s_old[..., None] * ns + s_new[..., None] * n_c
        return (C_new, n_new, m_new), (Cs, ns, m)

    init = (jnp.zeros_like(kv_loc[:, 0]), jnp.zeros_like(n_loc[:, 0]), jnp.zeros_like(G[:, 0]))
    xs = (jnp.moveaxis(G, 1, 0), jnp.moveaxis(a_max, 1, 0),
          jnp.moveaxis(kv_loc, 1, 0), jnp.moveaxis(n_loc, 1, 0))
    _, (C_prev, n_prev, m_prev) = lax.scan(step, init, xs)
    C_prev = jnp.moveaxis(C_prev, 0, 1)
    n_prev = jnp.moveaxis(n_prev, 0, 1)
    m_prev = jnp.moveaxis(m_prev, 0, 1)

    tril = jnp.tril(jnp.ones((L, L), dtype=bool))
    Dm = jnp.where(tril, F[..., :, None] - F[..., None, :] + li[..., None, :], -jnp.inf)
    m_inter = F + m_prev[..., None]
    m_t = jnp.maximum(m_inter, jnp.max(Dm, axis=-1))
    S = jnp.exp(Dm - m_t[..., None]) * jnp.einsum("bclhk,bcmhk->bchlm", qc, kc)
    inter = jnp.exp(m_inter - m_t)
    num = (jnp.einsum("bchlm,bcmhv->bchlv", S, vc)
           + inter[..., None] * jnp.einsum("bclhk,bchkv->bchlv", qc, C_prev))
    den = jnp.sum(S, axis=-1) + inter * jnp.einsum("bclhk,bchk->bchl", qc, n_prev)
    hout = num / jnp.maximum(jnp.abs(den), jnp.exp(-m_t))[..., None]
    return jnp.moveaxis(hout, 2, 3).reshape(b, s, h, dv)


def mlstm_mixer(u, w_in, gate_bias, head_norm, w_out):
    b, s, _ = u.shape
    proj = u @ w_in
    q, k, v, o, gates = jnp.split(
        proj, [MLSTM_QK_DIM, 2 * MLSTM_QK_DIM, 2 * MLSTM_QK_DIM + MLSTM_V_DIM,
               2 * MLSTM_QK_DIM + 2 * MLSTM_V_DIM], axis=-1)
    q = q.reshape(b, s, MLSTM_HEADS, MLSTM_QK_HEAD)
    k = k.reshape(b, s, MLSTM_HEADS, MLSTM_QK_HEAD) * MLSTM_QK_HEAD ** -0.5
    v = v.reshape(b, s, MLSTM_HEADS, MLSTM_V_HEAD)
    gates = gates.astype(jnp.float32).reshape(b, s, 4, MLSTM_HEADS) + gate_bias
    h_f = mlstm_chunked(q, k, v, gates[:, :, 0], gates[:, :, 1])
    h_b = flip(mlstm_chunked(flip(q), flip(k), flip(v), flip(gates[:, :, 2]), flip(gates[:, :, 3])))
    hh = rmsnorm(h_f + h_b, head_norm.reshape(MLSTM_HEADS, MLSTM_V_HEAD))
    hh = hh.reshape(b, s, MLSTM_V_DIM) * jax.nn.sigmoid(o)
    return hh.astype(u.dtype) @ w_out


def linear_scan(a, bx):
    def combine(l, r):
        return (l[0] * r[0], r[0] * l[1] + r[1])
    _, hs = lax.associative_scan(combine, (a, bx), axis=1)
    return hs


def rglru_mixer(u, w_in, conv_w, conv_b, gate_w, gate_b, lam, w_out):
    b, s, _ = u.shape
    gate_br, xr = jnp.split(u @ w_in, [RG_WIDTH], axis=-1)
    xr = dwconv(xr, conv_w, conv_b)
    xb = xr.reshape(b, s, RG_BLOCKS, RG_BLOCK_DIM)
    hsum = jnp.zeros((b, s, RG_WIDTH), jnp.float32)
    for d in range(2):
        g = (jnp.einsum("bsnk,nkj->bsnj", xb, gate_w[d]) + gate_b[d]).astype(jnp.float32)
        r = jax.nn.sigmoid(g[..., :RG_BLOCK_DIM]).reshape(b, s, RG_WIDTH)
        i = jax.nn.sigmoid(g[..., RG_BLOCK_DIM:]).reshape(b, s, RG_WIDTH)
        log_a = -RG_C * r * jax.nn.softplus(-lam[d].astype(jnp.float32))
        a = jnp.exp(log_a)
        bx = jnp.sqrt(-jnp.expm1(2.0 * log_a)) * (i * xr)
        if d == 0:
            hsum = hsum + linear_scan(a, bx)
        else:
            hsum = hsum + flip(linear_scan(flip(a), flip(bx)))
    y = hsum * jax.nn.gelu(gate_br)
    return y.astype(u.dtype) @ w_out


def mem_cross_attention(u, mem_n, wq, wkv, wo):
    b, s, _ = u.shape
    m = mem_n.shape[1]
    q = (u @ wq).reshape(b, s, XA_HEADS, XA_HEAD_DIM)
    k, v = jnp.split(mem_n @ wkv, 2, axis=-1)
    k = k.reshape(b, m, XA_HEADS, XA_HEAD_DIM)
    v = v.reshape(b, m, XA_HEADS, XA_HEAD_DIM)
    scores = jnp.einsum("bshd,bmhd->bhsm", q, k).astype(jnp.float32) * XA_HEAD_DIM ** -0.5
    p = jax.nn.softmax(scores, axis=-1).astype(v.dtype)
    o = jnp.einsum("bhsm,bmhd->bshd", p, v).reshape(b, s, D_MODEL)
    return o @ wo


def conv_glu_ffn(u, w_up, conv_w, conv_b, w_down):
    gate, val = jnp.split(u @ w_up, 2, axis=-1)
    return (jax.nn.silu(dwconv(gate, conv_w, conv_b)) * val) @ w_down


def setup_inputs(seed: int = 0) -> dict:
    key = jax.random.key(seed)
    ks = iter(jax.random.split(key, 64))

    def normal(shape, scale):
        return jax.random.normal(next(ks), shape, jnp.float32) * scale

    def gain(shape):
        return 1.0 + normal(shape, 0.02)

    n_ssd = len(range(0, DEPTH, N_MIXERS))
    n_mlstm = len(range(1, DEPTH, N_MIXERS))
    n_rglru = len(range(2, DEPTH, N_MIXERS))

    dt0 = jnp.exp(jax.random.uniform(next(ks), (n_ssd, 2, SSD_HEADS), jnp.float32,
                                     math.log(DT_MIN), math.log(DT_MAX)))
    ssd_dt_bias = dt0 + jnp.log(-jnp.expm1(-dt0))
    ssd_a_log = jnp.log(jax.random.uniform(next(ks), (n_ssd, 2, SSD_HEADS), jnp.float32, 1.0, 16.0))
    a_c = jax.random.uniform(next(ks), (n_rglru, 2, RG_WIDTH), jnp.float32, RG_A_MIN, RG_A_MAX)
    base = a_c ** (1.0 / RG_C)
    rglru_lambda = jnp.log(base) - jnp.log1p(-base)
    mlstm_gate_bias = (jnp.array([0.0, 3.0, 0.0, 3.0], jnp.float32)[None, :, None]
                       + normal((n_mlstm, 4, MLSTM_HEADS), 0.1))

    D = D_MODEL
    return {
        "x": normal((BATCH, SEQ, D), 1.0),
        "mem": normal((BATCH, N_MEM, D), 1.0),
        "mem_norm": gain((D,)),
        "mix_norm": gain((DEPTH, D)),
        "xattn_norm": gain((DEPTH, D)),
        "xattn_wq": normal((DEPTH, D, D), D ** -0.5),
        "xattn_wkv": normal((DEPTH, D, 2 * D), D ** -0.5),
        "xattn_wo": normal((DEPTH, D, D), D ** -0.5),
        "ffn_norm": gain((DEPTH, D)),
        "ffn_w_up": normal((DEPTH, D, 2 * FFN_DIM), D ** -0.5),
        "ffn_conv_w": normal((DEPTH, FFN_CONV, FFN_DIM), FFN_CONV ** -0.5),
        "ffn_conv_b": normal((DEPTH, FFN_DIM), 0.02),
        "ffn_w_down": normal((DEPTH, FFN_DIM, D), FFN_DIM ** -0.5),
        "ssd_w_in": normal((n_ssd, D, SSD_IN_DIM), D ** -0.5),
        "ssd_conv_w": normal((n_ssd, SSD_CONV, SSD_CONV_DIM), SSD_CONV ** -0.5),
        "ssd_conv_b": normal((n_ssd, SSD_CONV_DIM), 0.02),
        "ssd_dt_bias": ssd_dt_bias,
        "ssd_a_log": ssd_a_log,
        "ssd_d_skip": 1.0 + normal((n_ssd, SSD_HEADS), 0.1),
        "ssd_norm": gain((n_ssd, SSD_INNER)),
        "ssd_w_out": normal((n_ssd, SSD_INNER, D), SSD_INNER ** -0.5),
        "mlstm_w_in": normal((n_mlstm, D, MLSTM_IN_DIM), D ** -0.5),
        "mlstm_gate_bias": mlstm_gate_bias,
        "mlstm_head_norm": gain((n_mlstm, MLSTM_V_DIM)),
        "mlstm_w_out": normal((n_mlstm, MLSTM_V_DIM, D), MLSTM_V_DIM ** -0.5),
        "rglru_w_in": normal((n_rglru, D, 2 * RG_WIDTH), D ** -0.5),
        "rglru_conv_w": normal((n_rglru, RG_CONV, RG_WIDTH), RG_CONV ** -0.5),
        "rglru_conv_b": normal((n_rglru, RG_WIDTH), 0.02),
        "rglru_gate_w": normal((n_rglru, 2, RG_BLOCKS, RG_BLOCK_DIM, 2 * RG_BLOCK_DIM), RG_BLOCK_DIM ** -0.5),
        "rglru_gate_b": normal((n_rglru, 2, RG_BLOCKS, 2 * RG_BLOCK_DIM), 0.02),
        "rglru_lambda": rglru_lambda,
        "rglru_w_out": normal((n_rglru, RG_WIDTH, D), RG_WIDTH ** -0.5),
        "final_norm": gain((D,)),
    }


def reference(x, mem, mem_norm, mix_norm, xattn_norm, xattn_wq, xattn_wkv, xattn_wo,
              ffn_norm, ffn_w_up, ffn_conv_w, ffn_conv_b, ffn_w_down,
              ssd_w_in, ssd_conv_w, ssd_conv_b, ssd_dt_bias, ssd_a_log, ssd_d_skip, ssd_norm, ssd_w_out,
              mlstm_w_in, mlstm_gate_bias, mlstm_head_norm, mlstm_w_out,
              rglru_w_in, rglru_conv_w, rglru_conv_b, rglru_gate_w, rglru_gate_b, rglru_lambda, rglru_w_out,
              final_norm):
    mem_n = rmsnorm(mem, mem_norm)
    h = x
    for i in range(DEPTH):
        kind = i % N_MIXERS
        j = i // N_MIXERS
        u = rmsnorm(h, mix_norm[i])
        if kind == 0:
            mix = ssd_mixer(u, ssd_w_in[j], ssd_conv_w[j], ssd_conv_b[j], ssd_dt_bias[j],
                            ssd_a_log[j], ssd_d_skip[j], ssd_norm[j], ssd_w_out[j])
        elif kind == 1:
            mix = mlstm_mixer(u, mlstm_w_in[j], mlstm_gate_bias[j], mlstm_head_norm[j], mlstm_w_out[j])
        else:
            mix = rglru_mixer(u, rglru_w_in[j], rglru_conv_w[j], rglru_conv_b[j], rglru_gate_w[j],
                              rglru_gate_b[j], rglru_lambda[j], rglru_w_out[j])
        h = h + mix
        h = h + mem_cross_attention(rmsnorm(h, xattn_norm[i]), mem_n,
                                    xattn_wq[i], xattn_wkv[i], xattn_wo[i])
        h = h + conv_glu_ffn(rmsnorm(h, ffn_norm[i]), ffn_w_up[i], ffn_conv_w[i],
                             ffn_conv_b[i], ffn_w_down[i])
    return rmsnorm(h, final_norm)
```

```python
import contextlib
import numpy as np
import concourse.bass as bass
import concourse.mybir as mybir
from concourse.bass_utils import run_bass_kernel_spmd

F32 = mybir.dt.float32
BF16 = mybir.dt.bfloat16
ALU = mybir.AluOpType
AF = mybir.ActivationFunctionType
AX = mybir.AxisListType

D_MODEL = 2048
NCH = 16
FFN_DIM = 5632
FFN_BLK = 44
EPS = 1e-6
import os
SEM_LIMIT = int(os.environ.get('KB_SEM_LIMIT', '24000'))
KB_STATS = {}


class KB:
    def __init__(self, nc, stack):
        self.nc = nc
        self.stack = stack
        self.semstack = stack
        self.eng = dict(pe=nc.tensor, dve=nc.vector, act=nc.scalar, pool=nc.gpsimd, sp=nc.sync)
        self.esem = {}
        self.ecnt = {}
        self.seen = {e: {} for e in self.eng}
        self.lastw = {}
        self.reads = {}
        self.dq = {}
        self.nsem = 0
        self.sems = {}
        for e in ("pe", "dve", "act", "pool"):
            self._new_esem(e)
        self.pending_noinc = {e: False for e in self.eng}

    def new_sem(self, name):
        s = self.semstack.enter_context(self.nc.semaphore(f"{name}_{self.nsem}"))
        self.nsem += 1
        self.sems[id(s)] = s
        return s

    def _new_esem(self, e):
        self.esem[e] = self.new_sem("e" + e)
        self.ecnt[e] = 0
        KB_STATS["nsem"] = self.nsem
        KB_STATS[e] = KB_STATS.get(e, 0) + 1

    def _waits(self, e, reads, writes):
        need = {}
        def add(ev):
            if ev is None:
                return
            s, v = ev
            if need.get(id(s), (None, 0))[1] < v:
                need[id(s)] = (s, v)
        for t in reads:
            add(self.lastw.get(t))
        for t in writes:
            add(self.lastw.get(t))
            for ev in self.reads.get(t, {}).values():
                add(ev)
        engine = self.eng[e]
        for sid, (s, v) in need.items():
            if e == "pe" and s is self.esem["pe"]:
                continue
            if self.seen[e].get(sid, 0) >= v:
                continue
            engine.wait_ge(s, v)
            self.seen[e][sid] = v

    def _record(self, ev, reads, writes):
        for t in reads:
            d = self.reads.setdefault(t, {})
            s, v = ev
            if d.get(id(s), (None, 0))[1] < v:
                d[id(s)] = ev
        for t in writes:
            self.lastw[t] = ev
            self.reads[t] = {}

    def op(self, e, fn, reads=(), writes=(), inc=True):
        self._waits(e, reads, writes)
        if self.ecnt[e] >= SEM_LIMIT and not self.pending_noinc[e]:
            self._new_esem(e)
        ins = fn()
        ev = (self.esem[e], self.ecnt[e] + 1)
        if inc:
            ins.then_inc(self.esem[e], 1)
            self.ecnt[e] += 1
            self.pending_noinc[e] = False
        else:
            self.pending_noinc[e] = True
        self._record(ev, reads, writes)
        return ins

    def dma(self, q, out, in_, reads=(), writes=(), nslots=8):
        st = self.dq.setdefault(q, dict(sems=[], vals=[], idx=0, old=[]))
        i = st["idx"] % nslots
        if len(st["sems"]) <= i:
            st["sems"].append(self.new_sem("d" + q))
            st["vals"].append(0)
        elif st["vals"][i] + 16 > SEM_LIMIT:
            st["old"].append((st["sems"][i], st["vals"][i]))
            engine = self.eng[q]
            if self.seen[q].get(id(st["sems"][i]), 0) < st["vals"][i]:
                engine.wait_ge(st["sems"][i], st["vals"][i])
                self.seen[q][id(st["sems"][i])] = st["vals"][i]
            st["sems"][i] = self.new_sem("d" + q)
            st["vals"][i] = 0
        st["idx"] += 1
        s = st["sems"][i]
        engine = self.eng[q]
        if st["vals"][i] > 0 and self.seen[q].get(id(s), 0) < st["vals"][i]:
            engine.wait_ge(s, st["vals"][i])
            self.seen[q][id(s)] = st["vals"][i]
        self._waits(q, reads, writes)
        ins = engine.dma_start(out=out, in_=in_)
        ins.then_inc(s, 16)
        st["vals"][i] += 16
        ev = (s, st["vals"][i])
        self._record(ev, reads, writes)
        return ins

    def barrier(self):
        evs = []
        for e in ("pe", "dve", "act", "pool"):
            if self.ecnt[e] > 0 or self.pending_noinc[e]:
                assert not self.pending_noinc[e], f"engine {e} has trailing non-inc instructions"
                evs.append((self.esem[e], self.ecnt[e]))
        for q, st in self.dq.items():
            for s_, v in list(zip(st["sems"], st["vals"])) + st["old"]:
                if v > 0:
                    evs.append((s_, v))
            st["old"] = []
        for e in self.eng:
            engine = self.eng[e]
            for (s_, v) in evs:
                if e in self.esem and s_ is self.esem.get(e):
                    continue
                if self.seen[e].get(id(s_), 0) >= v:
                    continue
                engine.wait_ge(s_, v)
                self.seen[e][id(s_)] = v
        self.lastw = {}
        self.reads = {}

    @contextlib.contextmanager
    def phase(self):
        outer = self.stack
        with contextlib.ExitStack() as st:
            self.stack = st
            try:
                yield
            finally:
                self.barrier()
                self.stack = outer

    def finish(self, out_tokens):
        self._waits("sp", out_tokens, out_tokens)


def _bcast_free(ap, n):
    return ap


class PsumPool:
    def __init__(self, kb, nbanks=8):
        self.kb = kb
        self.tiles = [kb.stack.enter_context(kb.nc.psum_tensor(f"ps{i}", [128, 512], F32)) for i in range(nbanks)]
        self.i = 0

    def next(self):
        t = self.tiles[self.i % len(self.tiles)]
        tok = ("ps", self.i % len(self.tiles))
        self.i += 1
        return t, tok


def psum_bf16(kb, name, cols=1024):
    return kb.stack.enter_context(kb.nc.psum_tensor(name, [128, cols], BF16))


def make_ident(kb, name="ident"):
    nc = kb.nc
    identf = sb(kb, name + "f", [128, 128], F32)
    ident = sb(kb, name, [128, 128], BF16)
    kb.op("pool", lambda: nc.gpsimd.memset(identf[:, :], 1.0), writes=["identf"])
    kb.op("pool", lambda: nc.gpsimd.affine_select(out=identf[:, :], in_=identf[:, :], pattern=[[-1, 128]],
                                                  compare_op=ALU.is_equal, fill=0.0, base=0, channel_multiplier=1),
          reads=["identf"], writes=["identf"])
    kb.op("dve", lambda: nc.vector.tensor_copy(out=ident[:, :], in_=identf[:, :]), reads=["identf"], writes=["ident"])
    return identf, ident


_SBN = [0]


def sb(kb, name, shape, dt):
    _SBN[0] += 1
    return kb.stack.enter_context(kb.nc.sbuf_tensor(f"{name}_{_SBN[0]}", list(shape), dt))


def emit_rsqrt_inplace(kb, ap, tok):
    nc = kb.nc
    kb.op("act", lambda: nc.scalar.activation(out=ap, in_=ap, func=AF.Sqrt), reads=[tok], writes=[tok])
    kb.op("dve", lambda: nc.vector.reciprocal(out=ap, in_=ap), reads=[tok], writes=[tok])


def emit_rmsnorm_fm(kb, pp, hT, h_tok, g_sb, uT, u_tok, ones_bf, ncols, sq_bufs, rstd):
    nc = kb.nc
    groups = []
    c0 = 0
    while c0 < ncols:
        w = min(512, ncols - c0)
        groups.append((c0, w))
        c0 += w
    ps_list = [pp.next() for _ in groups]
    for c in range(NCH):
        sq, sq_tok = sq_bufs[c % len(sq_bufs)]
        kb.op("act", lambda: nc.scalar.activation(out=sq[:, :ncols], in_=hT[:, c, :ncols], func=AF.Square),
              reads=[h_tok], writes=[sq_tok])
        for gi, (c0, w) in enumerate(groups):
            ps, ps_tok = ps_list[gi]
            last = (c == NCH - 1) and (gi == len(groups) - 1)
            kb.op("pe", lambda: nc.tensor.matmul(ps[:, :w], lhsT=ones_bf[:, :], rhs=sq[:, c0:c0 + w],
                                                 start=(c == 0), stop=(c == NCH - 1)),
                  reads=[sq_tok, "ones"], writes=[ps_tok], inc=(gi == len(groups) - 1))
    rstd_t, rstd_tok = rstd
    for gi, (c0, w) in enumerate(groups):
        ps, ps_tok = ps_list[gi]
        kb.op("dve", lambda: nc.vector.tensor_scalar(out=rstd_t[:, c0:c0 + w], in0=ps[:, :w], scalar1=1.0 / D_MODEL,
                                                     scalar2=EPS, op0=ALU.mult, op1=ALU.add),
              reads=[ps_tok], writes=[rstd_tok])
    emit_rsqrt_inplace(kb, rstd_t[:, :ncols], rstd_tok)
    for c in range(NCH):
        kb.op("dve", lambda: nc.vector.scalar_tensor_tensor(out=uT[:, c, :ncols], in0=hT[:, c, :ncols],
                                                            scalar=g_sb[:, c:c + 1], in1=rstd_t[:, :ncols],
                                                            op0=ALU.mult, op1=ALU.mult),
              reads=[h_tok, rstd_tok, "g"], writes=[u_tok])


def build_ffn(Tn, final_norm=False, NT=512):
    assert Tn % NT == 0
    nt = Tn // NT
    W = NT + 2
    nc = bass.Bass("TRN2", target_bir_lowering=False)
    hin = nc.dram_tensor("hin", [D_MODEL, Tn + 2], F32, kind="ExternalInput").ap()
    wup = nc.dram_tensor("wup", [FFN_BLK, 128, NCH, 256], F32, kind="ExternalInput").ap()
    wdn = nc.dram_tensor("wdn", [NCH, 128, FFN_BLK, 128], F32, kind="ExternalInput").ap()
    gnorm = nc.dram_tensor("gnorm", [128, NCH], F32, kind="ExternalInput").ap()
    cw = nc.dram_tensor("cw", [128, FFN_BLK, 4], F32, kind="ExternalInput").ap()
    if final_norm:
        gfin = nc.dram_tensor("gfin", [128, NCH], F32, kind="ExternalInput").ap()
    hout = nc.dram_tensor("hout", [D_MODEL, Tn], F32, kind="ExternalOutput").ap()
    hin_v = hin.rearrange("(c p) t -> p c t", p=128)
    hout_v = hout.rearrange("(c p) t -> p c t", p=128)

    with contextlib.ExitStack() as stack:
        kb = KB(nc, stack)
        pp = PsumPool(kb)
        hT = sb(kb, "hT", [128, NCH, W], F32)
        uT = sb(kb, "uT", [128, NCH, W], BF16)
        actT = sb(kb, "actT", [128, FFN_BLK, NT], BF16)
        ones_bf = sb(kb, "ones", [128, 128], BF16)
        g_sb = sb(kb, "g_sb", [128, NCH], F32)
        cw_sb = sb(kb, "cw_sb", [128, FFN_BLK, 4], F32)
        rstd_t = sb(kb, "rstd", [128, W], F32)
        sq_bufs = [(sb(kb, f"sq{i}", [128, W], BF16), ("sq", i)) for i in range(3)]
        wup_sb = [sb(kb, f"wup{i}", [128, NCH, 256], BF16) for i in range(2)]
        wdn_sb = [sb(kb, f"wdn{i}", [128, FFN_BLK, 128], BF16) for i in range(2)]
        gate_sb = [sb(kb, f"gate{i}", [128, W], F32) for i in range(2)]
        acc_sb = [sb(kb, f"acc{i}", [128, NT], F32) for i in range(2)]
        sil_sb = [sb(kb, f"sil{i}", [128, NT], F32) for i in range(2)]
        ho_sb = [sb(kb, f"ho{i}", [128, NT], F32) for i in range(3)]
        if final_norm:
            gf_sb = sb(kb, "gf_sb", [128, NCH], F32)
            hn = sb(kb, "hn", [128, NCH, NT], F32)

        kb.op("pool", lambda: nc.gpsimd.memset(ones_bf[:, :], 1.0), writes=["ones"])
        kb.dma("sp", g_sb[:, :], gnorm[:, :], writes=["g"])
        kb.dma("sp", cw_sb[:, :, :], cw[:, :, :], writes=["cw"])
        if final_norm:
            kb.dma("sp", gf_sb[:, :], gfin[:, :], writes=["gf"])

        out_toks = []
        wi = 0
        di = 0
        for it in range(nt):
            t0 = it * NT
            for c in range(NCH):
                kb.dma("sp", hT[:, c, :], hin_v[:, c, t0:t0 + W], writes=["hT"])
            emit_rmsnorm_fm(kb, pp, hT, "hT", g_sb, uT, "uT", ones_bf, W, sq_bufs, (rstd_t, "rstd"))
            for j in range(FFN_BLK):
                wb = wup_sb[wi % 2]
                wtok = ("wup", wi % 2)
                wi += 1
                kb.dma("pool", wb[:, :, :], wup[j], writes=[wtok])
                ps_g, tg = pp.next()
                ps_x, tx = pp.next()
                ps_v, tv = pp.next()
                for kc in range(NCH):
                    kb.op("pe", lambda: nc.tensor.matmul(ps_g[:, :NT], lhsT=wb[:, kc, 0:128], rhs=uT[:, kc, 0:NT],
                                                         start=(kc == 0), stop=(kc == NCH - 1)),
                          reads=[wtok, "uT", "ones", "g"], writes=[tg], inc=(kc == NCH - 1))
                for kc in range(NCH):
                    kb.op("pe", lambda: nc.tensor.matmul(ps_x[:, :2], lhsT=wb[:, kc, 0:128], rhs=uT[:, kc, NT:NT + 2],
                                                         start=(kc == 0), stop=(kc == NCH - 1)),
                          reads=[wtok, "uT"], writes=[tx], inc=(kc == NCH - 1))
                for kc in range(NCH):
                    kb.op("pe", lambda: nc.tensor.matmul(ps_v[:, :NT], lhsT=wb[:, kc, 128:256], rhs=uT[:, kc, 1:NT + 1],
                                                         start=(kc == 0), stop=(kc == NCH - 1)),
                          reads=[wtok, "uT"], writes=[tv], inc=(kc == NCH - 1))
                gs = gate_sb[j % 2]
                gtok = ("gate", j % 2)
                kb.op("act", lambda: nc.scalar.copy(out=gs[:, 0:NT], in_=ps_g[:, :NT]), reads=[tg], writes=[gtok])
                kb.op("act", lambda: nc.scalar.copy(out=gs[:, NT:NT + 2], in_=ps_x[:, :2]), reads=[tx], writes=[gtok])
                ac = acc_sb[j % 2]
                atok = ("acc", j % 2)
                kb.op("dve", lambda: nc.vector.tensor_scalar(out=ac[:, :], in0=gs[:, 0:NT], scalar1=cw_sb[:, j, 0:1],
                                                             scalar2=cw_sb[:, j, 3:4], op0=ALU.mult, op1=ALU.add),
                      reads=[gtok, "cw"], writes=[atok])
                kb.op("dve", lambda: nc.vector.scalar_tensor_tensor(out=ac[:, :], in0=gs[:, 1:NT + 1],
                                                                    scalar=cw_sb[:, j, 1:2], in1=ac[:, :],
                                                                    op0=ALU.mult, op1=ALU.add),
                      reads=[gtok, atok], writes=[atok])
                kb.op("dve", lambda: nc.vector.scalar_tensor_tensor(out=ac[:, :], in0=gs[:, 2:NT + 2],
                                                                    scalar=cw_sb[:, j, 2:3], in1=ac[:, :],
                                                                    op0=ALU.mult, op1=ALU.add),
                      reads=[gtok, atok], writes=[atok])
                sl = sil_sb[j % 2]
                stok = ("sil", j % 2)
                kb.op("act", lambda: nc.scalar.activation(out=sl[:, :], in_=ac[:, :], func=AF.Silu),
                      reads=[atok], writes=[stok])
                kb.op("dve", lambda: nc.vector.tensor_tensor(out=actT[:, j, :], in0=sl[:, :], in1=ps_v[:, :NT], op=ALU.mult),
                      reads=[stok, tv], writes=[("actT", j)])
            for mb in range(NCH):
                wd = wdn_sb[di % 2]
                dtok = ("wdn", di % 2)
                di += 1
                kb.dma("pool", wd[:, :, :], wdn[mb], writes=[dtok])
                ps_o, to = pp.next()
                for kbk in range(FFN_BLK):
                    kb.op("pe", lambda: nc.tensor.matmul(ps_o[:, :NT], lhsT=wd[:, kbk, :], rhs=actT[:, kbk, :],
                                                         start=(kbk == 0), stop=(kbk == FFN_BLK - 1)),
                          reads=[dtok, ("actT", kbk)], writes=[to], inc=(kbk == FFN_BLK - 1))
                if not final_norm:
                    ho = ho_sb[mb % 3]
                    htok = ("ho", mb % 3)
                    kb.op("dve", lambda: nc.vector.tensor_tensor(out=ho[:, :], in0=ps_o[:, :NT], in1=hT[:, mb, 1:NT + 1], op=ALU.add),
                          reads=[to, "hT"], writes=[htok])
                    otok = ("hout", it, mb)
                    kb.dma("sp", hout_v[:, mb, t0:t0 + NT], ho[:, :], reads=[htok], writes=[otok])
                    out_toks.append(otok)
                else:
                    kb.op("dve", lambda: nc.vector.tensor_tensor(out=hn[:, mb, :], in0=ps_o[:, :NT], in1=hT[:, mb, 1:NT + 1], op=ALU.add),
                          reads=[to, "hT"], writes=["hn"])
            if final_norm:
                groups = [(0, NT)]
                ps, ps_tok = pp.next()
                for c in range(NCH):
                    sq, sq_tok = sq_bufs[c % len(sq_bufs)]
                    kb.op("act", lambda: nc.scalar.activation(out=sq[:, :NT], in_=hn[:, c, :], func=AF.Square),
                          reads=["hn"], writes=[sq_tok])
                    kb.op("pe", lambda: nc.tensor.matmul(ps[:, :NT], lhsT=ones_bf[:, :], rhs=sq[:, :NT],
                                                         start=(c == 0), stop=(c == NCH - 1)),
                          reads=[sq_tok, "ones"], writes=[ps_tok], inc=True)
                kb.op("dve", lambda: nc.vector.tensor_scalar(out=rstd_t[:, :NT], in0=ps[:, :NT], scalar1=1.0 / D_MODEL,
                                                             scalar2=EPS, op0=ALU.mult, op1=ALU.add),
                      reads=[ps_tok], writes=["rstd"])
                emit_rsqrt_inplace(kb, rstd_t[:, :NT], "rstd")
                for c in range(NCH):
                    ho = ho_sb[c % 3]
                    htok = ("ho", c % 3)
                    kb.op("dve", lambda: nc.vector.scalar_tensor_tensor(out=ho[:, :], in0=hn[:, c, :],
                                                                        scalar=gf_sb[:, c:c + 1], in1=rstd_t[:, :NT],
                                                                        op0=ALU.mult, op1=ALU.mult),
                          reads=["hn", "rstd", "gf"], writes=[htok])
                    otok = ("hout", it, c)
                    kb.dma("sp", hout_v[:, c, t0:t0 + NT], ho[:, :], reads=[htok], writes=[otok])
                    out_toks.append(otok)
        kb.finish(out_toks)
    return nc


def prep_ffn_weights(w_up, conv_w, conv_b, w_down, g):
    wg = w_up[:, :FFN_DIM].reshape(NCH, 128, FFN_BLK, 128)
    wv = w_up[:, FFN_DIM:].reshape(NCH, 128, FFN_BLK, 128)
    wup = np.concatenate([wg, wv], axis=-1).transpose(2, 1, 0, 3)
    wdn = w_down.reshape(FFN_BLK, 128, NCH, 128).transpose(2, 1, 0, 3)
    cw = np.concatenate([conv_w, conv_b[None, :]], axis=0)
    cw = cw.reshape(4, FFN_BLK, 128).transpose(2, 1, 0)
    return dict(wup=np.ascontiguousarray(wup, dtype=np.float32), wdn=np.ascontiguousarray(wdn, dtype=np.float32),
                cw=np.ascontiguousarray(cw, dtype=np.float32), gnorm=np.ascontiguousarray(g.reshape(NCH, 128).T, dtype=np.float32))


def emit_outproj_residual(kb, pp, actT, act_tok_fn, KBK, w_dram, wbufs, wstate, hT, h_tok, h_col0, NT,
                          dst_v, dst_col0, ho_bufs, out_toks, tag):
    nc = kb.nc
    for mb in range(NCH):
        wd = wbufs[wstate[0] % len(wbufs)]
        dtok = (tag + "w", wstate[0] % len(wbufs))
        wstate[0] += 1
        kb.dma("pool", wd[:, :KBK, :], w_dram[mb], writes=[dtok])
        ps_o, to = pp.next()
        for k in range(KBK):
            kb.op("pe", lambda: nc.tensor.matmul(ps_o[:, :NT], lhsT=wd[:, k, :], rhs=actT[:, k, :NT],
                                                 start=(k == 0), stop=(k == KBK - 1)),
                  reads=[dtok, act_tok_fn(k)], writes=[to], inc=(k == KBK - 1))
        ho = ho_bufs[mb % len(ho_bufs)]
        htok = (tag + "ho", mb % len(ho_bufs))
        kb.op("dve", lambda: nc.vector.tensor_tensor(out=ho[:, :NT], in0=ps_o[:, :NT], in1=hT[:, mb, h_col0:h_col0 + NT], op=ALU.add),
              reads=[to, h_tok], writes=[htok])
        otok = (tag + "out", dst_col0, mb)
        kb.dma("sp", dst_v[:, mb, dst_col0:dst_col0 + NT], ho[:, :NT], reads=[htok], writes=[otok])
        out_toks.append(otok)


XA_H = 4
XA_HD = 512
N_MEM = 256


class Ctx:
    pass


def emit_mem_norm(kb, pp, cx, memT_dram, gmem_dram):
    cx.memn = sb(kb, "memn", [128, NCH, N_MEM], BF16)
    with kb.phase():
        tile_ctx(kb, cx)
        memf = cx.hT
        gm = cx.g_sb
        kb.dma("sp", gm[:, :], gmem_dram[:, :], writes=["g"])
        kb.dma("sp", memf[:, :, :N_MEM], memT_dram.rearrange("(c p) m -> p c m", p=128), writes=["hT"])
        emit_rmsnorm_fm(kb, pp, memf, "hT", gm, cx.memn, "memn", cx.ones_bf, N_MEM, cx.sq_bufs, (cx.rstd_t, "rstd"))


def emit_xattn(kb, pp, cx, T, src_v, src_pad, dst_v, dst_pad, w, out_toks, NT=512):
    with kb.phase():
        tile_ctx(kb, cx)
        ctx_add_xattn(kb, cx)
        _emit_xattn_body(kb, pp, cx, T, src_v, src_pad, dst_v, dst_pad, w, out_toks, NT)


def _emit_xattn_body(kb, pp, cx, T, src_v, src_pad, dst_v, dst_pad, w, out_toks, NT):
    nc = kb.nc
    nt = T // NT
    KT = cx.xa_KT
    V = cx.xa_V
    g_sb = cx.g_sb
    kb.dma("sp", g_sb[:, :], w["g"][:, :], writes=["g"])
    for blk in range(NCH):
        wb = cx.wA[cx.wAi[0] % 2]; wtok = ("wA", cx.wAi[0] % 2); cx.wAi[0] += 1
        kb.dma("pool", wb[:, :, 0:128], w["wk"][blk], writes=[wtok])
        ps, pt = pp.next()
        for kc in range(NCH):
            kb.op("pe", lambda: nc.tensor.matmul(ps[:, :N_MEM], lhsT=wb[:, kc, 0:128], rhs=cx.memn[:, kc, :],
                                                 start=(kc == 0), stop=(kc == NCH - 1)),
                  reads=[wtok, "memn"], writes=[pt], inc=(kc == NCH - 1))
        kb.op("act", lambda: nc.scalar.copy(out=KT[:, blk, :], in_=ps[:, :N_MEM]), reads=[pt], writes=["KT"])
    for cg in range(4):
        wb = cx.wB[cx.wBi[0] % 2]; wtok = ("wB", cx.wBi[0] % 2); cx.wBi[0] += 1
        kb.dma("pool", wb[:, :, :], w["wv"][cg], writes=[wtok])
        for mblk in range(2):
            ps, pt = pp.next()
            for kc in range(NCH):
                kb.op("pe", lambda: nc.tensor.matmul(ps[:, :512], lhsT=cx.memn[:, kc, mblk * 128:(mblk + 1) * 128], rhs=wb[:, kc, :],
                                                     start=(kc == 0), stop=(kc == NCH - 1)),
                      reads=[wtok, "memn"], writes=[pt], inc=(kc == NCH - 1))
            kb.op("act", lambda: nc.scalar.copy(out=V[:, mblk, cg * 512:(cg + 1) * 512], in_=ps[:, :512]), reads=[pt], writes=["V"])
    hT, uT, qT, oT, PT = cx.hT, cx.uT, cx.xa_qT, cx.xa_oT, cx.xa_PT
    scale = float(XA_HD) ** -0.5
    for it in range(nt):
        t0 = it * NT
        for c in range(NCH):
            kb.dma("sp", hT[:, c, :NT], src_v[:, c, src_pad + t0:src_pad + t0 + NT], writes=["hT"])
        emit_rmsnorm_fm(kb, pp, hT, "hT", g_sb, uT, "uT", cx.ones_bf, NT, cx.sq_bufs, (cx.rstd_t, "rstd"))
        for blk in range(NCH):
            wb = cx.wA[cx.wAi[0] % 2]; wtok = ("wA", cx.wAi[0] % 2); cx.wAi[0] += 1
            kb.dma("pool", wb[:, :, 0:128], w["wq"][blk], writes=[wtok])
            ps, pt = pp.next()
            for kc in range(NCH):
                kb.op("pe", lambda: nc.tensor.matmul(ps[:, :NT], lhsT=wb[:, kc, 0:128], rhs=uT[:, kc, :NT],
                                                     start=(kc == 0), stop=(kc == NCH - 1)),
                      reads=[wtok, "uT"], writes=[pt], inc=(kc == NCH - 1))
            kb.op("act", lambda: nc.scalar.activation(out=qT[:, blk, :NT], in_=ps[:, :NT], func=AF.Copy, scale=scale),
                  reads=[pt], writes=[("qT", blk)])
        for hd in range(XA_H):
            for sub in range(NT // 128):
                ps, pt = pp.next()
                for j in range(4):
                    kb.op("pe", lambda: nc.tensor.matmul(ps[:, :N_MEM], lhsT=qT[:, hd * 4 + j, sub * 128:(sub + 1) * 128],
                                                         rhs=KT[:, hd * 4 + j, :], start=(j == 0), stop=(j == 3)),
                          reads=[("qT", hd * 4 + j), "KT"], writes=[pt], inc=(j == 3))
                i2 = cx.xai[0] % 2; cx.xai[0] += 1
                mx = cx.xa_mx[i2]; mtok = ("xamx", i2)
                kb.op("dve", lambda: nc.vector.reduce_max(out=mx[:, 0:1], in_=ps[:, :N_MEM], axis=AX.X), reads=[pt], writes=[mtok])
                kb.op("dve", lambda: nc.vector.tensor_scalar(out=mx[:, 1:2], in0=mx[:, 0:1], scalar1=-1.0, scalar2=None, op0=ALU.mult),
                      reads=[mtok], writes=[mtok])
                P = cx.xa_P[i2]; ptok = ("xaP", i2)
                kb.op("act", lambda: nc.scalar.activation(out=P[:, :], in_=ps[:, :N_MEM], func=AF.Exp, bias=mx[:, 1:2], scale=1.0,
                                                          accum_out=mx[:, 2:3]),
                      reads=[pt, mtok], writes=[ptok, mtok])
                kb.op("dve", lambda: nc.vector.reciprocal(out=mx[:, 3:4], in_=mx[:, 2:3]), reads=[mtok], writes=[mtok])
                Pn = cx.xa_Pn[i2]; pntok = ("xaPn", i2)
                kb.op("dve", lambda: nc.vector.tensor_scalar(out=Pn[:, :], in0=P[:, :], scalar1=mx[:, 3:4], scalar2=None, op0=ALU.mult),
                      reads=[ptok, mtok], writes=[pntok])
                pb = cx.psb[cx.psbi[0] % 2]; pbtok = ("psb", cx.psbi[0] % 2); cx.psbi[0] += 1
                for mblk in range(2):
                    kb.op("pe", lambda: nc.tensor.transpose(out=pb[:, mblk * 128:(mblk + 1) * 128], in_=Pn[:, mblk * 128:(mblk + 1) * 128],
                                                            identity=cx.ident[:, :]),
                          reads=[pntok, "ident"], writes=[pbtok], inc=(mblk == 1))
                kb.op("act", lambda: nc.scalar.copy(out=PT[:, :, hd, sub * 128:(sub + 1) * 128],
                                                    in_=pb[:, 0:256].rearrange("p (b t) -> p b t", b=2)),
                      reads=[pbtok], writes=[("PT", hd)])
            for dblk in range(4):
                ps, pt = pp.next()
                for mblk in range(2):
                    kb.op("pe", lambda: nc.tensor.matmul(ps[:, :NT], lhsT=V[:, mblk, hd * 512 + dblk * 128:hd * 512 + (dblk + 1) * 128],
                                                         rhs=PT[:, mblk, hd, :NT], start=(mblk == 0), stop=(mblk == 1)),
                          reads=["V", ("PT", hd)], writes=[pt], inc=(mblk == 1))
                kb.op("act", lambda: nc.scalar.copy(out=oT[:, hd * 4 + dblk, :NT], in_=ps[:, :NT]), reads=[pt], writes=[("oT", hd * 4 + dblk)])
        emit_outproj_residual(kb, pp, oT, lambda k: ("oT", k), NCH, w["wo"], cx.wO, cx.wOi, hT, "hT", 0, NT,
                              dst_v, dst_pad + t0, cx.ho_bufs, out_toks, "xa")


def make_ctx(kb, NT=512, halo=3):
    nc = kb.nc
    cx = Ctx()
    cx.NT, cx.W = NT, NT + halo
    cx.ones_bf = sb(kb, "ones", [128, 128], BF16)
    cx.onesf = sb(kb, "onesf", [128, 128], F32)
    cx.identf, cx.ident = make_ident(kb)
    cx.psb = [psum_bf16(kb, f"psb{i}") for i in range(2)]; cx.psbi = [0]
    cx.wAi = [0]; cx.wOi = [0]
    kb.op("pool", lambda: nc.gpsimd.memset(cx.ones_bf[:, :], 1.0), writes=["ones"])
    kb.op("pool", lambda: nc.gpsimd.memset(cx.onesf[:, :], 1.0), writes=["onesf"])
    return cx


def tile_ctx(kb, cx, need_wO=True):
    NT, W = cx.NT, cx.W
    cx.hT = sb(kb, "hT", [128, NCH, W], F32)
    cx.uT = sb(kb, "uT", [128, NCH, W], BF16)
    cx.g_sb = sb(kb, "g_sb", [128, NCH], F32)
    cx.rstd_t = sb(kb, "rstd", [128, W], F32)
    cx.sq_bufs = [(sb(kb, f"sq{i}", [128, W], BF16), ("sq", i)) for i in range(3)]
    cx.ho_bufs = [sb(kb, f"ho{i}", [128, NT], F32) for i in range(3)]
    cx.wA = [sb(kb, f"wA{i}", [128, NCH, 256], BF16) for i in range(2)]
    if need_wO:
        cx.wO = [sb(kb, f"wO{i}", [128, FFN_BLK, 128], BF16) for i in range(2)]


def ctx_add_xattn(kb, cx):
    NT = cx.NT
    cx.wB = [sb(kb, f"wB{i}", [128, NCH, 512], BF16) for i in range(2)]; cx.wBi = [0]
    cx.xa_KT = sb(kb, "xaKT", [128, NCH, N_MEM], BF16)
    cx.xa_V = sb(kb, "xaV", [128, 2, D_MODEL], BF16)
    cx.xa_qT = sb(kb, "xaqT", [128, NCH, NT], BF16)
    cx.xa_oT = sb(kb, "xaoT", [128, NCH, NT], BF16)
    cx.xa_PT = sb(kb, "xaPT", [128, 2, XA_H, NT], BF16)
    cx.xa_mx = [sb(kb, f"xamx{i}", [128, 4], F32) for i in range(2)]
    cx.xa_P = [sb(kb, f"xaP{i}", [128, N_MEM], F32) for i in range(2)]
    cx.xa_Pn = [sb(kb, f"xaPn{i}", [128, N_MEM], BF16) for i in range(2)]
    cx.xai = [0]


def prep_xattn_weights(wq, wkv, wo, g):
    def fm_blocks(wm):
        n = wm.shape[1] // 128
        return np.ascontiguousarray(wm.reshape(NCH, 128, n, 128).transpose(2, 1, 0, 3), dtype=np.float32)
    wk = wkv[:, :D_MODEL]
    wv = wkv[:, D_MODEL:]
    wvr = np.ascontiguousarray(wv.reshape(NCH, 128, 4, 512).transpose(2, 1, 0, 3), dtype=np.float32)
    return dict(wq=fm_blocks(wq), wk=fm_blocks(wk), wv=wvr, wo=fm_blocks(wo),
                g=np.ascontiguousarray(g.reshape(NCH, 128).T, dtype=np.float32))


def build_xattn_test(T, NT=512):
    nc = bass.Bass("TRN2", target_bir_lowering=False)
    hin = nc.dram_tensor("hin", [D_MODEL, T], F32, kind="ExternalInput").ap()
    memT = nc.dram_tensor("memT", [D_MODEL, N_MEM], F32, kind="ExternalInput").ap()
    gmem = nc.dram_tensor("gmem", [128, NCH], F32, kind="ExternalInput").ap()
    w = dict(wq=nc.dram_tensor("wq", [NCH, 128, NCH, 128], F32, kind="ExternalInput").ap(),
             wk=nc.dram_tensor("wk", [NCH, 128, NCH, 128], F32, kind="ExternalInput").ap(),
             wv=nc.dram_tensor("wv", [4, 128, NCH, 512], F32, kind="ExternalInput").ap(),
             wo=nc.dram_tensor("wo", [NCH, 128, NCH, 128], F32, kind="ExternalInput").ap(),
             g=nc.dram_tensor("g", [128, NCH], F32, kind="ExternalInput").ap())
    hout = nc.dram_tensor("hout", [D_MODEL, T], F32, kind="ExternalOutput").ap()
    with contextlib.ExitStack() as stack:
        kb = KB(nc, stack)
        pp = PsumPool(kb, nbanks=6)
        cx = make_ctx(kb, NT)
        emit_mem_norm(kb, pp, cx, memT, gmem)
        outs = []
        emit_xattn(kb, pp, cx, T, hin.rearrange("(c p) t -> p c t", p=128), 0, hout.rearrange("(c p) t -> p c t", p=128), 0, w, outs, NT)
        kb.finish(outs)
    return nc


def emit_proj_block(kb, pp, wb, wtok, wcol0, uT, u_tok, col_ranges):
    nc = kb.nc
    outs = []
    for (c0, w) in col_ranges:
        ps, pt = pp.next()
        for kc in range(NCH):
            kb.op("pe", lambda: nc.tensor.matmul(ps[:, :w], lhsT=wb[:, kc, wcol0:wcol0 + 128], rhs=uT[:, kc, c0:c0 + w],
                                                 start=(kc == 0), stop=(kc == NCH - 1)),
                  reads=[wtok, u_tok], writes=[pt], inc=(kc == NCH - 1))
        outs.append((ps, pt))
    return outs


def emit_softplus_small(kb, out, x, tmp, tok, neg_in=False):
    nc = kb.nc
    sgn = -1.0 if neg_in else 1.0
    kb.op("act", lambda: nc.scalar.activation(out=tmp, in_=x, func=AF.Abs), reads=[tok], writes=[tok])
    kb.op("act", lambda: nc.scalar.activation(out=tmp, in_=tmp, func=AF.Exp, scale=-1.0), reads=[tok], writes=[tok])
    kb.op("act", lambda: nc.scalar.activation(out=tmp, in_=tmp, func=AF.Ln, bias=1.0, scale=1.0), reads=[tok], writes=[tok])
    kb.op("dve", lambda: nc.vector.tensor_scalar(out=out, in0=x, scalar1=sgn, scalar2=0.0, op0=ALU.mult, op1=ALU.max), reads=[tok], writes=[tok])
    kb.op("dve", lambda: nc.vector.tensor_tensor(out=out, in0=out, in1=tmp, op=ALU.add), reads=[tok], writes=[tok])


RG_BLOCKS = 8


def emit_rglru(kb, pp, cx, T, src_v, src_pad, dst_v, dst_pad, w, scr, out_toks, NT=512):
    nc = kb.nc
    nt = T // NT
    W = NT + 3
    ggv = scr["ggT"].rearrange("(c p) t -> p c t", p=128)
    hfv = scr["hfT"].rearrange("(c p) t -> p c t", p=128)
    abv = scr["abT"].rearrange("(c p) t -> p c t", p=128)
    bxv = scr["bxbT"].rearrange("(c p) t -> p c t", p=128)
    with kb.phase():
        tile_ctx(kb, cx, need_wO=False)
        hT, uT, g_sb = cx.hT, cx.uT, cx.g_sb
        gw = sb(kb, "rg_gw", [128, 2, RG_BLOCKS, 2, 512], BF16)
        cw = sb(kb, "rg_cw", [128, NCH, 5], F32)
        gb = sb(kb, "rg_gb", [128, 2, NCH, 2], F32)
        cd = sb(kb, "rg_cd", [128, 2, NCH], F32)
        cdt = sb(kb, "rg_cdt", [128, 2, NCH], F32)
        xcf = sb(kb, "rg_xcf", [128, NCH, NT], F32)
        xcb = sb(kb, "rg_xcb", [128, NCH, NT], BF16)
        G = [sb(kb, f"rg_G{i}", [128, W], F32) for i in range(2)]
        gg = [sb(kb, f"rg_gg{i}", [128, NT], BF16) for i in range(2)]
        t1 = [sb(kb, f"rg_t1{i}", [128, NT], F32) for i in range(2)]
        t2 = [sb(kb, f"rg_t2{i}", [128, NT], F32) for i in range(2)]
        ta = [sb(kb, f"rg_ta{i}", [128, NT], F32) for i in range(2)]
        tb = [sb(kb, f"rg_tb{i}", [128, NT], F32) for i in range(2)]
        th = [sb(kb, f"rg_th{i}", [128, NT], F32) for i in range(2)]
        carry = sb(kb, "rg_carry", [128, NCH], F32)
        kb.dma("sp", g_sb[:, :], w["g"][:, :], writes=["g"])
        kb.dma("pool", gw[:, :, :, :, :], w["gw"], writes=["gw"])
        kb.dma("sp", cw[:, :, :], w["cw"], writes=["cw"])
        kb.dma("sp", gb[:, :, :, :], w["gb"], writes=["gb"])
        kb.dma("sp", cd[:, :, :], w["lam"], writes=["cd"])
        cdf = cd[:, :, :].rearrange("p a b -> p (a b)")
        cdtf = cdt[:, :, :].rearrange("p a b -> p (a b)")
        emit_softplus_small(kb, cdf, cdf, cdtf, "cd", neg_in=True)
        kb.op("dve", lambda: nc.vector.tensor_scalar(out=cdf, in0=cdf, scalar1=-8.0, scalar2=None, op0=ALU.mult), reads=["cd"], writes=["cd"])
        kb.op("dve", lambda: nc.vector.memset(carry[:, :], 0.0), writes=["carry"])
        k = 0
        for it in range(nt):
            t0 = it * NT
            for c in range(NCH):
                kb.dma("sp", hT[:, c, :W], src_v[:, c, src_pad + t0 - 2:src_pad + t0 - 2 + W], writes=["hT"])
            emit_rmsnorm_fm(kb, pp, hT, "hT", g_sb, uT, "uT", cx.ones_bf, W, cx.sq_bufs, (cx.rstd_t, "rstd"))
            for blk in range(32):
                wb = cx.wA[cx.wAi[0] % 2]; wtok = ("wA", cx.wAi[0] % 2); cx.wAi[0] += 1
                kb.dma("pool", wb[:, :, 0:128], w["win"][blk], writes=[wtok])
                i2 = blk % 2
                if blk < 16:
                    (ps, pt), = emit_proj_block(kb, pp, wb, wtok, 0, uT, "uT", [(2, NT)])
                    a1, a2 = t1[i2], t2[i2]
                    kb.op("act", lambda: nc.scalar.copy(out=a1[:, :], in_=ps[:, :NT]), reads=[pt], writes=[("t1", i2)])
                    kb.op("dve", lambda: nc.vector.tensor_tensor(out=a2[:, :], in0=a1[:, :], in1=a1[:, :], op=ALU.mult), reads=[("t1", i2)], writes=[("t2", i2)])
                    kb.op("dve", lambda: nc.vector.tensor_scalar(out=a2[:, :], in0=a2[:, :], scalar1=0.044715, scalar2=1.0, op0=ALU.mult, op1=ALU.add),
                          reads=[("t2", i2)], writes=[("t2", i2)])
                    kb.op("dve", lambda: nc.vector.tensor_tensor(out=a2[:, :], in0=a2[:, :], in1=a1[:, :], op=ALU.mult), reads=[("t2", i2), ("t1", i2)], writes=[("t2", i2)])
                    kb.op("act", lambda: nc.scalar.activation(out=a2[:, :], in_=a2[:, :], func=AF.Sigmoid, scale=1.5957691216), reads=[("t2", i2)], writes=[("t2", i2)])
                    kb.op("dve", lambda: nc.vector.tensor_tensor(out=gg[i2][:, :], in0=a2[:, :], in1=a1[:, :], op=ALU.mult), reads=[("t2", i2), ("t1", i2)], writes=[("gg", i2)])
                    kb.dma("sp", ggv[:, blk, t0:t0 + NT], gg[i2][:, :], reads=[("gg", i2)], writes=[("ggd", it, blk)])
                else:
                    c = blk - 16
                    (psm, ptm), (psx, ptx) = emit_proj_block(kb, pp, wb, wtok, 0, uT, "uT", [(0, NT), (NT, 3)])
                    Gt = G[i2]; gtok = ("G", i2)
                    kb.op("act", lambda: nc.scalar.copy(out=Gt[:, 0:NT], in_=psm[:, :NT]), reads=[ptm], writes=[gtok])
                    kb.op("act", lambda: nc.scalar.copy(out=Gt[:, NT:NT + 3], in_=psx[:, :3]), reads=[ptx], writes=[gtok])
                    kb.op("dve", lambda: nc.vector.tensor_scalar(out=xcf[:, c, :], in0=Gt[:, 0:NT], scalar1=cw[:, c, 0:1], scalar2=cw[:, c, 4:5],
                                                                 op0=ALU.mult, op1=ALU.add), reads=[gtok, "cw"], writes=[("xcf", c)])
                    for tap in range(1, 4):
                        kb.op("dve", lambda: nc.vector.scalar_tensor_tensor(out=xcf[:, c, :], in0=Gt[:, tap:tap + NT], scalar=cw[:, c, tap:tap + 1],
                                                                            in1=xcf[:, c, :], op0=ALU.mult, op1=ALU.add),
                              reads=[gtok, ("xcf", c)], writes=[("xcf", c)])
                    kb.op("act", lambda: nc.scalar.copy(out=xcb[:, c, :], in_=xcf[:, c, :]), reads=[("xcf", c)], writes=[("xcb", c)])
            for d in range(2):
                for n in range(RG_BLOCKS):
                    for jb in range(2):
                        c = 2 * n + jb
                        i2 = k % 2; k += 1
                        psr, ptr = pp.next()
                        psi, pti = pp.next()
                        for (ps_, pt_, col) in ((psr, ptr, jb * 128), (psi, pti, 256 + jb * 128)):
                            for kc in range(2):
                                kb.op("pe", lambda: nc.tensor.matmul(ps_[:, :NT], lhsT=gw[:, d, n, kc, col:col + 128], rhs=xcb[:, 2 * n + kc, :],
                                                                     start=(kc == 0), stop=(kc == 1)),
                                      reads=["gw", ("xcb", 2 * n + kc)], writes=[pt_], inc=(kc == 1))
                        A, B_, R1, R2 = ta[i2], tb[i2], t1[i2], t2[i2]
                        kb.op("act", lambda: nc.scalar.activation(out=R1[:, :], in_=psr[:, :NT], func=AF.Sigmoid, bias=gb[:, d, c, 0:1], scale=1.0),
                              reads=[ptr, "gb"], writes=[("t1", i2)])
                        kb.op("act", lambda: nc.scalar.activation(out=A[:, :], in_=R1[:, :], func=AF.Exp, scale=cd[:, d, c:c + 1]),
                              reads=[("t1", i2), "cd"], writes=[("ta", i2)])
                        kb.op("act", lambda: nc.scalar.activation(out=R2[:, :], in_=psi[:, :NT], func=AF.Sigmoid, bias=gb[:, d, c, 1:2], scale=1.0),
                              reads=[pti, "gb"], writes=[("t2", i2)])
                        kb.op("dve", lambda: nc.vector.tensor_tensor(out=R1[:, :], in0=A[:, :], in1=A[:, :], op=ALU.mult), reads=[("ta", i2), ("t1", i2)], writes=[("t1", i2)])
                        kb.op("dve", lambda: nc.vector.tensor_scalar(out=R1[:, :], in0=R1[:, :], scalar1=-1.0, scalar2=1.0, op0=ALU.mult, op1=ALU.add),
                              reads=[("t1", i2)], writes=[("t1", i2)])
                        kb.op("act", lambda: nc.scalar.activation(out=R1[:, :], in_=R1[:, :], func=AF.Sqrt), reads=[("t1", i2)], writes=[("t1", i2)])
                        kb.op("dve", lambda: nc.vector.tensor_tensor(out=R2[:, :], in0=R2[:, :], in1=xcf[:, c, :], op=ALU.mult), reads=[("t2", i2), ("xcf", c)], writes=[("t2", i2)])
                        kb.op("dve", lambda: nc.vector.tensor_tensor(out=B_[:, :], in0=R2[:, :], in1=R1[:, :], op=ALU.mult), reads=[("t2", i2), ("t1", i2)], writes=[("tb", i2)])
                        if d == 0:
                            H = th[i2]
                            kb.op("dve", lambda: nc.vector.tensor_tensor_scan(out=H[:, :], data0=A[:, :], data1=B_[:, :], initial=carry[:, c:c + 1],
                                                                              op0=ALU.mult, op1=ALU.add),
                                  reads=[("ta", i2), ("tb", i2), "carry"], writes=[("th", i2)])
                            kb.op("dve", lambda: nc.vector.tensor_copy(out=carry[:, c:c + 1], in_=H[:, NT - 1:NT]), reads=[("th", i2)], writes=["carry"])
                            kb.dma("sp", hfv[:, c, t0:t0 + NT], H[:, :], reads=[("th", i2)], writes=[("hfd", it, c)])
                        else:
                            kb.dma("sp", abv[:, c, t0:t0 + NT], A[:, :], reads=[("ta", i2)], writes=[("abd", it, c)])
                            kb.dma("sp", bxv[:, c, t0:t0 + NT], B_[:, :], reads=[("tb", i2)], writes=[("bxd", it, c)])
    with kb.phase():
        tile_ctx(kb, cx)
        hT = cx.hT
        yT = sb(kb, "rg_yT", [128, NCH, NT], BF16)
        A = [sb(kb, f"rg2_a{i}", [128, NT], F32) for i in range(2)]
        Bx = [sb(kb, f"rg2_b{i}", [128, NT], F32) for i in range(2)]
        Hf = [sb(kb, f"rg2_h{i}", [128, NT], F32) for i in range(2)]
        Gg = [sb(kb, f"rg2_g{i}", [128, NT], BF16) for i in range(2)]
        Hb = [sb(kb, f"rg2_hb{i}", [128, NT], F32) for i in range(2)]
        carry = sb(kb, "rg2_carry", [128, NCH], F32)
        kb.op("dve", lambda: nc.vector.memset(carry[:, :], 0.0), writes=["carry2"])
        k = 0
        for it in reversed(range(nt)):
            t0 = it * NT
            for c in range(NCH):
                kb.dma("sp", hT[:, c, :NT], src_v[:, c, src_pad + t0:src_pad + t0 + NT], writes=["hT"])
            for c in range(NCH):
                i2 = k % 2; k += 1
                kb.dma("sp", A[i2][:, :], abv[:, c, t0:t0 + NT], reads=[("abd", it, c)], writes=[("A2", i2)])
                kb.dma("sp", Bx[i2][:, :], bxv[:, c, t0:t0 + NT], reads=[("bxd", it, c)], writes=[("B2", i2)])
                kb.dma("sp", Hf[i2][:, :], hfv[:, c, t0:t0 + NT], reads=[("hfd", it, c)], writes=[("H2", i2)])
                kb.dma("sp", Gg[i2][:, :], ggv[:, c, t0:t0 + NT], reads=[("ggd", it, c)], writes=[("G2", i2)])
                kb.op("dve", lambda: nc.vector.tensor_tensor_scan(out=Hb[i2][:, ::-1], data0=A[i2][:, ::-1], data1=Bx[i2][:, ::-1],
                                                                  initial=carry[:, c:c + 1], op0=ALU.mult, op1=ALU.add),
                      reads=[("A2", i2), ("B2", i2), "carry2"], writes=[("Hb2", i2)])
                kb.op("dve", lambda: nc.vector.tensor_copy(out=carry[:, c:c + 1], in_=Hb[i2][:, 0:1]), reads=[("Hb2", i2)], writes=["carry2"])
                kb.op("dve", lambda: nc.vector.tensor_tensor(out=Hb[i2][:, :], in0=Hb[i2][:, :], in1=Hf[i2][:, :], op=ALU.add),
                      reads=[("Hb2", i2), ("H2", i2)], writes=[("Hb2", i2)])
                kb.op("dve", lambda: nc.vector.tensor_tensor(out=yT[:, c, :], in0=Hb[i2][:, :], in1=Gg[i2][:, :], op=ALU.mult),
                      reads=[("Hb2", i2), ("G2", i2)], writes=[("yT", c)])
            emit_outproj_residual(kb, pp, yT, lambda k_: ("yT", k_), NCH, w["wout"], cx.wO, cx.wOi, hT, "hT", 0, NT,
                                  dst_v, dst_pad + t0, cx.ho_bufs, out_toks, "rg")


def prep_rglru_weights(w_in, conv_w, conv_b, gate_w, gate_b, lam, w_out, g):
    def fm_blocks(wm):
        n = wm.shape[1] // 128
        return np.ascontiguousarray(wm.reshape(-1, 128, n, 128).transpose(2, 1, 0, 3), dtype=np.float32)
    gw = gate_w.reshape(2, RG_BLOCKS, 2, 128, 512).transpose(3, 0, 1, 2, 4)
    cw = np.concatenate([conv_w, conv_b[None]], 0).reshape(5, NCH, 128).transpose(2, 1, 0)
    gbr = gate_b.reshape(2, RG_BLOCKS, 2, 2, 128)
    gb = gbr.transpose(4, 0, 1, 3, 2).reshape(128, 2, NCH, 2)
    lm = lam.reshape(2, NCH, 128).transpose(2, 0, 1)
    f = lambda a: np.ascontiguousarray(a, dtype=np.float32)
    return dict(win=fm_blocks(w_in), gw=f(gw), cw=f(cw), gb=f(gb), lam=f(lm), wout=fm_blocks(w_out),
                g=f(g.reshape(NCH, 128).T))


def build_rglru_test(T, NT=512):
    nc = bass.Bass("TRN2", target_bir_lowering=False)
    PAD = 2
    hin = nc.dram_tensor("hin", [D_MODEL, T + 2 * PAD], F32, kind="ExternalInput").ap()
    w = dict(win=nc.dram_tensor("win", [32, 128, NCH, 128], F32, kind="ExternalInput").ap(),
             gw=nc.dram_tensor("gw", [128, 2, RG_BLOCKS, 2, 512], F32, kind="ExternalInput").ap(),
             cw=nc.dram_tensor("cw", [128, NCH, 5], F32, kind="ExternalInput").ap(),
             gb=nc.dram_tensor("gb", [128, 2, NCH, 2], F32, kind="ExternalInput").ap(),
             lam=nc.dram_tensor("lam", [128, 2, NCH], F32, kind="ExternalInput").ap(),
             wout=nc.dram_tensor("wout", [NCH, 128, NCH, 128], F32, kind="ExternalInput").ap(),
             g=nc.dram_tensor("g", [128, NCH], F32, kind="ExternalInput").ap())
    hout = nc.dram_tensor("hout", [D_MODEL, T], F32, kind="ExternalOutput").ap()
    scr = dict(ggT=nc.dram_tensor("ggT", [D_MODEL, T], BF16).ap(), hfT=nc.dram_tensor("hfT", [D_MODEL, T], F32).ap(),
               abT=nc.dram_tensor("abT", [D_MODEL, T], F32).ap(), bxbT=nc.dram_tensor("bxbT", [D_MODEL, T], F32).ap())
    with contextlib.ExitStack() as stack:
        kb = KB(nc, stack)
        pp = PsumPool(kb, nbanks=6)
        cx = make_ctx(kb, NT)
        outs = []
        emit_rglru(kb, pp, cx, T, hin.rearrange("(c p) t -> p c t", p=128), PAD, hout.rearrange("(c p) t -> p c t", p=128), 0, w, scr, outs, NT)
        kb.finish(outs)
    return nc


SSD_INNER = 4096
SSD_HEADS = 64
SSD_G = 8
L = 128


def make_masks(kb, cx):
    nc = kb.nc
    cx.mask = []
    for d in range(2):
        m = sb(kb, f"mask{d}", [128, 128], F32)
        kb.op("pool", lambda: nc.gpsimd.memset(m[:, :], 1.0), writes=[("mask", d)])
        cm, coef = (-1, 1) if d == 0 else (1, -1)
        kb.op("pool", lambda: nc.gpsimd.affine_select(out=m[:, :], in_=m[:, :], pattern=[[coef, 128]], compare_op=ALU.is_ge,
                                                      fill=0.0, base=0, channel_multiplier=cm),
              reads=[("mask", d)], writes=[("mask", d)])
        cx.mask.append(m)


def emit_ssd(kb, pp, cx, T, src_v, src_pad, dst_v, dst_pad, w, scr, out_toks, NT=512):
    nc = kb.nc
    nt = T // NT
    nchunk = T // L
    W = NT + 3
    xcv = scr["xcT"].rearrange("(b p) t -> p b t", p=128)
    ynv = scr["ynT"].rearrange("(b p) t -> p b t", p=128)
    with kb.phase():
        tile_ctx(kb, cx)
        hT, uT, g_sb = cx.hT, cx.uT, cx.g_sb
        wB = [sb(kb, f"s1wB{i}", [128, NCH, 512], BF16) for i in range(2)]
        cw = sb(kb, "s1cw", [128, 48, 5], F32)
        dtb = sb(kb, "s1dtb", [128, 2], F32)
        A_sb = sb(kb, "s1A", [128, 2], F32)
        zs = [sb(kb, f"s1zs{i}", [128, 512], BF16) for i in range(2)]
        G = [sb(kb, f"s1G{i}", [128, W], F32) for i in range(2)]
        acc = [sb(kb, f"s1acc{i}", [128, NT], F32) for i in range(2)]
        xo = [sb(kb, f"s1xo{i}", [128, NT], BF16) for i in range(2)]
        d1 = sb(kb, "s1d1", [128, NT], F32)
        d2 = sb(kb, "s1d2", [128, NT], F32)
        d3 = sb(kb, "s1d3", [128, NT], F32)
        kb.dma("sp", g_sb[:, :], w["g"][:, :], writes=["g"])
        kb.dma("sp", cw[:, :, :], w["cw"], writes=["cw"])
        kb.dma("sp", dtb[:, 0:1], w["dtb"], writes=["dtb"])
        kb.dma("sp", A_sb[:, 0:1], w["alog"], writes=["A"])
        kb.op("act", lambda: nc.scalar.activation(out=A_sb[:, 1:2], in_=A_sb[:, 0:1], func=AF.Exp), reads=["A"], writes=["A"])
        kb.op("dve", lambda: nc.vector.tensor_scalar(out=A_sb[:, 1:2], in0=A_sb[:, 1:2], scalar1=-1.0, scalar2=None, op0=ALU.mult), reads=["A"], writes=["A"])
        wbi = 0
        for it in range(nt):
            t0 = it * NT
            for c in range(NCH):
                kb.dma("sp", hT[:, c, :W], src_v[:, c, src_pad + t0 - 2:src_pad + t0 - 2 + W], writes=["hT"])
            emit_rmsnorm_fm(kb, pp, hT, "hT", g_sb, uT, "uT", cx.ones_bf, W, cx.sq_bufs, (cx.rstd_t, "rstd"))
            for cg in range(8):
                wb = wB[wbi % 2]; wtok = ("s1wB", wbi % 2); wbi += 1
                kb.dma("pool", wb[:, :, :], w["wz"][cg], writes=[wtok])
                for sub in range(NT // 128):
                    ps, pt = pp.next()
                    for kc in range(NCH):
                        kb.op("pe", lambda: nc.tensor.matmul(ps[:, :512], lhsT=uT[:, kc, 2 + sub * 128:2 + (sub + 1) * 128], rhs=wb[:, kc, :],
                                                             start=(kc == 0), stop=(kc == NCH - 1)),
                              reads=[wtok, "uT"], writes=[pt], inc=(kc == NCH - 1))
                    i2 = (cg * 4 + sub) % 2
                    kb.op("act", lambda: nc.scalar.activation(out=zs[i2][:, :], in_=ps[:, :512], func=AF.Silu), reads=[pt], writes=[("zs", i2)])
                    r0 = t0 + sub * 128
                    kb.dma("sp", scr["zs"][r0:r0 + 128, cg * 512:(cg + 1) * 512], zs[i2][:, :], reads=[("zs", i2)], writes=[("zsd", r0 // 128, cg)])
            for blk in range(48):
                wb = cx.wA[cx.wAi[0] % 2]; wtok = ("wA", cx.wAi[0] % 2); cx.wAi[0] += 1
                kb.dma("pool", wb[:, :, 0:128], w["wx"][blk], writes=[wtok])
                i2 = blk % 2
                (psm, ptm), (psx, ptx) = emit_proj_block(kb, pp, wb, wtok, 0, uT, "uT", [(0, NT), (NT, 3)])
                Gt = G[i2]; gtok = ("G", i2)
                kb.op("act", lambda: nc.scalar.copy(out=Gt[:, 0:NT], in_=psm[:, :NT]), reads=[ptm], writes=[gtok])
                kb.op("act", lambda: nc.scalar.copy(out=Gt[:, NT:NT + 3], in_=psx[:, :3]), reads=[ptx], writes=[gtok])
                ac = acc[i2]; atok = ("acc", i2)
                kb.op("dve", lambda: nc.vector.tensor_scalar(out=ac[:, :], in0=Gt[:, 0:NT], scalar1=cw[:, blk, 0:1], scalar2=cw[:, blk, 4:5],
                                                             op0=ALU.mult, op1=ALU.add), reads=[gtok, "cw"], writes=[atok])
                for tap in range(1, 4):
                    kb.op("dve", lambda: nc.vector.scalar_tensor_tensor(out=ac[:, :], in0=Gt[:, tap:tap + NT], scalar=cw[:, blk, tap:tap + 1],
                                                                        in1=ac[:, :], op0=ALU.mult, op1=ALU.add),
                          reads=[gtok, atok], writes=[atok])
                kb.op("act", lambda: nc.scalar.activation(out=xo[i2][:, :], in_=ac[:, :], func=AF.Silu), reads=[atok], writes=[("xo", i2)])
                kb.dma("sp", xcv[:, blk, t0:t0 + NT], xo[i2][:, :], reads=[("xo", i2)], writes=[("xcd", it, blk)])
            wb = cx.wA[cx.wAi[0] % 2]; wtok = ("wA", cx.wAi[0] % 2); cx.wAi[0] += 1
            kb.dma("pool", wb[:, :, 0:128], w["wdt"][0], writes=[wtok])
            (ps, pt), = emit_proj_block(kb, pp, wb, wtok, 0, uT, "uT", [(2, NT)])
            kb.op("act", lambda: nc.scalar.activation(out=d1[:, :], in_=ps[:, :NT], func=AF.Identity, bias=dtb[:, 0:1], scale=1.0),
                  reads=[pt, "dtb"], writes=["d1"])
            emit_softplus_small(kb, d2[:, :], d1[:, :], d3[:, :], "d1")
            kb.dma("sp", scr["dtT"][:, t0:t0 + NT], d2[:, :], reads=["d1"], writes=[("dtd", it)])
            kb.op("dve", lambda: nc.vector.tensor_scalar(out=d1[:, :], in0=d2[:, :], scalar1=A_sb[:, 1:2], scalar2=None, op0=ALU.mult),
                  reads=["d1", "A"], writes=["d1"])
            for ch in range(NT // L):
                sl = slice(ch * L, (ch + 1) * L)
                rsl = slice((ch + 1) * L - 1, ch * L - 1 if ch > 0 else None, -1)
                kb.op("dve", lambda: nc.vector.tensor_tensor_scan(out=d3[0:64, sl], data0=cx.onesf[0:64, 0:L], data1=d1[0:64, sl], initial=0.0,
                                                                  op0=ALU.mult, op1=ALU.add), reads=["d1", "onesf"], writes=["d1"])
                kb.op("dve", lambda: nc.vector.tensor_tensor_scan(out=d3[64:128, rsl], data0=cx.onesf[64:128, 0:L], data1=d1[64:128, rsl], initial=0.0,
                                                                  op0=ALU.mult, op1=ALU.add), reads=["d1", "onesf"], writes=["d1"])
            kb.dma("sp", scr["cumT"][:, t0:t0 + NT], d3[:, :], reads=["d1"], writes=[("cumd", it)])
    with kb.phase():
        make_masks(kb, cx)
        xT_in = sb(kb, "s2xTin", [128, 40, L], BF16)
        CT = [sb(kb, f"s2CT{i}", [128, 8, L], BF16) for i in range(2)]
        BT = [sb(kb, f"s2BT{i}", [128, 8, L], BF16) for i in range(2)]
        xtm = [sb(kb, f"s2xtm{i}", [128, SSD_INNER], BF16) for i in range(2)]
        btm = [sb(kb, f"s2btm{i}", [128, 1024], BF16) for i in range(2)]
        cdt = [sb(kb, f"s2cdt{i}", [64, 2, L], F32) for i in range(2)]
        wT = sb(kb, "s2wT", [64, L], F32)
        eg = sb(kb, "s2eg", [64, 2], F32)
        dg = sb(kb, "s2dg", [64, 64], F32)
        tm = [sb(kb, f"s2tm{i}", [128, 4, 64], F32) for i in range(2)]
        cb = [sb(kb, f"s2cb{i}", [128, 8, L], F32) for i in range(2)]
        Dc = [sb(kb, f"s2Dc{i}", [128, 8, L], F32) for i in range(2)]
        ecb = [sb(kb, f"s2ecb{i}", [128, 8, L], F32) for i in range(2)]
        CBm = [sb(kb, f"s2CBm{i}", [128, L], F32) for i in range(2)]
        Wt = [sb(kb, f"s2Wt{i}", [128, 8, L], BF16) for i in range(2)]
        CsT = [sb(kb, f"s2CsT{i}", [128, 8, L], BF16) for i in range(2)]
        xw = [sb(kb, f"s2xw{i}", [128, 512], BF16) for i in range(2)]
        Sf = sb(kb, "s2Sf", [128, SSD_G, 512], F32)
        Sb = sb(kb, "s2Sb", [128, SSD_G, 512], BF16)
        yf = sb(kb, "s2yf", [128, SSD_INNER], F32)
        yg = sb(kb, "s2yg", [128, SSD_INNER], F32)
        zsc = sb(kb, "s2zs", [128, SSD_INNER], BF16)
        gN = sb(kb, "s2gN", [128, SSD_INNER], F32)
        dsk = sb(kb, "s2dsk", [128, SSD_INNER], F32)
        ynb = sb(kb, "s2ynb", [128, SSD_INNER], BF16)
        ynT_sb = [sb(kb, f"s2ynT{i}", [128, 8, L], BF16) for i in range(2)]
        st = sb(kb, "s2st", [128, 4], F32)
        kb.dma("sp", gN[:, :], w["gn"].partition_broadcast(128), writes=["gN"])
        kb.dma("sp", dsk[:, :], w["dsk"].partition_broadcast(128), writes=["dsk"])
        ci = 0
        for d in range(2):
            kb.op("dve", lambda: nc.vector.memset(Sf[:, :, :], 0.0), reads=[], writes=[("Sf", g_) for g_ in range(SSD_G)])
            kb.op("pool", lambda: nc.gpsimd.memset(Sb[:, :, :], 0.0), reads=[], writes=[("Sb", g_) for g_ in range(SSD_G)])
            order = range(nchunk) if d == 0 else reversed(range(nchunk))
            last = L - 1 if d == 0 else 0
            for c in order:
                c0 = c * L
                i2 = ci % 2; ci += 1
                it = c0 // NT
                kb.dma("sp", CT[i2][:, :, :], xcv[:, 40:48, c0:c0 + L], reads=[("xcd", it, b_) for b_ in range(40, 48)], writes=[("CT", i2)])
                kb.dma("sp", BT[i2][:, :, :], xcv[:, 32:40, c0:c0 + L], reads=[("xcd", it, b_) for b_ in range(32, 40)], writes=[("BT", i2)])
                kb.dma("sp", cdt[i2][:, 0, :], scr["cumT"][d * 64:(d + 1) * 64, c0:c0 + L], reads=[("cumd", it)], writes=[("cdt", i2)])
                kb.dma("sp", cdt[i2][:, 1, :], scr["dtT"][d * 64:(d + 1) * 64, c0:c0 + L], reads=[("dtd", it)], writes=[("cdt", i2)])
                X, Bm = xtm[i2], btm[i2]
                if d == 0:
                    kb.dma("sp", xT_in[:, 0:32, :], xcv[:, 0:32, c0:c0 + L], reads=[("xcd", it, b_) for b_ in range(32)], writes=["xTin"])
                    for grp in range(5):
                        pb = cx.psb[cx.psbi[0] % 2]; pbtok = ("psb", cx.psbi[0] % 2); cx.psbi[0] += 1
                        for j in range(8):
                            src_ap = xT_in[:, grp * 8 + j, :] if grp < 4 else BT[i2][:, j, :]
                            kb.op("pe", lambda: nc.tensor.transpose(out=pb[:, j * 128:(j + 1) * 128], in_=src_ap, identity=cx.ident[:, :]),
                                  reads=["xTin" if grp < 4 else ("BT", i2), "ident"], writes=[pbtok], inc=(j == 7))
                        if grp < 4:
                            kb.op("act", lambda: nc.scalar.copy(out=X[:, grp * 1024:(grp + 1) * 1024], in_=pb[:, :]), reads=[pbtok], writes=[("xtm", i2)])
                        else:
                            kb.op("act", lambda: nc.scalar.copy(out=Bm[:, :], in_=pb[:, :]), reads=[pbtok], writes=[("btm", i2)])
                    kb.dma("sp", scr["xtm"][c0:c0 + L, :], X[:, :], reads=[("xtm", i2)], writes=[("xtmd", c)])
                    kb.dma("sp", scr["btm"][c0:c0 + L, :], Bm[:, :], reads=[("btm", i2)], writes=[("btmd", c)])
                else:
                    kb.dma("sp", X[:, :], scr["xtm"][c0:c0 + L, :], reads=[("xtmd", c)], writes=[("xtm", i2)])
                    kb.dma("sp", Bm[:, :], scr["btm"][c0:c0 + L, :], reads=[("btmd", c)], writes=[("btm", i2)])
                    kb.dma("sp", yf[:, :], scr["yf"][c0:c0 + L, :], reads=[("yfd", c, g_) for g_ in range(SSD_G)], writes=["yf"])
                    kb.dma("sp", zsc[:, :], scr["zs"][c0:c0 + L, :], reads=[("zsd", c, cg_) for cg_ in range(8)], writes=["zsc"])
                cd_ = cdt[i2]
                kb.op("act", lambda: nc.scalar.activation(out=wT[:, :], in_=cd_[:, 0, :], func=AF.Exp, bias=cd_[:, 0, last:last + 1], scale=-1.0),
                      reads=[("cdt", i2)], writes=["wT"])
                kb.op("dve", lambda: nc.vector.tensor_tensor(out=wT[:, :], in0=wT[:, :], in1=cd_[:, 1, :], op=ALU.mult), reads=["wT", ("cdt", i2)], writes=["wT"])
                kb.op("act", lambda: nc.scalar.activation(out=eg[:, 0:1], in_=cd_[:, 0, last:last + 1], func=AF.Exp), reads=[("cdt", i2)], writes=["eg"])
                kb.op("dve", lambda: nc.vector.tensor_scalar(out=dg[:, :], in0=cx.identf[0:64, 0:64], scalar1=eg[:, 0:1], scalar2=None, op0=ALU.mult),
                      reads=["eg", "identf"], writes=["dg"])
                ps, pt = pp.next()
                kb.op("pe", lambda: nc.tensor.matmul(ps[:, 0:64], lhsT=cd_[:, 0, :], rhs=cx.identf[0:64, 0:64], start=True, stop=True),
                      reads=[("cdt", i2), "identf"], writes=[pt], inc=False)
                kb.op("pe", lambda: nc.tensor.matmul(ps[:, 64:128], lhsT=cd_[:, 1, :], rhs=cx.identf[0:64, 0:64], start=True, stop=True),
                      reads=[("cdt", i2)], writes=[pt], inc=False)
                kb.op("pe", lambda: nc.tensor.matmul(ps[:, 128:192], lhsT=wT[:, :], rhs=cx.identf[0:64, 0:64], start=True, stop=True),
                      reads=["wT"], writes=[pt], inc=False)
                kb.op("pe", lambda: nc.tensor.matmul(ps[:, 192:256], lhsT=cx.onesf[0:64, :], rhs=dg[:, :], start=True, stop=True),
                      reads=["dg", "onesf"], writes=[pt], inc=True)
                TM = tm[i2]
                kb.op("act", lambda: nc.scalar.copy(out=TM[:, :, :].rearrange("p a b -> p (a b)"), in_=ps[:, 0:256]), reads=[pt], writes=[("tm", i2)])
                ps_y = None
                for g_ in range(SSD_G):
                    j2 = (ci * SSD_G + g_) % 2
                    r0 = d * 64 + g_ * 8
                    kb.dma("sp", cb[j2][:, :, :], scr["cumT"][r0:r0 + 8, c0:c0 + L].partition_broadcast(128), reads=[("cumd", it)], writes=[("cb", j2)])
                    ps_cb, pt_cb = pp.next()
                    kb.op("pe", lambda: nc.tensor.matmul(ps_cb[:, :L], lhsT=BT[i2][:, g_, :], rhs=CT[i2][:, g_, :], start=True, stop=True),
                          reads=[("BT", i2), ("CT", i2)], writes=[pt_cb], inc=True)
                    kb.op("dve", lambda: nc.vector.tensor_tensor(out=CBm[j2][:, :], in0=ps_cb[:, :L], in1=cx.mask[d][:, :], op=ALU.mult),
                          reads=[pt_cb, ("mask", d)], writes=[("CBm", j2)])
                    cum_b = TM[:, 0, g_ * 8:(g_ + 1) * 8].unsqueeze(2).to_broadcast([128, 8, L])
                    dt_b = TM[:, 1, g_ * 8:(g_ + 1) * 8].unsqueeze(2).to_broadcast([128, 8, L])
                    kb.op("dve", lambda: nc.vector.tensor_tensor(out=Dc[j2][:, :, :], in0=cb[j2][:, :, :], in1=cum_b, op=ALU.subtract),
                          reads=[("cb", j2), ("tm", i2)], writes=[("Dc", j2)])
                    kb.op("dve", lambda: nc.vector.tensor_scalar(out=Dc[j2][:, :, :], in0=Dc[j2][:, :, :], scalar1=0.0, scalar2=None, op0=ALU.min),
                          reads=[("Dc", j2)], writes=[("Dc", j2)])
                    kb.op("act", lambda: nc.scalar.activation(out=Dc[j2][:, :, :], in_=Dc[j2][:, :, :], func=AF.Exp), reads=[("Dc", j2)], writes=[("Dc", j2)])
                    kb.op("dve", lambda: nc.vector.tensor_tensor(out=Dc[j2][:, :, :], in0=Dc[j2][:, :, :],
                                                                 in1=CBm[j2][:, :].unsqueeze(1).to_broadcast([128, 8, L]), op=ALU.mult),
                          reads=[("Dc", j2), ("CBm", j2)], writes=[("Dc", j2)])
                    kb.op("dve", lambda: nc.vector.tensor_tensor(out=Wt[j2][:, :, :], in0=Dc[j2][:, :, :], in1=dt_b, op=ALU.mult),
                          reads=[("Dc", j2), ("tm", i2)], writes=[("Wt", j2)])
                    kb.op("act", lambda: nc.scalar.activation(out=ecb[j2][:, :, :], in_=cb[j2][:, :, :], func=AF.Exp), reads=[("cb", j2)], writes=[("ecb", j2)])
                    kb.op("dve", lambda: nc.vector.tensor_tensor(out=CsT[j2][:, :, :], in0=ecb[j2][:, :, :],
                                                                 in1=CT[i2][:, g_, :].unsqueeze(1).to_broadcast([128, 8, L]), op=ALU.mult),
                          reads=[("ecb", j2), ("CT", i2)], writes=[("CsT", j2)])
                    ps_y, pt_y = pp.next()
                    for h_ in range(8):
                        H = g_ * 8 + h_
                        kb.op("pe", lambda: nc.tensor.matmul(ps_y[:, h_ * 64:(h_ + 1) * 64], lhsT=Wt[j2][:, h_, :], rhs=X[:, H * 64:(H + 1) * 64],
                                                             start=True, stop=False),
                              reads=[("Wt", j2), ("xtm", i2)], writes=[pt_y], inc=False)
                        kb.op("pe", lambda: nc.tensor.matmul(ps_y[:, h_ * 64:(h_ + 1) * 64], lhsT=CsT[j2][:, h_, :], rhs=Sb[:, g_, h_ * 64:(h_ + 1) * 64],
                                                             start=False, stop=True),
                              reads=[("CsT", j2), ("Sb", g_)], writes=[pt_y], inc=(h_ == 7))
                    w_b = TM[:, 2, g_ * 8:(g_ + 1) * 8].unsqueeze(2).to_broadcast([128, 8, 64])
                    e_b = TM[:, 3, g_ * 8:(g_ + 1) * 8].unsqueeze(2).to_broadcast([128, 8, 64])
                    kb.op("dve", lambda: nc.vector.tensor_tensor(out=xw[j2][:, :].rearrange("p (h q) -> p h q", h=8),
                                                                 in0=X[:, g_ * 512:(g_ + 1) * 512].rearrange("p (h q) -> p h q", h=8), in1=w_b, op=ALU.mult),
                          reads=[("xtm", i2), ("tm", i2)], writes=[("xw", j2)])
                    ps_s, pt_s = pp.next()
                    kb.op("pe", lambda: nc.tensor.matmul(ps_s[:, :512], lhsT=Bm[:, g_ * 128:(g_ + 1) * 128], rhs=xw[j2][:, :], start=True, stop=True),
                          reads=[("btm", i2), ("xw", j2)], writes=[pt_s], inc=True)
                    kb.op("dve", lambda: nc.vector.tensor_tensor(out=Sf[:, g_, :].rearrange("p (h q) -> p h q", h=8),
                                                                 in0=Sf[:, g_, :].rearrange("p (h q) -> p h q", h=8), in1=e_b, op=ALU.mult),
                          reads=[("Sf", g_), ("tm", i2)], writes=[("Sf", g_)])
                    kb.op("dve", lambda: nc.vector.tensor_tensor(out=Sf[:, g_, :], in0=Sf[:, g_, :], in1=ps_s[:, :512], op=ALU.add),
                          reads=[("Sf", g_), pt_s], writes=[("Sf", g_)])
                    kb.op("act", lambda: nc.scalar.copy(out=Sb[:, g_, :], in_=Sf[:, g_, :]), reads=[("Sf", g_)], writes=[("Sb", g_)])
                    gs = slice(g_ * 512, (g_ + 1) * 512)
                    if d == 0:
                        kb.op("act", lambda: nc.scalar.copy(out=yg[:, gs], in_=ps_y[:, :512]), reads=[pt_y], writes=[("yg", g_)])
                        kb.dma("sp", scr["yf"][c0:c0 + L, gs], yg[:, gs], reads=[("yg", g_)], writes=[("yfd", c, g_)])
                    else:
                        kb.op("dve", lambda: nc.vector.tensor_tensor(out=yg[:, gs], in0=ps_y[:, :512], in1=yf[:, gs], op=ALU.add),
                              reads=[pt_y, "yf"], writes=[("yg", g_)])
                if d == 1:
                    allg = [("yg", g_) for g_ in range(SSD_G)]
                    kb.op("pool", lambda: nc.gpsimd.tensor_tensor(out=yf[:, :], in0=X[:, :], in1=dsk[:, :], op=ALU.mult), reads=[("xtm", i2), "dsk"], writes=["yf"])
                    kb.op("dve", lambda: nc.vector.tensor_tensor(out=yg[:, :], in0=yg[:, :], in1=yf[:, :], op=ALU.add), reads=allg + ["yf"], writes=allg)
                    kb.op("dve", lambda: nc.vector.tensor_tensor(out=yg[:, :], in0=yg[:, :], in1=zsc[:, :], op=ALU.mult), reads=allg + ["zsc"], writes=allg)
                    kb.op("act", lambda: nc.scalar.activation(out=yf[:, :], in_=yg[:, :], func=AF.Square, accum_out=st[:, 0:1]), reads=allg, writes=["yf", "st"])
                    kb.op("dve", lambda: nc.vector.tensor_scalar(out=st[:, 1:2], in0=st[:, 0:1], scalar1=1.0 / SSD_INNER, scalar2=EPS, op0=ALU.mult, op1=ALU.add),
                          reads=["st"], writes=["st"])
                    emit_rsqrt_inplace(kb, st[:, 1:2], "st")
                    kb.op("dve", lambda: nc.vector.scalar_tensor_tensor(out=ynb[:, :], in0=yg[:, :], scalar=st[:, 1:2], in1=gN[:, :], op0=ALU.mult, op1=ALU.mult),
                          reads=allg + ["st", "gN"], writes=["ynb"])
                    for grp in range(4):
                        pb = cx.psb[cx.psbi[0] % 2]; pbtok = ("psb", cx.psbi[0] % 2); cx.psbi[0] += 1
                        for j in range(8):
                            blk = grp * 8 + j
                            kb.op("pe", lambda: nc.tensor.transpose(out=pb[:, j * 128:(j + 1) * 128], in_=ynb[:, blk * 128:(blk + 1) * 128], identity=cx.ident[:, :]),
                                  reads=["ynb", "ident"], writes=[pbtok], inc=(j == 7))
                        yo = ynT_sb[grp % 2]
                        kb.op("act", lambda: nc.scalar.copy(out=yo[:, :, :].rearrange("p a b -> p (a b)"), in_=pb[:, :]), reads=[pbtok], writes=[("ynT", grp % 2)])
                        kb.dma("sp", ynv[:, grp * 8:(grp + 1) * 8, c0:c0 + L], yo[:, :, :], reads=[("ynT", grp % 2)], writes=[("ynd", c, grp)])
    with kb.phase():
        tile_ctx(kb, cx)
        hT = cx.hT
        yT = sb(kb, "s3yT", [128, 32, NT], BF16)
        for it in range(nt):
            t0 = it * NT
            for c in range(NCH):
                kb.dma("sp", hT[:, c, :NT], src_v[:, c, src_pad + t0:src_pad + t0 + NT], writes=["hT"])
            kb.dma("sp", yT[:, :, :], ynv[:, :, t0:t0 + NT], writes=[("yT", k_) for k_ in range(32)])
            emit_outproj_residual(kb, pp, yT, lambda k_: ("yT", k_), 32, w["wout"], cx.wO, cx.wOi, hT, "hT", 0, NT,
                                  dst_v, dst_pad + t0, cx.ho_bufs, out_toks, "ssd")


def prep_ssd_weights(w_in, conv_w, conv_b, dt_bias, a_log, d_skip, norm_g, w_out, g):
    f = lambda a: np.ascontiguousarray(a, dtype=np.float32)
    def fm_blocks(wm):
        n = wm.shape[1] // 128
        return f(wm.reshape(-1, 128, n, 128).transpose(2, 1, 0, 3))
    wz = w_in[:, :SSD_INNER].reshape(NCH, 128, 8, 512).transpose(2, 1, 0, 3)
    wx = fm_blocks(w_in[:, SSD_INNER:SSD_INNER + 6144])
    wdt = fm_blocks(w_in[:, SSD_INNER + 6144:])
    cw = np.concatenate([conv_w, conv_b[None]], 0).reshape(5, 48, 128).transpose(2, 1, 0)
    return dict(wz=f(wz), wx=wx, wdt=wdt, cw=f(cw), dtb=f(dt_bias.reshape(128, 1)), alog=f(a_log.reshape(128, 1)),
                dsk=f(np.repeat(d_skip, 64).reshape(1, SSD_INNER)), gn=f(norm_g.reshape(1, SSD_INNER)),
                wout=fm_blocks(w_out), g=f(g.reshape(NCH, 128).T))


SSD_W_SHAPES = dict(wz=[8, 128, NCH, 512], wx=[48, 128, NCH, 128], wdt=[1, 128, NCH, 128], cw=[128, 48, 5], dtb=[128, 1], alog=[128, 1],
                    dsk=[1, SSD_INNER], gn=[1, SSD_INNER], wout=[NCH, 128, 32, 128], g=[128, NCH])


def ssd_scratch(nc, T, pfx=""):
    return dict(zs=nc.dram_tensor(pfx + "zs", [T, SSD_INNER], BF16).ap(), xcT=nc.dram_tensor(pfx + "xcT", [6144, T], BF16).ap(),
                dtT=nc.dram_tensor(pfx + "dtT", [128, T], F32).ap(), cumT=nc.dram_tensor(pfx + "cumT", [128, T], F32).ap(),
                xtm=nc.dram_tensor(pfx + "xtm", [T, SSD_INNER], BF16).ap(), btm=nc.dram_tensor(pfx + "btm", [T, 1024], BF16).ap(),
                yf=nc.dram_tensor(pfx + "yf", [T, SSD_INNER], F32).ap(), ynT=nc.dram_tensor(pfx + "ynT", [SSD_INNER, T], BF16).ap())


def build_ssd_test(T, NT=512):
    nc = bass.Bass("TRN2", target_bir_lowering=False)
    PAD = 2
    hin = nc.dram_tensor("hin", [D_MODEL, T + 2 * PAD], F32, kind="ExternalInput").ap()
    w = {k: nc.dram_tensor(k, shp, F32, kind="ExternalInput").ap() for k, shp in SSD_W_SHAPES.items()}
    hout = nc.dram_tensor("hout", [D_MODEL, T], F32, kind="ExternalOutput").ap()
    scr = ssd_scratch(nc, T)
    with contextlib.ExitStack() as stack:
        kb = KB(nc, stack)
        pp = PsumPool(kb, nbanks=6)
        cx = make_ctx(kb, NT)
        outs = []
        emit_ssd(kb, pp, cx, T, hin.rearrange("(c p) t -> p c t", p=128), PAD, hout.rearrange("(c p) t -> p c t", p=128), 0, w, scr, outs, NT)
        kb.finish(outs)
    return nc


ML_H = 8
ML_DK = 128
ML_DV = 256


def emit_mlstm(kb, pp, cx, T, src_v, src_pad, dst_v, dst_pad, w, scr, out_toks, NT=512):
    nc = kb.nc
    nt = T // NT
    nchunk = T // L
    qv = scr["qT"].rearrange("(h p) t -> p h t", p=128)
    kv = scr["kT"].rearrange("(h p) t -> p h t", p=128)
    hhv = scr["hhT"].rearrange("(b p) t -> p b t", p=128)
    kscale = float(ML_DK) ** -0.5
    with kb.phase():
        tile_ctx(kb, cx)
        hT, uT, g_sb = cx.hT, cx.uT, cx.g_sb
        wB = [sb(kb, f"m1wB{i}", [128, NCH, 512], BF16) for i in range(2)]
        wg = sb(kb, "m1wg", [128, NCH, 32], BF16)
        gbias = sb(kb, "m1gb", [8, 4], F32)
        ob = [sb(kb, f"m1ob{i}", [128, 512], BF16) for i in range(2)]
        gt = [sb(kb, f"m1gt{i}", [8, NT], F32) for i in range(3)]
        ones8 = sb(kb, "m1ones8", [8, L], F32)
        kb.op("dve", lambda: nc.vector.memset(ones8[:, :], 1.0), writes=["ones8"])
        kb.dma("sp", g_sb[:, :], w["g"][:, :], writes=["g"])
        kb.dma("pool", wg[:, :, :], w["wg"], writes=["wg"])
        kb.dma("sp", gbias[:, :], w["gbias"], writes=["gbias"])
        wbi = 0
        oi = 0
        for it in range(nt):
            t0 = it * NT
            for c in range(NCH):
                kb.dma("sp", hT[:, c, :NT], src_v[:, c, src_pad + t0:src_pad + t0 + NT], writes=["hT"])
            emit_rmsnorm_fm(kb, pp, hT, "hT", g_sb, uT, "uT", cx.ones_bf, NT, cx.sq_bufs, (cx.rstd_t, "rstd"))
            for blk in range(16):
                wb = cx.wA[cx.wAi[0] % 2]; wtok = ("wA", cx.wAi[0] % 2); cx.wAi[0] += 1
                kb.dma("pool", wb[:, :, 0:128], w["wqk"][blk], writes=[wtok])
                (ps, pt), = emit_proj_block(kb, pp, wb, wtok, 0, uT, "uT", [(0, NT)])
                i2 = oi % 2; oi += 1
                kb.op("act", lambda: nc.scalar.activation(out=ob[i2][:, :NT], in_=ps[:, :NT], func=AF.Copy, scale=(1.0 if blk < 8 else kscale)),
                      reads=[pt], writes=[("ob", i2)])
                dstv = qv if blk < 8 else kv
                kb.dma("sp", dstv[:, blk % 8, t0:t0 + NT], ob[i2][:, :NT], reads=[("ob", i2)], writes=[("qkd", it, blk)])
            for cg in range(10):
                wb = wB[wbi % 2]; wtok = ("m1wB", wbi % 2); wbi += 1
                kb.dma("pool", wb[:, :, :], w["wtm"][cg], writes=[wtok])
                for sub in range(NT // 128):
                    ps, pt = pp.next()
                    for kc in range(NCH):
                        kb.op("pe", lambda: nc.tensor.matmul(ps[:, :512], lhsT=uT[:, kc, sub * 128:(sub + 1) * 128], rhs=wb[:, kc, :],
                                                             start=(kc == 0), stop=(kc == NCH - 1)),
                              reads=[wtok, "uT"], writes=[pt], inc=(kc == NCH - 1))
                    i2 = oi % 2; oi += 1
                    r0 = t0 + sub * 128
                    if cg < 2:
                        kb.op("act", lambda: nc.scalar.activation(out=ob[i2][:, :], in_=ps[:, :512], func=AF.Copy, scale=kscale), reads=[pt], writes=[("ob", i2)])
                        kb.dma("sp", scr["ktm"][r0:r0 + 128, cg * 512:(cg + 1) * 512], ob[i2][:, :], reads=[("ob", i2)], writes=[("ktmd", r0 // 128, cg)])
                    elif cg < 6:
                        kb.op("act", lambda: nc.scalar.copy(out=ob[i2][:, :], in_=ps[:, :512]), reads=[pt], writes=[("ob", i2)])
                        kb.dma("sp", scr["vtm"][r0:r0 + 128, (cg - 2) * 512:(cg - 1) * 512], ob[i2][:, :], reads=[("ob", i2)], writes=[("vtmd", r0 // 128, cg - 2)])
                    else:
                        kb.op("act", lambda: nc.scalar.activation(out=ob[i2][:, :], in_=ps[:, :512], func=AF.Sigmoid), reads=[pt], writes=[("ob", i2)])
                        kb.dma("sp", scr["so"][r0:r0 + 128, (cg - 6) * 512:(cg - 5) * 512], ob[i2][:, :], reads=[("ob", i2)], writes=[("sod", r0 // 128, cg - 6)])
            for d in range(2):
                pss = []
                for j in (2 * d, 2 * d + 1):
                    ps, pt = pp.next()
                    for kc in range(NCH):
                        kb.op("pe", lambda: nc.tensor.matmul(ps[0:8, :NT], lhsT=wg[:, kc, 8 * j:8 * j + 8], rhs=uT[:, kc, :NT],
                                                             start=(kc == 0), stop=(kc == NCH - 1)),
                              reads=["wg", "uT"], writes=[pt], inc=(kc == NCH - 1))
                    pss.append((ps, pt))
                (psi, pti), (psf, ptf) = pss
                g0, g1, g2 = gt
                kb.op("act", lambda: nc.scalar.activation(out=g0[:, :], in_=psi[0:8, :NT], func=AF.Exp, bias=gbias[:, 2 * d:2 * d + 1], scale=1.0),
                      reads=[pti, "gbias"], writes=["g0"])
                kb.dma("sp", scr["eiT"][d * 8:(d + 1) * 8, t0:t0 + NT], g0[:, :], reads=["g0"], writes=[("eid", it, d)])
                kb.op("act", lambda: nc.scalar.activation(out=g1[:, :], in_=psf[0:8, :NT], func=AF.Identity, bias=gbias[:, 2 * d + 1:2 * d + 2], scale=1.0),
                      reads=[ptf, "gbias"], writes=["g1"])
                emit_softplus_small(kb, g2[:, :], g1[:, :], g1[:, :], "g1", neg_in=True) if False else None
                kb.op("act", lambda: nc.scalar.activation(out=g2[:, :], in_=g1[:, :], func=AF.Abs), reads=["g1"], writes=["g2"])
                kb.op("act", lambda: nc.scalar.activation(out=g2[:, :], in_=g2[:, :], func=AF.Exp, scale=-1.0), reads=["g2"], writes=["g2"])
                kb.op("act", lambda: nc.scalar.activation(out=g2[:, :], in_=g2[:, :], func=AF.Ln, bias=1.0, scale=1.0), reads=["g2"], writes=["g2"])
                kb.op("dve", lambda: nc.vector.tensor_scalar(out=g1[:, :], in0=g1[:, :], scalar1=-1.0, scalar2=0.0, op0=ALU.mult, op1=ALU.max), reads=["g1"], writes=["g1"])
                kb.op("dve", lambda: nc.vector.tensor_tensor(out=g1[:, :], in0=g1[:, :], in1=g2[:, :], op=ALU.add), reads=["g1", "g2"], writes=["g1"])
                kb.op("dve", lambda: nc.vector.tensor_scalar(out=g1[:, :], in0=g1[:, :], scalar1=-1.0, scalar2=None, op0=ALU.mult), reads=["g1"], writes=["g1"])
                for ch in range(NT // L):
                    sl = slice(ch * L, (ch + 1) * L)
                    rsl = slice((ch + 1) * L - 1, ch * L - 1 if ch > 0 else None, -1)
                    use = sl if d == 0 else rsl
                    kb.op("dve", lambda: nc.vector.tensor_tensor_scan(out=g2[:, use], data0=ones8[:, 0:L], data1=g1[:, use], initial=0.0,
                                                                      op0=ALU.mult, op1=ALU.add), reads=["g1", "g2", "ones8"], writes=["g2"])
                kb.dma("sp", scr["cumT"][d * 8:(d + 1) * 8, t0:t0 + NT], g2[:, :], reads=["g2"], writes=[("cumd", it, d)])
    with kb.phase():
        make_masks(kb, cx)
        qT = [sb(kb, f"m2qT{i}", [128, ML_H, L], BF16) for i in range(2)]
        kT = [sb(kb, f"m2kT{i}", [128, ML_H, L], BF16) for i in range(2)]
        ktm = [sb(kb, f"m2ktm{i}", [128, ML_H, ML_DK], BF16) for i in range(2)]
        Vx = [sb(kb, f"m2Vx{i}", [128, ML_H, ML_DV + 1], BF16) for i in range(2)]
        cdt = [sb(kb, f"m2cdt{i}", [8, 2, L], F32) for i in range(2)]
        wT = sb(kb, "m2wT", [8, L], F32)
        eg = sb(kb, "m2eg", [8, 2], F32)
        dg = sb(kb, "m2dg", [8, 8], F32)
        tm = [sb(kb, f"m2tm{i}", [128, 4, 8], F32) for i in range(2)]
        cb = [sb(kb, f"m2cb{i}", [128, ML_H, L], F32) for i in range(2)]
        Dc = [sb(kb, f"m2Dc{i}", [128, ML_H, L], F32) for i in range(2)]
        ecb = [sb(kb, f"m2ecb{i}", [128, ML_H, L], F32) for i in range(2)]
        Wt = [sb(kb, f"m2Wt{i}", [128, ML_H, L], BF16) for i in range(2)]
        qsT = [sb(kb, f"m2qsT{i}", [128, ML_H, L], BF16) for i in range(2)]
        kw = [sb(kb, f"m2kw{i}", [128, ML_H, ML_DK], BF16) for i in range(2)]
        Cf = sb(kb, "m2Cf", [128, ML_H, ML_DV + 1], F32)
        Cb = sb(kb, "m2Cb", [128, ML_H, ML_DV + 1], BF16)
        num = sb(kb, "m2num", [128, ML_H, ML_DV], F32)
        den = sb(kb, "m2den", [128, 2, ML_H], F32)
        hfl = sb(kb, "m2hf", [128, ML_H, ML_DV], F32)
        sq = sb(kb, "m2sq", [128, ML_H, ML_DV], F32)
        so = sb(kb, "m2so", [128, D_MODEL], BF16)
        gN = sb(kb, "m2gN", [128, D_MODEL], F32)
        hhb = sb(kb, "m2hhb", [128, D_MODEL], BF16)
        hhT_sb = [sb(kb, f"m2hhT{i}", [128, 8, L], BF16) for i in range(2)]
        ss = sb(kb, "m2ss", [128, 2, ML_H], F32)
        kb.dma("sp", gN[:, :], w["hn"].partition_broadcast(128), writes=["gN"])
        for i in range(2):
            kb.op("pool", lambda: nc.gpsimd.memset(Vx[i][:, :, ML_DV:ML_DV + 1], 1.0), writes=[("Vx1", i)])
        ci = 0
        for d in range(2):
            kb.op("dve", lambda: nc.vector.memset(Cf[:, :, :], 0.0), writes=[("Cf", h_) for h_ in range(ML_H)])
            kb.op("pool", lambda: nc.gpsimd.memset(Cb[:, :, :], 0.0), writes=[("Cb", h_) for h_ in range(ML_H)])
            order = range(nchunk) if d == 0 else reversed(range(nchunk))
            last = L - 1 if d == 0 else 0
            for c in order:
                c0 = c * L
                i2 = ci % 2; ci += 1
                it = c0 // NT
                kb.dma("sp", qT[i2][:, :, :], qv[:, :, c0:c0 + L], reads=[("qkd", it, b_) for b_ in range(8)], writes=[("qT", i2)])
                kb.dma("sp", kT[i2][:, :, :], kv[:, :, c0:c0 + L], reads=[("qkd", it, b_) for b_ in range(8, 16)], writes=[("kT", i2)])
                kb.dma("sp", ktm[i2][:, :, :], scr["ktm"][c0:c0 + L, :].rearrange("t (h k) -> t h k", h=ML_H), reads=[("ktmd", c, 0), ("ktmd", c, 1)], writes=[("ktm", i2)])
                kb.dma("sp", Vx[i2][:, :, 0:ML_DV], scr["vtm"][c0:c0 + L, :].rearrange("t (h k) -> t h k", h=ML_H),
                       reads=[("vtmd", c, j_) for j_ in range(4)], writes=[("Vx", i2)])
                kb.dma("sp", cdt[i2][:, 0, :], scr["cumT"][d * 8:(d + 1) * 8, c0:c0 + L], reads=[("cumd", it, d)], writes=[("cdt", i2)])
                kb.dma("sp", cdt[i2][:, 1, :], scr["eiT"][d * 8:(d + 1) * 8, c0:c0 + L], reads=[("eid", it, d)], writes=[("cdt", i2)])
                kb.dma("sp", cb[i2][:, :, :], scr["cumT"][d * 8:(d + 1) * 8, c0:c0 + L].partition_broadcast(128), reads=[("cumd", it, d)], writes=[("cb", i2)])
                if d == 1:
                    kb.dma("sp", hfl[:, :, :], scr["hf"][c0:c0 + L, :].rearrange("t (h k) -> t h k", h=ML_H), reads=[("hfd", c)], writes=["hfl"])
                    kb.dma("sp", so[:, :], scr["so"][c0:c0 + L, :], reads=[("sod", c, j_) for j_ in range(4)], writes=["so"])
                cd_ = cdt[i2]
                kb.op("act", lambda: nc.scalar.activation(out=wT[:, :], in_=cd_[:, 0, :], func=AF.Exp, bias=cd_[:, 0, last:last + 1], scale=-1.0),
                      reads=[("cdt", i2)], writes=["wT"])
                kb.op("dve", lambda: nc.vector.tensor_tensor(out=wT[:, :], in0=wT[:, :], in1=cd_[:, 1, :], op=ALU.mult), reads=["wT", ("cdt", i2)], writes=["wT"])
                kb.op("act", lambda: nc.scalar.activation(out=eg[:, 0:1], in_=cd_[:, 0, last:last + 1], func=AF.Exp), reads=[("cdt", i2)], writes=["eg"])
                kb.op("dve", lambda: nc.vector.tensor_scalar(out=dg[:, :], in0=cx.identf[0:8, 0:8], scalar1=eg[:, 0:1], scalar2=None, op0=ALU.mult),
                      reads=["eg", "identf"], writes=["dg"])
                ps, pt = pp.next()
                kb.op("pe", lambda: nc.tensor.matmul(ps[:, 0:8], lhsT=cd_[:, 0, :], rhs=cx.identf[0:8, 0:8], start=True, stop=True),
                      reads=[("cdt", i2), "identf"], writes=[pt], inc=False)
                kb.op("pe", lambda: nc.tensor.matmul(ps[:, 8:16], lhsT=cd_[:, 1, :], rhs=cx.identf[0:8, 0:8], start=True, stop=True),
                      reads=[("cdt", i2)], writes=[pt], inc=False)
                kb.op("pe", lambda: nc.tensor.matmul(ps[:, 16:24], lhsT=wT[:, :], rhs=cx.identf[0:8, 0:8], start=True, stop=True),
                      reads=["wT"], writes=[pt], inc=False)
                kb.op("pe", lambda: nc.tensor.matmul(ps[:, 24:32], lhsT=cx.onesf[0:8, :], rhs=dg[:, :], start=True, stop=True),
                      reads=["dg", "onesf"], writes=[pt], inc=True)
                TM = tm[i2]
                kb.op("act", lambda: nc.scalar.copy(out=TM[:, :, :].rearrange("p a b -> p (a b)"), in_=ps[:, 0:32]), reads=[pt], writes=[("tm", i2)])
                qk = []
                for half in range(2):
                    psq, ptq = pp.next()
                    for hh_ in range(4):
                        h_ = half * 4 + hh_
                        kb.op("pe", lambda: nc.tensor.matmul(psq[:, hh_ * L:(hh_ + 1) * L], lhsT=kT[i2][:, h_, :], rhs=qT[i2][:, h_, :], start=True, stop=True),
                              reads=[("kT", i2), ("qT", i2)], writes=[ptq], inc=(hh_ == 3))
                    qk.append((psq, ptq))
                cum_b = TM[:, 0, :].unsqueeze(2).to_broadcast([128, ML_H, L])
                ei_b = TM[:, 1, :].unsqueeze(2).to_broadcast([128, ML_H, L])
                D_ = Dc[i2]
                kb.op("dve", lambda: nc.vector.tensor_tensor(out=D_[:, :, :], in0=cb[i2][:, :, :], in1=cum_b, op=ALU.subtract),
                      reads=[("cb", i2), ("tm", i2)], writes=[("Dc", i2)])
                kb.op("dve", lambda: nc.vector.tensor_scalar(out=D_[:, :, :], in0=D_[:, :, :], scalar1=0.0, scalar2=None, op0=ALU.min), reads=[("Dc", i2)], writes=[("Dc", i2)])
                kb.op("act", lambda: nc.scalar.activation(out=D_[:, :, :], in_=D_[:, :, :], func=AF.Exp), reads=[("Dc", i2)], writes=[("Dc", i2)])
                kb.op("dve", lambda: nc.vector.tensor_tensor(out=D_[:, :, :], in0=D_[:, :, :], in1=cx.mask[d][:, :].unsqueeze(1).to_broadcast([128, ML_H, L]), op=ALU.mult),
                      reads=[("Dc", i2), ("mask", d)], writes=[("Dc", i2)])
                kb.op("dve", lambda: nc.vector.tensor_tensor(out=D_[:, :, :], in0=D_[:, :, :], in1=ei_b, op=ALU.mult), reads=[("Dc", i2), ("tm", i2)], writes=[("Dc", i2)])
                for half in range(2):
                    psq, ptq = qk[half]
                    kb.op("dve", lambda: nc.vector.tensor_tensor(out=Wt[i2][:, half * 4:(half + 1) * 4, :], in0=D_[:, half * 4:(half + 1) * 4, :],
                                                                 in1=psq[:, :512].rearrange("p (h l) -> p h l", h=4), op=ALU.mult),
                          reads=[("Dc", i2), ptq], writes=[("Wt", i2)])
                kb.op("act", lambda: nc.scalar.activation(out=ecb[i2][:, :, :], in_=cb[i2][:, :, :], func=AF.Exp), reads=[("cb", i2)], writes=[("ecb", i2)])
                kb.op("dve", lambda: nc.vector.tensor_tensor(out=qsT[i2][:, :, :], in0=ecb[i2][:, :, :], in1=qT[i2][:, :, :], op=ALU.mult),
                      reads=[("ecb", i2), ("qT", i2)], writes=[("qsT", i2)])
                w_b = TM[:, 2, :].unsqueeze(2).to_broadcast([128, ML_H, ML_DK])
                kb.op("dve", lambda: nc.vector.tensor_tensor(out=kw[i2][:, :, :], in0=ktm[i2][:, :, :], in1=w_b, op=ALU.mult),
                      reads=[("ktm", i2), ("tm", i2)], writes=[("kw", i2)])
                for h_ in range(ML_H):
                    ps_o, pt_o = pp.next()
                    kb.op("pe", lambda: nc.tensor.matmul(ps_o[:, :ML_DV + 1], lhsT=Wt[i2][:, h_, :], rhs=Vx[i2][:, h_, :], start=True, stop=False),
                          reads=[("Wt", i2), ("Vx", i2), ("Vx1", i2)], writes=[pt_o], inc=False)
                    kb.op("pe", lambda: nc.tensor.matmul(ps_o[:, :ML_DV + 1], lhsT=qsT[i2][:, h_, :], rhs=Cb[:, h_, :], start=False, stop=True),
                          reads=[("qsT", i2), ("Cb", h_)], writes=[pt_o], inc=True)
                    kb.op("act", lambda: nc.scalar.copy(out=num[:, h_, :], in_=ps_o[:, 0:ML_DV]), reads=[pt_o], writes=[("num", h_)])
                    kb.op("act", lambda: nc.scalar.activation(out=den[:, 0, h_:h_ + 1], in_=ps_o[:, ML_DV:ML_DV + 1], func=AF.Abs), reads=[pt_o], writes=[("den", h_)])
                    ps_s, pt_s = pp.next()
                    kb.op("pe", lambda: nc.tensor.matmul(ps_s[:, :ML_DV + 1], lhsT=kw[i2][:, h_, :], rhs=Vx[i2][:, h_, :], start=True, stop=True),
                          reads=[("kw", i2), ("Vx", i2), ("Vx1", i2)], writes=[pt_s], inc=True)
                    kb.op("dve", lambda: nc.vector.scalar_tensor_tensor(out=Cf[:, h_, :], in0=Cf[:, h_, :], scalar=TM[:, 3, h_:h_ + 1], in1=ps_s[:, :ML_DV + 1],
                                                                        op0=ALU.mult, op1=ALU.add),
                          reads=[("Cf", h_), ("tm", i2), pt_s], writes=[("Cf", h_)])
                    kb.op("act", lambda: nc.scalar.copy(out=Cb[:, h_, :], in_=Cf[:, h_, :]), reads=[("Cf", h_)], writes=[("Cb", h_)])
                allnum = [("num", h_) for h_ in range(ML_H)]
                allden = [("den", h_) for h_ in range(ML_H)]
                kb.op("dve", lambda: nc.vector.tensor_scalar(out=den[:, 1, :], in0=den[:, 0, :], scalar1=1.0, scalar2=None, op0=ALU.max), reads=allden, writes=["den1"])
                kb.op("dve", lambda: nc.vector.reciprocal(out=den[:, 1, :], in_=den[:, 1, :]), reads=["den1"], writes=["den1"])
                r_b = den[:, 1, :].unsqueeze(2).to_broadcast([128, ML_H, ML_DV])
                kb.op("dve", lambda: nc.vector.tensor_tensor(out=num[:, :, :], in0=num[:, :, :], in1=r_b, op=ALU.mult), reads=allnum + ["den1"], writes=allnum)
                if d == 0:
                    kb.dma("sp", scr["hf"][c0:c0 + L, :].rearrange("t (h k) -> t h k", h=ML_H), num[:, :, :], reads=allnum, writes=[("hfd", c)])
                else:
                    kb.op("dve", lambda: nc.vector.tensor_tensor(out=num[:, :, :], in0=num[:, :, :], in1=hfl[:, :, :], op=ALU.add), reads=allnum + ["hfl"], writes=allnum)
                    kb.op("pool", lambda: nc.gpsimd.tensor_tensor(out=sq[:, :, :], in0=num[:, :, :], in1=num[:, :, :], op=ALU.mult), reads=allnum, writes=["sq2"])
                    kb.op("dve", lambda: nc.vector.tensor_reduce(out=ss[:, 0, :], in_=sq[:, :, :], axis=AX.X, op=ALU.add), reads=["sq2"], writes=["ss"])
                    kb.op("dve", lambda: nc.vector.tensor_scalar(out=ss[:, 1, :], in0=ss[:, 0, :], scalar1=1.0 / ML_DV, scalar2=EPS, op0=ALU.mult, op1=ALU.add),
                          reads=["ss"], writes=["ss"])
                    emit_rsqrt_inplace(kb, ss[:, 1, :], "ss")
                    rs_b = ss[:, 1, :].unsqueeze(2).to_broadcast([128, ML_H, ML_DV])
                    kb.op("dve", lambda: nc.vector.tensor_tensor(out=num[:, :, :], in0=num[:, :, :], in1=rs_b, op=ALU.mult), reads=allnum + ["ss"], writes=allnum)
                    numf = num[:, :, :].rearrange("p h k -> p (h k)")
                    kb.op("pool", lambda: nc.gpsimd.tensor_tensor(out=numf, in0=numf, in1=gN[:, :], op=ALU.mult), reads=allnum + ["gN"], writes=allnum)
                    kb.op("dve", lambda: nc.vector.tensor_tensor(out=hhb[:, :], in0=numf, in1=so[:, :], op=ALU.mult), reads=allnum + ["so"], writes=["hhb"])
                    for grp in range(2):
                        pb = cx.psb[cx.psbi[0] % 2]; pbtok = ("psb", cx.psbi[0] % 2); cx.psbi[0] += 1
                        for j in range(8):
                            blk = grp * 8 + j
                            kb.op("pe", lambda: nc.tensor.transpose(out=pb[:, j * 128:(j + 1) * 128], in_=hhb[:, blk * 128:(blk + 1) * 128], identity=cx.ident[:, :]),
                                  reads=["hhb", "ident"], writes=[pbtok], inc=(j == 7))
                        yo = hhT_sb[grp % 2]
                        kb.op("act", lambda: nc.scalar.copy(out=yo[:, :, :].rearrange("p a b -> p (a b)"), in_=pb[:, :]), reads=[pbtok], writes=[("hhT", grp % 2)])
                        kb.dma("sp", hhv[:, grp * 8:(grp + 1) * 8, c0:c0 + L], yo[:, :, :], reads=[("hhT", grp % 2)], writes=[("hhd", c, grp)])
    with kb.phase():
        tile_ctx(kb, cx)
        hT = cx.hT
        yT = sb(kb, "m3yT", [128, NCH, NT], BF16)
        for it in range(nt):
            t0 = it * NT
            for c in range(NCH):
                kb.dma("sp", hT[:, c, :NT], src_v[:, c, src_pad + t0:src_pad + t0 + NT], writes=["hT"])
            kb.dma("sp", yT[:, :, :], hhv[:, :, t0:t0 + NT], writes=[("yT", k_) for k_ in range(NCH)])
            emit_outproj_residual(kb, pp, yT, lambda k_: ("yT", k_), NCH, w["wout"], cx.wO, cx.wOi, hT, "hT", 0, NT,
                                  dst_v, dst_pad + t0, cx.ho_bufs, out_toks, "ml")


def prep_mlstm_weights(w_in, gate_bias, head_norm, w_out, g):
    f = lambda a: np.ascontiguousarray(a, dtype=np.float32)
    def fm_blocks(wm):
        n = wm.shape[1] // 128
        return f(wm.reshape(-1, 128, n, 128).transpose(2, 1, 0, 3))
    def tm_groups(wm):
        n = wm.shape[1] // 512
        return wm.reshape(NCH, 128, n, 512).transpose(2, 1, 0, 3)
    wqk = fm_blocks(w_in[:, 0:2048])
    wtm = np.concatenate([tm_groups(w_in[:, 1024:2048]), tm_groups(w_in[:, 2048:4096]), tm_groups(w_in[:, 4096:6144])], 0)
    wg = w_in[:, 6144:6176].reshape(NCH, 128, 32).transpose(1, 0, 2)
    return dict(wqk=wqk, wtm=f(wtm), wg=f(wg), gbias=f(gate_bias.T), hn=f(head_norm.reshape(1, D_MODEL)),
                wout=fm_blocks(w_out), g=f(g.reshape(NCH, 128).T))


ML_W_SHAPES = dict(wqk=[16, 128, NCH, 128], wtm=[10, 128, NCH, 512], wg=[128, NCH, 32], gbias=[8, 4], hn=[1, D_MODEL],
                   wout=[NCH, 128, NCH, 128], g=[128, NCH])


def mlstm_scratch(nc, T, pfx=""):
    dtn = lambda n, s, dt: nc.dram_tensor(pfx + n, s, dt).ap()
    return dict(qT=dtn("qT", [1024, T], BF16), kT=dtn("kT", [1024, T], BF16), ktm=dtn("ktm", [T, 1024], BF16),
                vtm=dtn("vtm", [T, D_MODEL], BF16), so=dtn("so", [T, D_MODEL], BF16), cumT=dtn("mcumT", [16, T], F32),
                eiT=dtn("meiT", [16, T], F32), hf=dtn("mhf", [T, D_MODEL], F32), hhT=dtn("hhT", [D_MODEL, T], BF16))


def build_mlstm_test(T, NT=512):
    nc = bass.Bass("TRN2", target_bir_lowering=False)
    hin = nc.dram_tensor("hin", [D_MODEL, T], F32, kind="ExternalInput").ap()
    w = {k: nc.dram_tensor(k, shp, F32, kind="ExternalInput").ap() for k, shp in ML_W_SHAPES.items()}
    hout = nc.dram_tensor("hout", [D_MODEL, T], F32, kind="ExternalOutput").ap()
    scr = mlstm_scratch(nc, T)
    with contextlib.ExitStack() as stack:
        kb = KB(nc, stack)
        pp = PsumPool(kb, nbanks=6)
        cx = make_ctx(kb, NT)
        outs = []
        emit_mlstm(kb, pp, cx, T, hin.rearrange("(c p) t -> p c t", p=128), 0, hout.rearrange("(c p) t -> p c t", p=128), 0, w, scr, outs, NT)
        kb.finish(outs)
    return nc


def emit_ffn(kb, pp, cx, T, src_v, src_pad, dst_v, dst_pad, w, out_toks, gfin=None, NT=512):
    nc = kb.nc
    nt = T // NT
    W = NT + 2
    with kb.phase():
        tile_ctx(kb, cx)
        hT, uT, g_sb = cx.hT, cx.uT, cx.g_sb
        actT = sb(kb, "f_actT", [128, FFN_BLK, NT], BF16)
        cw_sb = sb(kb, "f_cw", [128, FFN_BLK, 4], F32)
        gate_sb = [sb(kb, f"f_gate{i}", [128, W], F32) for i in range(2)]
        acc_sb = [sb(kb, f"f_acc{i}", [128, NT], F32) for i in range(2)]
        sil_sb = [sb(kb, f"f_sil{i}", [128, NT], F32) for i in range(2)]
        kb.dma("sp", g_sb[:, :], w["g"][:, :], writes=["g"])
        kb.dma("sp", cw_sb[:, :, :], w["cw"], writes=["cw"])
        if gfin is not None:
            gf_sb = sb(kb, "f_gf", [128, NCH], F32)
            hn = sb(kb, "f_hn", [128, NCH, NT], F32)
            kb.dma("sp", gf_sb[:, :], gfin[:, :], writes=["gf"])
        for it in range(nt):
            t0 = it * NT
            for c in range(NCH):
                kb.dma("sp", hT[:, c, :W], src_v[:, c, src_pad + t0 - 1:src_pad + t0 - 1 + W], writes=["hT"])
            emit_rmsnorm_fm(kb, pp, hT, "hT", g_sb, uT, "uT", cx.ones_bf, W, cx.sq_bufs, (cx.rstd_t, "rstd"))
            for j in range(FFN_BLK):
                wb = cx.wA[cx.wAi[0] % 2]; wtok = ("wA", cx.wAi[0] % 2); cx.wAi[0] += 1
                kb.dma("pool", wb[:, :, :], w["wup"][j], writes=[wtok])
                (ps_g, tg), (ps_x, tx) = emit_proj_block(kb, pp, wb, wtok, 0, uT, "uT", [(0, NT), (NT, 2)])
                (ps_v, tv), = emit_proj_block(kb, pp, wb, wtok, 128, uT, "uT", [(1, NT)])
                gs = gate_sb[j % 2]; gtok = ("gate", j % 2)
                kb.op("act", lambda: nc.scalar.copy(out=gs[:, 0:NT], in_=ps_g[:, :NT]), reads=[tg], writes=[gtok])
                kb.op("act", lambda: nc.scalar.copy(out=gs[:, NT:NT + 2], in_=ps_x[:, :2]), reads=[tx], writes=[gtok])
                ac = acc_sb[j % 2]; atok = ("acc", j % 2)
                kb.op("dve", lambda: nc.vector.tensor_scalar(out=ac[:, :], in0=gs[:, 0:NT], scalar1=cw_sb[:, j, 0:1], scalar2=cw_sb[:, j, 3:4],
                                                             op0=ALU.mult, op1=ALU.add), reads=[gtok, "cw"], writes=[atok])
                for tap in (1, 2):
                    kb.op("dve", lambda: nc.vector.scalar_tensor_tensor(out=ac[:, :], in0=gs[:, tap:tap + NT], scalar=cw_sb[:, j, tap:tap + 1],
                                                                        in1=ac[:, :], op0=ALU.mult, op1=ALU.add), reads=[gtok, atok], writes=[atok])
                sl = sil_sb[j % 2]; stok = ("sil", j % 2)
                kb.op("act", lambda: nc.scalar.activation(out=sl[:, :], in_=ac[:, :], func=AF.Silu), reads=[atok], writes=[stok])
                kb.op("dve", lambda: nc.vector.tensor_tensor(out=actT[:, j, :], in0=sl[:, :], in1=ps_v[:, :NT], op=ALU.mult),
                      reads=[stok, tv], writes=[("actT", j)])
            if gfin is None:
                emit_outproj_residual(kb, pp, actT, lambda k_: ("actT", k_), FFN_BLK, w["wdn"], cx.wO, cx.wOi, hT, "hT", 1, NT,
                                      dst_v, dst_pad + t0, cx.ho_bufs, out_toks, "ffn")
            else:
                for mb in range(NCH):
                    wd = cx.wO[cx.wOi[0] % 2]; dtok = ("ffnw", cx.wOi[0] % 2); cx.wOi[0] += 1
                    kb.dma("pool", wd[:, :, :], w["wdn"][mb], writes=[dtok])
                    ps_o, to = pp.next()
                    for k_ in range(FFN_BLK):
                        kb.op("pe", lambda: nc.tensor.matmul(ps_o[:, :NT], lhsT=wd[:, k_, :], rhs=actT[:, k_, :], start=(k_ == 0), stop=(k_ == FFN_BLK - 1)),
                              reads=[dtok, ("actT", k_)], writes=[to], inc=(k_ == FFN_BLK - 1))
                    kb.op("dve", lambda: nc.vector.tensor_tensor(out=hn[:, mb, :], in0=ps_o[:, :NT], in1=hT[:, mb, 1:NT + 1], op=ALU.add),
                          reads=[to, "hT"], writes=["hn"])
                ps, ps_tok = pp.next()
                for c in range(NCH):
                    sq, sq_tok = cx.sq_bufs[c % len(cx.sq_bufs)]
                    kb.op("act", lambda: nc.scalar.activation(out=sq[:, :NT], in_=hn[:, c, :], func=AF.Square), reads=["hn"], writes=[sq_tok])
                    kb.op("pe", lambda: nc.tensor.matmul(ps[:, :NT], lhsT=cx.ones_bf[:, :], rhs=sq[:, :NT], start=(c == 0), stop=(c == NCH - 1)),
                          reads=[sq_tok, "ones"], writes=[ps_tok], inc=True)
                rstd_t = cx.rstd_t
                kb.op("dve", lambda: nc.vector.tensor_scalar(out=rstd_t[:, :NT], in0=ps[:, :NT], scalar1=1.0 / D_MODEL, scalar2=EPS, op0=ALU.mult, op1=ALU.add),
                      reads=[ps_tok], writes=["rstd"])
                emit_rsqrt_inplace(kb, rstd_t[:, :NT], "rstd")
                for c in range(NCH):
                    ho = cx.ho_bufs[c % 3]; htok = ("ho", c % 3)
                    kb.op("dve", lambda: nc.vector.scalar_tensor_tensor(out=ho[:, :], in0=hn[:, c, :], scalar=gf_sb[:, c:c + 1], in1=rstd_t[:, :NT],
                                                                        op0=ALU.mult, op1=ALU.mult), reads=["hn", "rstd", "gf"], writes=[htok])
                    otok = ("fout", it, c)
                    kb.dma("sp", dst_v[:, c, dst_pad + t0:dst_pad + t0 + NT], ho[:, :], reads=[htok], writes=[otok])
                    out_toks.append(otok)


FFN_W_SHAPES = dict(wup=[FFN_BLK, 128, NCH, 256], wdn=[NCH, 128, FFN_BLK, 128], cw=[128, FFN_BLK, 4], g=[128, NCH])
XA_W_SHAPES = dict(wq=[NCH, 128, NCH, 128], wk=[NCH, 128, NCH, 128], wv=[4, 128, NCH, 512], wo=[NCH, 128, NCH, 128], g=[128, NCH])
RG_W_SHAPES = dict(win=[32, 128, NCH, 128], gw=[128, 2, RG_BLOCKS, 2, 512], cw=[128, NCH, 5], gb=[128, 2, NCH, 2], lam=[128, 2, NCH],
                   wout=[NCH, 128, NCH, 128], g=[128, NCH])
DEPTH = 4
PAD = 2


def rglru_scratch(nc, T, pfx=""):
    return dict(ggT=nc.dram_tensor(pfx + "ggT", [D_MODEL, T], BF16).ap(), hfT=nc.dram_tensor(pfx + "hfT", [D_MODEL, T], F32).ap(),
                abT=nc.dram_tensor(pfx + "abT", [D_MODEL, T], F32).ap(), bxbT=nc.dram_tensor(pfx + "bxbT", [D_MODEL, T], F32).ap())


def build_prog(T, plist, NT=512, depth=DEPTH):
    nc = bass.Bass("TRN2", target_bir_lowering=False)
    ext = lambda n, s: nc.dram_tensor(n, s, F32, kind="ExternalInput").ap()
    xT = ext("xT", [D_MODEL, T + 2 * PAD])
    has_xa = any(k == "xa" for _, k in plist)
    has_fin = any(k == "ffn" and i == depth - 1 for i, k in plist)
    if has_xa:
        memT = ext("memT", [D_MODEL, N_MEM])
        gmem = ext("gmem", [128, NCH])
    gfin = ext("gfin", [128, NCH]) if has_fin else None
    lw = {}
    for (i, kind) in plist:
        if kind == "mix":
            shapes = [SSD_W_SHAPES, ML_W_SHAPES, RG_W_SHAPES][i % 3]
        elif kind == "xa":
            shapes = XA_W_SHAPES
        else:
            shapes = FFN_W_SHAPES
        lw[(i, kind)] = {k: ext(f"l{i}_{kind}_{k}", s) for k, s in shapes.items()}
    outT = nc.dram_tensor("outT", [D_MODEL, T], F32, kind="ExternalOutput").ap()
    nint = max(0, len(plist) - 1)
    hbufs = [nc.dram_tensor(f"hI{j}", [D_MODEL, T + 2 * PAD], F32).ap() for j in range(min(2, nint))]
    kinds = set(i % 3 for i, k in plist if k == "mix")
    scr_ssd = ssd_scratch(nc, T, "s_") if 0 in kinds else None
    scr_ml = mlstm_scratch(nc, T, "m_") if 1 in kinds else None
    scr_rg = rglru_scratch(nc, T, "r_") if 2 in kinds else None
    view = lambda ap: ap.rearrange("(c p) t -> p c t", p=128)
    with contextlib.ExitStack() as stack:
        kb = KB(nc, stack)
        pp = PsumPool(kb, nbanks=6)
        cx = make_ctx(kb, NT)
        outs = []
        if hbufs:
            with kb.phase():
                z = sb(kb, "zpad", [128, NCH, PAD], F32)
                kb.op("dve", lambda: nc.vector.memset(z[:, :, :], 0.0), writes=["z"])
                for hb in hbufs:
                    kb.dma("sp", view(hb)[:, :, 0:PAD], z[:, :, :], reads=["z"], writes=[("zp", id(hb), 0)])
                    kb.dma("sp", view(hb)[:, :, T + PAD:T + 2 * PAD], z[:, :, :], reads=["z"], writes=[("zp", id(hb), 1)])
        if has_xa:
            emit_mem_norm(kb, pp, cx, memT, gmem)
        cur, cur_pad = xT, PAD
        for n, (i, kind) in enumerate(plist):
            lastp = (n == len(plist) - 1)
            dst, dpad = (outT, 0) if lastp else (hbufs[n % 2], PAD)
            w = lw[(i, kind)]
            if kind == "mix":
                if i % 3 == 0:
                    emit_ssd(kb, pp, cx, T, view(cur), cur_pad, view(dst), dpad, w, scr_ssd, outs, NT)
                elif i % 3 == 1:
                    emit_mlstm(kb, pp, cx, T, view(cur), cur_pad, view(dst), dpad, w, scr_ml, outs, NT)
                else:
                    emit_rglru(kb, pp, cx, T, view(cur), cur_pad, view(dst), dpad, w, scr_rg, outs, NT)
            elif kind == "xa":
                emit_xattn(kb, pp, cx, T, view(cur), cur_pad, view(dst), dpad, w, outs, NT)
            else:
                emit_ffn(kb, pp, cx, T, view(cur), cur_pad, view(dst), dpad, w, outs, gfin if i == depth - 1 else None, NT)
            cur, cur_pad = dst, dpad
        kb.finish(outs)
    return nc


def all_phases(depth=DEPTH):
    return [(i, k) for i in range(depth) for k in ("mix", "xa", "ffn")]


def build_full(T, NT=512, depth=DEPTH):
    return build_prog(T, all_phases(depth), NT, depth)


def prep_ffn_w(w_up, conv_w, conv_b, w_down, g):
    d = prep_ffn_weights(w_up, conv_w, conv_b, w_down, g)
    return dict(wup=d["wup"], wdn=d["wdn"], cw=d["cw"], g=d["gnorm"])


def prep_all(inp, depth=DEPTH):
    f = lambda a: np.ascontiguousarray(a, dtype=np.float32)
    col = lambda v: f(np.asarray(v).reshape(NCH, 128).T)
    out = dict(gmem=col(inp["mem_norm"]), gfin=col(inp["final_norm"]))
    for i in range(depth):
        kind, j = i % 3, i // 3
        g = np.asarray(inp["mix_norm"][i])
        if kind == 0:
            mix = prep_ssd_weights(*[np.asarray(inp[k][j]) for k in ["ssd_w_in", "ssd_conv_w", "ssd_conv_b", "ssd_dt_bias", "ssd_a_log",
                                                                      "ssd_d_skip", "ssd_norm", "ssd_w_out"]], g)
        elif kind == 1:
            mix = prep_mlstm_weights(*[np.asarray(inp[k][j]) for k in ["mlstm_w_in", "mlstm_gate_bias", "mlstm_head_norm", "mlstm_w_out"]], g)
        else:
            mix = prep_rglru_weights(*[np.asarray(inp[k][j]) for k in ["rglru_w_in", "rglru_conv_w", "rglru_conv_b", "rglru_gate_w",
                                                                        "rglru_gate_b", "rglru_lambda", "rglru_w_out"]], g)
        xa = prep_xattn_weights(np.asarray(inp["xattn_wq"][i]), np.asarray(inp["xattn_wkv"][i]), np.asarray(inp["xattn_wo"][i]),
                                np.asarray(inp["xattn_norm"][i]))
        ff = prep_ffn_w(np.asarray(inp["ffn_w_up"][i]), np.asarray(inp["ffn_conv_w"][i]), np.asarray(inp["ffn_conv_b"][i]),
                        np.asarray(inp["ffn_w_down"][i]), np.asarray(inp["ffn_norm"][i]))
        for k, v in mix.items():
            out[f"l{i}_mix_{k}"] = v
        for k, v in xa.items():
            out[f"l{i}_xa_{k}"] = v
        for k, v in ff.items():
            out[f"l{i}_ffn_{k}"] = v
    return out


FUSED_GROUPS = None


def run_full(inputs, T, depth=DEPTH, NT=512, groups=None):
    x = np.asarray(inputs["x"])
    mem = np.asarray(inputs["mem"])
    B = x.shape[0]
    shared = prep_all(inputs, depth)
    if groups is None:
        groups = [all_phases(depth)]
    cur = [np.ascontiguousarray(x[b, :T].T, dtype=np.float32) for b in range(B)]
    for plist in groups:
        nc = build_prog(T, plist, NT, depth)
        names = set()
        for alloc_name in shared:
            names.add(alloc_name)
        need = lambda k: any(k.startswith(f"l{i}_{kind}_") for (i, kind) in plist)
        in_maps = []
        for b in range(B):
            xp = np.zeros((D_MODEL, T + 2 * PAD), np.float32)
            xp[:, PAD:PAD + T] = cur[b]
            m = {k: v for k, v in shared.items() if need(k)}
            if any(k == "xa" for _, k in plist):
                m["memT"] = np.ascontiguousarray(mem[b].T, dtype=np.float32)
                m["gmem"] = shared["gmem"]
            if any(k == "ffn" and i == depth - 1 for i, k in plist):
                m["gfin"] = shared["gfin"]
            m["xT"] = xp
            in_maps.append(m)
        res = run_bass_kernel_spmd(nc, in_maps, core_ids=list(range(B)))
        cur = [res.results[b]["outT"] for b in range(B)]
        if VERBOSE:
            print("launch done", plist, float(np.abs(cur[0]).mean()), flush=True)
    out = np.stack([cur[b].T for b in range(B)], 0)
    return np.ascontiguousarray(out, dtype=np.float32)


VERBOSE = False
SPLIT = True


def kernel(**inputs):
    groups = [[p] for p in all_phases()] if SPLIT else None
    return run_full(inputs, T=16384, groups=groups)
```

```python
import contextlib
import numpy as np
import concourse.bass as bass
import concourse.mybir as mybir
from concourse.bass_utils import run_bass_kernel_spmd

F32 = mybir.dt.float32
BF16 = mybir.dt.bfloat16
ALU = mybir.AluOpType
AF = mybir.ActivationFunctionType
AX = mybir.AxisListType

D_MODEL = 2048
NCH = 16
FFN_DIM = 5632
FFN_BLK = 44
EPS = 1e-6
import os
SEM_LIMIT = int(os.environ.get('KB_SEM_LIMIT', '24000'))
KB_STATS = {}


class KB:
    def __init__(self, nc, stack):
        self.nc = nc
        self.stack = stack
        self.semstack = stack
        self.eng = dict(pe=nc.tensor, dve=nc.vector, act=nc.scalar, pool=nc.gpsimd, sp=nc.sync)
        self.esem = {}
        self.ecnt = {}
        self.seen = {e: {} for e in self.eng}
        self.lastw = {}
        self.reads = {}
        self.dq = {}
        self.nsem = 0
        self.sems = {}
        for e in ("pe", "dve", "act", "pool"):
            self._new_esem(e)
        self.pending_noinc = {e: False for e in self.eng}

    def new_sem(self, name):
        s = self.semstack.enter_context(self.nc.semaphore(f"{name}_{self.nsem}"))
        self.nsem += 1
        self.sems[id(s)] = s
        return s

    def _new_esem(self, e):
        self.esem[e] = self.new_sem("e" + e)
        self.ecnt[e] = 0
        KB_STATS["nsem"] = self.nsem
        KB_STATS[e] = KB_STATS.get(e, 0) + 1

    def _waits(self, e, reads, writes):
        need = {}
        def add(ev):
            if ev is None:
                return
            s, v = ev
            if need.get(id(s), (None, 0))[1] < v:
                need[id(s)] = (s, v)
        for t in reads:
            add(self.lastw.get(t))
        for t in writes:
            add(self.lastw.get(t))
            for ev in self.reads.get(t, {}).values():
                add(ev)
        engine = self.eng[e]
        for sid, (s, v) in need.items():
            if e == "pe" and s is self.esem["pe"]:
                continue
            if self.seen[e].get(sid, 0) >= v:
                continue
            engine.wait_ge(s, v)
            self.seen[e][sid] = v

    def _record(self, ev, reads, writes):
        for t in reads:
            d = self.reads.setdefault(t, {})
            s, v = ev
            if d.get(id(s), (None, 0))[1] < v:
                d[id(s)] = ev
        for t in writes:
            self.lastw[t] = ev
            self.reads[t] = {}

    def op(self, e, fn, reads=(), writes=(), inc=True):
        self._waits(e, reads, writes)
        if self.ecnt[e] >= SEM_LIMIT and not self.pending_noinc[e]:
            self._new_esem(e)
        ins = fn()
        ev = (self.esem[e], self.ecnt[e] + 1)
        if inc:
            ins.then_inc(self.esem[e], 1)
            self.ecnt[e] += 1
            self.pending_noinc[e] = False
        else:
            self.pending_noinc[e] = True
        self._record(ev, reads, writes)
        return ins

    def dma(self, q, out, in_, reads=(), writes=(), nslots=8):
        st = self.dq.setdefault(q, dict(sems=[], vals=[], idx=0, old=[]))
        i = st["idx"] % nslots
        if len(st["sems"]) <= i:
            st["sems"].append(self.new_sem("d" + q))
            st["vals"].append(0)
        elif st["vals"][i] + 16 > SEM_LIMIT:
            st["old"].append((st["sems"][i], st["vals"][i]))
            engine = self.eng[q]
            if self.seen[q].get(id(st["sems"][i]), 0) < st["vals"][i]:
                engine.wait_ge(st["sems"][i], st["vals"][i])
                self.seen[q][id(st["sems"][i])] = st["vals"][i]
            st["sems"][i] = self.new_sem("d" + q)
            st["vals"][i] = 0
        st["idx"] += 1
        s = st["sems"][i]
        engine = self.eng[q]
        if st["vals"][i] > 0 and self.seen[q].get(id(s), 0) < st["vals"][i]:
            engine.wait_ge(s, st["vals"][i])
            self.seen[q][id(s)] = st["vals"][i]
        self._waits(q, reads, writes)
        ins = engine.dma_start(out=out, in_=in_)
        ins.then_inc(s, 16)
        st["vals"][i] += 16
        ev = (s, st["vals"][i])
        self._record(ev, reads, writes)
        return ins

    def barrier(self):
        evs = []
        for e in ("pe", "dve", "act", "pool"):
            if self.ecnt[e] > 0 or self.pending_noinc[e]:
                assert not self.pending_noinc[e], f"engine {e} has trailing non-inc instructions"
                evs.append((self.esem[e], self.ecnt[e]))
        for q, st in self.dq.items():
            for s_, v in list(zip(st["sems"], st["vals"])) + st["old"]:
                if v > 0:
                    evs.append((s_, v))
            st["old"] = []
        for e in self.eng:
            engine = self.eng[e]
            for (s_, v) in evs:
                if e in self.esem and s_ is self.esem.get(e):
                    continue
                if self.seen[e].get(id(s_), 0) >= v:
                    continue
                engine.wait_ge(s_, v)
                self.seen[e][id(s_)] = v
        self.lastw = {}
        self.reads = {}

    @contextlib.contextmanager
    def phase(self):
        outer = self.stack
        with contextlib.ExitStack() as st:
            self.stack = st
            try:
                yield
            finally:
                self.barrier()
                self.stack = outer

    def finish(self, out_tokens):
        self._waits("sp", out_tokens, out_tokens)


def _bcast_free(ap, n):
    return ap


class PsumPool:
    def __init__(self, kb, nbanks=8):
        self.kb = kb
        self.tiles = [kb.stack.enter_context(kb.nc.psum_tensor(f"ps{i}", [128, 512], F32)) for i in range(nbanks)]
        self.i = 0

    def next(self):
        t = self.tiles[self.i % len(self.tiles)]
        tok = ("ps", self.i % len(self.tiles))
        self.i += 1
        return t, tok


def psum_bf16(kb, name, cols=1024):
    return kb.stack.enter_context(kb.nc.psum_tensor(name, [128, cols], BF16))


def make_ident(kb, name="ident"):
    nc = kb.nc
    identf = sb(kb, name + "f", [128, 128], F32)
    ident = sb(kb, name, [128, 128], BF16)
    kb.op("pool", lambda: nc.gpsimd.memset(identf[:, :], 1.0), writes=["identf"])
    kb.op("pool", lambda: nc.gpsimd.affine_select(out=identf[:, :], in_=identf[:, :], pattern=[[-1, 128]],
                                                  compare_op=ALU.is_equal, fill=0.0, base=0, channel_multiplier=1),
          reads=["identf"], writes=["identf"])
    kb.op("dve", lambda: nc.vector.tensor_copy(out=ident[:, :], in_=identf[:, :]), reads=["identf"], writes=["ident"])
    return identf, ident


_SBN = [0]


def sb(kb, name, shape, dt):
    _SBN[0] += 1
    return kb.stack.enter_context(kb.nc.sbuf_tensor(f"{name}_{_SBN[0]}", list(shape), dt))


def emit_rsqrt_inplace(kb, ap, tok):
    nc = kb.nc
    kb.op("act", lambda: nc.scalar.activation(out=ap, in_=ap, func=AF.Sqrt), reads=[tok], writes=[tok])
    kb.op("dve", lambda: nc.vector.reciprocal(out=ap, in_=ap), reads=[tok], writes=[tok])


def emit_rmsnorm_fm(kb, pp, hT, h_tok, g_sb, uT, u_tok, ones_bf, ncols, sq_bufs, rstd):
    nc = kb.nc
    groups = []
    c0 = 0
    while c0 < ncols:
        w = min(512, ncols - c0)
        groups.append((c0, w))
        c0 += w
    ps_list = [pp.next() for _ in groups]
    for c in range(NCH):
        sq, sq_tok = sq_bufs[c % len(sq_bufs)]
        kb.op("act", lambda: nc.scalar.activation(out=sq[:, :ncols], in_=hT[:, c, :ncols], func=AF.Square),
              reads=[h_tok], writes=[sq_tok])
        for gi, (c0, w) in enumerate(groups):
            ps, ps_tok = ps_list[gi]
            last = (c == NCH - 1) and (gi == len(groups) - 1)
            kb.op("pe", lambda: nc.tensor.matmul(ps[:, :w], lhsT=ones_bf[:, :], rhs=sq[:, c0:c0 + w],
                                                 start=(c == 0), stop=(c == NCH - 1)),
                  reads=[sq_tok, "ones"], writes=[ps_tok], inc=(gi == len(groups) - 1))
    rstd_t, rstd_tok = rstd
    for gi, (c0, w) in enumerate(groups):
        ps, ps_tok = ps_list[gi]
        kb.op("dve", lambda: nc.vector.tensor_scalar(out=rstd_t[:, c0:c0 + w], in0=ps[:, :w], scalar1=1.0 / D_MODEL,
                                                     scalar2=EPS, op0=ALU.mult, op1=ALU.add),
              reads=[ps_tok], writes=[rstd_tok])
    emit_rsqrt_inplace(kb, rstd_t[:, :ncols], rstd_tok)
    for c in range(NCH):
        kb.op("dve", lambda: nc.vector.scalar_tensor_tensor(out=uT[:, c, :ncols], in0=hT[:, c, :ncols],
                                                            scalar=g_sb[:, c:c + 1], in1=rstd_t[:, :ncols],
                                                            op0=ALU.mult, op1=ALU.mult),
              reads=[h_tok, rstd_tok, "g"], writes=[u_tok])


def build_ffn(Tn, final_norm=False, NT=512):
    assert Tn % NT == 0
    nt = Tn // NT
    W = NT + 2
    nc = bass.Bass("TRN2", target_bir_lowering=False)
    hin = nc.dram_tensor("hin", [D_MODEL, Tn + 2], F32, kind="ExternalInput").ap()
    wup = nc.dram_tensor("wup", [FFN_BLK, 128, NCH, 256], F32, kind="ExternalInput").ap()
    wdn = nc.dram_tensor("wdn", [NCH, 128, FFN_BLK, 128], F32, kind="ExternalInput").ap()
    gnorm = nc.dram_tensor("gnorm", [128, NCH], F32, kind="ExternalInput").ap()
    cw = nc.dram_tensor("cw", [128, FFN_BLK, 4], F32, kind="ExternalInput").ap()
    if final_norm:
        gfin = nc.dram_tensor("gfin", [128, NCH], F32, kind="ExternalInput").ap()
    hout = nc.dram_tensor("hout", [D_MODEL, Tn], F32, kind="ExternalOutput").ap()
    hin_v = hin.rearrange("(c p) t -> p c t", p=128)
    hout_v = hout.rearrange("(c p) t -> p c t", p=128)

    with contextlib.ExitStack() as stack:
        kb = KB(nc, stack)
        pp = PsumPool(kb)
        hT = sb(kb, "hT", [128, NCH, W], F32)
        uT = sb(kb, "uT", [128, NCH, W], BF16)
        actT = sb(kb, "actT", [128, FFN_BLK, NT], BF16)
        ones_bf = sb(kb, "ones", [128, 128], BF16)
        g_sb = sb(kb, "g_sb", [128, NCH], F32)
        cw_sb = sb(kb, "cw_sb", [128, FFN_BLK, 4], F32)
        rstd_t = sb(kb, "rstd", [128, W], F32)
        sq_bufs = [(sb(kb, f"sq{i}", [128, W], BF16), ("sq", i)) for i in range(3)]
        wup_sb = [sb(kb, f"wup{i}", [128, NCH, 256], BF16) for i in range(2)]
        wdn_sb = [sb(kb, f"wdn{i}", [128, FFN_BLK, 128], BF16) for i in range(2)]
        gate_sb = [sb(kb, f"gate{i}", [128, W], F32) for i in range(2)]
        acc_sb = [sb(kb, f"acc{i}", [128, NT], F32) for i in range(2)]
        sil_sb = [sb(kb, f"sil{i}", [128, NT], F32) for i in range(2)]
        ho_sb = [sb(kb, f"ho{i}", [128, NT], F32) for i in range(3)]
        if final_norm:
            gf_sb = sb(kb, "gf_sb", [128, NCH], F32)
            hn = sb(kb, "hn", [128, NCH, NT], F32)

        kb.op("pool", lambda: nc.gpsimd.memset(ones_bf[:, :], 1.0), writes=["ones"])
        kb.dma("sp", g_sb[:, :], gnorm[:, :], writes=["g"])
        kb.dma("sp", cw_sb[:, :, :], cw[:, :, :], writes=["cw"])
        if final_norm:
            kb.dma("sp", gf_sb[:, :], gfin[:, :], writes=["gf"])

        out_toks = []
        wi = 0
        di = 0
        for it in range(nt):
            t0 = it * NT
            for c in range(NCH):
                kb.dma("sp", hT[:, c, :], hin_v[:, c, t0:t0 + W], writes=["hT"])
            emit_rmsnorm_fm(kb, pp, hT, "hT", g_sb, uT, "uT", ones_bf, W, sq_bufs, (rstd_t, "rstd"))
            for j in range(FFN_BLK):
                wb = wup_sb[wi % 2]
                wtok = ("wup", wi % 2)
                wi += 1
                kb.dma("pool", wb[:, :, :], wup[j], writes=[wtok])
                ps_g, tg = pp.next()
                ps_x, tx = pp.next()
                ps_v, tv = pp.next()
                for kc in range(NCH):
                    kb.op("pe", lambda: nc.tensor.matmul(ps_g[:, :NT], lhsT=wb[:, kc, 0:128], rhs=uT[:, kc, 0:NT],
                                                         start=(kc == 0), stop=(kc == NCH - 1)),
                          reads=[wtok, "uT", "ones", "g"], writes=[tg], inc=(kc == NCH - 1))
                for kc in range(NCH):
                    kb.op("pe", lambda: nc.tensor.matmul(ps_x[:, :2], lhsT=wb[:, kc, 0:128], rhs=uT[:, kc, NT:NT + 2],
                                                         start=(kc == 0), stop=(kc == NCH - 1)),
                          reads=[wtok, "uT"], writes=[tx], inc=(kc == NCH - 1))
                for kc in range(NCH):
                    kb.op("pe", lambda: nc.tensor.matmul(ps_v[:, :NT], lhsT=wb[:, kc, 128:256], rhs=uT[:, kc, 1:NT + 1],
                                                         start=(kc == 0), stop=(kc == NCH - 1)),
                          reads=[wtok, "uT"], writes=[tv], inc=(kc == NCH - 1))
                gs = gate_sb[j % 2]
                gtok = ("gate", j % 2)
                kb.op("act", lambda: nc.scalar.copy(out=gs[:, 0:NT], in_=ps_g[:, :NT]), reads=[tg], writes=[gtok])
                kb.op("act", lambda: nc.scalar.copy(out=gs[:, NT:NT + 2], in_=ps_x[:, :2]), reads=[tx], writes=[gtok])
                ac = acc_sb[j % 2]
                atok = ("acc", j % 2)
                kb.op("dve", lambda: nc.vector.tensor_scalar(out=ac[:, :], in0=gs[:, 0:NT], scalar1=cw_sb[:, j, 0:1],
                                                             scalar2=cw_sb[:, j, 3:4], op0=ALU.mult, op1=ALU.add),
                      reads=[gtok, "cw"], writes=[atok])
                kb.op("dve", lambda: nc.vector.scalar_tensor_tensor(out=ac[:, :], in0=gs[:, 1:NT + 1],
                                                                    scalar=cw_sb[:, j, 1:2], in1=ac[:, :],
                                                                    op0=ALU.mult, op1=ALU.add),
                      reads=[gtok, atok], writes=[atok])
                kb.op("dve", lambda: nc.vector.scalar_tensor_tensor(out=ac[:, :], in0=gs[:, 2:NT + 2],
                                                                    scalar=cw_sb[:, j, 2:3], in1=ac[:, :],
                                                                    op0=ALU.mult, op1=ALU.add),
                      reads=[gtok, atok], writes=[atok])
                sl = sil_sb[j % 2]
                stok = ("sil", j % 2)
                kb.op("act", lambda: nc.scalar.activation(out=sl[:, :], in_=ac[:, :], func=AF.Silu),
                      reads=[atok], writes=[stok])
                kb.op("dve", lambda: nc.vector.tensor_tensor(out=actT[:, j, :], in0=sl[:, :], in1=ps_v[:, :NT], op=ALU.mult),
                      reads=[stok, tv], writes=[("actT", j)])
            for mb in range(NCH):
                wd = wdn_sb[di % 2]
                dtok = ("wdn", di % 2)
                di += 1
                kb.dma("pool", wd[:, :, :], wdn[mb], writes=[dtok])
                ps_o, to = pp.next()
                for kbk in range(FFN_BLK):
                    kb.op("pe", lambda: nc.tensor.matmul(ps_o[:, :NT], lhsT=wd[:, kbk, :], rhs=actT[:, kbk, :],
                                                         start=(kbk == 0), stop=(kbk == FFN_BLK - 1)),
                          reads=[dtok, ("actT", kbk)], writes=[to], inc=(kbk == FFN_BLK - 1))
                if not final_norm:
                    ho = ho_sb[mb % 3]
                    htok = ("ho", mb % 3)
                    kb.op("dve", lambda: nc.vector.tensor_tensor(out=ho[:, :], in0=ps_o[:, :NT], in1=hT[:, mb, 1:NT + 1], op=ALU.add),
                          reads=[to, "hT"], writes=[htok])
                    otok = ("hout", it, mb)
                    kb.dma("sp", hout_v[:, mb, t0:t0 + NT], ho[:, :], reads=[htok], writes=[otok])
                    out_toks.append(otok)
                else:
                    kb.op("dve", lambda: nc.vector.tensor_tensor(out=hn[:, mb, :], in0=ps_o[:, :NT], in1=hT[:, mb, 1:NT + 1], op=ALU.add),
                          reads=[to, "hT"], writes=["hn"])
            if final_norm:
                groups = [(0, NT)]
                ps, ps_tok = pp.next()
                for c in range(NCH):
                    sq, sq_tok = sq_bufs[c % len(sq_bufs)]
                    kb.op("act", lambda: nc.scalar.activation(out=sq[:, :NT], in_=hn[:, c, :], func=AF.Square),
                          reads=["hn"], writes=[sq_tok])
                    kb.op("pe", lambda: nc.tensor.matmul(ps[:, :NT], lhsT=ones_bf[:, :], rhs=sq[:, :NT],
                                                         start=(c == 0), stop=(c == NCH - 1)),
                          reads=[sq_tok, "ones"], writes=[ps_tok], inc=True)
                kb.op("dve", lambda: nc.vector.tensor_scalar(out=rstd_t[:, :NT], in0=ps[:, :NT], scalar1=1.0 / D_MODEL,
                                                             scalar2=EPS, op0=ALU.mult, op1=ALU.add),
                      reads=[ps_tok], writes=["rstd"])
                emit_rsqrt_inplace(kb, rstd_t[:, :NT], "rstd")
                for c in range(NCH):
                    ho = ho_sb[c % 3]
                    htok = ("ho", c % 3)
                    kb.op("dve", lambda: nc.vector.scalar_tensor_tensor(out=ho[:, :], in0=hn[:, c, :],
                                                                        scalar=gf_sb[:, c:c + 1], in1=rstd_t[:, :NT],
                                                                        op0=ALU.mult, op1=ALU.mult),
                          reads=["hn", "rstd", "gf"], writes=[htok])
                    otok = ("hout", it, c)
                    kb.dma("sp", hout_v[:, c, t0:t0 + NT], ho[:, :], reads=[htok], writes=[otok])
                    out_toks.append(otok)
        kb.finish(out_toks)
    return nc


def prep_ffn_weights(w_up, conv_w, conv_b, w_down, g):
    wg = w_up[:, :FFN_DIM].reshape(NCH, 128, FFN_BLK, 128)
    wv = w_up[:, FFN_DIM:].reshape(NCH, 128, FFN_BLK, 128)
    wup = np.concatenate([wg, wv], axis=-1).transpose(2, 1, 0, 3)
    wdn = w_down.reshape(FFN_BLK, 128, NCH, 128).transpose(2, 1, 0, 3)
    cw = np.concatenate([conv_w, conv_b[None, :]], axis=0)
    cw = cw.reshape(4, FFN_BLK, 128).transpose(2, 1, 0)
    return dict(wup=np.ascontiguousarray(wup, dtype=np.float32), wdn=np.ascontiguousarray(wdn, dtype=np.float32),
                cw=np.ascontiguousarray(cw, dtype=np.float32), gnorm=np.ascontiguousarray(g.reshape(NCH, 128).T, dtype=np.float32))


def emit_outproj_residual(kb, pp, actT, act_tok_fn, KBK, w_dram, wbufs, wstate, hT, h_tok, h_col0, NT,
                          dst_v, dst_col0, ho_bufs, out_toks, tag):
    nc = kb.nc
    for mb in range(NCH):
        wd = wbufs[wstate[0] % len(wbufs)]
        dtok = (tag + "w", wstate[0] % len(wbufs))
        wstate[0] += 1
        kb.dma("pool", wd[:, :KBK, :], w_dram[mb], writes=[dtok])
        ps_o, to = pp.next()
        for k in range(KBK):
            kb.op("pe", lambda: nc.tensor.matmul(ps_o[:, :NT], lhsT=wd[:, k, :], rhs=actT[:, k, :NT],
                                                 start=(k == 0), stop=(k == KBK - 1)),
                  reads=[dtok, act_tok_fn(k)], writes=[to], inc=(k == KBK - 1))
        ho = ho_bufs[mb % len(ho_bufs)]
        htok = (tag + "ho", mb % len(ho_bufs))
        kb.op("dve", lambda: nc.vector.tensor_tensor(out=ho[:, :NT], in0=ps_o[:, :NT], in1=hT[:, mb, h_col0:h_col0 + NT], op=ALU.add),
              reads=[to, h_tok], writes=[htok])
        otok = (tag + "out", dst_col0, mb)
        kb.dma("sp", dst_v[:, mb, dst_col0:dst_col0 + NT], ho[:, :NT], reads=[htok], writes=[otok])
        out_toks.append(otok)


XA_H = 4
XA_HD = 512
N_MEM = 256


class Ctx:
    pass


def emit_mem_norm(kb, pp, cx, memT_dram, gmem_dram):
    cx.memn = sb(kb, "memn", [128, NCH, N_MEM], BF16)
    with kb.phase():
        tile_ctx(kb, cx)
        memf = cx.hT
        gm = cx.g_sb
        kb.dma("sp", gm[:, :], gmem_dram[:, :], writes=["g"])
        kb.dma("sp", memf[:, :, :N_MEM], memT_dram.rearrange("(c p) m -> p c m", p=128), writes=["hT"])
        emit_rmsnorm_fm(kb, pp, memf, "hT", gm, cx.memn, "memn", cx.ones_bf, N_MEM, cx.sq_bufs, (cx.rstd_t, "rstd"))


def emit_xattn(kb, pp, cx, T, src_v, src_pad, dst_v, dst_pad, w, out_toks, NT=512):
    with kb.phase():
        tile_ctx(kb, cx)
        ctx_add_xattn(kb, cx)
        _emit_xattn_body(kb, pp, cx, T, src_v, src_pad, dst_v, dst_pad, w, out_toks, NT)


def _emit_xattn_body(kb, pp, cx, T, src_v, src_pad, dst_v, dst_pad, w, out_toks, NT):
    nc = kb.nc
    nt = T // NT
    KT = cx.xa_KT
    V = cx.xa_V
    g_sb = cx.g_sb
    kb.dma("sp", g_sb[:, :], w["g"][:, :], writes=["g"])
    for blk in range(NCH):
        wb = cx.wA[cx.wAi[0] % 2]; wtok = ("wA", cx.wAi[0] % 2); cx.wAi[0] += 1
        kb.dma("pool", wb[:, :, 0:128], w["wk"][blk], writes=[wtok])
        ps, pt = pp.next()
        for kc in range(NCH):
            kb.op("pe", lambda: nc.tensor.matmul(ps[:, :N_MEM], lhsT=wb[:, kc, 0:128], rhs=cx.memn[:, kc, :],
                                                 start=(kc == 0), stop=(kc == NCH - 1)),
                  reads=[wtok, "memn"], writes=[pt], inc=(kc == NCH - 1))
        kb.op("act", lambda: nc.scalar.copy(out=KT[:, blk, :], in_=ps[:, :N_MEM]), reads=[pt], writes=["KT"])
    for cg in range(4):
        wb = cx.wB[cx.wBi[0] % 2]; wtok = ("wB", cx.wBi[0] % 2); cx.wBi[0] += 1
        kb.dma("pool", wb[:, :, :], w["wv"][cg], writes=[wtok])
        for mblk in range(2):
            ps, pt = pp.next()
            for kc in range(NCH):
                kb.op("pe", lambda: nc.tensor.matmul(ps[:, :512], lhsT=cx.memn[:, kc, mblk * 128:(mblk + 1) * 128], rhs=wb[:, kc, :],
                                                     start=(kc == 0), stop=(kc == NCH - 1)),
                      reads=[wtok, "memn"], writes=[pt], inc=(kc == NCH - 1))
            kb.op("act", lambda: nc.scalar.copy(out=V[:, mblk, cg * 512:(cg + 1) * 512], in_=ps[:, :512]), reads=[pt], writes=["V"])
    hT, uT, qT, oT, PT = cx.hT, cx.uT, cx.xa_qT, cx.xa_oT, cx.xa_PT
    scale = float(XA_HD) ** -0.5
    for it in range(nt):
        t0 = it * NT
        for c in range(NCH):
            kb.dma("sp", hT[:, c, :NT], src_v[:, c, src_pad + t0:src_pad + t0 + NT], writes=["hT"])
        emit_rmsnorm_fm(kb, pp, hT, "hT", g_sb, uT, "uT", cx.ones_bf, NT, cx.sq_bufs, (cx.rstd_t, "rstd"))
        for blk in range(NCH):
            wb = cx.wA[cx.wAi[0] % 2]; wtok = ("wA", cx.wAi[0] % 2); cx.wAi[0] += 1
            kb.dma("pool", wb[:, :, 0:128], w["wq"][blk], writes=[wtok])
            ps, pt = pp.next()
            for kc in range(NCH):
                kb.op("pe", lambda: nc.tensor.matmul(ps[:, :NT], lhsT=wb[:, kc, 0:128], rhs=uT[:, kc, :NT],
                                                     start=(kc == 0), stop=(kc == NCH - 1)),
                      reads=[wtok, "uT"], writes=[pt], inc=(kc == NCH - 1))
            kb.op("act", lambda: nc.scalar.activation(out=qT[:, blk, :NT], in_=ps[:, :NT], func=AF.Copy, scale=scale),
                  reads=[pt], writes=[("qT", blk)])
        for hd in range(XA_H):
            for sub in range(NT // 128):
                ps, pt = pp.next()
                for j in range(4):
                    kb.op("pe", lambda: nc.tensor.matmul(ps[:, :N_MEM], lhsT=qT[:, hd * 4 + j, sub * 128:(sub + 1) * 128],
                                                         rhs=KT[:, hd * 4 + j, :], start=(j == 0), stop=(j == 3)),
                          reads=[("qT", hd * 4 + j), "KT"], writes=[pt], inc=(j == 3))
                i2 = cx.xai[0] % 2; cx.xai[0] += 1
                mx = cx.xa_mx[i2]; mtok = ("xamx", i2)
                kb.op("dve", lambda: nc.vector.reduce_max(out=mx[:, 0:1], in_=ps[:, :N_MEM], axis=AX.X), reads=[pt], writes=[mtok])
                kb.op("dve", lambda: nc.vector.tensor_scalar(out=mx[:, 1:2], in0=mx[:, 0:1], scalar1=-1.0, scalar2=None, op0=ALU.mult),
                      reads=[mtok], writes=[mtok])
                P = cx.xa_P[i2]; ptok = ("xaP", i2)
                kb.op("act", lambda: nc.scalar.activation(out=P[:, :], in_=ps[:, :N_MEM], func=AF.Exp, bias=mx[:, 1:2], scale=1.0,
                                                          accum_out=mx[:, 2:3]),
                      reads=[pt, mtok], writes=[ptok, mtok])
                kb.op("dve", lambda: nc.vector.reciprocal(out=mx[:, 3:4], in_=mx[:, 2:3]), reads=[mtok], writes=[mtok])
                Pn = cx.xa_Pn[i2]; pntok = ("xaPn", i2)
                kb.op("dve", lambda: nc.vector.tensor_scalar(out=Pn[:, :], in0=P[:, :], scalar1=mx[:, 3:4], scalar2=None, op0=ALU.mult),
                      reads=[ptok, mtok], writes=[pntok])
                pb = cx.psb[cx.psbi[0] % 2]; pbtok = ("psb", cx.psbi[0] % 2); cx.psbi[0] += 1
                for mblk in range(2):
                    kb.op("pe", lambda: nc.tensor.transpose(out=pb[:, mblk * 128:(mblk + 1) * 128], in_=Pn[:, mblk * 128:(mblk + 1) * 128],
                                                            identity=cx.ident[:, :]),
                          reads=[pntok, "ident"], writes=[pbtok], inc=(mblk == 1))
                kb.op("act", lambda: nc.scalar.copy(out=PT[:, :, hd, sub * 128:(sub + 1) * 128],
                                                    in_=pb[:, 0:256].rearrange("p (b t) -> p b t", b=2)),
                      reads=[pbtok], writes=[("PT", hd)])
            for dblk in range(4):
                ps, pt = pp.next()
                for mblk in range(2):
                    kb.op("pe", lambda: nc.tensor.matmul(ps[:, :NT], lhsT=V[:, mblk, hd * 512 + dblk * 128:hd * 512 + (dblk + 1) * 128],
                                                         rhs=PT[:, mblk, hd, :NT], start=(mblk == 0), stop=(mblk == 1)),
                          reads=["V", ("PT", hd)], writes=[pt], inc=(mblk == 1))
                kb.op("act", lambda: nc.scalar.copy(out=oT[:, hd * 4 + dblk, :NT], in_=ps[:, :NT]), reads=[pt], writes=[("oT", hd * 4 + dblk)])
        emit_outproj_residual(kb, pp, oT, lambda k: ("oT", k), NCH, w["wo"], cx.wO, cx.wOi, hT, "hT", 0, NT,
                              dst_v, dst_pad + t0, cx.ho_bufs, out_toks, "xa")


def make_ctx(kb, NT=512, halo=3):
    nc = kb.nc
    cx = Ctx()
    cx.NT, cx.W = NT, NT + halo
    cx.ones_bf = sb(kb, "ones", [128, 128], BF16)
    cx.onesf = sb(kb, "onesf", [128, 128], F32)
    cx.identf, cx.ident = make_ident(kb)
    cx.psb = [psum_bf16(kb, f"psb{i}") for i in range(2)]; cx.psbi = [0]
    cx.wAi = [0]; cx.wOi = [0]
    kb.op("pool", lambda: nc.gpsimd.memset(cx.ones_bf[:, :], 1.0), writes=["ones"])
    kb.op("pool", lambda: nc.gpsimd.memset(cx.onesf[:, :], 1.0), writes=["onesf"])
    return cx


def tile_ctx(kb, cx, need_wO=True):
    NT, W = cx.NT, cx.W
    cx.hT = sb(kb, "hT", [128, NCH, W], F32)
    cx.uT = sb(kb, "uT", [128, NCH, W], BF16)
    cx.g_sb = sb(kb, "g_sb", [128, NCH], F32)
    cx.rstd_t = sb(kb, "rstd", [128, W], F32)
    cx.sq_bufs = [(sb(kb, f"sq{i}", [128, W], BF16), ("sq", i)) for i in range(3)]
    cx.ho_bufs = [sb(kb, f"ho{i}", [128, NT], F32) for i in range(3)]
    cx.wA = [sb(kb, f"wA{i}", [128, NCH, 256], BF16) for i in range(2)]
    if need_wO:
        cx.wO = [sb(kb, f"wO{i}", [128, FFN_BLK, 128], BF16) for i in range(2)]


def ctx_add_xattn(kb, cx):
    NT = cx.NT
    cx.wB = [sb(kb, f"wB{i}", [128, NCH, 512], BF16) for i in range(2)]; cx.wBi = [0]
    cx.xa_KT = sb(kb, "xaKT", [128, NCH, N_MEM], BF16)
    cx.xa_V = sb(kb, "xaV", [128, 2, D_MODEL], BF16)
    cx.xa_qT = sb(kb, "xaqT", [128, NCH, NT], BF16)
    cx.xa_oT = sb(kb, "xaoT", [128, NCH, NT], BF16)
    cx.xa_PT = sb(kb, "xaPT", [128, 2, XA_H, NT], BF16)
    cx.xa_mx = [sb(kb, f"xamx{i}", [128, 4], F32) for i in range(2)]
    cx.xa_P = [sb(kb, f"xaP{i}", [128, N_MEM], F32) for i in range(2)]
    cx.xa_Pn = [sb(kb, f"xaPn{i}", [128, N_MEM], BF16) for i in range(2)]
    cx.xai = [0]


def prep_xattn_weights(wq, wkv, wo, g):
    def fm_blocks(wm):
        n = wm.shape[1] // 128
        return np.ascontiguousarray(wm.reshape(NCH, 128, n, 128).transpose(2, 1, 0, 3), dtype=np.float32)
    wk = wkv[:, :D_MODEL]
    wv = wkv[:, D_MODEL:]
    wvr = np.ascontiguousarray(wv.reshape(NCH, 128, 4, 512).transpose(2, 1, 0, 3), dtype=np.float32)
    return dict(wq=fm_blocks(wq), wk=fm_blocks(wk), wv=wvr, wo=fm_blocks(wo),
                g=np.ascontiguousarray(g.reshape(NCH, 128).T, dtype=np.float32))


def build_xattn_test(T, NT=512):
    nc = bass.Bass("TRN2", target_bir_lowering=False)
    hin = nc.dram_tensor("hin", [D_MODEL, T], F32, kind="ExternalInput").ap()
    memT = nc.dram_tensor("memT", [D_MODEL, N_MEM], F32, kind="ExternalInput").ap()
    gmem = nc.dram_tensor("gmem", [128, NCH], F32, kind="ExternalInput").ap()
    w = dict(wq=nc.dram_tensor("wq", [NCH, 128, NCH, 128], F32, kind="ExternalInput").ap(),
             wk=nc.dram_tensor("wk", [NCH, 128, NCH, 128], F32, kind="ExternalInput").ap(),
             wv=nc.dram_tensor("wv", [4, 128, NCH, 512], F32, kind="ExternalInput").ap(),
             wo=nc.dram_tensor("wo", [NCH, 128, NCH, 128], F32, kind="ExternalInput").ap(),
             g=nc.dram_tensor("g", [128, NCH], F32, kind="ExternalInput").ap())
    hout = nc.dram_tensor("hout", [D_MODEL, T], F32, kind="ExternalOutput").ap()
    with contextlib.ExitStack() as stack:
        kb = KB(nc, stack)
        pp = PsumPool(kb, nbanks=6)
        cx = make_ctx(kb, NT)
        emit_mem_norm(kb, pp, cx, memT, gmem)
        outs = []
        emit_xattn(kb, pp, cx, T, hin.rearrange("(c p) t -> p c t", p=128), 0, hout.rearrange("(c p) t -> p c t", p=128), 0, w, outs, NT)
        kb.finish(outs)
    return nc


def emit_proj_block(kb, pp, wb, wtok, wcol0, uT, u_tok, col_ranges):
    nc = kb.nc
    outs = []
    for (c0, w) in col_ranges:
        ps, pt = pp.next()
        for kc in range(NCH):
            kb.op("pe", lambda: nc.tensor.matmul(ps[:, :w], lhsT=wb[:, kc, wcol0:wcol0 + 128], rhs=uT[:, kc, c0:c0 + w],
                                                 start=(kc == 0), stop=(kc == NCH - 1)),
                  reads=[wtok, u_tok], writes=[pt], inc=(kc == NCH - 1))
        outs.append((ps, pt))
    return outs


def emit_softplus_small(kb, out, x, tmp, tok, neg_in=False):
    nc = kb.nc
    sgn = -1.0 if neg_in else 1.0
    kb.op("act", lambda: nc.scalar.activation(out=tmp, in_=x, func=AF.Abs), reads=[tok], writes=[tok])
    kb.op("act", lambda: nc.scalar.activation(out=tmp, in_=tmp, func=AF.Exp, scale=-1.0), reads=[tok], writes=[tok])
    kb.op("act", lambda: nc.scalar.activation(out=tmp, in_=tmp, func=AF.Ln, bias=1.0, scale=1.0), reads=[tok], writes=[tok])
    kb.op("dve", lambda: nc.vector.tensor_scalar(out=out, in0=x, scalar1=sgn, scalar2=0.0, op0=ALU.mult, op1=ALU.max), reads=[tok], writes=[tok])
    kb.op("dve", lambda: nc.vector.tensor_tensor(out=out, in0=out, in1=tmp, op=ALU.add), reads=[tok], writes=[tok])


RG_BLOCKS = 8


def emit_rglru(kb, pp, cx, T, src_v, src_pad, dst_v, dst_pad, w, scr, out_toks, NT=512):
    nc = kb.nc
    nt = T // NT
    W = NT + 3
    ggv = scr["ggT"].rearrange("(c p) t -> p c t", p=128)
    hfv = scr["hfT"].rearrange("(c p) t -> p c t", p=128)
    abv = scr["abT"].rearrange("(c p) t -> p c t", p=128)
    bxv = scr["bxbT"].rearrange("(c p) t -> p c t", p=128)
    with kb.phase():
        tile_ctx(kb, cx, need_wO=False)
        hT, uT, g_sb = cx.hT, cx.uT, cx.g_sb
        gw = sb(kb, "rg_gw", [128, 2, RG_BLOCKS, 2, 512], BF16)
        cw = sb(kb, "rg_cw", [128, NCH, 5], F32)
        gb = sb(kb, "rg_gb", [128, 2, NCH, 2], F32)
        cd = sb(kb, "rg_cd", [128, 2, NCH], F32)
        cdt = sb(kb, "rg_cdt", [128, 2, NCH], F32)
        xcf = sb(kb, "rg_xcf", [128, NCH, NT], F32)
        xcb = sb(kb, "rg_xcb", [128, NCH, NT], BF16)
        G = [sb(kb, f"rg_G{i}", [128, W], F32) for i in range(2)]
        gg = [sb(kb, f"rg_gg{i}", [128, NT], BF16) for i in range(2)]
        t1 = [sb(kb, f"rg_t1{i}", [128, NT], F32) for i in range(2)]
        t2 = [sb(kb, f"rg_t2{i}", [128, NT], F32) for i in range(2)]
        ta = [sb(kb, f"rg_ta{i}", [128, NT], F32) for i in range(2)]
        tb = [sb(kb, f"rg_tb{i}", [128, NT], F32) for i in range(2)]
        th = [sb(kb, f"rg_th{i}", [128, NT], F32) for i in range(2)]
        carry = sb(kb, "rg_carry", [128, NCH], F32)
        kb.dma("sp", g_sb[:, :], w["g"][:, :], writes=["g"])
        kb.dma("pool", gw[:, :, :, :, :], w["gw"], writes=["gw"])
        kb.dma("sp", cw[:, :, :], w["cw"], writes=["cw"])
        kb.dma("sp", gb[:, :, :, :], w["gb"], writes=["gb"])
        kb.dma("sp", cd[:, :, :], w["lam"], writes=["cd"])
        cdf = cd[:, :, :].rearrange("p a b -> p (a b)")
        cdtf = cdt[:, :, :].rearrange("p a b -> p (a b)")
        emit_softplus_small(kb, cdf, cdf, cdtf, "cd", neg_in=True)
        kb.op("dve", lambda: nc.vector.tensor_scalar(out=cdf, in0=cdf, scalar1=-8.0, scalar2=None, op0=ALU.mult), reads=["cd"], writes=["cd"])
        kb.op("dve", lambda: nc.vector.memset(carry[:, :], 0.0), writes=["carry"])
        k = 0
        for it in range(nt):
            t0 = it * NT
            for c in range(NCH):
                kb.dma("sp", hT[:, c, :W], src_v[:, c, src_pad + t0 - 2:src_pad + t0 - 2 + W], writes=["hT"])
            emit_rmsnorm_fm(kb, pp, hT, "hT", g_sb, uT, "uT", cx.ones_bf, W, cx.sq_bufs, (cx.rstd_t, "rstd"))
            for blk in range(32):
                wb = cx.wA[cx.wAi[0] % 2]; wtok = ("wA", cx.wAi[0] % 2); cx.wAi[0] += 1
                kb.dma("pool", wb[:, :, 0:128], w["win"][blk], writes=[wtok])
                i2 = blk % 2
                if blk < 16:
                    (ps, pt), = emit_proj_block(kb, pp, wb, wtok, 0, uT, "uT", [(2, NT)])
                    a1, a2 = t1[i2], t2[i2]
                    kb.op("act", lambda: nc.scalar.copy(out=a1[:, :], in_=ps[:, :NT]), reads=[pt], writes=[("t1", i2)])
                    kb.op("dve", lambda: nc.vector.tensor_tensor(out=a2[:, :], in0=a1[:, :], in1=a1[:, :], op=ALU.mult), reads=[("t1", i2)], writes=[("t2", i2)])
                    kb.op("dve", lambda: nc.vector.tensor_scalar(out=a2[:, :], in0=a2[:, :], scalar1=0.044715, scalar2=1.0, op0=ALU.mult, op1=ALU.add),
                          reads=[("t2", i2)], writes=[("t2", i2)])
                    kb.op("dve", lambda: nc.vector.tensor_tensor(out=a2[:, :], in0=a2[:, :], in1=a1[:, :], op=ALU.mult), reads=[("t2", i2), ("t1", i2)], writes=[("t2", i2)])
                    kb.op("act", lambda: nc.scalar.activation(out=a2[:, :], in_=a2[:, :], func=AF.Sigmoid, scale=1.5957691216), reads=[("t2", i2)], writes=[("t2", i2)])
                    kb.op("dve", lambda: nc.vector.tensor_tensor(out=gg[i2][:, :], in0=a2[:, :], in1=a1[:, :], op=ALU.mult), reads=[("t2", i2), ("t1", i2)], writes=[("gg", i2)])
                    kb.dma("sp", ggv[:, blk, t0:t0 + NT], gg[i2][:, :], reads=[("gg", i2)], writes=[("ggd", it, blk)])
                else:
                    c = blk - 16
                    (psm, ptm), (psx, ptx) = emit_proj_block(kb, pp, wb, wtok, 0, uT, "uT", [(0, NT), (NT, 3)])
                    Gt = G[i2]; gtok = ("G", i2)
                    kb.op("act", lambda: nc.scalar.copy(out=Gt[:, 0:NT], in_=psm[:, :NT]), reads=[ptm], writes=[gtok])
                    kb.op("act", lambda: nc.scalar.copy(out=Gt[:, NT:NT + 3], in_=psx[:, :3]), reads=[ptx], writes=[gtok])
                    kb.op("dve", lambda: nc.vector.tensor_scalar(out=xcf[:, c, :], in0=Gt[:, 0:NT], scalar1=cw[:, c, 0:1], scalar2=cw[:, c, 4:5],
                                                                 op0=ALU.mult, op1=ALU.add), reads=[gtok, "cw"], writes=[("xcf", c)])
                    for tap in range(1, 4):
                        kb.op("dve", lambda: nc.vector.scalar_tensor_tensor(out=xcf[:, c, :], in0=Gt[:, tap:tap + NT], scalar=cw[:, c, tap:tap + 1],
                                                                            in1=xcf[:, c, :], op0=ALU.mult, op1=ALU.add),
                              reads=[gtok, ("xcf", c)], writes=[("xcf", c)])
                    kb.op("act", lambda: nc.scalar.copy(out=xcb[:, c, :], in_=xcf[:, c, :]), reads=[("xcf", c)], writes=[("xcb", c)])
            for d in range(2):
                for n in range(RG_BLOCKS):
                    for jb in range(2):
                        c = 2 * n + jb
                        i2 = k % 2; k += 1
                        psr, ptr = pp.next()
                        psi, pti = pp.next()
                        for (ps_, pt_, col) in ((psr, ptr, jb * 128), (psi, pti, 256 + jb * 128)):
                            for kc in range(2):
                                kb.op("pe", lambda: nc.tensor.matmul(ps_[:, :NT], lhsT=gw[:, d, n, kc, col:col + 128], rhs=xcb[:, 2 * n + kc, :],
                                                                     start=(kc == 0), stop=(kc == 1)),
                                      reads=["gw", ("xcb", 2 * n + kc)], writes=[pt_], inc=(kc == 1))
                        A, B_, R1, R2 = ta[i2], tb[i2], t1[i2], t2[i2]
                        kb.op("act", lambda: nc.scalar.activation(out=R1[:, :], in_=psr[:, :NT], func=AF.Sigmoid, bias=gb[:, d, c, 0:1], scale=1.0),
                              reads=[ptr, "gb"], writes=[("t1", i2)])
                        kb.op("act", lambda: nc.scalar.activation(out=A[:, :], in_=R1[:, :], func=AF.Exp, scale=cd[:, d, c:c + 1]),
                              reads=[("t1", i2), "cd"], writes=[("ta", i2)])
                        kb.op("act", lambda: nc.scalar.activation(out=R2[:, :], in_=psi[:, :NT], func=AF.Sigmoid, bias=gb[:, d, c, 1:2], scale=1.0),
                              reads=[pti, "gb"], writes=[("t2", i2)])
                        kb.op("dve", lambda: nc.vector.tensor_tensor(out=R1[:, :], in0=A[:, :], in1=A[:, :], op=ALU.mult), reads=[("ta", i2), ("t1", i2)], writes=[("t1", i2)])
                        kb.op("dve", lambda: nc.vector.tensor_scalar(out=R1[:, :], in0=R1[:, :], scalar1=-1.0, scalar2=1.0, op0=ALU.mult, op1=ALU.add),
                              reads=[("t1", i2)], writes=[("t1", i2)])
                        kb.op("act", lambda: nc.scalar.activation(out=R1[:, :], in_=R1[:, :], func=AF.Sqrt), reads=[("t1", i2)], writes=[("t1", i2)])
                        kb.op("dve", lambda: nc.vector.tensor_tensor(out=R2[:, :], in0=R2[:, :], in1=xcf[:, c, :], op=ALU.mult), reads=[("t2", i2), ("xcf", c)], writes=[("t2", i2)])
                        kb.op("dve", lambda: nc.vector.tensor_tensor(out=B_[:, :], in0=R2[:, :], in1=R1[:, :], op=ALU.mult), reads=[("t2", i2), ("t1", i2)], writes=[("tb", i2)])
                        if d == 0:
                            H = th[i2]
                            kb.op("dve", lambda: nc.vector.tensor_tensor_scan(out=H[:, :], data0=A[:, :], data1=B_[:, :], initial=carry[:, c:c + 1],
                                                                              op0=ALU.mult, op1=ALU.add),
                                  reads=[("ta", i2), ("tb", i2), "carry"], writes=[("th", i2)])
                            kb.op("dve", lambda: nc.vector.tensor_copy(out=carry[:, c:c + 1], in_=H[:, NT - 1:NT]), reads=[("th", i2)], writes=["carry"])
                            kb.dma("sp", hfv[:, c, t0:t0 + NT], H[:, :], reads=[("th", i2)], writes=[("hfd", it, c)])
                        else:
                            kb.dma("sp", abv[:, c, t0:t0 + NT], A[:, :], reads=[("ta", i2)], writes=[("abd", it, c)])
                            kb.dma("sp", bxv[:, c, t0:t0 + NT], B_[:, :], reads=[("tb", i2)], writes=[("bxd", it, c)])
    with kb.phase():
        tile_ctx(kb, cx)
        hT = cx.hT
        yT = sb(kb, "rg_yT", [128, NCH, NT], BF16)
        A = [sb(kb, f"rg2_a{i}", [128, NT], F32) for i in range(2)]
        Bx = [sb(kb, f"rg2_b{i}", [128, NT], F32) for i in range(2)]
        Hf = [sb(kb, f"rg2_h{i}", [128, NT], F32) for i in range(2)]
        Gg = [sb(kb, f"rg2_g{i}", [128, NT], BF16) for i in range(2)]
        Hb = [sb(kb, f"rg2_hb{i}", [128, NT], F32) for i in range(2)]
        carry = sb(kb, "rg2_carry", [128, NCH], F32)
        kb.op("dve", lambda: nc.vector.memset(carry[:, :], 0.0), writes=["carry2"])
        k = 0
        for it in reversed(range(nt)):
            t0 = it * NT
            for c in range(NCH):
                kb.dma("sp", hT[:, c, :NT], src_v[:, c, src_pad + t0:src_pad + t0 + NT], writes=["hT"])
            for c in range(NCH):
                i2 = k % 2; k += 1
                kb.dma("sp", A[i2][:, :], abv[:, c, t0:t0 + NT], reads=[("abd", it, c)], writes=[("A2", i2)])
                kb.dma("sp", Bx[i2][:, :], bxv[:, c, t0:t0 + NT], reads=[("bxd", it, c)], writes=[("B2", i2)])
                kb.dma("sp", Hf[i2][:, :], hfv[:, c, t0:t0 + NT], reads=[("hfd", it, c)], writes=[("H2", i2)])
                kb.dma("sp", Gg[i2][:, :], ggv[:, c, t0:t0 + NT], reads=[("ggd", it, c)], writes=[("G2", i2)])
                kb.op("dve", lambda: nc.vector.tensor_tensor_scan(out=Hb[i2][:, ::-1], data0=A[i2][:, ::-1], data1=Bx[i2][:, ::-1],
                                                                  initial=carry[:, c:c + 1], op0=ALU.mult, op1=ALU.add),
                      reads=[("A2", i2), ("B2", i2), "carry2"], writes=[("Hb2", i2)])
                kb.op("dve", lambda: nc.vector.tensor_copy(out=carry[:, c:c + 1], in_=Hb[i2][:, 0:1]), reads=[("Hb2", i2)], writes=["carry2"])
                kb.op("dve", lambda: nc.vector.tensor_tensor(out=Hb[i2][:, :], in0=Hb[i2][:, :], in1=Hf[i2][:, :], op=ALU.add),
                      reads=[("Hb2", i2), ("H2", i2)], writes=[("Hb2", i2)])
                kb.op("dve", lambda: nc.vector.tensor_tensor(out=yT[:, c, :], in0=Hb[i2][:, :], in1=Gg[i2][:, :], op=ALU.mult),
                      reads=[("Hb2", i2), ("G2", i2)], writes=[("yT", c)])
            emit_outproj_residual(kb, pp, yT, lambda k_: ("yT", k_), NCH, w["wout"], cx.wO, cx.wOi, hT, "hT", 0, NT,
                                  dst_v, dst_pad + t0, cx.ho_bufs, out_toks, "rg")


def prep_rglru_weights(w_in, conv_w, conv_b, gate_w, gate_b, lam, w_out, g):
    def fm_blocks(wm):
        n = wm.shape[1] // 128
        return np.ascontiguousarray(wm.reshape(-1, 128, n, 128).transpose(2, 1, 0, 3), dtype=np.float32)
    gw = gate_w.reshape(2, RG_BLOCKS, 2, 128, 512).transpose(3, 0, 1, 2, 4)
    cw = np.concatenate([conv_w, conv_b[None]], 0).reshape(5, NCH, 128).transpose(2, 1, 0)
    gbr = gate_b.reshape(2, RG_BLOCKS, 2, 2, 128)
    gb = gbr.transpose(4, 0, 1, 3, 2).reshape(128, 2, NCH, 2)
    lm = lam.reshape(2, NCH, 128).transpose(2, 0, 1)
    f = lambda a: np.ascontiguousarray(a, dtype=np.float32)
    return dict(win=fm_blocks(w_in), gw=f(gw), cw=f(cw), gb=f(gb), lam=f(lm), wout=fm_blocks(w_out),
                g=f(g.reshape(NCH, 128).T))


def build_rglru_test(T, NT=512):
    nc = bass.Bass("TRN2", target_bir_lowering=False)
    PAD = 2
    hin = nc.dram_tensor("hin", [D_MODEL, T + 2 * PAD], F32, kind="ExternalInput").ap()
    w = dict(win=nc.dram_tensor("win", [32, 128, NCH, 128], F32, kind="ExternalInput").ap(),
             gw=nc.dram_tensor("gw", [128, 2, RG_BLOCKS, 2, 512], F32, kind="ExternalInput").ap(),
             cw=nc.dram_tensor("cw", [128, NCH, 5], F32, kind="ExternalInput").ap(),
             gb=nc.dram_tensor("gb", [128, 2, NCH, 2], F32, kind="ExternalInput").ap(),
             lam=nc.dram_tensor("lam", [128, 2, NCH], F32, kind="ExternalInput").ap(),
             wout=nc.dram_tensor("wout", [NCH, 128, NCH, 128], F32, kind="ExternalInput").ap(),
             g=nc.dram_tensor("g", [128, NCH], F32, kind="ExternalInput").ap())
    hout = nc.dram_tensor("hout", [D_MODEL, T], F32, kind="ExternalOutput").ap()
    scr = dict(ggT=nc.dram_tensor("ggT", [D_MODEL, T], BF16).ap(), hfT=nc.dram_tensor("hfT", [D_MODEL, T], F32).ap(),
               abT=nc.dram_tensor("abT", [D_MODEL, T], F32).ap(), bxbT=nc.dram_tensor("bxbT", [D_MODEL, T], F32).ap())
    with contextlib.ExitStack() as stack:
        kb = KB(nc, stack)
        pp = PsumPool(kb, nbanks=6)
        cx = make_ctx(kb, NT)
        outs = []
        emit_rglru(kb, pp, cx, T, hin.rearrange("(c p) t -> p c t", p=128), PAD, hout.rearrange("(c p) t -> p c t", p=128), 0, w, scr, outs, NT)
        kb.finish(outs)
    return nc


SSD_INNER = 4096
SSD_HEADS = 64
SSD_G = 8
L = 128


def make_masks(kb, cx):
    nc = kb.nc
    cx.mask = []
    for d in range(2):
        m = sb(kb, f"mask{d}", [128, 128], F32)
        kb.op("pool", lambda: nc.gpsimd.memset(m[:, :], 1.0), writes=[("mask", d)])
        cm, coef = (-1, 1) if d == 0 else (1, -1)
        kb.op("pool", lambda: nc.gpsimd.affine_select(out=m[:, :], in_=m[:, :], pattern=[[coef, 128]], compare_op=ALU.is_ge,
                                                      fill=0.0, base=0, channel_multiplier=cm),
              reads=[("mask", d)], writes=[("mask", d)])
        cx.mask.append(m)


def emit_ssd(kb, pp, cx, T, src_v, src_pad, dst_v, dst_pad, w, scr, out_toks, NT=512):
    nc = kb.nc
    nt = T // NT
    nchunk = T // L
    W = NT + 3
    xcv = scr["xcT"].rearrange("(b p) t -> p b t", p=128)
    ynv = scr["ynT"].rearrange("(b p) t -> p b t", p=128)
    with kb.phase():
        tile_ctx(kb, cx)
        hT, uT, g_sb = cx.hT, cx.uT, cx.g_sb
        wB = [sb(kb, f"s1wB{i}", [128, NCH, 512], BF16) for i in range(2)]
        cw = sb(kb, "s1cw", [128, 48, 5], F32)
        dtb = sb(kb, "s1dtb", [128, 2], F32)
        A_sb = sb(kb, "s1A", [128, 2], F32)
        zs = [sb(kb, f"s1zs{i}", [128, 512], BF16) for i in range(2)]
        G = [sb(kb, f"s1G{i}", [128, W], F32) for i in range(2)]
        acc = [sb(kb, f"s1acc{i}", [128, NT], F32) for i in range(2)]
        xo = [sb(kb, f"s1xo{i}", [128, NT], BF16) for i in range(2)]
        d1 = sb(kb, "s1d1", [128, NT], F32)
        d2 = sb(kb, "s1d2", [128, NT], F32)
        d3 = sb(kb, "s1d3", [128, NT], F32)
        kb.dma("sp", g_sb[:, :], w["g"][:, :], writes=["g"])
        kb.dma("sp", cw[:, :, :], w["cw"], writes=["cw"])
        kb.dma("sp", dtb[:, 0:1], w["dtb"], writes=["dtb"])
        kb.dma("sp", A_sb[:, 0:1], w["alog"], writes=["A"])
        kb.op("act", lambda: nc.scalar.activation(out=A_sb[:, 1:2], in_=A_sb[:, 0:1], func=AF.Exp), reads=["A"], writes=["A"])
        kb.op("dve", lambda: nc.vector.tensor_scalar(out=A_sb[:, 1:2], in0=A_sb[:, 1:2], scalar1=-1.0, scalar2=None, op0=ALU.mult), reads=["A"], writes=["A"])
        wbi = 0
        for it in range(nt):
            t0 = it * NT
            for c in range(NCH):
                kb.dma("sp", hT[:, c, :W], src_v[:, c, src_pad + t0 - 2:src_pad + t0 - 2 + W], writes=["hT"])
            emit_rmsnorm_fm(kb, pp, hT, "hT", g_sb, uT, "uT", cx.ones_bf, W, cx.sq_bufs, (cx.rstd_t, "rstd"))
            for cg in range(8):
                wb = wB[wbi % 2]; wtok = ("s1wB", wbi % 2); wbi += 1
                kb.dma("pool", wb[:, :, :], w["wz"][cg], writes=[wtok])
                for sub in range(NT // 128):
                    ps, pt = pp.next()
                    for kc in range(NCH):
                        kb.op("pe", lambda: nc.tensor.matmul(ps[:, :512], lhsT=uT[:, kc, 2 + sub * 128:2 + (sub + 1) * 128], rhs=wb[:, kc, :],
                                                             start=(kc == 0), stop=(kc == NCH - 1)),
                              reads=[wtok, "uT"], writes=[pt], inc=(kc == NCH - 1))
                    i2 = (cg * 4 + sub) % 2
                    kb.op("act", lambda: nc.scalar.activation(out=zs[i2][:, :], in_=ps[:, :512], func=AF.Silu), reads=[pt], writes=[("zs", i2)])
                    r0 = t0 + sub * 128
                    kb.dma("sp", scr["zs"][r0:r0 + 128, cg * 512:(cg + 1) * 512], zs[i2][:, :], reads=[("zs", i2)], writes=[("zsd", r0 // 128, cg)])
            for blk in range(48):
                wb = cx.wA[cx.wAi[0] % 2]; wtok = ("wA", cx.wAi[0] % 2); cx.wAi[0] += 1
                kb.dma("pool", wb[:, :, 0:128], w["wx"][blk], writes=[wtok])
                i2 = blk % 2
                (psm, ptm), (psx, ptx) = emit_proj_block(kb, pp, wb, wtok, 0, uT, "uT", [(0, NT), (NT, 3)])
                Gt = G[i2]; gtok = ("G", i2)
                kb.op("act", lambda: nc.scalar.copy(out=Gt[:, 0:NT], in_=psm[:, :NT]), reads=[ptm], writes=[gtok])
                kb.op("act", lambda: nc.scalar.copy(out=Gt[:, NT:NT + 3], in_=psx[:, :3]), reads=[ptx], writes=[gtok])
                ac = acc[i2]; atok = ("acc", i2)
                kb.op("dve", lambda: nc.vector.tensor_scalar(out=ac[:, :], in0=Gt[:, 0:NT], scalar1=cw[:, blk, 0:1], scalar2=cw[:, blk, 4:5],
                                                             op0=ALU.mult, op1=ALU.add), reads=[gtok, "cw"], writes=[atok])
                for tap in range(1, 4):
                    kb.op("dve", lambda: nc.vector.scalar_tensor_tensor(out=ac[:, :], in0=Gt[:, tap:tap + NT], scalar=cw[:, blk, tap:tap + 1],
                                                                        in1=ac[:, :], op0=ALU.mult, op1=ALU.add),
                          reads=[gtok, atok], writes=[atok])
                kb.op("act", lambda: nc.scalar.activation(out=xo[i2][:, :], in_=ac[:, :], func=AF.Silu), reads=[atok], writes=[("xo", i2)])
                kb.dma("sp", xcv[:, blk, t0:t0 + NT], xo[i2][:, :], reads=[("xo", i2)], writes=[("xcd", it, blk)])
            wb = cx.wA[cx.wAi[0] % 2]; wtok = ("wA", cx.wAi[0] % 2); cx.wAi[0] += 1
            kb.dma("pool", wb[:, :, 0:128], w["wdt"][0], writes=[wtok])
            (ps, pt), = emit_proj_block(kb, pp, wb, wtok, 0, uT, "uT", [(2, NT)])
            kb.op("act", lambda: nc.scalar.activation(out=d1[:, :], in_=ps[:, :NT], func=AF.Identity, bias=dtb[:, 0:1], scale=1.0),
                  reads=[pt, "dtb"], writes=["d1"])
            emit_softplus_small(kb, d2[:, :], d1[:, :], d3[:, :], "d1")
            kb.dma("sp", scr["dtT"][:, t0:t0 + NT], d2[:, :], reads=["d1"], writes=[("dtd", it)])
            kb.op("dve", lambda: nc.vector.tensor_scalar(out=d1[:, :], in0=d2[:, :], scalar1=A_sb[:, 1:2], scalar2=None, op0=ALU.mult),
                  reads=["d1", "A"], writes=["d1"])
            for ch in range(NT // L):
                sl = slice(ch * L, (ch + 1) * L)
                rsl = slice((ch + 1) * L - 1, ch * L - 1 if ch > 0 else None, -1)
                kb.op("dve", lambda: nc.vector.tensor_tensor_scan(out=d3[0:64, sl], data0=cx.onesf[0:64, 0:L], data1=d1[0:64, sl], initial=0.0,
                                                                  op0=ALU.mult, op1=ALU.add), reads=["d1", "onesf"], writes=["d1"])
                kb.op("dve", lambda: nc.vector.tensor_tensor_scan(out=d3[64:128, rsl], data0=cx.onesf[64:128, 0:L], data1=d1[64:128, rsl], initial=0.0,
                                                                  op0=ALU.mult, op1=ALU.add), reads=["d1", "onesf"], writes=["d1"])
            kb.dma("sp", scr["cumT"][:, t0:t0 + NT], d3[:, :], reads=["d1"], writes=[("cumd", it)])
    with kb.phase():
        make_masks(kb, cx)
        xT_in = sb(kb, "s2xTin", [128, 40, L], BF16)
        CT = [sb(kb, f"s2CT{i}", [128, 8, L], BF16) for i in range(2)]
        BT = [sb(kb, f"s2BT{i}", [128, 8, L], BF16) for i in range(2)]
        xtm = [sb(kb, f"s2xtm{i}", [128, SSD_INNER], BF16) for i in range(2)]
        btm = [sb(kb, f"s2btm{i}", [128, 1024], BF16) for i in range(2)]
        cdt = [sb(kb, f"s2cdt{i}", [64, 2, L], F32) for i in range(2)]
        wT = sb(kb, "s2wT", [64, L], F32)
        eg = sb(kb, "s2eg", [64, 2], F32)
        dg = sb(kb, "s2dg", [64, 64], F32)
        tm = [sb(kb, f"s2tm{i}", [128, 4, 64], F32) for i in range(2)]
        cb = [sb(kb, f"s2cb{i}", [128, 8, L], F32) for i in range(4)]
        Dc = [sb(kb, f"s2Dc{i}", [128, 8, L], F32) for i in range(2)]
        ecb = [sb(kb, f"s2ecb{i}", [128, 8, L], F32) for i in range(2)]
        CBm = [sb(kb, f"s2CBm{i}", [128, L], F32) for i in range(2)]
        Wt = [sb(kb, f"s2Wt{i}", [128, 8, L], BF16) for i in range(2)]
        CsT = [sb(kb, f"s2CsT{i}", [128, 8, L], BF16) for i in range(2)]
        xw = [sb(kb, f"s2xw{i}", [128, 512], BF16) for i in range(2)]
        Sf = sb(kb, "s2Sf", [128, SSD_G, 512], F32)
        Sb = sb(kb, "s2Sb", [128, SSD_G, 512], BF16)
        yf = sb(kb, "s2yf", [128, SSD_INNER], F32)
        yg = sb(kb, "s2yg", [128, SSD_INNER], F32)
        zsc = sb(kb, "s2zs", [128, SSD_INNER], BF16)
        gN = sb(kb, "s2gN", [128, SSD_INNER], F32)
        dsk = sb(kb, "s2dsk", [128, SSD_INNER], F32)
        ynb = sb(kb, "s2ynb", [128, SSD_INNER], BF16)
        ynT_sb = [sb(kb, f"s2ynT{i}", [128, 8, L], BF16) for i in range(2)]
        st = sb(kb, "s2st", [128, 4], F32)
        kb.dma("sp", gN[:, :], w["gn"].partition_broadcast(128), writes=["gN"])
        kb.dma("sp", dsk[:, :], w["dsk"].partition_broadcast(128), writes=["dsk"])
        ci = 0
        for d in range(2):
            kb.op("dve", lambda: nc.vector.memset(Sf[:, :, :], 0.0), reads=[], writes=[("Sf", g_) for g_ in range(SSD_G)])
            kb.op("pool", lambda: nc.gpsimd.memset(Sb[:, :, :], 0.0), reads=[], writes=[("Sb", g_) for g_ in range(SSD_G)])
            order = range(nchunk) if d == 0 else reversed(range(nchunk))
            last = L - 1 if d == 0 else 0
            for c in order:
                c0 = c * L
                i2 = ci % 2; ci += 1
                it = c0 // NT
                kb.dma("sp", CT[i2][:, :, :], xcv[:, 40:48, c0:c0 + L], reads=[("xcd", it, b_) for b_ in range(40, 48)], writes=[("CT", i2)])
                kb.dma("sp", BT[i2][:, :, :], xcv[:, 32:40, c0:c0 + L], reads=[("xcd", it, b_) for b_ in range(32, 40)], writes=[("BT", i2)])
                kb.dma("sp", cdt[i2][:, 0, :], scr["cumT"][d * 64:(d + 1) * 64, c0:c0 + L], reads=[("cumd", it)], writes=[("cdt", i2)])
                kb.dma("sp", cdt[i2][:, 1, :], scr["dtT"][d * 64:(d + 1) * 64, c0:c0 + L], reads=[("dtd", it)], writes=[("cdt", i2)])
                X, Bm = xtm[i2], btm[i2]
                if d == 0:
                    kb.dma("sp", xT_in[:, 0:32, :], xcv[:, 0:32, c0:c0 + L], reads=[("xcd", it, b_) for b_ in range(32)], writes=["xTin"])
                    for grp in range(5):
                        pb = cx.psb[cx.psbi[0] % 2]; pbtok = ("psb", cx.psbi[0] % 2); cx.psbi[0] += 1
                        for j in range(8):
                            src_ap = xT_in[:, grp * 8 + j, :] if grp < 4 else BT[i2][:, j, :]
                            kb.op("pe", lambda: nc.tensor.transpose(out=pb[:, j * 128:(j + 1) * 128], in_=src_ap, identity=cx.ident[:, :]),
                                  reads=["xTin" if grp < 4 else ("BT", i2), "ident"], writes=[pbtok], inc=(j == 7))
                        if grp < 4:
                            kb.op("act", lambda: nc.scalar.copy(out=X[:, grp * 1024:(grp + 1) * 1024], in_=pb[:, :]), reads=[pbtok], writes=[("xtm", i2)])
                        else:
                            kb.op("act", lambda: nc.scalar.copy(out=Bm[:, :], in_=pb[:, :]), reads=[pbtok], writes=[("btm", i2)])
                    kb.dma("pool", scr["xtm"][c0:c0 + L, :], X[:, :], reads=[("xtm", i2)], writes=[("xtmd", c)])
                    kb.dma("pool", scr["btm"][c0:c0 + L, :], Bm[:, :], reads=[("btm", i2)], writes=[("btmd", c)])
                else:
                    kb.dma("sp", X[:, :], scr["xtm"][c0:c0 + L, :], reads=[("xtmd", c)], writes=[("xtm", i2)])
                    kb.dma("sp", Bm[:, :], scr["btm"][c0:c0 + L, :], reads=[("btmd", c)], writes=[("btm", i2)])
                    kb.dma("sp", yf[:, :], scr["yf"][c0:c0 + L, :], reads=[("yfd", c, g_) for g_ in range(SSD_G)], writes=["yf"])
                    kb.dma("sp", zsc[:, :], scr["zs"][c0:c0 + L, :], reads=[("zsd", c, cg_) for cg_ in range(8)], writes=["zsc"])
                cd_ = cdt[i2]
                kb.op("act", lambda: nc.scalar.activation(out=wT[:, :], in_=cd_[:, 0, :], func=AF.Exp, bias=cd_[:, 0, last:last + 1], scale=-1.0),
                      reads=[("cdt", i2)], writes=["wT"])
                kb.op("dve", lambda: nc.vector.tensor_tensor(out=wT[:, :], in0=wT[:, :], in1=cd_[:, 1, :], op=ALU.mult), reads=["wT", ("cdt", i2)], writes=["wT"])
                kb.op("act", lambda: nc.scalar.activation(out=eg[:, 0:1], in_=cd_[:, 0, last:last + 1], func=AF.Exp), reads=[("cdt", i2)], writes=["eg"])
                kb.op("dve", lambda: nc.vector.tensor_scalar(out=dg[:, :], in0=cx.identf[0:64, 0:64], scalar1=eg[:, 0:1], scalar2=None, op0=ALU.mult),
                      reads=["eg", "identf"], writes=["dg"])
                ps, pt = pp.next()
                kb.op("pe", lambda: nc.tensor.matmul(ps[:, 0:64], lhsT=cd_[:, 0, :], rhs=cx.identf[0:64, 0:64], start=True, stop=True),
                      reads=[("cdt", i2), "identf"], writes=[pt], inc=False)
                kb.op("pe", lambda: nc.tensor.matmul(ps[:, 64:128], lhsT=cd_[:, 1, :], rhs=cx.identf[0:64, 0:64], start=True, stop=True),
                      reads=[("cdt", i2)], writes=[pt], inc=False)
                kb.op("pe", lambda: nc.tensor.matmul(ps[:, 128:192], lhsT=wT[:, :], rhs=cx.identf[0:64, 0:64], start=True, stop=True),
                      reads=["wT"], writes=[pt], inc=False)
                kb.op("pe", lambda: nc.tensor.matmul(ps[:, 192:256], lhsT=cx.onesf[0:64, :], rhs=dg[:, :], start=True, stop=True),
                      reads=["dg", "onesf"], writes=[pt], inc=True)
                TM = tm[i2]
                kb.op("act", lambda: nc.scalar.copy(out=TM[:, :, :].rearrange("p a b -> p (a b)"), in_=ps[:, 0:256]), reads=[pt], writes=[("tm", i2)])
                st2 = {}
                def stage0(g_):
                    j4 = g_ % 4
                    r0 = d * 64 + g_ * 8
                    kb.dma("sp", cb[j4][:, :, :], scr["cumT"][r0:r0 + 8, c0:c0 + L].partition_broadcast(128), reads=[("cumd", it)], writes=[("cb", j4)])
                def stage1(g_):
                    j2 = (ci * SSD_G + g_) % 2
                    j4 = g_ % 4
                    ps_cb, pt_cb = pp.next()
                    kb.op("pe", lambda: nc.tensor.matmul(ps_cb[:, :L], lhsT=BT[i2][:, g_, :], rhs=CT[i2][:, g_, :], start=True, stop=True),
                          reads=[("BT", i2), ("CT", i2)], writes=[pt_cb], inc=True)
                    kb.op("dve", lambda: nc.vector.tensor_tensor(out=CBm[j2][:, :], in0=ps_cb[:, :L], in1=cx.mask[d][:, :], op=ALU.mult),
                          reads=[pt_cb, ("mask", d)], writes=[("CBm", j2)])
                    cum_b = TM[:, 0, g_ * 8:(g_ + 1) * 8].unsqueeze(2).to_broadcast([128, 8, L])
                    dt_b = TM[:, 1, g_ * 8:(g_ + 1) * 8].unsqueeze(2).to_broadcast([128, 8, L])
                    kb.op("dve", lambda: nc.vector.tensor_tensor(out=Dc[j2][:, :, :], in0=cb[j4][:, :, :], in1=cum_b, op=ALU.subtract),
                          reads=[("cb", j4), ("tm", i2)], writes=[("Dc", j2)])
                    kb.op("dve", lambda: nc.vector.tensor_scalar(out=Dc[j2][:, :, :], in0=Dc[j2][:, :, :], scalar1=0.0, scalar2=None, op0=ALU.min),
                          reads=[("Dc", j2)], writes=[("Dc", j2)])
                    kb.op("act", lambda: nc.scalar.activation(out=Dc[j2][:, :, :], in_=Dc[j2][:, :, :], func=AF.Exp), reads=[("Dc", j2)], writes=[("Dc", j2)])
                    kb.op("act", lambda: nc.scalar.activation(out=ecb[j2][:, :, :], in_=cb[j4][:, :, :], func=AF.Exp), reads=[("cb", j4)], writes=[("ecb", j2)])
                def stage2(g_):
                    j2 = g_ % 2
                    cum_b = TM[:, 0, g_ * 8:(g_ + 1) * 8].unsqueeze(2).to_broadcast([128, 8, L])
                    dt_b = TM[:, 1, g_ * 8:(g_ + 1) * 8].unsqueeze(2).to_broadcast([128, 8, L])
                    kb.op("dve", lambda: nc.vector.tensor_tensor(out=Dc[j2][:, :, :], in0=Dc[j2][:, :, :],
                                                                 in1=CBm[j2][:, :].unsqueeze(1).to_broadcast([128, 8, L]), op=ALU.mult),
                          reads=[("Dc", j2), ("CBm", j2)], writes=[("Dc", j2)])
                    kb.op("dve", lambda: nc.vector.tensor_tensor(out=Wt[j2][:, :, :], in0=Dc[j2][:, :, :], in1=dt_b, op=ALU.mult),
                          reads=[("Dc", j2), ("tm", i2)], writes=[("Wt", j2)])
                    kb.op("dve", lambda: nc.vector.tensor_tensor(out=CsT[j2][:, :, :], in0=ecb[j2][:, :, :],
                                                                 in1=CT[i2][:, g_, :].unsqueeze(1).to_broadcast([128, 8, L]), op=ALU.mult),
                          reads=[("ecb", j2), ("CT", i2)], writes=[("CsT", j2)])
                    ps_y, pt_y = pp.next()
                    for h_ in range(8):
                        H = g_ * 8 + h_
                        kb.op("pe", lambda: nc.tensor.matmul(ps_y[:, h_ * 64:(h_ + 1) * 64], lhsT=Wt[j2][:, h_, :], rhs=X[:, H * 64:(H + 1) * 64],
                                                             start=True, stop=False),
                              reads=[("Wt", j2), ("xtm", i2)], writes=[pt_y], inc=False)
                        kb.op("pe", lambda: nc.tensor.matmul(ps_y[:, h_ * 64:(h_ + 1) * 64], lhsT=CsT[j2][:, h_, :], rhs=Sb[:, g_, h_ * 64:(h_ + 1) * 64],
                                                             start=False, stop=True),
                              reads=[("CsT", j2), ("Sb", g_)], writes=[pt_y], inc=(h_ == 7))
                    w_b = TM[:, 2, g_ * 8:(g_ + 1) * 8].unsqueeze(2).to_broadcast([128, 8, 64])
                    e_b = TM[:, 3, g_ * 8:(g_ + 1) * 8].unsqueeze(2).to_broadcast([128, 8, 64])
                    kb.op("dve", lambda: nc.vector.tensor_tensor(out=xw[j2][:, :].rearrange("p (h q) -> p h q", h=8),
                                                                 in0=X[:, g_ * 512:(g_ + 1) * 512].rearrange("p (h q) -> p h q", h=8), in1=w_b, op=ALU.mult),
                          reads=[("xtm", i2), ("tm", i2)], writes=[("xw", j2)])
                    ps_s, pt_s = pp.next()
                    kb.op("pe", lambda: nc.tensor.matmul(ps_s[:, :512], lhsT=Bm[:, g_ * 128:(g_ + 1) * 128], rhs=xw[j2][:, :], start=True, stop=True),
                          reads=[("btm", i2), ("xw", j2)], writes=[pt_s], inc=True)
                    st2[g_] = (ps_y, pt_y, ps_s, pt_s)
                def stage3(g_):
                    ps_y, pt_y, ps_s, pt_s = st2[g_]
                    e_b = TM[:, 3, g_ * 8:(g_ + 1) * 8].unsqueeze(2).to_broadcast([128, 8, 64])
                    kb.op("dve", lambda: nc.vector.tensor_tensor(out=Sf[:, g_, :].rearrange("p (h q) -> p h q", h=8),
                                                                 in0=Sf[:, g_, :].rearrange("p (h q) -> p h q", h=8), in1=e_b, op=ALU.mult),
                          reads=[("Sf", g_), ("tm", i2)], writes=[("Sf", g_)])
                    kb.op("dve", lambda: nc.vector.tensor_tensor(out=Sf[:, g_, :], in0=Sf[:, g_, :], in1=ps_s[:, :512], op=ALU.add),
                          reads=[("Sf", g_), pt_s], writes=[("Sf", g_)])
                    kb.op("act", lambda: nc.scalar.copy(out=Sb[:, g_, :], in_=Sf[:, g_, :]), reads=[("Sf", g_)], writes=[("Sb", g_)])
                    gs = slice(g_ * 512, (g_ + 1) * 512)
                    if d == 0:
                        kb.op("act", lambda: nc.scalar.copy(out=yg[:, gs], in_=ps_y[:, :512]), reads=[pt_y], writes=[("yg", g_)])
                        kb.dma("pool", scr["yf"][c0:c0 + L, gs], yg[:, gs], reads=[("yg", g_)], writes=[("yfd", c, g_)])
                    else:
                        kb.op("dve", lambda: nc.vector.tensor_tensor(out=yg[:, gs], in0=ps_y[:, :512], in1=yf[:, gs], op=ALU.add),
                              reads=[pt_y, "yf"], writes=[("yg", g_)])

                for g0_ in range(4):
                    stage0(g0_)
                stage1(0)
                stage1(1)
                stage2(0)
                for g_ in range(SSD_G):
                    if g_ + 4 < SSD_G:
                        stage0(g_ + 4)
                    if g_ + 2 < SSD_G:
                        stage1(g_ + 2)
                    if g_ + 1 < SSD_G:
                        stage2(g_ + 1)
                    stage3(g_)
                if d == 1:
                    allg = [("yg", g_) for g_ in range(SSD_G)]
                    kb.op("pool", lambda: nc.gpsimd.tensor_tensor(out=yf[:, :], in0=X[:, :], in1=dsk[:, :], op=ALU.mult), reads=[("xtm", i2), "dsk"], writes=["yf"])
                    kb.op("dve", lambda: nc.vector.tensor_tensor(out=yg[:, :], in0=yg[:, :], in1=yf[:, :], op=ALU.add), reads=allg + ["yf"], writes=allg)
                    kb.op("dve", lambda: nc.vector.tensor_tensor(out=yg[:, :], in0=yg[:, :], in1=zsc[:, :], op=ALU.mult), reads=allg + ["zsc"], writes=allg)
                    kb.op("act", lambda: nc.scalar.activation(out=yf[:, :], in_=yg[:, :], func=AF.Square, accum_out=st[:, 0:1]), reads=allg, writes=["yf", "st"])
                    kb.op("dve", lambda: nc.vector.tensor_scalar(out=st[:, 1:2], in0=st[:, 0:1], scalar1=1.0 / SSD_INNER, scalar2=EPS, op0=ALU.mult, op1=ALU.add),
                          reads=["st"], writes=["st"])
                    emit_rsqrt_inplace(kb, st[:, 1:2], "st")
                    kb.op("dve", lambda: nc.vector.scalar_tensor_tensor(out=ynb[:, :], in0=yg[:, :], scalar=st[:, 1:2], in1=gN[:, :], op0=ALU.mult, op1=ALU.mult),
                          reads=allg + ["st", "gN"], writes=["ynb"])
                    for grp in range(4):
                        pb = cx.psb[cx.psbi[0] % 2]; pbtok = ("psb", cx.psbi[0] % 2); cx.psbi[0] += 1
                        for j in range(8):
                            blk = grp * 8 + j
                            kb.op("pe", lambda: nc.tensor.transpose(out=pb[:, j * 128:(j + 1) * 128], in_=ynb[:, blk * 128:(blk + 1) * 128], identity=cx.ident[:, :]),
                                  reads=["ynb", "ident"], writes=[pbtok], inc=(j == 7))
                        yo = ynT_sb[grp % 2]
                        kb.op("act", lambda: nc.scalar.copy(out=yo[:, :, :].rearrange("p a b -> p (a b)"), in_=pb[:, :]), reads=[pbtok], writes=[("ynT", grp % 2)])
                        kb.dma("pool", ynv[:, grp * 8:(grp + 1) * 8, c0:c0 + L], yo[:, :, :], reads=[("ynT", grp % 2)], writes=[("ynd", c, grp)])
    with kb.phase():
        tile_ctx(kb, cx)
        hT = cx.hT
        yT = sb(kb, "s3yT", [128, 32, NT], BF16)
        for it in range(nt):
            t0 = it * NT
            for c in range(NCH):
                kb.dma("sp", hT[:, c, :NT], src_v[:, c, src_pad + t0:src_pad + t0 + NT], writes=["hT"])
            kb.dma("sp", yT[:, :, :], ynv[:, :, t0:t0 + NT], writes=[("yT", k_) for k_ in range(32)])
            emit_outproj_residual(kb, pp, yT, lambda k_: ("yT", k_), 32, w["wout"], cx.wO, cx.wOi, hT, "hT", 0, NT,
                                  dst_v, dst_pad + t0, cx.ho_bufs, out_toks, "ssd")


def prep_ssd_weights(w_in, conv_w, conv_b, dt_bias, a_log, d_skip, norm_g, w_out, g):
    f = lambda a: np.ascontiguousarray(a, dtype=np.float32)
    def fm_blocks(wm):
        n = wm.shape[1] // 128
        return f(wm.reshape(-1, 128, n, 128).transpose(2, 1, 0, 3))
    wz = w_in[:, :SSD_INNER].reshape(NCH, 128, 8, 512).transpose(2, 1, 0, 3)
    wx = fm_blocks(w_in[:, SSD_INNER:SSD_INNER + 6144])
    wdt = fm_blocks(w_in[:, SSD_INNER + 6144:])
    cw = np.concatenate([conv_w, conv_b[None]], 0).reshape(5, 48, 128).transpose(2, 1, 0)
    return dict(wz=f(wz), wx=wx, wdt=wdt, cw=f(cw), dtb=f(dt_bias.reshape(128, 1)), alog=f(a_log.reshape(128, 1)),
                dsk=f(np.repeat(d_skip, 64).reshape(1, SSD_INNER)), gn=f(norm_g.reshape(1, SSD_INNER)),
                wout=fm_blocks(w_out), g=f(g.reshape(NCH, 128).T))


SSD_W_SHAPES = dict(wz=[8, 128, NCH, 512], wx=[48, 128, NCH, 128], wdt=[1, 128, NCH, 128], cw=[128, 48, 5], dtb=[128, 1], alog=[128, 1],
                    dsk=[1, SSD_INNER], gn=[1, SSD_INNER], wout=[NCH, 128, 32, 128], g=[128, NCH])


def ssd_scratch(nc, T, pfx=""):
    return dict(zs=nc.dram_tensor(pfx + "zs", [T, SSD_INNER], BF16).ap(), xcT=nc.dram_tensor(pfx + "xcT", [6144, T], BF16).ap(),
                dtT=nc.dram_tensor(pfx + "dtT", [128, T], F32).ap(), cumT=nc.dram_tensor(pfx + "cumT", [128, T], F32).ap(),
                xtm=nc.dram_tensor(pfx + "xtm", [T, SSD_INNER], BF16).ap(), btm=nc.dram_tensor(pfx + "btm", [T, 1024], BF16).ap(),
                yf=nc.dram_tensor(pfx + "yf", [T, SSD_INNER], F32).ap(), ynT=nc.dram_tensor(pfx + "ynT", [SSD_INNER, T], BF16).ap())


def build_ssd_test(T, NT=512):
    nc = bass.Bass("TRN2", target_bir_lowering=False)
    PAD = 2
    hin = nc.dram_tensor("hin", [D_MODEL, T + 2 * PAD], F32, kind="ExternalInput").ap()
    w = {k: nc.dram_tensor(k, shp, F32, kind="ExternalInput").ap() for k, shp in SSD_W_SHAPES.items()}
    hout = nc.dram_tensor("hout", [D_MODEL, T], F32, kind="ExternalOutput").ap()
    scr = ssd_scratch(nc, T)
    with contextlib.ExitStack() as stack:
        kb = KB(nc, stack)
        pp = PsumPool(kb, nbanks=6)
        cx = make_ctx(kb, NT)
        outs = []
        emit_ssd(kb, pp, cx, T, hin.rearrange("(c p) t -> p c t", p=128), PAD, hout.rearrange("(c p) t -> p c t", p=128), 0, w, scr, outs, NT)
        kb.finish(outs)
    return nc


ML_H = 8
ML_DK = 128
ML_DV = 256


def emit_mlstm(kb, pp, cx, T, src_v, src_pad, dst_v, dst_pad, w, scr, out_toks, NT=512):
    nc = kb.nc
    nt = T // NT
    nchunk = T // L
    qv = scr["qT"].rearrange("(h p) t -> p h t", p=128)
    kv = scr["kT"].rearrange("(h p) t -> p h t", p=128)
    hhv = scr["hhT"].rearrange("(b p) t -> p b t", p=128)
    kscale = float(ML_DK) ** -0.5
    with kb.phase():
        tile_ctx(kb, cx)
        hT, uT, g_sb = cx.hT, cx.uT, cx.g_sb
        wB = [sb(kb, f"m1wB{i}", [128, NCH, 512], BF16) for i in range(2)]
        wg = sb(kb, "m1wg", [128, NCH, 32], BF16)
        gbias = sb(kb, "m1gb", [8, 4], F32)
        ob = [sb(kb, f"m1ob{i}", [128, 512], BF16) for i in range(2)]
        gt = [sb(kb, f"m1gt{i}", [8, NT], F32) for i in range(3)]
        ones8 = sb(kb, "m1ones8", [8, L], F32)
        kb.op("dve", lambda: nc.vector.memset(ones8[:, :], 1.0), writes=["ones8"])
        kb.dma("sp", g_sb[:, :], w["g"][:, :], writes=["g"])
        kb.dma("pool", wg[:, :, :], w["wg"], writes=["wg"])
        kb.dma("sp", gbias[:, :], w["gbias"], writes=["gbias"])
        wbi = 0
        oi = 0
        for it in range(nt):
            t0 = it * NT
            for c in range(NCH):
                kb.dma("sp", hT[:, c, :NT], src_v[:, c, src_pad + t0:src_pad + t0 + NT], writes=["hT"])
            emit_rmsnorm_fm(kb, pp, hT, "hT", g_sb, uT, "uT", cx.ones_bf, NT, cx.sq_bufs, (cx.rstd_t, "rstd"))
            for blk in range(16):
                wb = cx.wA[cx.wAi[0] % 2]; wtok = ("wA", cx.wAi[0] % 2); cx.wAi[0] += 1
                kb.dma("pool", wb[:, :, 0:128], w["wqk"][blk], writes=[wtok])
                (ps, pt), = emit_proj_block(kb, pp, wb, wtok, 0, uT, "uT", [(0, NT)])
                i2 = oi % 2; oi += 1
                kb.op("act", lambda: nc.scalar.activation(out=ob[i2][:, :NT], in_=ps[:, :NT], func=AF.Copy, scale=(1.0 if blk < 8 else kscale)),
                      reads=[pt], writes=[("ob", i2)])
                dstv = qv if blk < 8 else kv
                kb.dma("sp", dstv[:, blk % 8, t0:t0 + NT], ob[i2][:, :NT], reads=[("ob", i2)], writes=[("qkd", it, blk)])
            for cg in range(10):
                wb = wB[wbi % 2]; wtok = ("m1wB", wbi % 2); wbi += 1
                kb.dma("pool", wb[:, :, :], w["wtm"][cg], writes=[wtok])
                for sub in range(NT // 128):
                    ps, pt = pp.next()
                    for kc in range(NCH):
                        kb.op("pe", lambda: nc.tensor.matmul(ps[:, :512], lhsT=uT[:, kc, sub * 128:(sub + 1) * 128], rhs=wb[:, kc, :],
                                                             start=(kc == 0), stop=(kc == NCH - 1)),
                              reads=[wtok, "uT"], writes=[pt], inc=(kc == NCH - 1))
                    i2 = oi % 2; oi += 1
                    r0 = t0 + sub * 128
                    if cg < 2:
                        kb.op("act", lambda: nc.scalar.activation(out=ob[i2][:, :], in_=ps[:, :512], func=AF.Copy, scale=kscale), reads=[pt], writes=[("ob", i2)])
                        kb.dma("sp", scr["ktm"][r0:r0 + 128, cg * 512:(cg + 1) * 512], ob[i2][:, :], reads=[("ob", i2)], writes=[("ktmd", r0 // 128, cg)])
                    elif cg < 6:
                        kb.op("act", lambda: nc.scalar.copy(out=ob[i2][:, :], in_=ps[:, :512]), reads=[pt], writes=[("ob", i2)])
                        kb.dma("sp", scr["vtm"][r0:r0 + 128, (cg - 2) * 512:(cg - 1) * 512], ob[i2][:, :], reads=[("ob", i2)], writes=[("vtmd", r0 // 128, cg - 2)])
                    else:
                        kb.op("act", lambda: nc.scalar.activation(out=ob[i2][:, :], in_=ps[:, :512], func=AF.Sigmoid), reads=[pt], writes=[("ob", i2)])
                        kb.dma("sp", scr["so"][r0:r0 + 128, (cg - 6) * 512:(cg - 5) * 512], ob[i2][:, :], reads=[("ob", i2)], writes=[("sod", r0 // 128, cg - 6)])
            for d in range(2):
                pss = []
                for j in (2 * d, 2 * d + 1):
                    ps, pt = pp.next()
                    for kc in range(NCH):
                        kb.op("pe", lambda: nc.tensor.matmul(ps[0:8, :NT], lhsT=wg[:, kc, 8 * j:8 * j + 8], rhs=uT[:, kc, :NT],
                                                             start=(kc == 0), stop=(kc == NCH - 1)),
                              reads=["wg", "uT"], writes=[pt], inc=(kc == NCH - 1))
                    pss.append((ps, pt))
                (psi, pti), (psf, ptf) = pss
                g0, g1, g2 = gt
                kb.op("act", lambda: nc.scalar.activation(out=g0[:, :], in_=psi[0:8, :NT], func=AF.Exp, bias=gbias[:, 2 * d:2 * d + 1], scale=1.0),
                      reads=[pti, "gbias"], writes=["g0"])
                kb.dma("sp", scr["eiT"][d * 8:(d + 1) * 8, t0:t0 + NT], g0[:, :], reads=["g0"], writes=[("eid", it, d)])
                kb.op("act", lambda: nc.scalar.activation(out=g1[:, :], in_=psf[0:8, :NT], func=AF.Identity, bias=gbias[:, 2 * d + 1:2 * d + 2], scale=1.0),
                      reads=[ptf, "gbias"], writes=["g1"])
                emit_softplus_small(kb, g2[:, :], g1[:, :], g1[:, :], "g1", neg_in=True) if False else None
                kb.op("act", lambda: nc.scalar.activation(out=g2[:, :], in_=g1[:, :], func=AF.Abs), reads=["g1"], writes=["g2"])
                kb.op("act", lambda: nc.scalar.activation(out=g2[:, :], in_=g2[:, :], func=AF.Exp, scale=-1.0), reads=["g2"], writes=["g2"])
                kb.op("act", lambda: nc.scalar.activation(out=g2[:, :], in_=g2[:, :], func=AF.Ln, bias=1.0, scale=1.0), reads=["g2"], writes=["g2"])
                kb.op("dve", lambda: nc.vector.tensor_scalar(out=g1[:, :], in0=g1[:, :], scalar1=-1.0, scalar2=0.0, op0=ALU.mult, op1=ALU.max), reads=["g1"], writes=["g1"])
                kb.op("dve", lambda: nc.vector.tensor_tensor(out=g1[:, :], in0=g1[:, :], in1=g2[:, :], op=ALU.add), reads=["g1", "g2"], writes=["g1"])
                kb.op("dve", lambda: nc.vector.tensor_scalar(out=g1[:, :], in0=g1[:, :], scalar1=-1.0, scalar2=None, op0=ALU.mult), reads=["g1"], writes=["g1"])
                for ch in range(NT // L):
                    sl = slice(ch * L, (ch + 1) * L)
                    rsl = slice((ch + 1) * L - 1, ch * L - 1 if ch > 0 else None, -1)
                    use = sl if d == 0 else rsl
                    kb.op("dve", lambda: nc.vector.tensor_tensor_scan(out=g2[:, use], data0=ones8[:, 0:L], data1=g1[:, use], initial=0.0,
                                                                      op0=ALU.mult, op1=ALU.add), reads=["g1", "g2", "ones8"], writes=["g2"])
                kb.dma("sp", scr["cumT"][d * 8:(d + 1) * 8, t0:t0 + NT], g2[:, :], reads=["g2"], writes=[("cumd", it, d)])
    with kb.phase():
        make_masks(kb, cx)
        qT = [sb(kb, f"m2qT{i}", [128, ML_H, L], BF16) for i in range(2)]
        kT = [sb(kb, f"m2kT{i}", [128, ML_H, L], BF16) for i in range(2)]
        ktm = [sb(kb, f"m2ktm{i}", [128, ML_H, ML_DK], BF16) for i in range(2)]
        Vx = [sb(kb, f"m2Vx{i}", [128, ML_H, ML_DV + 1], BF16) for i in range(2)]
        cdt = [sb(kb, f"m2cdt{i}", [8, 2, L], F32) for i in range(2)]
        wT = sb(kb, "m2wT", [8, L], F32)
        eg = sb(kb, "m2eg", [8, 2], F32)
        dg = sb(kb, "m2dg", [8, 8], F32)
        tm = [sb(kb, f"m2tm{i}", [128, 4, 8], F32) for i in range(2)]
        cb = [sb(kb, f"m2cb{i}", [128, ML_H, L], F32) for i in range(2)]
        Dc = [sb(kb, f"m2Dc{i}", [128, ML_H, L], F32) for i in range(2)]
        ecb = [sb(kb, f"m2ecb{i}", [128, ML_H, L], F32) for i in range(2)]
        Wt = [sb(kb, f"m2Wt{i}", [128, ML_H, L], BF16) for i in range(2)]
        qsT = [sb(kb, f"m2qsT{i}", [128, ML_H, L], BF16) for i in range(2)]
        kw = [sb(kb, f"m2kw{i}", [128, ML_H, ML_DK], BF16) for i in range(2)]
        Cf = sb(kb, "m2Cf", [128, ML_H, ML_DV + 1], F32)
        Cb = sb(kb, "m2Cb", [128, ML_H, ML_DV + 1], BF16)
        num = sb(kb, "m2num", [128, ML_H, ML_DV], F32)
        den = sb(kb, "m2den", [128, 2, ML_H], F32)
        hfl = sb(kb, "m2hf", [128, ML_H, ML_DV], F32)
        sq = sb(kb, "m2sq", [128, ML_H, ML_DV], F32)
        so = sb(kb, "m2so", [128, D_MODEL], BF16)
        gN = sb(kb, "m2gN", [128, D_MODEL], F32)
        hhb = sb(kb, "m2hhb", [128, D_MODEL], BF16)
        hhT_sb = [sb(kb, f"m2hhT{i}", [128, 8, L], BF16) for i in range(2)]
        ss = sb(kb, "m2ss", [128, 2, ML_H], F32)
        kb.dma("sp", gN[:, :], w["hn"].partition_broadcast(128), writes=["gN"])
        for i in range(2):
            kb.op("pool", lambda: nc.gpsimd.memset(Vx[i][:, :, ML_DV:ML_DV + 1], 1.0), writes=[("Vx1", i)])
        ci = 0
        for d in range(2):
            kb.op("dve", lambda: nc.vector.memset(Cf[:, :, :], 0.0), writes=[("Cf", h_) for h_ in range(ML_H)])
            kb.op("pool", lambda: nc.gpsimd.memset(Cb[:, :, :], 0.0), writes=[("Cb", h_) for h_ in range(ML_H)])
            order = range(nchunk) if d == 0 else reversed(range(nchunk))
            last = L - 1 if d == 0 else 0
            for c in order:
                c0 = c * L
                i2 = ci % 2; ci += 1
                it = c0 // NT
                kb.dma("sp", qT[i2][:, :, :], qv[:, :, c0:c0 + L], reads=[("qkd", it, b_) for b_ in range(8)], writes=[("qT", i2)])
                kb.dma("sp", kT[i2][:, :, :], kv[:, :, c0:c0 + L], reads=[("qkd", it, b_) for b_ in range(8, 16)], writes=[("kT", i2)])
                kb.dma("sp", ktm[i2][:, :, :], scr["ktm"][c0:c0 + L, :].rearrange("t (h k) -> t h k", h=ML_H), reads=[("ktmd", c, 0), ("ktmd", c, 1)], writes=[("ktm", i2)])
                kb.dma("sp", Vx[i2][:, :, 0:ML_DV], scr["vtm"][c0:c0 + L, :].rearrange("t (h k) -> t h k", h=ML_H),
                       reads=[("vtmd", c, j_) for j_ in range(4)], writes=[("Vx", i2)])
                kb.dma("sp", cdt[i2][:, 0, :], scr["cumT"][d * 8:(d + 1) * 8, c0:c0 + L], reads=[("cumd", it, d)], writes=[("cdt", i2)])
                kb.dma("sp", cdt[i2][:, 1, :], scr["eiT"][d * 8:(d + 1) * 8, c0:c0 + L], reads=[("eid", it, d)], writes=[("cdt", i2)])
                kb.dma("sp", cb[i2][:, :, :], scr["cumT"][d * 8:(d + 1) * 8, c0:c0 + L].partition_broadcast(128), reads=[("cumd", it, d)], writes=[("cb", i2)])
                if d == 1:
                    kb.dma("sp", hfl[:, :, :], scr["hf"][c0:c0 + L, :].rearrange("t (h k) -> t h k", h=ML_H), reads=[("hfd", c)], writes=["hfl"])
                    kb.dma("sp", so[:, :], scr["so"][c0:c0 + L, :], reads=[("sod", c, j_) for j_ in range(4)], writes=["so"])
                cd_ = cdt[i2]
                kb.op("act", lambda: nc.scalar.activation(out=wT[:, :], in_=cd_[:, 0, :], func=AF.Exp, bias=cd_[:, 0, last:last + 1], scale=-1.0),
                      reads=[("cdt", i2)], writes=["wT"])
                kb.op("dve", lambda: nc.vector.tensor_tensor(out=wT[:, :], in0=wT[:, :], in1=cd_[:, 1, :], op=ALU.mult), reads=["wT", ("cdt", i2)], writes=["wT"])
                kb.op("act", lambda: nc.scalar.activation(out=eg[:, 0:1], in_=cd_[:, 0, last:last + 1], func=AF.Exp), reads=[("cdt", i2)], writes=["eg"])
                kb.op("dve", lambda: nc.vector.tensor_scalar(out=dg[:, :], in0=cx.identf[0:8, 0:8], scalar1=eg[:, 0:1], scalar2=None, op0=ALU.mult),
                      reads=["eg", "identf"], writes=["dg"])
                ps, pt = pp.next()
                kb.op("pe", lambda: nc.tensor.matmul(ps[:, 0:8], lhsT=cd_[:, 0, :], rhs=cx.identf[0:8, 0:8], start=True, stop=True),
                      reads=[("cdt", i2), "identf"], writes=[pt], inc=False)
                kb.op("pe", lambda: nc.tensor.matmul(ps[:, 8:16], lhsT=cd_[:, 1, :], rhs=cx.identf[0:8, 0:8], start=True, stop=True),
                      reads=[("cdt", i2)], writes=[pt], inc=False)
                kb.op("pe", lambda: nc.tensor.matmul(ps[:, 16:24], lhsT=wT[:, :], rhs=cx.identf[0:8, 0:8], start=True, stop=True),
                      reads=["wT"], writes=[pt], inc=False)
                kb.op("pe", lambda: nc.tensor.matmul(ps[:, 24:32], lhsT=cx.onesf[0:8, :], rhs=dg[:, :], start=True, stop=True),
                      reads=["dg", "onesf"], writes=[pt], inc=True)
                TM = tm[i2]
                kb.op("act", lambda: nc.scalar.copy(out=TM[:, :, :].rearrange("p a b -> p (a b)"), in_=ps[:, 0:32]), reads=[pt], writes=[("tm", i2)])
                qk = []
                for half in range(2):
                    psq, ptq = pp.next()
                    for hh_ in range(4):
                        h_ = half * 4 + hh_
                        kb.op("pe", lambda: nc.tensor.matmul(psq[:, hh_ * L:(hh_ + 1) * L], lhsT=kT[i2][:, h_, :], rhs=qT[i2][:, h_, :], start=True, stop=True),
                              reads=[("kT", i2), ("qT", i2)], writes=[ptq], inc=(hh_ == 3))
                    qk.append((psq, ptq))
                cum_b = TM[:, 0, :].unsqueeze(2).to_broadcast([128, ML_H, L])
                ei_b = TM[:, 1, :].unsqueeze(2).to_broadcast([128, ML_H, L])
                D_ = Dc[i2]
                kb.op("dve", lambda: nc.vector.tensor_tensor(out=D_[:, :, :], in0=cb[i2][:, :, :], in1=cum_b, op=ALU.subtract),
                      reads=[("cb", i2), ("tm", i2)], writes=[("Dc", i2)])
                kb.op("dve", lambda: nc.vector.tensor_scalar(out=D_[:, :, :], in0=D_[:, :, :], scalar1=0.0, scalar2=None, op0=ALU.min), reads=[("Dc", i2)], writes=[("Dc", i2)])
                kb.op("act", lambda: nc.scalar.activation(out=D_[:, :, :], in_=D_[:, :, :], func=AF.Exp), reads=[("Dc", i2)], writes=[("Dc", i2)])
                kb.op("dve", lambda: nc.vector.tensor_tensor(out=D_[:, :, :], in0=D_[:, :, :], in1=cx.mask[d][:, :].unsqueeze(1).to_broadcast([128, ML_H, L]), op=ALU.mult),
                      reads=[("Dc", i2), ("mask", d)], writes=[("Dc", i2)])
                kb.op("dve", lambda: nc.vector.tensor_tensor(out=D_[:, :, :], in0=D_[:, :, :], in1=ei_b, op=ALU.mult), reads=[("Dc", i2), ("tm", i2)], writes=[("Dc", i2)])
                for half in range(2):
                    psq, ptq = qk[half]
                    kb.op("dve", lambda: nc.vector.tensor_tensor(out=Wt[i2][:, half * 4:(half + 1) * 4, :], in0=D_[:, half * 4:(half + 1) * 4, :],
                                                                 in1=psq[:, :512].rearrange("p (h l) -> p h l", h=4), op=ALU.mult),
                          reads=[("Dc", i2), ptq], writes=[("Wt", i2)])
                kb.op("act", lambda: nc.scalar.activation(out=ecb[i2][:, :, :], in_=cb[i2][:, :, :], func=AF.Exp), reads=[("cb", i2)], writes=[("ecb", i2)])
                kb.op("dve", lambda: nc.vector.tensor_tensor(out=qsT[i2][:, :, :], in0=ecb[i2][:, :, :], in1=qT[i2][:, :, :], op=ALU.mult),
                      reads=[("ecb", i2), ("qT", i2)], writes=[("qsT", i2)])
                w_b = TM[:, 2, :].unsqueeze(2).to_broadcast([128, ML_H, ML_DK])
                kb.op("dve", lambda: nc.vector.tensor_tensor(out=kw[i2][:, :, :], in0=ktm[i2][:, :, :], in1=w_b, op=ALU.mult),
                      reads=[("ktm", i2), ("tm", i2)], writes=[("kw", i2)])
                for h_ in range(ML_H):
                    ps_o, pt_o = pp.next()
                    kb.op("pe", lambda: nc.tensor.matmul(ps_o[:, :ML_DV + 1], lhsT=Wt[i2][:, h_, :], rhs=Vx[i2][:, h_, :], start=True, stop=False),
                          reads=[("Wt", i2), ("Vx", i2), ("Vx1", i2)], writes=[pt_o], inc=False)
                    kb.op("pe", lambda: nc.tensor.matmul(ps_o[:, :ML_DV + 1], lhsT=qsT[i2][:, h_, :], rhs=Cb[:, h_, :], start=False, stop=True),
                          reads=[("qsT", i2), ("Cb", h_)], writes=[pt_o], inc=True)
                    kb.op("act", lambda: nc.scalar.copy(out=num[:, h_, :], in_=ps_o[:, 0:ML_DV]), reads=[pt_o], writes=[("num", h_)])
                    kb.op("act", lambda: nc.scalar.activation(out=den[:, 0, h_:h_ + 1], in_=ps_o[:, ML_DV:ML_DV + 1], func=AF.Abs), reads=[pt_o], writes=[("den", h_)])
                    ps_s, pt_s = pp.next()
                    kb.op("pe", lambda: nc.tensor.matmul(ps_s[:, :ML_DV + 1], lhsT=kw[i2][:, h_, :], rhs=Vx[i2][:, h_, :], start=True, stop=True),
                          reads=[("kw", i2), ("Vx", i2), ("Vx1", i2)], writes=[pt_s], inc=True)
                    kb.op("dve", lambda: nc.vector.scalar_tensor_tensor(out=Cf[:, h_, :], in0=Cf[:, h_, :], scalar=TM[:, 3, h_:h_ + 1], in1=ps_s[:, :ML_DV + 1],
                                                                        op0=ALU.mult, op1=ALU.add),
                          reads=[("Cf", h_), ("tm", i2), pt_s], writes=[("Cf", h_)])
                    kb.op("act", lambda: nc.scalar.copy(out=Cb[:, h_, :], in_=Cf[:, h_, :]), reads=[("Cf", h_)], writes=[("Cb", h_)])
                allnum = [("num", h_) for h_ in range(ML_H)]
                allden = [("den", h_) for h_ in range(ML_H)]
                kb.op("dve", lambda: nc.vector.tensor_scalar(out=den[:, 1, :], in0=den[:, 0, :], scalar1=1.0, scalar2=None, op0=ALU.max), reads=allden, writes=["den1"])
                kb.op("dve", lambda: nc.vector.reciprocal(out=den[:, 1, :], in_=den[:, 1, :]), reads=["den1"], writes=["den1"])
                r_b = den[:, 1, :].unsqueeze(2).to_broadcast([128, ML_H, ML_DV])
                kb.op("dve", lambda: nc.vector.tensor_tensor(out=num[:, :, :], in0=num[:, :, :], in1=r_b, op=ALU.mult), reads=allnum + ["den1"], writes=allnum)
                if d == 0:
                    kb.dma("pool", scr["hf"][c0:c0 + L, :].rearrange("t (h k) -> t h k", h=ML_H), num[:, :, :], reads=allnum, writes=[("hfd", c)])
                else:
                    kb.op("dve", lambda: nc.vector.tensor_tensor(out=num[:, :, :], in0=num[:, :, :], in1=hfl[:, :, :], op=ALU.add), reads=allnum + ["hfl"], writes=allnum)
                    kb.op("pool", lambda: nc.gpsimd.tensor_tensor(out=sq[:, :, :], in0=num[:, :, :], in1=num[:, :, :], op=ALU.mult), reads=allnum, writes=["sq2"])
                    kb.op("dve", lambda: nc.vector.tensor_reduce(out=ss[:, 0, :], in_=sq[:, :, :], axis=AX.X, op=ALU.add), reads=["sq2"], writes=["ss"])
                    kb.op("dve", lambda: nc.vector.tensor_scalar(out=ss[:, 1, :], in0=ss[:, 0, :], scalar1=1.0 / ML_DV, scalar2=EPS, op0=ALU.mult, op1=ALU.add),
                          reads=["ss"], writes=["ss"])
                    emit_rsqrt_inplace(kb, ss[:, 1, :], "ss")
                    rs_b = ss[:, 1, :].unsqueeze(2).to_broadcast([128, ML_H, ML_DV])
                    kb.op("dve", lambda: nc.vector.tensor_tensor(out=num[:, :, :], in0=num[:, :, :], in1=rs_b, op=ALU.mult), reads=allnum + ["ss"], writes=allnum)
                    numf = num[:, :, :].rearrange("p h k -> p (h k)")
                    kb.op("pool", lambda: nc.gpsimd.tensor_tensor(out=numf, in0=numf, in1=gN[:, :], op=ALU.mult), reads=allnum + ["gN"], writes=allnum)
                    kb.op("dve", lambda: nc.vector.tensor_tensor(out=hhb[:, :], in0=numf, in1=so[:, :], op=ALU.mult), reads=allnum + ["so"], writes=["hhb"])
                    for grp in range(2):
                        pb = cx.psb[cx.psbi[0] % 2]; pbtok = ("psb", cx.psbi[0] % 2); cx.psbi[0] += 1
                        for j in range(8):
                            blk = grp * 8 + j
                            kb.op("pe", lambda: nc.tensor.transpose(out=pb[:, j * 128:(j + 1) * 128], in_=hhb[:, blk * 128:(blk + 1) * 128], identity=cx.ident[:, :]),
                                  reads=["hhb", "ident"], writes=[pbtok], inc=(j == 7))
                        yo = hhT_sb[grp % 2]
                        kb.op("act", lambda: nc.scalar.copy(out=yo[:, :, :].rearrange("p a b -> p (a b)"), in_=pb[:, :]), reads=[pbtok], writes=[("hhT", grp % 2)])
                        kb.dma("pool", hhv[:, grp * 8:(grp + 1) * 8, c0:c0 + L], yo[:, :, :], reads=[("hhT", grp % 2)], writes=[("hhd", c, grp)])
    with kb.phase():
        tile_ctx(kb, cx)
        hT = cx.hT
        yT = sb(kb, "m3yT", [128, NCH, NT], BF16)
        for it in range(nt):
            t0 = it * NT
            for c in range(NCH):
                kb.dma("sp", hT[:, c, :NT], src_v[:, c, src_pad + t0:src_pad + t0 + NT], writes=["hT"])
            kb.dma("sp", yT[:, :, :], hhv[:, :, t0:t0 + NT], writes=[("yT", k_) for k_ in range(NCH)])
            emit_outproj_residual(kb, pp, yT, lambda k_: ("yT", k_), NCH, w["wout"], cx.wO, cx.wOi, hT, "hT", 0, NT,
                                  dst_v, dst_pad + t0, cx.ho_bufs, out_toks, "ml")


def prep_mlstm_weights(w_in, gate_bias, head_norm, w_out, g):
    f = lambda a: np.ascontiguousarray(a, dtype=np.float32)
    def fm_blocks(wm):
        n = wm.shape[1] // 128
        return f(wm.reshape(-1, 128, n, 128).transpose(2, 1, 0, 3))
    def tm_groups(wm):
        n = wm.shape[1] // 512
        return wm.reshape(NCH, 128, n, 512).transpose(2, 1, 0, 3)
    wqk = fm_blocks(w_in[:, 0:2048])
    wtm = np.concatenate([tm_groups(w_in[:, 1024:2048]), tm_groups(w_in[:, 2048:4096]), tm_groups(w_in[:, 4096:6144])], 0)
    wg = w_in[:, 6144:6176].reshape(NCH, 128, 32).transpose(1, 0, 2)
    return dict(wqk=wqk, wtm=f(wtm), wg=f(wg), gbias=f(gate_bias.T), hn=f(head_norm.reshape(1, D_MODEL)),
                wout=fm_blocks(w_out), g=f(g.reshape(NCH, 128).T))


ML_W_SHAPES = dict(wqk=[16, 128, NCH, 128], wtm=[10, 128, NCH, 512], wg=[128, NCH, 32], gbias=[8, 4], hn=[1, D_MODEL],
                   wout=[NCH, 128, NCH, 128], g=[128, NCH])


def mlstm_scratch(nc, T, pfx=""):
    dtn = lambda n, s, dt: nc.dram_tensor(pfx + n, s, dt).ap()
    return dict(qT=dtn("qT", [1024, T], BF16), kT=dtn("kT", [1024, T], BF16), ktm=dtn("ktm", [T, 1024], BF16),
                vtm=dtn("vtm", [T, D_MODEL], BF16), so=dtn("so", [T, D_MODEL], BF16), cumT=dtn("mcumT", [16, T], F32),
                eiT=dtn("meiT", [16, T], F32), hf=dtn("mhf", [T, D_MODEL], F32), hhT=dtn("hhT", [D_MODEL, T], BF16))


def build_mlstm_test(T, NT=512):
    nc = bass.Bass("TRN2", target_bir_lowering=False)
    hin = nc.dram_tensor("hin", [D_MODEL, T], F32, kind="ExternalInput").ap()
    w = {k: nc.dram_tensor(k, shp, F32, kind="ExternalInput").ap() for k, shp in ML_W_SHAPES.items()}
    hout = nc.dram_tensor("hout", [D_MODEL, T], F32, kind="ExternalOutput").ap()
    scr = mlstm_scratch(nc, T)
    with contextlib.ExitStack() as stack:
        kb = KB(nc, stack)
        pp = PsumPool(kb, nbanks=6)
        cx = make_ctx(kb, NT)
        outs = []
        emit_mlstm(kb, pp, cx, T, hin.rearrange("(c p) t -> p c t", p=128), 0, hout.rearrange("(c p) t -> p c t", p=128), 0, w, scr, outs, NT)
        kb.finish(outs)
    return nc


def emit_ffn(kb, pp, cx, T, src_v, src_pad, dst_v, dst_pad, w, out_toks, gfin=None, NT=512):
    nc = kb.nc
    nt = T // NT
    W = NT + 2
    with kb.phase():
        tile_ctx(kb, cx)
        hT, uT, g_sb = cx.hT, cx.uT, cx.g_sb
        actT = sb(kb, "f_actT", [128, FFN_BLK, NT], BF16)
        cw_sb = sb(kb, "f_cw", [128, FFN_BLK, 4], F32)
        gate_sb = [sb(kb, f"f_gate{i}", [128, W], F32) for i in range(2)]
        acc_sb = [sb(kb, f"f_acc{i}", [128, NT], F32) for i in range(2)]
        sil_sb = [sb(kb, f"f_sil{i}", [128, NT], F32) for i in range(2)]
        kb.dma("sp", g_sb[:, :], w["g"][:, :], writes=["g"])
        kb.dma("sp", cw_sb[:, :, :], w["cw"], writes=["cw"])
        if gfin is not None:
            gf_sb = sb(kb, "f_gf", [128, NCH], F32)
            hn = sb(kb, "f_hn", [128, NCH, NT], F32)
            kb.dma("sp", gf_sb[:, :], gfin[:, :], writes=["gf"])
        for it in range(nt):
            t0 = it * NT
            for c in range(NCH):
                kb.dma("sp", hT[:, c, :W], src_v[:, c, src_pad + t0 - 1:src_pad + t0 - 1 + W], writes=["hT"])
            emit_rmsnorm_fm(kb, pp, hT, "hT", g_sb, uT, "uT", cx.ones_bf, W, cx.sq_bufs, (cx.rstd_t, "rstd"))
            for j in range(FFN_BLK):
                wb = cx.wA[cx.wAi[0] % 2]; wtok = ("wA", cx.wAi[0] % 2); cx.wAi[0] += 1
                kb.dma("pool", wb[:, :, :], w["wup"][j], writes=[wtok])
                (ps_g, tg), (ps_x, tx) = emit_proj_block(kb, pp, wb, wtok, 0, uT, "uT", [(0, NT), (NT, 2)])
                (ps_v, tv), = emit_proj_block(kb, pp, wb, wtok, 128, uT, "uT", [(1, NT)])
                gs = gate_sb[j % 2]; gtok = ("gate", j % 2)
                kb.op("act", lambda: nc.scalar.copy(out=gs[:, 0:NT], in_=ps_g[:, :NT]), reads=[tg], writes=[gtok])
                kb.op("act", lambda: nc.scalar.copy(out=gs[:, NT:NT + 2], in_=ps_x[:, :2]), reads=[tx], writes=[gtok])
                ac = acc_sb[j % 2]; atok = ("acc", j % 2)
                kb.op("dve", lambda: nc.vector.tensor_scalar(out=ac[:, :], in0=gs[:, 0:NT], scalar1=cw_sb[:, j, 0:1], scalar2=cw_sb[:, j, 3:4],
                                                             op0=ALU.mult, op1=ALU.add), reads=[gtok, "cw"], writes=[atok])
                for tap in (1, 2):
                    kb.op("dve", lambda: nc.vector.scalar_tensor_tensor(out=ac[:, :], in0=gs[:, tap:tap + NT], scalar=cw_sb[:, j, tap:tap + 1],
                                                                        in1=ac[:, :], op0=ALU.mult, op1=ALU.add), reads=[gtok, atok], writes=[atok])
                sl = sil_sb[j % 2]; stok = ("sil", j % 2)
                kb.op("act", lambda: nc.scalar.activation(out=sl[:, :], in_=ac[:, :], func=AF.Silu), reads=[atok], writes=[stok])
                kb.op("dve", lambda: nc.vector.tensor_tensor(out=actT[:, j, :], in0=sl[:, :], in1=ps_v[:, :NT], op=ALU.mult),
                      reads=[stok, tv], writes=[("actT", j)])
            if gfin is None:
                emit_outproj_residual(kb, pp, actT, lambda k_: ("actT", k_), FFN_BLK, w["wdn"], cx.wO, cx.wOi, hT, "hT", 1, NT,
                                      dst_v, dst_pad + t0, cx.ho_bufs, out_toks, "ffn")
            else:
                for mb in range(NCH):
                    wd = cx.wO[cx.wOi[0] % 2]; dtok = ("ffnw", cx.wOi[0] % 2); cx.wOi[0] += 1
                    kb.dma("pool", wd[:, :, :], w["wdn"][mb], writes=[dtok])
                    ps_o, to = pp.next()
                    for k_ in range(FFN_BLK):
                        kb.op("pe", lambda: nc.tensor.matmul(ps_o[:, :NT], lhsT=wd[:, k_, :], rhs=actT[:, k_, :], start=(k_ == 0), stop=(k_ == FFN_BLK - 1)),
                              reads=[dtok, ("actT", k_)], writes=[to], inc=(k_ == FFN_BLK - 1))
                    kb.op("dve", lambda: nc.vector.tensor_tensor(out=hn[:, mb, :], in0=ps_o[:, :NT], in1=hT[:, mb, 1:NT + 1], op=ALU.add),
                          reads=[to, "hT"], writes=["hn"])
                ps, ps_tok = pp.next()
                for c in range(NCH):
                    sq, sq_tok = cx.sq_bufs[c % len(cx.sq_bufs)]
                    kb.op("act", lambda: nc.scalar.activation(out=sq[:, :NT], in_=hn[:, c, :], func=AF.Square), reads=["hn"], writes=[sq_tok])
                    kb.op("pe", lambda: nc.tensor.matmul(ps[:, :NT], lhsT=cx.ones_bf[:, :], rhs=sq[:, :NT], start=(c == 0), stop=(c == NCH - 1)),
                          reads=[sq_tok, "ones"], writes=[ps_tok], inc=True)
                rstd_t = cx.rstd_t
                kb.op("dve", lambda: nc.vector.tensor_scalar(out=rstd_t[:, :NT], in0=ps[:, :NT], scalar1=1.0 / D_MODEL, scalar2=EPS, op0=ALU.mult, op1=ALU.add),
                      reads=[ps_tok], writes=["rstd"])
                emit_rsqrt_inplace(kb, rstd_t[:, :NT], "rstd")
                for c in range(NCH):
                    ho = cx.ho_bufs[c % 3]; htok = ("ho", c % 3)
                    kb.op("dve", lambda: nc.vector.scalar_tensor_tensor(out=ho[:, :], in0=hn[:, c, :], scalar=gf_sb[:, c:c + 1], in1=rstd_t[:, :NT],
                                                                        op0=ALU.mult, op1=ALU.mult), reads=["hn", "rstd", "gf"], writes=[htok])
                    otok = ("fout", it, c)
                    kb.dma("sp", dst_v[:, c, dst_pad + t0:dst_pad + t0 + NT], ho[:, :], reads=[htok], writes=[otok])
                    out_toks.append(otok)


FFN_W_SHAPES = dict(wup=[FFN_BLK, 128, NCH, 256], wdn=[NCH, 128, FFN_BLK, 128], cw=[128, FFN_BLK, 4], g=[128, NCH])
XA_W_SHAPES = dict(wq=[NCH, 128, NCH, 128], wk=[NCH, 128, NCH, 128], wv=[4, 128, NCH, 512], wo=[NCH, 128, NCH, 128], g=[128, NCH])
RG_W_SHAPES = dict(win=[32, 128, NCH, 128], gw=[128, 2, RG_BLOCKS, 2, 512], cw=[128, NCH, 5], gb=[128, 2, NCH, 2], lam=[128, 2, NCH],
                   wout=[NCH, 128, NCH, 128], g=[128, NCH])
DEPTH = 4
PAD = 2
PRECAST = True
PRECAST_KEYS = ('wup', 'wdn', 'wq', 'wk', 'wv', 'wo', 'win', 'wz', 'wx', 'wdt', 'wout', 'wqk', 'wtm')


def rglru_scratch(nc, T, pfx=""):
    return dict(ggT=nc.dram_tensor(pfx + "ggT", [D_MODEL, T], BF16).ap(), hfT=nc.dram_tensor(pfx + "hfT", [D_MODEL, T], F32).ap(),
                abT=nc.dram_tensor(pfx + "abT", [D_MODEL, T], F32).ap(), bxbT=nc.dram_tensor(pfx + "bxbT", [D_MODEL, T], F32).ap())


def build_prog(T, plist, NT=512, depth=DEPTH):
    nc = bass.Bass("TRN2", target_bir_lowering=False)
    ext = lambda n, s: nc.dram_tensor(n, s, F32, kind="ExternalInput").ap()
    xT = ext("xT", [D_MODEL, T + 2 * PAD])
    has_xa = any(k == "xa" for _, k in plist)
    has_fin = any(k == "ffn" and i == depth - 1 for i, k in plist)
    if has_xa:
        memT = ext("memT", [D_MODEL, N_MEM])
        gmem = ext("gmem", [128, NCH])
    gfin = ext("gfin", [128, NCH]) if has_fin else None
    lw = {}
    for (i, kind) in plist:
        if kind == "mix":
            shapes = [SSD_W_SHAPES, ML_W_SHAPES, RG_W_SHAPES][i % 3]
        elif kind == "xa":
            shapes = XA_W_SHAPES
        else:
            shapes = FFN_W_SHAPES
        lw[(i, kind)] = {k: ext(f"l{i}_{kind}_{k}", s) for k, s in shapes.items()}
    outT = nc.dram_tensor("outT", [D_MODEL, T], F32, kind="ExternalOutput").ap()
    nint = max(0, len(plist) - 1)
    hbufs = [nc.dram_tensor(f"hI{j}", [D_MODEL, T + 2 * PAD], F32).ap() for j in range(min(2, nint))]
    kinds = set(i % 3 for i, k in plist if k == "mix")
    scr_ssd = ssd_scratch(nc, T, "s_") if 0 in kinds else None
    scr_ml = mlstm_scratch(nc, T, "m_") if 1 in kinds else None
    scr_rg = rglru_scratch(nc, T, "r_") if 2 in kinds else None
    view = lambda ap: ap.rearrange("(c p) t -> p c t", p=128)
    with contextlib.ExitStack() as stack:
        kb = KB(nc, stack)
        pp = PsumPool(kb, nbanks=6)
        cx = make_ctx(kb, NT)
        outs = []
        if hbufs:
            with kb.phase():
                z = sb(kb, "zpad", [128, NCH, PAD], F32)
                kb.op("dve", lambda: nc.vector.memset(z[:, :, :], 0.0), writes=["z"])
                for hb in hbufs:
                    kb.dma("sp", view(hb)[:, :, 0:PAD], z[:, :, :], reads=["z"], writes=[("zp", id(hb), 0)])
                    kb.dma("sp", view(hb)[:, :, T + PAD:T + 2 * PAD], z[:, :, :], reads=["z"], writes=[("zp", id(hb), 1)])
        if PRECAST:
            ci_ = 0
            for key_, wd_ in lw.items():
                for nm_ in list(wd_):
                    if nm_ not in PRECAST_KEYS:
                        continue
                    src_ = wd_[nm_]
                    shp_ = list(src_.shape)
                    dstt_ = nc.dram_tensor(f"bf_l{key_[0]}_{key_[1]}_{nm_}", shp_, BF16).ap()
                    for j_ in range(shp_[0]):
                        kb.dma("pool", dstt_[j_], src_[j_], writes=[("precast", ci_)])
                        ci_ += 1
                    wd_[nm_] = dstt_
            kb.barrier()
        if has_xa:
            emit_mem_norm(kb, pp, cx, memT, gmem)
        cur, cur_pad = xT, PAD
        for n, (i, kind) in enumerate(plist):
            lastp = (n == len(plist) - 1)
            dst, dpad = (outT, 0) if lastp else (hbufs[n % 2], PAD)
            w = lw[(i, kind)]
            if kind == "mix":
                if i % 3 == 0:
                    emit_ssd(kb, pp, cx, T, view(cur), cur_pad, view(dst), dpad, w, scr_ssd, outs, NT)
                elif i % 3 == 1:
                    emit_mlstm(kb, pp, cx, T, view(cur), cur_pad, view(dst), dpad, w, scr_ml, outs, NT)
                else:
                    emit_rglru(kb, pp, cx, T, view(cur), cur_pad, view(dst), dpad, w, scr_rg, outs, NT)
            elif kind == "xa":
                emit_xattn(kb, pp, cx, T, view(cur), cur_pad, view(dst), dpad, w, outs, NT)
            else:
                emit_ffn(kb, pp, cx, T, view(cur), cur_pad, view(dst), dpad, w, outs, gfin if i == depth - 1 else None, NT)
            cur, cur_pad = dst, dpad
        kb.finish(outs)
    return nc


def all_phases(depth=DEPTH):
    return [(i, k) for i in range(depth) for k in ("mix", "xa", "ffn")]


def build_full(T, NT=512, depth=DEPTH):
    return build_prog(T, all_phases(depth), NT, depth)


def prep_ffn_w(w_up, conv_w, conv_b, w_down, g):
    d = prep_ffn_weights(w_up, conv_w, conv_b, w_down, g)
    return dict(wup=d["wup"], wdn=d["wdn"], cw=d["cw"], g=d["gnorm"])


def prep_all(inp, depth=DEPTH):
    f = lambda a: np.ascontiguousarray(a, dtype=np.float32)
    col = lambda v: f(np.asarray(v).reshape(NCH, 128).T)
    out = dict(gmem=col(inp["mem_norm"]), gfin=col(inp["final_norm"]))
    for i in range(depth):
        kind, j = i % 3, i // 3
        g = np.asarray(inp["mix_norm"][i])
        if kind == 0:
            mix = prep_ssd_weights(*[np.asarray(inp[k][j]) for k in ["ssd_w_in", "ssd_conv_w", "ssd_conv_b", "ssd_dt_bias", "ssd_a_log",
                                                                      "ssd_d_skip", "ssd_norm", "ssd_w_out"]], g)
        elif kind == 1:
            mix = prep_mlstm_weights(*[np.asarray(inp[k][j]) for k in ["mlstm_w_in", "mlstm_gate_bias", "mlstm_head_norm", "mlstm_w_out"]], g)
        else:
            mix = prep_rglru_weights(*[np.asarray(inp[k][j]) for k in ["rglru_w_in", "rglru_conv_w", "rglru_conv_b", "rglru_gate_w",
                                                                        "rglru_gate_b", "rglru_lambda", "rglru_w_out"]], g)
        xa = prep_xattn_weights(np.asarray(inp["xattn_wq"][i]), np.asarray(inp["xattn_wkv"][i]), np.asarray(inp["xattn_wo"][i]),
                                np.asarray(inp["xattn_norm"][i]))
        ff = prep_ffn_w(np.asarray(inp["ffn_w_up"][i]), np.asarray(inp["ffn_conv_w"][i]), np.asarray(inp["ffn_conv_b"][i]),
                        np.asarray(inp["ffn_w_down"][i]), np.asarray(inp["ffn_norm"][i]))
        for k, v in mix.items():
            out[f"l{i}_mix_{k}"] = v
        for k, v in xa.items():
            out[f"l{i}_xa_{k}"] = v
        for k, v in ff.items():
            out[f"l{i}_ffn_{k}"] = v
    return out


FUSED_GROUPS = None


def run_full(inputs, T, depth=DEPTH, NT=512, groups=None):
    x = np.asarray(inputs["x"])
    mem = np.asarray(inputs["mem"])
    B = x.shape[0]
    shared = prep_all(inputs, depth)
    if groups is None:
        groups = [all_phases(depth)]
    cur = [np.ascontiguousarray(x[b, :T].T, dtype=np.float32) for b in range(B)]
    for plist in groups:
        nc = build_prog(T, plist, NT, depth)
        names = set()
        for alloc_name in shared:
            names.add(alloc_name)
        need = lambda k: any(k.startswith(f"l{i}_{kind}_") for (i, kind) in plist)
        in_maps = []
        for b in range(B):
            xp = np.zeros((D_MODEL, T + 2 * PAD), np.float32)
            xp[:, PAD:PAD + T] = cur[b]
            m = {k: v for k, v in shared.items() if need(k)}
            if any(k == "xa" for _, k in plist):
                m["memT"] = np.ascontiguousarray(mem[b].T, dtype=np.float32)
                m["gmem"] = shared["gmem"]
            if any(k == "ffn" and i == depth - 1 for i, k in plist):
                m["gfin"] = shared["gfin"]
            m["xT"] = xp
            in_maps.append(m)
        res = run_bass_kernel_spmd(nc, in_maps, core_ids=list(range(B)))
        cur = [res.results[b]["outT"] for b in range(B)]
        if VERBOSE:
            print("launch done", plist, float(np.abs(cur[0]).mean()), flush=True)
    out = np.stack([cur[b].T for b in range(B)], 0)
    return np.ascontiguousarray(out, dtype=np.float32)


VERBOSE = False
SPLIT = False


def kernel(**inputs):
    groups = [[p] for p in all_phases()] if SPLIT else None
    return run_full(inputs, T=16384, groups=groups)
```

```python
import contextlib
import numpy as np
import concourse.bass as bass
import concourse.mybir as mybir
from concourse.bass_utils import run_bass_kernel_spmd

F32 = mybir.dt.float32
BF16 = mybir.dt.bfloat16
ALU = mybir.AluOpType
AF = mybir.ActivationFunctionType
AX = mybir.AxisListType

D_MODEL = 2048
NCH = 16
FFN_DIM = 5632
FFN_BLK = 44
EPS = 1e-6
import os
SEM_LIMIT = int(os.environ.get('KB_SEM_LIMIT', '24000'))
KB_STATS = {}


class KB:
    def __init__(self, nc, stack):
        self.nc = nc
        self.stack = stack
        self.semstack = stack
        self.eng = dict(pe=nc.tensor, dve=nc.vector, act=nc.scalar, pool=nc.gpsimd, sp=nc.sync)
        self.esem = {}
        self.ecnt = {}
        self.seen = {e: {} for e in self.eng}
        self.lastw = {}
        self.reads = {}
        self.dq = {}
        self.nsem = 0
        self.sems = {}
        for e in ("pe", "dve", "act", "pool"):
            self._new_esem(e)
        self.pending_noinc = {e: False for e in self.eng}

    def new_sem(self, name):
        s = self.semstack.enter_context(self.nc.semaphore(f"{name}_{self.nsem}"))
        self.nsem += 1
        self.sems[id(s)] = s
        return s

    def _new_esem(self, e):
        self.esem[e] = self.new_sem("e" + e)
        self.ecnt[e] = 0
        KB_STATS["nsem"] = self.nsem
        KB_STATS[e] = KB_STATS.get(e, 0) + 1

    def _waits(self, e, reads, writes):
        need = {}
        def add(ev):
            if ev is None:
                return
            s, v = ev
            if need.get(id(s), (None, 0))[1] < v:
                need[id(s)] = (s, v)
        for t in reads:
            add(self.lastw.get(t))
        for t in writes:
            add(self.lastw.get(t))
            for ev in self.reads.get(t, {}).values():
                add(ev)
        engine = self.eng[e]
        for sid, (s, v) in need.items():
            if e == "pe" and s is self.esem["pe"]:
                continue
            if self.seen[e].get(sid, 0) >= v:
                continue
            engine.wait_ge(s, v)
            self.seen[e][sid] = v

    def _record(self, ev, reads, writes):
        for t in reads:
            d = self.reads.setdefault(t, {})
            s, v = ev
            if d.get(id(s), (None, 0))[1] < v:
                d[id(s)] = ev
        for t in writes:
            self.lastw[t] = ev
            self.reads[t] = {}

    def op(self, e, fn, reads=(), writes=(), inc=True):
        self._waits(e, reads, writes)
        if self.ecnt[e] >= SEM_LIMIT and not self.pending_noinc[e]:
            self._new_esem(e)
        ins = fn()
        ev = (self.esem[e], self.ecnt[e] + 1)
        if inc:
            ins.then_inc(self.esem[e], 1)
            self.ecnt[e] += 1
            self.pending_noinc[e] = False
        else:
            self.pending_noinc[e] = True
        self._record(ev, reads, writes)
        return ins

    def dma(self, q, out, in_, reads=(), writes=(), nslots=8):
        st = self.dq.setdefault(q, dict(sems=[], vals=[], idx=0, old=[]))
        i = st["idx"] % nslots
        if len(st["sems"]) <= i:
            st["sems"].append(self.new_sem("d" + q))
            st["vals"].append(0)
        elif st["vals"][i] + 16 > SEM_LIMIT:
            st["old"].append((st["sems"][i], st["vals"][i]))
            engine = self.eng[q]
            if self.seen[q].get(id(st["sems"][i]), 0) < st["vals"][i]:
                engine.wait_ge(st["sems"][i], st["vals"][i])
                self.seen[q][id(st["sems"][i])] = st["vals"][i]
            st["sems"][i] = self.new_sem("d" + q)
            st["vals"][i] = 0
        st["idx"] += 1
        s = st["sems"][i]
        engine = self.eng[q]
        if st["vals"][i] > 0 and self.seen[q].get(id(s), 0) < st["vals"][i]:
            engine.wait_ge(s, st["vals"][i])
            self.seen[q][id(s)] = st["vals"][i]
        self._waits(q, reads, writes)
        ins = engine.dma_start(out=out, in_=in_)
        ins.then_inc(s, 16)
        st["vals"][i] += 16
        ev = (s, st["vals"][i])
        self._record(ev, reads, writes)
        return ins

    def barrier(self):
        evs = []
        for e in ("pe", "dve", "act", "pool"):
            if self.ecnt[e] > 0 or self.pending_noinc[e]:
                assert not self.pending_noinc[e], f"engine {e} has trailing non-inc instructions"
                evs.append((self.esem[e], self.ecnt[e]))
        for q, st in self.dq.items():
            for s_, v in list(zip(st["sems"], st["vals"])) + st["old"]:
                if v > 0:
                    evs.append((s_, v))
            st["old"] = []
        for e in self.eng:
            engine = self.eng[e]
            for (s_, v) in evs:
                if e in self.esem and s_ is self.esem.get(e):
                    continue
                if self.seen[e].get(id(s_), 0) >= v:
                    continue
                engine.wait_ge(s_, v)
                self.seen[e][id(s_)] = v
        self.lastw = {}
        self.reads = {}

    @contextlib.contextmanager
    def phase(self):
        outer = self.stack
        with contextlib.ExitStack() as st:
            self.stack = st
            try:
                yield
            finally:
                self.barrier()
                self.stack = outer

    def finish(self, out_tokens):
        self._waits("sp", out_tokens, out_tokens)


def _bcast_free(ap, n):
    return ap


class PsumPool:
    def __init__(self, kb, nbanks=8):
        self.kb = kb
        self.tiles = [kb.stack.enter_context(kb.nc.psum_tensor(f"ps{i}", [128, 512], F32)) for i in range(nbanks)]
        self.i = 0

    def next(self):
        t = self.tiles[self.i % len(self.tiles)]
        tok = ("ps", self.i % len(self.tiles))
        self.i += 1
        return t, tok


def psum_bf16(kb, name, cols=1024):
    return kb.stack.enter_context(kb.nc.psum_tensor(name, [128, cols], BF16))


def make_ident(kb, name="ident"):
    nc = kb.nc
    identf = sb(kb, name + "f", [128, 128], F32)
    ident = sb(kb, name, [128, 128], BF16)
    kb.op("pool", lambda: nc.gpsimd.memset(identf[:, :], 1.0), writes=["identf"])
    kb.op("pool", lambda: nc.gpsimd.affine_select(out=identf[:, :], in_=identf[:, :], pattern=[[-1, 128]],
                                                  compare_op=ALU.is_equal, fill=0.0, base=0, channel_multiplier=1),
          reads=["identf"], writes=["identf"])
    kb.op("dve", lambda: nc.vector.tensor_copy(out=ident[:, :], in_=identf[:, :]), reads=["identf"], writes=["ident"])
    return identf, ident


_SBN = [0]


def sb(kb, name, shape, dt):
    _SBN[0] += 1
    return kb.stack.enter_context(kb.nc.sbuf_tensor(f"{name}_{_SBN[0]}", list(shape), dt))


def emit_rsqrt_inplace(kb, ap, tok):
    nc = kb.nc
    kb.op("act", lambda: nc.scalar.activation(out=ap, in_=ap, func=AF.Sqrt), reads=[tok], writes=[tok])
    kb.op("dve", lambda: nc.vector.reciprocal(out=ap, in_=ap), reads=[tok], writes=[tok])


def emit_rmsnorm_fm(kb, pp, hT, h_tok, g_sb, uT, u_tok, ones_bf, ncols, sq_bufs, rstd):
    nc = kb.nc
    groups = []
    c0 = 0
    while c0 < ncols:
        w = min(512, ncols - c0)
        groups.append((c0, w))
        c0 += w
    ps_list = [pp.next() for _ in groups]
    for c in range(NCH):
        sq, sq_tok = sq_bufs[c % len(sq_bufs)]
        kb.op("act", lambda: nc.scalar.activation(out=sq[:, :ncols], in_=hT[:, c, :ncols], func=AF.Square),
              reads=[h_tok], writes=[sq_tok])
        for gi, (c0, w) in enumerate(groups):
            ps, ps_tok = ps_list[gi]
            last = (c == NCH - 1) and (gi == len(groups) - 1)
            kb.op("pe", lambda: nc.tensor.matmul(ps[:, :w], lhsT=ones_bf[:, :], rhs=sq[:, c0:c0 + w],
                                                 start=(c == 0), stop=(c == NCH - 1)),
                  reads=[sq_tok, "ones"], writes=[ps_tok], inc=(gi == len(groups) - 1))
    rstd_t, rstd_tok = rstd
    for gi, (c0, w) in enumerate(groups):
        ps, ps_tok = ps_list[gi]
        kb.op("dve", lambda: nc.vector.tensor_scalar(out=rstd_t[:, c0:c0 + w], in0=ps[:, :w], scalar1=1.0 / D_MODEL,
                                                     scalar2=EPS, op0=ALU.mult, op1=ALU.add),
              reads=[ps_tok], writes=[rstd_tok])
    emit_rsqrt_inplace(kb, rstd_t[:, :ncols], rstd_tok)
    for c in range(NCH):
        kb.op("dve", lambda: nc.vector.scalar_tensor_tensor(out=uT[:, c, :ncols], in0=hT[:, c, :ncols],
                                                            scalar=g_sb[:, c:c + 1], in1=rstd_t[:, :ncols],
                                                            op0=ALU.mult, op1=ALU.mult),
              reads=[h_tok, rstd_tok, "g"], writes=[u_tok])


def build_ffn(Tn, final_norm=False, NT=512):
    assert Tn % NT == 0
    nt = Tn // NT
    W = NT + 2
    nc = bass.Bass("TRN2", target_bir_lowering=False)
    hin = nc.dram_tensor("hin", [D_MODEL, Tn + 2], F32, kind="ExternalInput").ap()
    wup = nc.dram_tensor("wup", [FFN_BLK, 128, NCH, 256], F32, kind="ExternalInput").ap()
    wdn = nc.dram_tensor("wdn", [NCH, 128, FFN_BLK, 128], F32, kind="ExternalInput").ap()
    gnorm = nc.dram_tensor("gnorm", [128, NCH], F32, kind="ExternalInput").ap()
    cw = nc.dram_tensor("cw", [128, FFN_BLK, 4], F32, kind="ExternalInput").ap()
    if final_norm:
        gfin = nc.dram_tensor("gfin", [128, NCH], F32, kind="ExternalInput").ap()
    hout = nc.dram_tensor("hout", [D_MODEL, Tn], F32, kind="ExternalOutput").ap()
    hin_v = hin.rearrange("(c p) t -> p c t", p=128)
    hout_v = hout.rearrange("(c p) t -> p c t", p=128)

    with contextlib.ExitStack() as stack:
        kb = KB(nc, stack)
        pp = PsumPool(kb)
        hT = sb(kb, "hT", [128, NCH, W], F32)
        uT = sb(kb, "uT", [128, NCH, W], BF16)
        actT = sb(kb, "actT", [128, FFN_BLK, NT], BF16)
        ones_bf = sb(kb, "ones", [128, 128], BF16)
        g_sb = sb(kb, "g_sb", [128, NCH], F32)
        cw_sb = sb(kb, "cw_sb", [128, FFN_BLK, 4], F32)
        rstd_t = sb(kb, "rstd", [128, W], F32)
        sq_bufs = [(sb(kb, f"sq{i}", [128, W], BF16), ("sq", i)) for i in range(3)]
        wup_sb = [sb(kb, f"wup{i}", [128, NCH, 256], BF16) for i in range(2)]
        wdn_sb = [sb(kb, f"wdn{i}", [128, FFN_BLK, 128], BF16) for i in range(2)]
        gate_sb = [sb(kb, f"gate{i}", [128, W], F32) for i in range(2)]
        acc_sb = [sb(kb, f"acc{i}", [128, NT], F32) for i in range(2)]
        sil_sb = [sb(kb, f"sil{i}", [128, NT], F32) for i in range(2)]
        ho_sb = [sb(kb, f"ho{i}", [128, NT], F32) for i in range(3)]
        if final_norm:
            gf_sb = sb(kb, "gf_sb", [128, NCH], F32)
            hn = sb(kb, "hn", [128, NCH, NT], F32)

        kb.op("pool", lambda: nc.gpsimd.memset(ones_bf[:, :], 1.0), writes=["ones"])
        kb.dma("sp", g_sb[:, :], gnorm[:, :], writes=["g"])
        kb.dma("sp", cw_sb[:, :, :], cw[:, :, :], writes=["cw"])
        if final_norm:
            kb.dma("sp", gf_sb[:, :], gfin[:, :], writes=["gf"])

        out_toks = []
        wi = 0
        di = 0
        for it in range(nt):
            t0 = it * NT
            for c in range(NCH):
                kb.dma("sp", hT[:, c, :], hin_v[:, c, t0:t0 + W], writes=["hT"])
            emit_rmsnorm_fm(kb, pp, hT, "hT", g_sb, uT, "uT", ones_bf, W, sq_bufs, (rstd_t, "rstd"))
            for j in range(FFN_BLK):
                wb = wup_sb[wi % 2]
                wtok = ("wup", wi % 2)
                wi += 1
                kb.dma("pool", wb[:, :, :], wup[j], writes=[wtok])
                ps_g, tg = pp.next()
                ps_x, tx = pp.next()
                ps_v, tv = pp.next()
                for kc in range(NCH):
                    kb.op("pe", lambda: nc.tensor.matmul(ps_g[:, :NT], lhsT=wb[:, kc, 0:128], rhs=uT[:, kc, 0:NT],
                                                         start=(kc == 0), stop=(kc == NCH - 1)),
                          reads=[wtok, "uT", "ones", "g"], writes=[tg], inc=(kc == NCH - 1))
                for kc in range(NCH):
                    kb.op("pe", lambda: nc.tensor.matmul(ps_x[:, :2], lhsT=wb[:, kc, 0:128], rhs=uT[:, kc, NT:NT + 2],
                                                         start=(kc == 0), stop=(kc == NCH - 1)),
                          reads=[wtok, "uT"], writes=[tx], inc=(kc == NCH - 1))
                for kc in range(NCH):
                    kb.op("pe", lambda: nc.tensor.matmul(ps_v[:, :NT], lhsT=wb[:, kc, 128:256], rhs=uT[:, kc, 1:NT + 1],
                                                         start=(kc == 0), stop=(kc == NCH - 1)),
                          reads=[wtok, "uT"], writes=[tv], inc=(kc == NCH - 1))
                gs = gate_sb[j % 2]
                gtok = ("gate", j % 2)
                kb.op("act", lambda: nc.scalar.copy(out=gs[:, 0:NT], in_=ps_g[:, :NT]), reads=[tg], writes=[gtok])
                kb.op("act", lambda: nc.scalar.copy(out=gs[:, NT:NT + 2], in_=ps_x[:, :2]), reads=[tx], writes=[gtok])
                ac = acc_sb[j % 2]
                atok = ("acc", j % 2)
                kb.op("dve", lambda: nc.vector.tensor_scalar(out=ac[:, :], in0=gs[:, 0:NT], scalar1=cw_sb[:, j, 0:1],
                                                             scalar2=cw_sb[:, j, 3:4], op0=ALU.mult, op1=ALU.add),
                      reads=[gtok, "cw"], writes=[atok])
                kb.op("dve", lambda: nc.vector.scalar_tensor_tensor(out=ac[:, :], in0=gs[:, 1:NT + 1],
                                                                    scalar=cw_sb[:, j, 1:2], in1=ac[:, :],
                                                                    op0=ALU.mult, op1=ALU.add),
                      reads=[gtok, atok], writes=[atok])
                kb.op("dve", lambda: nc.vector.scalar_tensor_tensor(out=ac[:, :], in0=gs[:, 2:NT + 2],
                                                                    scalar=cw_sb[:, j, 2:3], in1=ac[:, :],
                                                                    op0=ALU.mult, op1=ALU.add),
                      reads=[gtok, atok], writes=[atok])
                sl = sil_sb[j % 2]
                stok = ("sil", j % 2)
                kb.op("act", lambda: nc.scalar.activation(out=sl[:, :], in_=ac[:, :], func=AF.Silu),
                      reads=[atok], writes=[stok])
                kb.op("dve", lambda: nc.vector.tensor_tensor(out=actT[:, j, :], in0=sl[:, :], in1=ps_v[:, :NT], op=ALU.mult),
                      reads=[stok, tv], writes=[("actT", j)])
            for mb in range(NCH):
                wd = wdn_sb[di % 2]
                dtok = ("wdn", di % 2)
                di += 1
                kb.dma("pool", wd[:, :, :], wdn[mb], writes=[dtok])
                ps_o, to = pp.next()
                for kbk in range(FFN_BLK):
                    kb.op("pe", lambda: nc.tensor.matmul(ps_o[:, :NT], lhsT=wd[:, kbk, :], rhs=actT[:, kbk, :],
                                                         start=(kbk == 0), stop=(kbk == FFN_BLK - 1)),
                          reads=[dtok, ("actT", kbk)], writes=[to], inc=(kbk == FFN_BLK - 1))
                if not final_norm:
                    ho = ho_sb[mb % 3]
                    htok = ("ho", mb % 3)
                    kb.op("dve", lambda: nc.vector.tensor_tensor(out=ho[:, :], in0=ps_o[:, :NT], in1=hT[:, mb, 1:NT + 1], op=ALU.add),
                          reads=[to, "hT"], writes=[htok])
                    otok = ("hout", it, mb)
                    kb.dma("sp", hout_v[:, mb, t0:t0 + NT], ho[:, :], reads=[htok], writes=[otok])
                    out_toks.append(otok)
                else:
                    kb.op("dve", lambda: nc.vector.tensor_tensor(out=hn[:, mb, :], in0=ps_o[:, :NT], in1=hT[:, mb, 1:NT + 1], op=ALU.add),
                          reads=[to, "hT"], writes=["hn"])
            if final_norm:
                groups = [(0, NT)]
                ps, ps_tok = pp.next()
                for c in range(NCH):
                    sq, sq_tok = sq_bufs[c % len(sq_bufs)]
                    kb.op("act", lambda: nc.scalar.activation(out=sq[:, :NT], in_=hn[:, c, :], func=AF.Square),
                          reads=["hn"], writes=[sq_tok])
                    kb.op("pe", lambda: nc.tensor.matmul(ps[:, :NT], lhsT=ones_bf[:, :], rhs=sq[:, :NT],
                                                         start=(c == 0), stop=(c == NCH - 1)),
                          reads=[sq_tok, "ones"], writes=[ps_tok], inc=True)
                kb.op("dve", lambda: nc.vector.tensor_scalar(out=rstd_t[:, :NT], in0=ps[:, :NT], scalar1=1.0 / D_MODEL,
                                                             scalar2=EPS, op0=ALU.mult, op1=ALU.add),
                      reads=[ps_tok], writes=["rstd"])
                emit_rsqrt_inplace(kb, rstd_t[:, :NT], "rstd")
                for c in range(NCH):
                    ho = ho_sb[c % 3]
                    htok = ("ho", c % 3)
                    kb.op("dve", lambda: nc.vector.scalar_tensor_tensor(out=ho[:, :], in0=hn[:, c, :],
                                                                        scalar=gf_sb[:, c:c + 1], in1=rstd_t[:, :NT],
                                                                        op0=ALU.mult, op1=ALU.mult),
                          reads=["hn", "rstd", "gf"], writes=[htok])
                    otok = ("hout", it, c)
                    kb.dma("sp", hout_v[:, c, t0:t0 + NT], ho[:, :], reads=[htok], writes=[otok])
                    out_toks.append(otok)
        kb.finish(out_toks)
    return nc


def prep_ffn_weights(w_up, conv_w, conv_b, w_down, g):
    wg = w_up[:, :FFN_DIM].reshape(NCH, 128, FFN_BLK, 128)
    wv = w_up[:, FFN_DIM:].reshape(NCH, 128, FFN_BLK, 128)
    wup = np.concatenate([wg, wv], axis=-1).transpose(2, 1, 0, 3)
    wdn = w_down.reshape(FFN_BLK, 128, NCH, 128).transpose(2, 1, 0, 3)
    cw = np.concatenate([conv_w, conv_b[None, :]], axis=0)
    cw = cw.reshape(4, FFN_BLK, 128).transpose(2, 1, 0)
    return dict(wup=np.ascontiguousarray(wup, dtype=np.float32), wdn=np.ascontiguousarray(wdn, dtype=np.float32),
                cw=np.ascontiguousarray(cw, dtype=np.float32), gnorm=np.ascontiguousarray(g.reshape(NCH, 128).T, dtype=np.float32))


def emit_outproj_residual(kb, pp, actT, act_tok_fn, KBK, w_dram, wbufs, wstate, hT, h_tok, h_col0, NT,
                          dst_v, dst_col0, ho_bufs, out_toks, tag):
    nc = kb.nc
    for mb in range(NCH):
        wd = wbufs[wstate[0] % len(wbufs)]
        dtok = (tag + "w", wstate[0] % len(wbufs))
        wstate[0] += 1
        kb.dma("pool", wd[:, :KBK, :], w_dram[mb], writes=[dtok])
        ps_o, to = pp.next()
        for k in range(KBK):
            kb.op("pe", lambda: nc.tensor.matmul(ps_o[:, :NT], lhsT=wd[:, k, :], rhs=actT[:, k, :NT],
                                                 start=(k == 0), stop=(k == KBK - 1)),
                  reads=[dtok, act_tok_fn(k)], writes=[to], inc=(k == KBK - 1))
        ho = ho_bufs[mb % len(ho_bufs)]
        htok = (tag + "ho", mb % len(ho_bufs))
        kb.op("dve", lambda: nc.vector.tensor_tensor(out=ho[:, :NT], in0=ps_o[:, :NT], in1=hT[:, mb, h_col0:h_col0 + NT], op=ALU.add),
              reads=[to, h_tok], writes=[htok])
        otok = (tag + "out", dst_col0, mb)
        kb.dma("sp", dst_v[:, mb, dst_col0:dst_col0 + NT], ho[:, :NT], reads=[htok], writes=[otok])
        out_toks.append(otok)


XA_H = 4
XA_HD = 512
N_MEM = 256


class Ctx:
    pass


def emit_mem_norm(kb, pp, cx, memT_dram, gmem_dram):
    cx.memn = sb(kb, "memn", [128, NCH, N_MEM], BF16)
    with kb.phase():
        tile_ctx(kb, cx)
        memf = cx.hT
        gm = cx.g_sb
        kb.dma("sp", gm[:, :], gmem_dram[:, :], writes=["g"])
        kb.dma("sp", memf[:, :, :N_MEM], memT_dram.rearrange("(c p) m -> p c m", p=128), writes=["hT"])
        emit_rmsnorm_fm(kb, pp, memf, "hT", gm, cx.memn, "memn", cx.ones_bf, N_MEM, cx.sq_bufs, (cx.rstd_t, "rstd"))


def emit_xattn(kb, pp, cx, T, src_v, src_pad, dst_v, dst_pad, w, out_toks, NT=512):
    with kb.phase():
        tile_ctx(kb, cx)
        ctx_add_xattn(kb, cx)
        _emit_xattn_body(kb, pp, cx, T, src_v, src_pad, dst_v, dst_pad, w, out_toks, NT)


def _emit_xattn_body(kb, pp, cx, T, src_v, src_pad, dst_v, dst_pad, w, out_toks, NT):
    nc = kb.nc
    nt = T // NT
    KT = cx.xa_KT
    V = cx.xa_V
    g_sb = cx.g_sb
    kb.dma("sp", g_sb[:, :], w["g"][:, :], writes=["g"])
    for blk in range(NCH):
        wb = cx.wA[cx.wAi[0] % 2]; wtok = ("wA", cx.wAi[0] % 2); cx.wAi[0] += 1
        kb.dma("pool", wb[:, :, 0:128], w["wk"][blk], writes=[wtok])
        ps, pt = pp.next()
        for kc in range(NCH):
            kb.op("pe", lambda: nc.tensor.matmul(ps[:, :N_MEM], lhsT=wb[:, kc, 0:128], rhs=cx.memn[:, kc, :],
                                                 start=(kc == 0), stop=(kc == NCH - 1)),
                  reads=[wtok, "memn"], writes=[pt], inc=(kc == NCH - 1))
        kb.op("act", lambda: nc.scalar.copy(out=KT[:, blk, :], in_=ps[:, :N_MEM]), reads=[pt], writes=["KT"])
    for cg in range(4):
        wb = cx.wB[cx.wBi[0] % 2]; wtok = ("wB", cx.wBi[0] % 2); cx.wBi[0] += 1
        kb.dma("pool", wb[:, :, :], w["wv"][cg], writes=[wtok])
        for mblk in range(2):
            ps, pt = pp.next()
            for kc in range(NCH):
                kb.op("pe", lambda: nc.tensor.matmul(ps[:, :512], lhsT=cx.memn[:, kc, mblk * 128:(mblk + 1) * 128], rhs=wb[:, kc, :],
                                                     start=(kc == 0), stop=(kc == NCH - 1)),
                      reads=[wtok, "memn"], writes=[pt], inc=(kc == NCH - 1))
            kb.op("act", lambda: nc.scalar.copy(out=V[:, mblk, cg * 512:(cg + 1) * 512], in_=ps[:, :512]), reads=[pt], writes=["V"])
    hT, uT, qT, oT, PT = cx.hT, cx.uT, cx.xa_qT, cx.xa_oT, cx.xa_PT
    scale = float(XA_HD) ** -0.5
    for it in range(nt):
        t0 = it * NT
        for c in range(NCH):
            kb.dma("sp", hT[:, c, :NT], src_v[:, c, src_pad + t0:src_pad + t0 + NT], writes=["hT"])
        emit_rmsnorm_fm(kb, pp, hT, "hT", g_sb, uT, "uT", cx.ones_bf, NT, cx.sq_bufs, (cx.rstd_t, "rstd"))
        for blk in range(NCH):
            wb = cx.wA[cx.wAi[0] % 2]; wtok = ("wA", cx.wAi[0] % 2); cx.wAi[0] += 1
            kb.dma("pool", wb[:, :, 0:128], w["wq"][blk], writes=[wtok])
            ps, pt = pp.next()
            for kc in range(NCH):
                kb.op("pe", lambda: nc.tensor.matmul(ps[:, :NT], lhsT=wb[:, kc, 0:128], rhs=uT[:, kc, :NT],
                                                     start=(kc == 0), stop=(kc == NCH - 1)),
                      reads=[wtok, "uT"], writes=[pt], inc=(kc == NCH - 1))
            kb.op("act", lambda: nc.scalar.activation(out=qT[:, blk, :NT], in_=ps[:, :NT], func=AF.Copy, scale=scale),
                  reads=[pt], writes=[("qT", blk)])
        nsub = NT // 128
        units = [(hd, sub) for hd in range(XA_H) for sub in range(nsub)]
        stA = {}

        def stageA(u):
            hd, sub = units[u]
            ps, pt = pp.next()
            for j in range(4):
                kb.op("pe", lambda: nc.tensor.matmul(ps[:, :N_MEM], lhsT=qT[:, hd * 4 + j, sub * 128:(sub + 1) * 128],
                                                     rhs=KT[:, hd * 4 + j, :], start=(j == 0), stop=(j == 3)),
                      reads=[("qT", hd * 4 + j), "KT"], writes=[pt], inc=(j == 3))
            i2 = cx.xai[0] % 2; cx.xai[0] += 1
            mx = cx.xa_mx[i2]; mtok = ("xamx", i2)
            kb.op("dve", lambda: nc.vector.reduce_max(out=mx[:, 0:1], in_=ps[:, :N_MEM], axis=AX.X), reads=[pt], writes=[mtok])
            kb.op("dve", lambda: nc.vector.tensor_scalar(out=mx[:, 1:2], in0=mx[:, 0:1], scalar1=-1.0, scalar2=None, op0=ALU.mult),
                  reads=[mtok], writes=[mtok])
            P = cx.xa_P[i2]; ptok = ("xaP", i2)
            kb.op("act", lambda: nc.scalar.activation(out=P[:, :], in_=ps[:, :N_MEM], func=AF.Exp, bias=mx[:, 1:2], scale=1.0,
                                                      accum_out=mx[:, 2:3]),
                  reads=[pt, mtok], writes=[ptok, mtok])
            stA[u] = i2

        def stageB(u):
            hd, sub = units[u]
            i2 = stA.pop(u)
            mx = cx.xa_mx[i2]; mtok = ("xamx", i2)
            P = cx.xa_P[i2]; ptok = ("xaP", i2)
            kb.op("dve", lambda: nc.vector.reciprocal(out=mx[:, 3:4], in_=mx[:, 2:3]), reads=[mtok], writes=[mtok])
            Pn = cx.xa_Pn[i2]; pntok = ("xaPn", i2)
            kb.op("dve", lambda: nc.vector.tensor_scalar(out=Pn[:, :], in0=P[:, :], scalar1=mx[:, 3:4], scalar2=None, op0=ALU.mult),
                  reads=[ptok, mtok], writes=[pntok])
            pb = cx.psb[cx.psbi[0] % 2]; pbtok = ("psb", cx.psbi[0] % 2); cx.psbi[0] += 1
            for mblk in range(2):
                kb.op("pe", lambda: nc.tensor.transpose(out=pb[:, mblk * 128:(mblk + 1) * 128], in_=Pn[:, mblk * 128:(mblk + 1) * 128],
                                                        identity=cx.ident[:, :]),
                      reads=[pntok, "ident"], writes=[pbtok], inc=(mblk == 1))
            kb.op("act", lambda: nc.scalar.copy(out=PT[:, :, hd, sub * 128:(sub + 1) * 128],
                                                in_=pb[:, 0:256].rearrange("p (b t) -> p b t", b=2)),
                  reads=[pbtok], writes=[("PT", hd)])
            if sub == nsub - 1:
                for dblk in range(4):
                    ps, pt = pp.next()
                    for mblk in range(2):
                        kb.op("pe", lambda: nc.tensor.matmul(ps[:, :NT], lhsT=V[:, mblk, hd * 512 + dblk * 128:hd * 512 + (dblk + 1) * 128],
                                                             rhs=PT[:, mblk, hd, :NT], start=(mblk == 0), stop=(mblk == 1)),
                              reads=["V", ("PT", hd)], writes=[pt], inc=(mblk == 1))
                    kb.op("act", lambda: nc.scalar.copy(out=oT[:, hd * 4 + dblk, :NT], in_=ps[:, :NT]), reads=[pt], writes=[("oT", hd * 4 + dblk)])

        stageA(0)
        for u in range(len(units)):
            if u + 1 < len(units):
                stageA(u + 1)
            stageB(u)
        emit_outproj_residual(kb, pp, oT, lambda k: ("oT", k), NCH, w["wo"], cx.wO, cx.wOi, hT, "hT", 0, NT,
                              dst_v, dst_pad + t0, cx.ho_bufs, out_toks, "xa")


def make_ctx(kb, NT=512, halo=3):
    nc = kb.nc
    cx = Ctx()
    cx.NT, cx.W = NT, NT + halo
    cx.ones_bf = sb(kb, "ones", [128, 128], BF16)
    cx.onesf = sb(kb, "onesf", [128, 128], F32)
    cx.identf, cx.ident = make_ident(kb)
    cx.psb = [psum_bf16(kb, f"psb{i}") for i in range(2)]; cx.psbi = [0]
    cx.wAi = [0]; cx.wOi = [0]
    kb.op("pool", lambda: nc.gpsimd.memset(cx.ones_bf[:, :], 1.0), writes=["ones"])
    kb.op("pool", lambda: nc.gpsimd.memset(cx.onesf[:, :], 1.0), writes=["onesf"])
    return cx


def tile_ctx(kb, cx, need_wO=True):
    NT, W = cx.NT, cx.W
    cx.hT = sb(kb, "hT", [128, NCH, W], F32)
    cx.uT = sb(kb, "uT", [128, NCH, W], BF16)
    cx.g_sb = sb(kb, "g_sb", [128, NCH], F32)
    cx.rstd_t = sb(kb, "rstd", [128, W], F32)
    cx.sq_bufs = [(sb(kb, f"sq{i}", [128, W], BF16), ("sq", i)) for i in range(3)]
    cx.ho_bufs = [sb(kb, f"ho{i}", [128, NT], F32) for i in range(3)]
    cx.wA = [sb(kb, f"wA{i}", [128, NCH, 256], BF16) for i in range(2)]
    if need_wO:
        cx.wO = [sb(kb, f"wO{i}", [128, FFN_BLK, 128], BF16) for i in range(2)]


def ctx_add_xattn(kb, cx):
    NT = cx.NT
    cx.wB = [sb(kb, f"wB{i}", [128, NCH, 512], BF16) for i in range(2)]; cx.wBi = [0]
    cx.xa_KT = sb(kb, "xaKT", [128, NCH, N_MEM], BF16)
    cx.xa_V = sb(kb, "xaV", [128, 2, D_MODEL], BF16)
    cx.xa_qT = sb(kb, "xaqT", [128, NCH, NT], BF16)
    cx.xa_oT = sb(kb, "xaoT", [128, NCH, NT], BF16)
    cx.xa_PT = sb(kb, "xaPT", [128, 2, XA_H, NT], BF16)
    cx.xa_mx = [sb(kb, f"xamx{i}", [128, 4], F32) for i in range(2)]
    cx.xa_P = [sb(kb, f"xaP{i}", [128, N_MEM], F32) for i in range(2)]
    cx.xa_Pn = [sb(kb, f"xaPn{i}", [128, N_MEM], BF16) for i in range(2)]
    cx.xai = [0]


def prep_xattn_weights(wq, wkv, wo, g):
    def fm_blocks(wm):
        n = wm.shape[1] // 128
        return np.ascontiguousarray(wm.reshape(NCH, 128, n, 128).transpose(2, 1, 0, 3), dtype=np.float32)
    wk = wkv[:, :D_MODEL]
    wv = wkv[:, D_MODEL:]
    wvr = np.ascontiguousarray(wv.reshape(NCH, 128, 4, 512).transpose(2, 1, 0, 3), dtype=np.float32)
    return dict(wq=fm_blocks(wq), wk=fm_blocks(wk), wv=wvr, wo=fm_blocks(wo),
                g=np.ascontiguousarray(g.reshape(NCH, 128).T, dtype=np.float32))


def build_xattn_test(T, NT=512):
    nc = bass.Bass("TRN2", target_bir_lowering=False)
    hin = nc.dram_tensor("hin", [D_MODEL, T], F32, kind="ExternalInput").ap()
    memT = nc.dram_tensor("memT", [D_MODEL, N_MEM], F32, kind="ExternalInput").ap()
    gmem = nc.dram_tensor("gmem", [128, NCH], F32, kind="ExternalInput").ap()
    w = dict(wq=nc.dram_tensor("wq", [NCH, 128, NCH, 128], F32, kind="ExternalInput").ap(),
             wk=nc.dram_tensor("wk", [NCH, 128, NCH, 128], F32, kind="ExternalInput").ap(),
             wv=nc.dram_tensor("wv", [4, 128, NCH, 512], F32, kind="ExternalInput").ap(),
             wo=nc.dram_tensor("wo", [NCH, 128, NCH, 128], F32, kind="ExternalInput").ap(),
             g=nc.dram_tensor("g", [128, NCH], F32, kind="ExternalInput").ap())
    hout = nc.dram_tensor("hout", [D_MODEL, T], F32, kind="ExternalOutput").ap()
    with contextlib.ExitStack() as stack:
        kb = KB(nc, stack)
        pp = PsumPool(kb, nbanks=6)
        cx = make_ctx(kb, NT)
        emit_mem_norm(kb, pp, cx, memT, gmem)
        outs = []
        emit_xattn(kb, pp, cx, T, hin.rearrange("(c p) t -> p c t", p=128), 0, hout.rearrange("(c p) t -> p c t", p=128), 0, w, outs, NT)
        kb.finish(outs)
    return nc


def emit_proj_block(kb, pp, wb, wtok, wcol0, uT, u_tok, col_ranges):
    nc = kb.nc
    outs = []
    for (c0, w) in col_ranges:
        ps, pt = pp.next()
        for kc in range(NCH):
            kb.op("pe", lambda: nc.tensor.matmul(ps[:, :w], lhsT=wb[:, kc, wcol0:wcol0 + 128], rhs=uT[:, kc, c0:c0 + w],
                                                 start=(kc == 0), stop=(kc == NCH - 1)),
                  reads=[wtok, u_tok], writes=[pt], inc=(kc == NCH - 1))
        outs.append((ps, pt))
    return outs


def emit_softplus_small(kb, out, x, tmp, tok, neg_in=False):
    nc = kb.nc
    sgn = -1.0 if neg_in else 1.0
    kb.op("act", lambda: nc.scalar.activation(out=tmp, in_=x, func=AF.Abs), reads=[tok], writes=[tok])
    kb.op("act", lambda: nc.scalar.activation(out=tmp, in_=tmp, func=AF.Exp, scale=-1.0), reads=[tok], writes=[tok])
    kb.op("act", lambda: nc.scalar.activation(out=tmp, in_=tmp, func=AF.Ln, bias=1.0, scale=1.0), reads=[tok], writes=[tok])
    kb.op("dve", lambda: nc.vector.tensor_scalar(out=out, in0=x, scalar1=sgn, scalar2=0.0, op0=ALU.mult, op1=ALU.max), reads=[tok], writes=[tok])
    kb.op("dve", lambda: nc.vector.tensor_tensor(out=out, in0=out, in1=tmp, op=ALU.add), reads=[tok], writes=[tok])


RG_BLOCKS = 8


def emit_rglru(kb, pp, cx, T, src_v, src_pad, dst_v, dst_pad, w, scr, out_toks, NT=512):
    nc = kb.nc
    nt = T // NT
    W = NT + 3
    ggv = scr["ggT"].rearrange("(c p) t -> p c t", p=128)
    hfv = scr["hfT"].rearrange("(c p) t -> p c t", p=128)
    abv = scr["abT"].rearrange("(c p) t -> p c t", p=128)
    bxv = scr["bxbT"].rearrange("(c p) t -> p c t", p=128)
    with kb.phase():
        tile_ctx(kb, cx, need_wO=False)
        hT, uT, g_sb = cx.hT, cx.uT, cx.g_sb
        gw = sb(kb, "rg_gw", [128, 2, RG_BLOCKS, 2, 512], BF16)
        cw = sb(kb, "rg_cw", [128, NCH, 5], F32)
        gb = sb(kb, "rg_gb", [128, 2, NCH, 2], F32)
        cd = sb(kb, "rg_cd", [128, 2, NCH], F32)
        cdt = sb(kb, "rg_cdt", [128, 2, NCH], F32)
        xcf = sb(kb, "rg_xcf", [128, NCH, NT], F32)
        xcb = sb(kb, "rg_xcb", [128, NCH, NT], BF16)
        G = [sb(kb, f"rg_G{i}", [128, W], F32) for i in range(2)]
        gg = [sb(kb, f"rg_gg{i}", [128, NT], BF16) for i in range(2)]
        t1 = [sb(kb, f"rg_t1{i}", [128, NT], F32) for i in range(2)]
        t2 = [sb(kb, f"rg_t2{i}", [128, NT], F32) for i in range(2)]
        ta = [sb(kb, f"rg_ta{i}", [128, NT], F32) for i in range(2)]
        tb = [sb(kb, f"rg_tb{i}", [128, NT], F32) for i in range(2)]
        th = [sb(kb, f"rg_th{i}", [128, NT], F32) for i in range(2)]
        carry = sb(kb, "rg_carry", [128, NCH], F32)
        kb.dma("sp", g_sb[:, :], w["g"][:, :], writes=["g"])
        kb.dma("pool", gw[:, :, :, :, :], w["gw"], writes=["gw"])
        kb.dma("sp", cw[:, :, :], w["cw"], writes=["cw"])
        kb.dma("sp", gb[:, :, :, :], w["gb"], writes=["gb"])
        kb.dma("sp", cd[:, :, :], w["lam"], writes=["cd"])
        cdf = cd[:, :, :].rearrange("p a b -> p (a b)")
        cdtf = cdt[:, :, :].rearrange("p a b -> p (a b)")
        emit_softplus_small(kb, cdf, cdf, cdtf, "cd", neg_in=True)
        kb.op("dve", lambda: nc.vector.tensor_scalar(out=cdf, in0=cdf, scalar1=-8.0, scalar2=None, op0=ALU.mult), reads=["cd"], writes=["cd"])
        kb.op("dve", lambda: nc.vector.memset(carry[:, :], 0.0), writes=["carry"])
        k = 0
        for it in range(nt):
            t0 = it * NT
            for c in range(NCH):
                kb.dma("sp", hT[:, c, :W], src_v[:, c, src_pad + t0 - 2:src_pad + t0 - 2 + W], writes=["hT"])
            emit_rmsnorm_fm(kb, pp, hT, "hT", g_sb, uT, "uT", cx.ones_bf, W, cx.sq_bufs, (cx.rstd_t, "rstd"))
            for blk in range(32):
                wb = cx.wA[cx.wAi[0] % 2]; wtok = ("wA", cx.wAi[0] % 2); cx.wAi[0] += 1
                kb.dma("pool", wb[:, :, 0:128], w["win"][blk], writes=[wtok])
                i2 = blk % 2
                if blk < 16:
                    (ps, pt), = emit_proj_block(kb, pp, wb, wtok, 0, uT, "uT", [(2, NT)])
                    a1, a2 = t1[i2], t2[i2]
                    kb.op("act", lambda: nc.scalar.copy(out=a1[:, :], in_=ps[:, :NT]), reads=[pt], writes=[("t1", i2)])
                    kb.op("dve", lambda: nc.vector.tensor_tensor(out=a2[:, :], in0=a1[:, :], in1=a1[:, :], op=ALU.mult), reads=[("t1", i2)], writes=[("t2", i2)])
                    kb.op("dve", lambda: nc.vector.tensor_scalar(out=a2[:, :], in0=a2[:, :], scalar1=0.044715, scalar2=1.0, op0=ALU.mult, op1=ALU.add),
                          reads=[("t2", i2)], writes=[("t2", i2)])
                    kb.op("dve", lambda: nc.vector.tensor_tensor(out=a2[:, :], in0=a2[:, :], in1=a1[:, :], op=ALU.mult), reads=[("t2", i2), ("t1", i2)], writes=[("t2", i2)])
                    kb.op("act", lambda: nc.scalar.activation(out=a2[:, :], in_=a2[:, :], func=AF.Sigmoid, scale=1.5957691216), reads=[("t2", i2)], writes=[("t2", i2)])
                    kb.op("dve", lambda: nc.vector.tensor_tensor(out=gg[i2][:, :], in0=a2[:, :], in1=a1[:, :], op=ALU.mult), reads=[("t2", i2), ("t1", i2)], writes=[("gg", i2)])
                    kb.dma("sp", ggv[:, blk, t0:t0 + NT], gg[i2][:, :], reads=[("gg", i2)], writes=[("ggd", it, blk)])
                else:
                    c = blk - 16
                    (psm, ptm), (psx, ptx) = emit_proj_block(kb, pp, wb, wtok, 0, uT, "uT", [(0, NT), (NT, 3)])
                    Gt = G[i2]; gtok = ("G", i2)
                    kb.op("act", lambda: nc.scalar.copy(out=Gt[:, 0:NT], in_=psm[:, :NT]), reads=[ptm], writes=[gtok])
                    kb.op("act", lambda: nc.scalar.copy(out=Gt[:, NT:NT + 3], in_=psx[:, :3]), reads=[ptx], writes=[gtok])
                    kb.op("dve", lambda: nc.vector.tensor_scalar(out=xcf[:, c, :], in0=Gt[:, 0:NT], scalar1=cw[:, c, 0:1], scalar2=cw[:, c, 4:5],
                                                                 op0=ALU.mult, op1=ALU.add), reads=[gtok, "cw"], writes=[("xcf", c)])
                    for tap in range(1, 4):
                        kb.op("dve", lambda: nc.vector.scalar_tensor_tensor(out=xcf[:, c, :], in0=Gt[:, tap:tap + NT], scalar=cw[:, c, tap:tap + 1],
                                                                            in1=xcf[:, c, :], op0=ALU.mult, op1=ALU.add),
                              reads=[gtok, ("xcf", c)], writes=[("xcf", c)])
                    kb.op("act", lambda: nc.scalar.copy(out=xcb[:, c, :], in_=xcf[:, c, :]), reads=[("xcf", c)], writes=[("xcb", c)])
            for d in range(2):
                for n in range(RG_BLOCKS):
                    for jb in range(2):
                        c = 2 * n + jb
                        i2 = k % 2; k += 1
                        psr, ptr = pp.next()
                        psi, pti = pp.next()
                        for (ps_, pt_, col) in ((psr, ptr, jb * 128), (psi, pti, 256 + jb * 128)):
                            for kc in range(2):
                                kb.op("pe", lambda: nc.tensor.matmul(ps_[:, :NT], lhsT=gw[:, d, n, kc, col:col + 128], rhs=xcb[:, 2 * n + kc, :],
                                                                     start=(kc == 0), stop=(kc == 1)),
                                      reads=["gw", ("xcb", 2 * n + kc)], writes=[pt_], inc=(kc == 1))
                        A, B_, R1, R2 = ta[i2], tb[i2], t1[i2], t2[i2]
                        kb.op("act", lambda: nc.scalar.activation(out=R1[:, :], in_=psr[:, :NT], func=AF.Sigmoid, bias=gb[:, d, c, 0:1], scale=1.0),
                              reads=[ptr, "gb"], writes=[("t1", i2)])
                        kb.op("act", lambda: nc.scalar.activation(out=A[:, :], in_=R1[:, :], func=AF.Exp, scale=cd[:, d, c:c + 1]),
                              reads=[("t1", i2), "cd"], writes=[("ta", i2)])
                        kb.op("act", lambda: nc.scalar.activation(out=R2[:, :], in_=psi[:, :NT], func=AF.Sigmoid, bias=gb[:, d, c, 1:2], scale=1.0),
                              reads=[pti, "gb"], writes=[("t2", i2)])
                        kb.op("dve", lambda: nc.vector.tensor_tensor(out=R1[:, :], in0=A[:, :], in1=A[:, :], op=ALU.mult), reads=[("ta", i2), ("t1", i2)], writes=[("t1", i2)])
                        kb.op("dve", lambda: nc.vector.tensor_scalar(out=R1[:, :], in0=R1[:, :], scalar1=-1.0, scalar2=1.0, op0=ALU.mult, op1=ALU.add),
                              reads=[("t1", i2)], writes=[("t1", i2)])
                        kb.op("act", lambda: nc.scalar.activation(out=R1[:, :], in_=R1[:, :], func=AF.Sqrt), reads=[("t1", i2)], writes=[("t1", i2)])
                        kb.op("dve", lambda: nc.vector.tensor_tensor(out=R2[:, :], in0=R2[:, :], in1=xcf[:, c, :], op=ALU.mult), reads=[("t2", i2), ("xcf", c)], writes=[("t2", i2)])
                        kb.op("dve", lambda: nc.vector.tensor_tensor(out=B_[:, :], in0=R2[:, :], in1=R1[:, :], op=ALU.mult), reads=[("t2", i2), ("t1", i2)], writes=[("tb", i2)])
                        if d == 0:
                            H = th[i2]
                            kb.op("dve", lambda: nc.vector.tensor_tensor_scan(out=H[:, :], data0=A[:, :], data1=B_[:, :], initial=carry[:, c:c + 1],
                                                                              op0=ALU.mult, op1=ALU.add),
                                  reads=[("ta", i2), ("tb", i2), "carry"], writes=[("th", i2)])
                            kb.op("dve", lambda: nc.vector.tensor_copy(out=carry[:, c:c + 1], in_=H[:, NT - 1:NT]), reads=[("th", i2)], writes=["carry"])
                            kb.dma("sp", hfv[:, c, t0:t0 + NT], H[:, :], reads=[("th", i2)], writes=[("hfd", it, c)])
                        else:
                            kb.dma("sp", abv[:, c, t0:t0 + NT], A[:, :], reads=[("ta", i2)], writes=[("abd", it, c)])
                            kb.dma("sp", bxv[:, c, t0:t0 + NT], B_[:, :], reads=[("tb", i2)], writes=[("bxd", it, c)])
    with kb.phase():
        tile_ctx(kb, cx)
        hT = cx.hT
        yT = sb(kb, "rg_yT", [128, NCH, NT], BF16)
        A = [sb(kb, f"rg2_a{i}", [128, NT], F32) for i in range(2)]
        Bx = [sb(kb, f"rg2_b{i}", [128, NT], F32) for i in range(2)]
        Hf = [sb(kb, f"rg2_h{i}", [128, NT], F32) for i in range(2)]
        Gg = [sb(kb, f"rg2_g{i}", [128, NT], BF16) for i in range(2)]
        Hb = [sb(kb, f"rg2_hb{i}", [128, NT], F32) for i in range(2)]
        carry = sb(kb, "rg2_carry", [128, NCH], F32)
        kb.op("dve", lambda: nc.vector.memset(carry[:, :], 0.0), writes=["carry2"])
        k = 0
        for it in reversed(range(nt)):
            t0 = it * NT
            for c in range(NCH):
                kb.dma("sp", hT[:, c, :NT], src_v[:, c, src_pad + t0:src_pad + t0 + NT], writes=["hT"])
            for c in range(NCH):
                i2 = k % 2; k += 1
                kb.dma("sp", A[i2][:, :], abv[:, c, t0:t0 + NT], reads=[("abd", it, c)], writes=[("A2", i2)])
                kb.dma("sp", Bx[i2][:, :], bxv[:, c, t0:t0 + NT], reads=[("bxd", it, c)], writes=[("B2", i2)])
                kb.dma("sp", Hf[i2][:, :], hfv[:, c, t0:t0 + NT], reads=[("hfd", it, c)], writes=[("H2", i2)])
                kb.dma("sp", Gg[i2][:, :], ggv[:, c, t0:t0 + NT], reads=[("ggd", it, c)], writes=[("G2", i2)])
                kb.op("dve", lambda: nc.vector.tensor_tensor_scan(out=Hb[i2][:, ::-1], data0=A[i2][:, ::-1], data1=Bx[i2][:, ::-1],
                                                                  initial=carry[:, c:c + 1], op0=ALU.mult, op1=ALU.add),
                      reads=[("A2", i2), ("B2", i2), "carry2"], writes=[("Hb2", i2)])
                kb.op("dve", lambda: nc.vector.tensor_copy(out=carry[:, c:c + 1], in_=Hb[i2][:, 0:1]), reads=[("Hb2", i2)], writes=["carry2"])
                kb.op("dve", lambda: nc.vector.tensor_tensor(out=Hb[i2][:, :], in0=Hb[i2][:, :], in1=Hf[i2][:, :], op=ALU.add),
                      reads=[("Hb2", i2), ("H2", i2)], writes=[("Hb2", i2)])
                kb.op("dve", lambda: nc.vector.tensor_tensor(out=yT[:, c, :], in0=Hb[i2][:, :], in1=Gg[i2][:, :], op=ALU.mult),
                      reads=[("Hb2", i2), ("G2", i2)], writes=[("yT", c)])
            emit_outproj_residual(kb, pp, yT, lambda k_: ("yT", k_), NCH, w["wout"], cx.wO, cx.wOi, hT, "hT", 0, NT,
                                  dst_v, dst_pad + t0, cx.ho_bufs, out_toks, "rg")


def prep_rglru_weights(w_in, conv_w, conv_b, gate_w, gate_b, lam, w_out, g):
    def fm_blocks(wm):
        n = wm.shape[1] // 128
        return np.ascontiguousarray(wm.reshape(-1, 128, n, 128).transpose(2, 1, 0, 3), dtype=np.float32)
    gw = gate_w.reshape(2, RG_BLOCKS, 2, 128, 512).transpose(3, 0, 1, 2, 4)
    cw = np.concatenate([conv_w, conv_b[None]], 0).reshape(5, NCH, 128).transpose(2, 1, 0)
    gbr = gate_b.reshape(2, RG_BLOCKS, 2, 2, 128)
    gb = gbr.transpose(4, 0, 1, 3, 2).reshape(128, 2, NCH, 2)
    lm = lam.reshape(2, NCH, 128).transpose(2, 0, 1)
    f = lambda a: np.ascontiguousarray(a, dtype=np.float32)
    return dict(win=fm_blocks(w_in), gw=f(gw), cw=f(cw), gb=f(gb), lam=f(lm), wout=fm_blocks(w_out),
                g=f(g.reshape(NCH, 128).T))


def build_rglru_test(T, NT=512):
    nc = bass.Bass("TRN2", target_bir_lowering=False)
    PAD = 2
    hin = nc.dram_tensor("hin", [D_MODEL, T + 2 * PAD], F32, kind="ExternalInput").ap()
    w = dict(win=nc.dram_tensor("win", [32, 128, NCH, 128], F32, kind="ExternalInput").ap(),
             gw=nc.dram_tensor("gw", [128, 2, RG_BLOCKS, 2, 512], F32, kind="ExternalInput").ap(),
             cw=nc.dram_tensor("cw", [128, NCH, 5], F32, kind="ExternalInput").ap(),
             gb=nc.dram_tensor("gb", [128, 2, NCH, 2], F32, kind="ExternalInput").ap(),
             lam=nc.dram_tensor("lam", [128, 2, NCH], F32, kind="ExternalInput").ap(),
             wout=nc.dram_tensor("wout", [NCH, 128, NCH, 128], F32, kind="ExternalInput").ap(),
             g=nc.dram_tensor("g", [128, NCH], F32, kind="ExternalInput").ap())
    hout = nc.dram_tensor("hout", [D_MODEL, T], F32, kind="ExternalOutput").ap()
    scr = dict(ggT=nc.dram_tensor("ggT", [D_MODEL, T], BF16).ap(), hfT=nc.dram_tensor("hfT", [D_MODEL, T], F32).ap(),
               abT=nc.dram_tensor("abT", [D_MODEL, T], F32).ap(), bxbT=nc.dram_tensor("bxbT", [D_MODEL, T], F32).ap())
    with contextlib.ExitStack() as stack:
        kb = KB(nc, stack)
        pp = PsumPool(kb, nbanks=6)
        cx = make_ctx(kb, NT)
        outs = []
        emit_rglru(kb, pp, cx, T, hin.rearrange("(c p) t -> p c t", p=128), PAD, hout.rearrange("(c p) t -> p c t", p=128), 0, w, scr, outs, NT)
        kb.finish(outs)
    return nc


SSD_INNER = 4096
SSD_HEADS = 64
SSD_G = 8
L = 128


def make_masks(kb, cx):
    nc = kb.nc
    cx.mask = []
    for d in range(2):
        m = sb(kb, f"mask{d}", [128, 128], F32)
        kb.op("pool", lambda: nc.gpsimd.memset(m[:, :], 1.0), writes=[("mask", d)])
        cm, coef = (-1, 1) if d == 0 else (1, -1)
        kb.op("pool", lambda: nc.gpsimd.affine_select(out=m[:, :], in_=m[:, :], pattern=[[coef, 128]], compare_op=ALU.is_ge,
                                                      fill=0.0, base=0, channel_multiplier=cm),
              reads=[("mask", d)], writes=[("mask", d)])
        cx.mask.append(m)


def emit_ssd(kb, pp, cx, T, src_v, src_pad, dst_v, dst_pad, w, scr, out_toks, NT=512):
    nc = kb.nc
    nt = T // NT
    nchunk = T // L
    W = NT + 3
    xcv = scr["xcT"].rearrange("(b p) t -> p b t", p=128)
    ynv = scr["ynT"].rearrange("(b p) t -> p b t", p=128)
    with kb.phase():
        tile_ctx(kb, cx)
        hT, uT, g_sb = cx.hT, cx.uT, cx.g_sb
        wB = [sb(kb, f"s1wB{i}", [128, NCH, 512], BF16) for i in range(2)]
        cw = sb(kb, "s1cw", [128, 48, 5], F32)
        dtb = sb(kb, "s1dtb", [128, 2], F32)
        A_sb = sb(kb, "s1A", [128, 2], F32)
        zs = [sb(kb, f"s1zs{i}", [128, 512], BF16) for i in range(2)]
        G = [sb(kb, f"s1G{i}", [128, W], F32) for i in range(2)]
        acc = [sb(kb, f"s1acc{i}", [128, NT], F32) for i in range(2)]
        xo = [sb(kb, f"s1xo{i}", [128, NT], BF16) for i in range(2)]
        d1 = sb(kb, "s1d1", [128, NT], F32)
        d2 = sb(kb, "s1d2", [128, NT], F32)
        d3 = sb(kb, "s1d3", [128, NT], F32)
        kb.dma("sp", g_sb[:, :], w["g"][:, :], writes=["g"])
        kb.dma("sp", cw[:, :, :], w["cw"], writes=["cw"])
        kb.dma("sp", dtb[:, 0:1], w["dtb"], writes=["dtb"])
        kb.dma("sp", A_sb[:, 0:1], w["alog"], writes=["A"])
        kb.op("act", lambda: nc.scalar.activation(out=A_sb[:, 1:2], in_=A_sb[:, 0:1], func=AF.Exp), reads=["A"], writes=["A"])
        kb.op("dve", lambda: nc.vector.tensor_scalar(out=A_sb[:, 1:2], in0=A_sb[:, 1:2], scalar1=-1.0, scalar2=None, op0=ALU.mult), reads=["A"], writes=["A"])
        wbi = 0
        for it in range(nt):
            t0 = it * NT
            for c in range(NCH):
                kb.dma("sp", hT[:, c, :W], src_v[:, c, src_pad + t0 - 2:src_pad + t0 - 2 + W], writes=["hT"])
            emit_rmsnorm_fm(kb, pp, hT, "hT", g_sb, uT, "uT", cx.ones_bf, W, cx.sq_bufs, (cx.rstd_t, "rstd"))
            for cg in range(8):
                wb = wB[wbi % 2]; wtok = ("s1wB", wbi % 2); wbi += 1
                kb.dma("pool", wb[:, :, :], w["wz"][cg], writes=[wtok])
                for sub in range(NT // 128):
                    ps, pt = pp.next()
                    for kc in range(NCH):
                        kb.op("pe", lambda: nc.tensor.matmul(ps[:, :512], lhsT=uT[:, kc, 2 + sub * 128:2 + (sub + 1) * 128], rhs=wb[:, kc, :],
                                                             start=(kc == 0), stop=(kc == NCH - 1)),
                              reads=[wtok, "uT"], writes=[pt], inc=(kc == NCH - 1))
                    i2 = (cg * 4 + sub) % 2
                    kb.op("act", lambda: nc.scalar.activation(out=zs[i2][:, :], in_=ps[:, :512], func=AF.Silu), reads=[pt], writes=[("zs", i2)])
                    r0 = t0 + sub * 128
                    kb.dma("sp", scr["zs"][r0:r0 + 128, cg * 512:(cg + 1) * 512], zs[i2][:, :], reads=[("zs", i2)], writes=[("zsd", r0 // 128, cg)])
            for blk in range(48):
                wb = cx.wA[cx.wAi[0] % 2]; wtok = ("wA", cx.wAi[0] % 2); cx.wAi[0] += 1
                kb.dma("pool", wb[:, :, 0:128], w["wx"][blk], writes=[wtok])
                i2 = blk % 2
                (psm, ptm), (psx, ptx) = emit_proj_block(kb, pp, wb, wtok, 0, uT, "uT", [(0, NT), (NT, 3)])
                Gt = G[i2]; gtok = ("G", i2)
                kb.op("act", lambda: nc.scalar.copy(out=Gt[:, 0:NT], in_=psm[:, :NT]), reads=[ptm], writes=[gtok])
                kb.op("act", lambda: nc.scalar.copy(out=Gt[:, NT:NT + 3], in_=psx[:, :3]), reads=[ptx], writes=[gtok])
                ac = acc[i2]; atok = ("acc", i2)
                kb.op("dve", lambda: nc.vector.tensor_scalar(out=ac[:, :], in0=Gt[:, 0:NT], scalar1=cw[:, blk, 0:1], scalar2=cw[:, blk, 4:5],
                                                             op0=ALU.mult, op1=ALU.add), reads=[gtok, "cw"], writes=[atok])
                for tap in range(1, 4):
                    kb.op("dve", lambda: nc.vector.scalar_tensor_tensor(out=ac[:, :], in0=Gt[:, tap:tap + NT], scalar=cw[:, blk, tap:tap + 1],
                                                                        in1=ac[:, :], op0=ALU.mult, op1=ALU.add),
                          reads=[gtok, atok], writes=[atok])
                kb.op("act", lambda: nc.scalar.activation(out=xo[i2][:, :], in_=ac[:, :], func=AF.Silu), reads=[atok], writes=[("xo", i2)])
                kb.dma("sp", xcv[:, blk, t0:t0 + NT], xo[i2][:, :], reads=[("xo", i2)], writes=[("xcd", it, blk)])
            wb = cx.wA[cx.wAi[0] % 2]; wtok = ("wA", cx.wAi[0] % 2); cx.wAi[0] += 1
            kb.dma("pool", wb[:, :, 0:128], w["wdt"][0], writes=[wtok])
            (ps, pt), = emit_proj_block(kb, pp, wb, wtok, 0, uT, "uT", [(2, NT)])
            kb.op("act", lambda: nc.scalar.activation(out=d1[:, :], in_=ps[:, :NT], func=AF.Identity, bias=dtb[:, 0:1], scale=1.0),
                  reads=[pt, "dtb"], writes=["d1"])
            emit_softplus_small(kb, d2[:, :], d1[:, :], d3[:, :], "d1")
            kb.dma("sp", scr["dtT"][:, t0:t0 + NT], d2[:, :], reads=["d1"], writes=[("dtd", it)])
            kb.op("dve", lambda: nc.vector.tensor_scalar(out=d1[:, :], in0=d2[:, :], scalar1=A_sb[:, 1:2], scalar2=None, op0=ALU.mult),
                  reads=["d1", "A"], writes=["d1"])
            for ch in range(NT // L):
                sl = slice(ch * L, (ch + 1) * L)
                rsl = slice((ch + 1) * L - 1, ch * L - 1 if ch > 0 else None, -1)
                kb.op("dve", lambda: nc.vector.tensor_tensor_scan(out=d3[0:64, sl], data0=cx.onesf[0:64, 0:L], data1=d1[0:64, sl], initial=0.0,
                                                                  op0=ALU.mult, op1=ALU.add), reads=["d1", "onesf"], writes=["d1"])
                kb.op("dve", lambda: nc.vector.tensor_tensor_scan(out=d3[64:128, rsl], data0=cx.onesf[64:128, 0:L], data1=d1[64:128, rsl], initial=0.0,
                                                                  op0=ALU.mult, op1=ALU.add), reads=["d1", "onesf"], writes=["d1"])
            kb.dma("sp", scr["cumT"][:, t0:t0 + NT], d3[:, :], reads=["d1"], writes=[("cumd", it)])
    with kb.phase():
        make_masks(kb, cx)
        xT_in = sb(kb, "s2xTin", [128, 40, L], BF16)
        CT = [sb(kb, f"s2CT{i}", [128, 8, L], BF16) for i in range(2)]
        BT = [sb(kb, f"s2BT{i}", [128, 8, L], BF16) for i in range(2)]
        xtm = [sb(kb, f"s2xtm{i}", [128, SSD_INNER], BF16) for i in range(2)]
        btm = [sb(kb, f"s2btm{i}", [128, 1024], BF16) for i in range(2)]
        cdt = [sb(kb, f"s2cdt{i}", [64, 2, L], F32) for i in range(2)]
        wT = sb(kb, "s2wT", [64, L], F32)
        eg = sb(kb, "s2eg", [64, 2], F32)
        dg = sb(kb, "s2dg", [64, 64], F32)
        tm = [sb(kb, f"s2tm{i}", [128, 4, 64], F32) for i in range(2)]
        cb = [sb(kb, f"s2cb{i}", [128, 8, L], F32) for i in range(4)]
        Dc = [sb(kb, f"s2Dc{i}", [128, 8, L], F32) for i in range(2)]
        ecb = [sb(kb, f"s2ecb{i}", [128, 8, L], F32) for i in range(2)]
        CBm = [sb(kb, f"s2CBm{i}", [128, L], F32) for i in range(2)]
        Wt = [sb(kb, f"s2Wt{i}", [128, 8, L], BF16) for i in range(2)]
        CsT = [sb(kb, f"s2CsT{i}", [128, 8, L], BF16) for i in range(2)]
        xw = [sb(kb, f"s2xw{i}", [128, 512], BF16) for i in range(2)]
        Sf = sb(kb, "s2Sf", [128, SSD_G, 512], F32)
        Sb = sb(kb, "s2Sb", [128, SSD_G, 512], BF16)
        yf = sb(kb, "s2yf", [128, SSD_INNER], F32)
        yg = sb(kb, "s2yg", [128, SSD_INNER], F32)
        zsc = sb(kb, "s2zs", [128, SSD_INNER], BF16)
        gN = sb(kb, "s2gN", [128, SSD_INNER], F32)
        dsk = sb(kb, "s2dsk", [128, SSD_INNER], F32)
        ynb = sb(kb, "s2ynb", [128, SSD_INNER], BF16)
        ynT_sb = [sb(kb, f"s2ynT{i}", [128, 8, L], BF16) for i in range(2)]
        st = sb(kb, "s2st", [128, 4], F32)
        kb.dma("sp", gN[:, :], w["gn"].partition_broadcast(128), writes=["gN"])
        kb.dma("sp", dsk[:, :], w["dsk"].partition_broadcast(128), writes=["dsk"])
        ci = 0
        for d in range(2):
            kb.op("dve", lambda: nc.vector.memset(Sf[:, :, :], 0.0), reads=[], writes=[("Sf", g_) for g_ in range(SSD_G)])
            kb.op("pool", lambda: nc.gpsimd.memset(Sb[:, :, :], 0.0), reads=[], writes=[("Sb", g_) for g_ in range(SSD_G)])
            order = range(nchunk) if d == 0 else reversed(range(nchunk))
            last = L - 1 if d == 0 else 0
            for c in order:
                c0 = c * L
                i2 = ci % 2; ci += 1
                it = c0 // NT
                kb.dma("sp", CT[i2][:, :, :], xcv[:, 40:48, c0:c0 + L], reads=[("xcd", it, b_) for b_ in range(40, 48)], writes=[("CT", i2)])
                kb.dma("sp", BT[i2][:, :, :], xcv[:, 32:40, c0:c0 + L], reads=[("xcd", it, b_) for b_ in range(32, 40)], writes=[("BT", i2)])
                kb.dma("sp", cdt[i2][:, 0, :], scr["cumT"][d * 64:(d + 1) * 64, c0:c0 + L], reads=[("cumd", it)], writes=[("cdt", i2)])
                kb.dma("sp", cdt[i2][:, 1, :], scr["dtT"][d * 64:(d + 1) * 64, c0:c0 + L], reads=[("dtd", it)], writes=[("cdt", i2)])
                X, Bm = xtm[i2], btm[i2]
                if d == 0:
                    kb.dma("sp", xT_in[:, 0:32, :], xcv[:, 0:32, c0:c0 + L], reads=[("xcd", it, b_) for b_ in range(32)], writes=["xTin"])
                    for grp in range(5):
                        pb = cx.psb[cx.psbi[0] % 2]; pbtok = ("psb", cx.psbi[0] % 2); cx.psbi[0] += 1
                        for j in range(8):
                            src_ap = xT_in[:, grp * 8 + j, :] if grp < 4 else BT[i2][:, j, :]
                            kb.op("pe", lambda: nc.tensor.transpose(out=pb[:, j * 128:(j + 1) * 128], in_=src_ap, identity=cx.ident[:, :]),
                                  reads=["xTin" if grp < 4 else ("BT", i2), "ident"], writes=[pbtok], inc=(j == 7))
                        if grp < 4:
                            kb.op("act", lambda: nc.scalar.copy(out=X[:, grp * 1024:(grp + 1) * 1024], in_=pb[:, :]), reads=[pbtok], writes=[("xtm", i2)])
                        else:
                            kb.op("act", lambda: nc.scalar.copy(out=Bm[:, :], in_=pb[:, :]), reads=[pbtok], writes=[("btm", i2)])
                    kb.dma("pool", scr["xtm"][c0:c0 + L, :], X[:, :], reads=[("xtm", i2)], writes=[("xtmd", c)])
                    kb.dma("pool", scr["btm"][c0:c0 + L, :], Bm[:, :], reads=[("btm", i2)], writes=[("btmd", c)])
                else:
                    kb.dma("sp", X[:, :], scr["xtm"][c0:c0 + L, :], reads=[("xtmd", c)], writes=[("xtm", i2)])
                    kb.dma("sp", Bm[:, :], scr["btm"][c0:c0 + L, :], reads=[("btmd", c)], writes=[("btm", i2)])
                    kb.dma("sp", yf[:, :], scr["yf"][c0:c0 + L, :], reads=[("yfd", c, g_) for g_ in range(SSD_G)], writes=["yf"])
                    kb.dma("sp", zsc[:, :], scr["zs"][c0:c0 + L, :], reads=[("zsd", c, cg_) for cg_ in range(8)], writes=["zsc"])
                cd_ = cdt[i2]
                kb.op("act", lambda: nc.scalar.activation(out=wT[:, :], in_=cd_[:, 0, :], func=AF.Exp, bias=cd_[:, 0, last:last + 1], scale=-1.0),
                      reads=[("cdt", i2)], writes=["wT"])
                kb.op("dve", lambda: nc.vector.tensor_tensor(out=wT[:, :], in0=wT[:, :], in1=cd_[:, 1, :], op=ALU.mult), reads=["wT", ("cdt", i2)], writes=["wT"])
                kb.op("act", lambda: nc.scalar.activation(out=eg[:, 0:1], in_=cd_[:, 0, last:last + 1], func=AF.Exp), reads=[("cdt", i2)], writes=["eg"])
                kb.op("dve", lambda: nc.vector.tensor_scalar(out=dg[:, :], in0=cx.identf[0:64, 0:64], scalar1=eg[:, 0:1], scalar2=None, op0=ALU.mult),
                      reads=["eg", "identf"], writes=["dg"])
                ps, pt = pp.next()
                kb.op("pe", lambda: nc.tensor.matmul(ps[:, 0:64], lhsT=cd_[:, 0, :], rhs=cx.identf[0:64, 0:64], start=True, stop=True),
                      reads=[("cdt", i2), "identf"], writes=[pt], inc=False)
                kb.op("pe", lambda: nc.tensor.matmul(ps[:, 64:128], lhsT=cd_[:, 1, :], rhs=cx.identf[0:64, 0:64], start=True, stop=True),
                      reads=[("cdt", i2)], writes=[pt], inc=False)
                kb.op("pe", lambda: nc.tensor.matmul(ps[:, 128:192], lhsT=wT[:, :], rhs=cx.identf[0:64, 0:64], start=True, stop=True),
                      reads=["wT"], writes=[pt], inc=False)
                kb.op("pe", lambda: nc.tensor.matmul(ps[:, 192:256], lhsT=cx.onesf[0:64, :], rhs=dg[:, :], start=True, stop=True),
                      reads=["dg", "onesf"], writes=[pt], inc=True)
                TM = tm[i2]
                kb.op("act", lambda: nc.scalar.copy(out=TM[:, :, :].rearrange("p a b -> p (a b)"), in_=ps[:, 0:256]), reads=[pt], writes=[("tm", i2)])
                st2 = {}
                def stage0(g_):
                    j4 = g_ % 4
                    r0 = d * 64 + g_ * 8
                    kb.dma("sp", cb[j4][:, :, :], scr["cumT"][r0:r0 + 8, c0:c0 + L].partition_broadcast(128), reads=[("cumd", it)], writes=[("cb", j4)])
                def stage1(g_):
                    j2 = (ci * SSD_G + g_) % 2
                    j4 = g_ % 4
                    ps_cb, pt_cb = pp.next()
                    kb.op("pe", lambda: nc.tensor.matmul(ps_cb[:, :L], lhsT=BT[i2][:, g_, :], rhs=CT[i2][:, g_, :], start=True, stop=True),
                          reads=[("BT", i2), ("CT", i2)], writes=[pt_cb], inc=True)
                    kb.op("dve", lambda: nc.vector.tensor_tensor(out=CBm[j2][:, :], in0=ps_cb[:, :L], in1=cx.mask[d][:, :], op=ALU.mult),
                          reads=[pt_cb, ("mask", d)], writes=[("CBm", j2)])
                    cum_b = TM[:, 0, g_ * 8:(g_ + 1) * 8].unsqueeze(2).to_broadcast([128, 8, L])
                    dt_b = TM[:, 1, g_ * 8:(g_ + 1) * 8].unsqueeze(2).to_broadcast([128, 8, L])
                    kb.op("dve", lambda: nc.vector.tensor_tensor(out=Dc[j2][:, :, :], in0=cb[j4][:, :, :], in1=cum_b, op=ALU.subtract),
                          reads=[("cb", j4), ("tm", i2)], writes=[("Dc", j2)])
                    kb.op("dve", lambda: nc.vector.tensor_scalar(out=Dc[j2][:, :, :], in0=Dc[j2][:, :, :], scalar1=0.0, scalar2=None, op0=ALU.min),
                          reads=[("Dc", j2)], writes=[("Dc", j2)])
                    kb.op("act", lambda: nc.scalar.activation(out=Dc[j2][:, :, :], in_=Dc[j2][:, :, :], func=AF.Exp), reads=[("Dc", j2)], writes=[("Dc", j2)])
                    kb.op("act", lambda: nc.scalar.activation(out=ecb[j2][:, :, :], in_=cb[j4][:, :, :], func=AF.Exp), reads=[("cb", j4)], writes=[("ecb", j2)])
                def stage2(g_):
                    j2 = g_ % 2
                    cum_b = TM[:, 0, g_ * 8:(g_ + 1) * 8].unsqueeze(2).to_broadcast([128, 8, L])
                    dt_b = TM[:, 1, g_ * 8:(g_ + 1) * 8].unsqueeze(2).to_broadcast([128, 8, L])
                    kb.op("dve", lambda: nc.vector.tensor_tensor(out=Dc[j2][:, :, :], in0=Dc[j2][:, :, :],
                                                                 in1=CBm[j2][:, :].unsqueeze(1).to_broadcast([128, 8, L]), op=ALU.mult),
                          reads=[("Dc", j2), ("CBm", j2)], writes=[("Dc", j2)])
                    kb.op("dve", lambda: nc.vector.tensor_tensor(out=Wt[j2][:, :, :], in0=Dc[j2][:, :, :], in1=dt_b, op=ALU.mult),
                          reads=[("Dc", j2), ("tm", i2)], writes=[("Wt", j2)])
                    kb.op("dve", lambda: nc.vector.tensor_tensor(out=CsT[j2][:, :, :], in0=ecb[j2][:, :, :],
                                                                 in1=CT[i2][:, g_, :].unsqueeze(1).to_broadcast([128, 8, L]), op=ALU.mult),
                          reads=[("ecb", j2), ("CT", i2)], writes=[("CsT", j2)])
                    ps_y, pt_y = pp.next()
                    for h_ in range(8):
                        H = g_ * 8 + h_
                        kb.op("pe", lambda: nc.tensor.matmul(ps_y[:, h_ * 64:(h_ + 1) * 64], lhsT=Wt[j2][:, h_, :], rhs=X[:, H * 64:(H + 1) * 64],
                                                             start=True, stop=False),
                              reads=[("Wt", j2), ("xtm", i2)], writes=[pt_y], inc=False)
                        kb.op("pe", lambda: nc.tensor.matmul(ps_y[:, h_ * 64:(h_ + 1) * 64], lhsT=CsT[j2][:, h_, :], rhs=Sb[:, g_, h_ * 64:(h_ + 1) * 64],
                                                             start=False, stop=True),
                              reads=[("CsT", j2), ("Sb", g_)], writes=[pt_y], inc=(h_ == 7))
                    w_b = TM[:, 2, g_ * 8:(g_ + 1) * 8].unsqueeze(2).to_broadcast([128, 8, 64])
                    e_b = TM[:, 3, g_ * 8:(g_ + 1) * 8].unsqueeze(2).to_broadcast([128, 8, 64])
                    kb.op("dve", lambda: nc.vector.tensor_tensor(out=xw[j2][:, :].rearrange("p (h q) -> p h q", h=8),
                                                                 in0=X[:, g_ * 512:(g_ + 1) * 512].rearrange("p (h q) -> p h q", h=8), in1=w_b, op=ALU.mult),
                          reads=[("xtm", i2), ("tm", i2)], writes=[("xw", j2)])
                    ps_s, pt_s = pp.next()
                    kb.op("pe", lambda: nc.tensor.matmul(ps_s[:, :512], lhsT=Bm[:, g_ * 128:(g_ + 1) * 128], rhs=xw[j2][:, :], start=True, stop=True),
                          reads=[("btm", i2), ("xw", j2)], writes=[pt_s], inc=True)
                    st2[g_] = (ps_y, pt_y, ps_s, pt_s)
                def stage3(g_):
                    ps_y, pt_y, ps_s, pt_s = st2[g_]
                    e_b = TM[:, 3, g_ * 8:(g_ + 1) * 8].unsqueeze(2).to_broadcast([128, 8, 64])
                    kb.op("dve", lambda: nc.vector.tensor_tensor(out=Sf[:, g_, :].rearrange("p (h q) -> p h q", h=8),
                                                                 in0=Sf[:, g_, :].rearrange("p (h q) -> p h q", h=8), in1=e_b, op=ALU.mult),
                          reads=[("Sf", g_), ("tm", i2)], writes=[("Sf", g_)])
                    kb.op("dve", lambda: nc.vector.tensor_tensor(out=Sf[:, g_, :], in0=Sf[:, g_, :], in1=ps_s[:, :512], op=ALU.add),
                          reads=[("Sf", g_), pt_s], writes=[("Sf", g_)])
                    kb.op("act", lambda: nc.scalar.copy(out=Sb[:, g_, :], in_=Sf[:, g_, :]), reads=[("Sf", g_)], writes=[("Sb", g_)])
                    gs = slice(g_ * 512, (g_ + 1) * 512)
                    if d == 0:
                        kb.op("act", lambda: nc.scalar.copy(out=yg[:, gs], in_=ps_y[:, :512]), reads=[pt_y], writes=[("yg", g_)])
                        kb.dma("pool", scr["yf"][c0:c0 + L, gs], yg[:, gs], reads=[("yg", g_)], writes=[("yfd", c, g_)])
                    else:
                        kb.op("dve", lambda: nc.vector.tensor_tensor(out=yg[:, gs], in0=ps_y[:, :512], in1=yf[:, gs], op=ALU.add),
                              reads=[pt_y, "yf"], writes=[("yg", g_)])

                for g0_ in range(4):
                    stage0(g0_)
                stage1(0)
                stage1(1)
                stage2(0)
                for g_ in range(SSD_G):
                    if g_ + 4 < SSD_G:
                        stage0(g_ + 4)
                    if g_ + 2 < SSD_G:
                        stage1(g_ + 2)
                    if g_ + 1 < SSD_G:
                        stage2(g_ + 1)
                    stage3(g_)
                if d == 1:
                    allg = [("yg", g_) for g_ in range(SSD_G)]
                    kb.op("pool", lambda: nc.gpsimd.tensor_tensor(out=yf[:, :], in0=X[:, :], in1=dsk[:, :], op=ALU.mult), reads=[("xtm", i2), "dsk"], writes=["yf"])
                    kb.op("dve", lambda: nc.vector.tensor_tensor(out=yg[:, :], in0=yg[:, :], in1=yf[:, :], op=ALU.add), reads=allg + ["yf"], writes=allg)
                    kb.op("dve", lambda: nc.vector.tensor_tensor(out=yg[:, :], in0=yg[:, :], in1=zsc[:, :], op=ALU.mult), reads=allg + ["zsc"], writes=allg)
                    kb.op("act", lambda: nc.scalar.activation(out=yf[:, :], in_=yg[:, :], func=AF.Square, accum_out=st[:, 0:1]), reads=allg, writes=["yf", "st"])
                    kb.op("dve", lambda: nc.vector.tensor_scalar(out=st[:, 1:2], in0=st[:, 0:1], scalar1=1.0 / SSD_INNER, scalar2=EPS, op0=ALU.mult, op1=ALU.add),
                          reads=["st"], writes=["st"])
                    emit_rsqrt_inplace(kb, st[:, 1:2], "st")
                    kb.op("dve", lambda: nc.vector.scalar_tensor_tensor(out=ynb[:, :], in0=yg[:, :], scalar=st[:, 1:2], in1=gN[:, :], op0=ALU.mult, op1=ALU.mult),
                          reads=allg + ["st", "gN"], writes=["ynb"])
                    for grp in range(4):
                        pb = cx.psb[cx.psbi[0] % 2]; pbtok = ("psb", cx.psbi[0] % 2); cx.psbi[0] += 1
                        for j in range(8):
                            blk = grp * 8 + j
                            kb.op("pe", lambda: nc.tensor.transpose(out=pb[:, j * 128:(j + 1) * 128], in_=ynb[:, blk * 128:(blk + 1) * 128], identity=cx.ident[:, :]),
                                  reads=["ynb", "ident"], writes=[pbtok], inc=(j == 7))
                        yo = ynT_sb[grp % 2]
                        kb.op("act", lambda: nc.scalar.copy(out=yo[:, :, :].rearrange("p a b -> p (a b)"), in_=pb[:, :]), reads=[pbtok], writes=[("ynT", grp % 2)])
                        kb.dma("pool", ynv[:, grp * 8:(grp + 1) * 8, c0:c0 + L], yo[:, :, :], reads=[("ynT", grp % 2)], writes=[("ynd", c, grp)])
    with kb.phase():
        tile_ctx(kb, cx)
        hT = cx.hT
        yT = sb(kb, "s3yT", [128, 32, NT], BF16)
        for it in range(nt):
            t0 = it * NT
            for c in range(NCH):
                kb.dma("sp", hT[:, c, :NT], src_v[:, c, src_pad + t0:src_pad + t0 + NT], writes=["hT"])
            kb.dma("sp", yT[:, :, :], ynv[:, :, t0:t0 + NT], writes=[("yT", k_) for k_ in range(32)])
            emit_outproj_residual(kb, pp, yT, lambda k_: ("yT", k_), 32, w["wout"], cx.wO, cx.wOi, hT, "hT", 0, NT,
                                  dst_v, dst_pad + t0, cx.ho_bufs, out_toks, "ssd")


def prep_ssd_weights(w_in, conv_w, conv_b, dt_bias, a_log, d_skip, norm_g, w_out, g):
    f = lambda a: np.ascontiguousarray(a, dtype=np.float32)
    def fm_blocks(wm):
        n = wm.shape[1] // 128
        return f(wm.reshape(-1, 128, n, 128).transpose(2, 1, 0, 3))
    wz = w_in[:, :SSD_INNER].reshape(NCH, 128, 8, 512).transpose(2, 1, 0, 3)
    wx = fm_blocks(w_in[:, SSD_INNER:SSD_INNER + 6144])
    wdt = fm_blocks(w_in[:, SSD_INNER + 6144:])
    cw = np.concatenate([conv_w, conv_b[None]], 0).reshape(5, 48, 128).transpose(2, 1, 0)
    return dict(wz=f(wz), wx=wx, wdt=wdt, cw=f(cw), dtb=f(dt_bias.reshape(128, 1)), alog=f(a_log.reshape(128, 1)),
                dsk=f(np.repeat(d_skip, 64).reshape(1, SSD_INNER)), gn=f(norm_g.reshape(1, SSD_INNER)),
                wout=fm_blocks(w_out), g=f(g.reshape(NCH, 128).T))


SSD_W_SHAPES = dict(wz=[8, 128, NCH, 512], wx=[48, 128, NCH, 128], wdt=[1, 128, NCH, 128], cw=[128, 48, 5], dtb=[128, 1], alog=[128, 1],
                    dsk=[1, SSD_INNER], gn=[1, SSD_INNER], wout=[NCH, 128, 32, 128], g=[128, NCH])


def ssd_scratch(nc, T, pfx=""):
    return dict(zs=nc.dram_tensor(pfx + "zs", [T, SSD_INNER], BF16).ap(), xcT=nc.dram_tensor(pfx + "xcT", [6144, T], BF16).ap(),
                dtT=nc.dram_tensor(pfx + "dtT", [128, T], F32).ap(), cumT=nc.dram_tensor(pfx + "cumT", [128, T], F32).ap(),
                xtm=nc.dram_tensor(pfx + "xtm", [T, SSD_INNER], BF16).ap(), btm=nc.dram_tensor(pfx + "btm", [T, 1024], BF16).ap(),
                yf=nc.dram_tensor(pfx + "yf", [T, SSD_INNER], F32).ap(), ynT=nc.dram_tensor(pfx + "ynT", [SSD_INNER, T], BF16).ap())


def build_ssd_test(T, NT=512):
    nc = bass.Bass("TRN2", target_bir_lowering=False)
    PAD = 2
    hin = nc.dram_tensor("hin", [D_MODEL, T + 2 * PAD], F32, kind="ExternalInput").ap()
    w = {k: nc.dram_tensor(k, shp, F32, kind="ExternalInput").ap() for k, shp in SSD_W_SHAPES.items()}
    hout = nc.dram_tensor("hout", [D_MODEL, T], F32, kind="ExternalOutput").ap()
    scr = ssd_scratch(nc, T)
    with contextlib.ExitStack() as stack:
        kb = KB(nc, stack)
        pp = PsumPool(kb, nbanks=6)
        cx = make_ctx(kb, NT)
        outs = []
        emit_ssd(kb, pp, cx, T, hin.rearrange("(c p) t -> p c t", p=128), PAD, hout.rearrange("(c p) t -> p c t", p=128), 0, w, scr, outs, NT)
        kb.finish(outs)
    return nc


ML_H = 8
ML_DK = 128
ML_DV = 256


def emit_mlstm(kb, pp, cx, T, src_v, src_pad, dst_v, dst_pad, w, scr, out_toks, NT=512):
    nc = kb.nc
    nt = T // NT
    nchunk = T // L
    qv = scr["qT"].rearrange("(h p) t -> p h t", p=128)
    kv = scr["kT"].rearrange("(h p) t -> p h t", p=128)
    hhv = scr["hhT"].rearrange("(b p) t -> p b t", p=128)
    kscale = float(ML_DK) ** -0.5
    with kb.phase():
        tile_ctx(kb, cx)
        hT, uT, g_sb = cx.hT, cx.uT, cx.g_sb
        wB = [sb(kb, f"m1wB{i}", [128, NCH, 512], BF16) for i in range(2)]
        wg = sb(kb, "m1wg", [128, NCH, 32], BF16)
        gbias = sb(kb, "m1gb", [8, 4], F32)
        ob = [sb(kb, f"m1ob{i}", [128, 512], BF16) for i in range(2)]
        gt = [sb(kb, f"m1gt{i}", [8, NT], F32) for i in range(3)]
        ones8 = sb(kb, "m1ones8", [8, L], F32)
        kb.op("dve", lambda: nc.vector.memset(ones8[:, :], 1.0), writes=["ones8"])
        kb.dma("sp", g_sb[:, :], w["g"][:, :], writes=["g"])
        kb.dma("pool", wg[:, :, :], w["wg"], writes=["wg"])
        kb.dma("sp", gbias[:, :], w["gbias"], writes=["gbias"])
        wbi = 0
        oi = 0
        for it in range(nt):
            t0 = it * NT
            for c in range(NCH):
                kb.dma("sp", hT[:, c, :NT], src_v[:, c, src_pad + t0:src_pad + t0 + NT], writes=["hT"])
            emit_rmsnorm_fm(kb, pp, hT, "hT", g_sb, uT, "uT", cx.ones_bf, NT, cx.sq_bufs, (cx.rstd_t, "rstd"))
            for blk in range(16):
                wb = cx.wA[cx.wAi[0] % 2]; wtok = ("wA", cx.wAi[0] % 2); cx.wAi[0] += 1
                kb.dma("pool", wb[:, :, 0:128], w["wqk"][blk], writes=[wtok])
                (ps, pt), = emit_proj_block(kb, pp, wb, wtok, 0, uT, "uT", [(0, NT)])
                i2 = oi % 2; oi += 1
                kb.op("act", lambda: nc.scalar.activation(out=ob[i2][:, :NT], in_=ps[:, :NT], func=AF.Copy, scale=(1.0 if blk < 8 else kscale)),
                      reads=[pt], writes=[("ob", i2)])
                dstv = qv if blk < 8 else kv
                kb.dma("sp", dstv[:, blk % 8, t0:t0 + NT], ob[i2][:, :NT], reads=[("ob", i2)], writes=[("qkd", it, blk)])
            for cg in range(10):
                wb = wB[wbi % 2]; wtok = ("m1wB", wbi % 2); wbi += 1
                kb.dma("pool", wb[:, :, :], w["wtm"][cg], writes=[wtok])
                for sub in range(NT // 128):
                    ps, pt = pp.next()
                    for kc in range(NCH):
                        kb.op("pe", lambda: nc.tensor.matmul(ps[:, :512], lhsT=uT[:, kc, sub * 128:(sub + 1) * 128], rhs=wb[:, kc, :],
                                                             start=(kc == 0), stop=(kc == NCH - 1)),
                              reads=[wtok, "uT"], writes=[pt], inc=(kc == NCH - 1))
                    i2 = oi % 2; oi += 1
                    r0 = t0 + sub * 128
                    if cg < 2:
                        kb.op("act", lambda: nc.scalar.activation(out=ob[i2][:, :], in_=ps[:, :512], func=AF.Copy, scale=kscale), reads=[pt], writes=[("ob", i2)])
                        kb.dma("sp", scr["ktm"][r0:r0 + 128, cg * 512:(cg + 1) * 512], ob[i2][:, :], reads=[("ob", i2)], writes=[("ktmd", r0 // 128, cg)])
                    elif cg < 6:
                        kb.op("act", lambda: nc.scalar.copy(out=ob[i2][:, :], in_=ps[:, :512]), reads=[pt], writes=[("ob", i2)])
                        kb.dma("sp", scr["vtm"][r0:r0 + 128, (cg - 2) * 512:(cg - 1) * 512], ob[i2][:, :], reads=[("ob", i2)], writes=[("vtmd", r0 // 128, cg - 2)])
                    else:
                        kb.op("act", lambda: nc.scalar.activation(out=ob[i2][:, :], in_=ps[:, :512], func=AF.Sigmoid), reads=[pt], writes=[("ob", i2)])
                        kb.dma("sp", scr["so"][r0:r0 + 128, (cg - 6) * 512:(cg - 5) * 512], ob[i2][:, :], reads=[("ob", i2)], writes=[("sod", r0 // 128, cg - 6)])
            for d in range(2):
                pss = []
                for j in (2 * d, 2 * d + 1):
                    ps, pt = pp.next()
                    for kc in range(NCH):
                        kb.op("pe", lambda: nc.tensor.matmul(ps[0:8, :NT], lhsT=wg[:, kc, 8 * j:8 * j + 8], rhs=uT[:, kc, :NT],
                                                             start=(kc == 0), stop=(kc == NCH - 1)),
                              reads=["wg", "uT"], writes=[pt], inc=(kc == NCH - 1))
                    pss.append((ps, pt))
                (psi, pti), (psf, ptf) = pss
                g0, g1, g2 = gt
                kb.op("act", lambda: nc.scalar.activation(out=g0[:, :], in_=psi[0:8, :NT], func=AF.Exp, bias=gbias[:, 2 * d:2 * d + 1], scale=1.0),
                      reads=[pti, "gbias"], writes=["g0"])
                kb.dma("sp", scr["eiT"][d * 8:(d + 1) * 8, t0:t0 + NT], g0[:, :], reads=["g0"], writes=[("eid", it, d)])
                kb.op("act", lambda: nc.scalar.activation(out=g1[:, :], in_=psf[0:8, :NT], func=AF.Identity, bias=gbias[:, 2 * d + 1:2 * d + 2], scale=1.0),
                      reads=[ptf, "gbias"], writes=["g1"])
                emit_softplus_small(kb, g2[:, :], g1[:, :], g1[:, :], "g1", neg_in=True) if False else None
                kb.op("act", lambda: nc.scalar.activation(out=g2[:, :], in_=g1[:, :], func=AF.Abs), reads=["g1"], writes=["g2"])
                kb.op("act", lambda: nc.scalar.activation(out=g2[:, :], in_=g2[:, :], func=AF.Exp, scale=-1.0), reads=["g2"], writes=["g2"])
                kb.op("act", lambda: nc.scalar.activation(out=g2[:, :], in_=g2[:, :], func=AF.Ln, bias=1.0, scale=1.0), reads=["g2"], writes=["g2"])
                kb.op("dve", lambda: nc.vector.tensor_scalar(out=g1[:, :], in0=g1[:, :], scalar1=-1.0, scalar2=0.0, op0=ALU.mult, op1=ALU.max), reads=["g1"], writes=["g1"])
                kb.op("dve", lambda: nc.vector.tensor_tensor(out=g1[:, :], in0=g1[:, :], in1=g2[:, :], op=ALU.add), reads=["g1", "g2"], writes=["g1"])
                kb.op("dve", lambda: nc.vector.tensor_scalar(out=g1[:, :], in0=g1[:, :], scalar1=-1.0, scalar2=None, op0=ALU.mult), reads=["g1"], writes=["g1"])
                for ch in range(NT // L):
                    sl = slice(ch * L, (ch + 1) * L)
                    rsl = slice((ch + 1) * L - 1, ch * L - 1 if ch > 0 else None, -1)
                    use = sl if d == 0 else rsl
                    kb.op("dve", lambda: nc.vector.tensor_tensor_scan(out=g2[:, use], data0=ones8[:, 0:L], data1=g1[:, use], initial=0.0,
                                                                      op0=ALU.mult, op1=ALU.add), reads=["g1", "g2", "ones8"], writes=["g2"])
                kb.dma("sp", scr["cumT"][d * 8:(d + 1) * 8, t0:t0 + NT], g2[:, :], reads=["g2"], writes=[("cumd", it, d)])
    with kb.phase():
        make_masks(kb, cx)
        qT = [sb(kb, f"m2qT{i}", [128, ML_H, L], BF16) for i in range(2)]
        kT = [sb(kb, f"m2kT{i}", [128, ML_H, L], BF16) for i in range(2)]
        ktm = [sb(kb, f"m2ktm{i}", [128, ML_H, ML_DK], BF16) for i in range(2)]
        Vx = [sb(kb, f"m2Vx{i}", [128, ML_H, ML_DV + 1], BF16) for i in range(2)]
        cdt = [sb(kb, f"m2cdt{i}", [8, 2, L], F32) for i in range(2)]
        wT = sb(kb, "m2wT", [8, L], F32)
        eg = sb(kb, "m2eg", [8, 2], F32)
        dg = sb(kb, "m2dg", [8, 8], F32)
        tm = [sb(kb, f"m2tm{i}", [128, 4, 8], F32) for i in range(2)]
        cb = [sb(kb, f"m2cb{i}", [128, ML_H, L], F32) for i in range(2)]
        Dc = [sb(kb, f"m2Dc{i}", [128, ML_H, L], F32) for i in range(2)]
        ecb = [sb(kb, f"m2ecb{i}", [128, ML_H, L], F32) for i in range(2)]
        Wt = [sb(kb, f"m2Wt{i}", [128, ML_H, L], BF16) for i in range(2)]
        qsT = [sb(kb, f"m2qsT{i}", [128, ML_H, L], BF16) for i in range(2)]
        kw = [sb(kb, f"m2kw{i}", [128, ML_H, ML_DK], BF16) for i in range(2)]
        Cf = sb(kb, "m2Cf", [128, ML_H, ML_DV + 1], F32)
        Cb = sb(kb, "m2Cb", [128, ML_H, ML_DV + 1], BF16)
        num = sb(kb, "m2num", [128, ML_H, ML_DV], F32)
        den = sb(kb, "m2den", [128, 2, ML_H], F32)
        hfl = sb(kb, "m2hf", [128, ML_H, ML_DV], F32)
        sq = sb(kb, "m2sq", [128, ML_H, ML_DV], F32)
        so = sb(kb, "m2so", [128, D_MODEL], BF16)
        gN = sb(kb, "m2gN", [128, D_MODEL], F32)
        hhb = sb(kb, "m2hhb", [128, D_MODEL], BF16)
        hhT_sb = [sb(kb, f"m2hhT{i}", [128, 8, L], BF16) for i in range(2)]
        ss = sb(kb, "m2ss", [128, 2, ML_H], F32)
        kb.dma("sp", gN[:, :], w["hn"].partition_broadcast(128), writes=["gN"])
        for i in range(2):
            kb.op("pool", lambda: nc.gpsimd.memset(Vx[i][:, :, ML_DV:ML_DV + 1], 1.0), writes=[("Vx1", i)])
        ci = 0
        for d in range(2):
            kb.op("dve", lambda: nc.vector.memset(Cf[:, :, :], 0.0), writes=[("Cf", h_) for h_ in range(ML_H)])
            kb.op("pool", lambda: nc.gpsimd.memset(Cb[:, :, :], 0.0), writes=[("Cb", h_) for h_ in range(ML_H)])
            order = range(nchunk) if d == 0 else reversed(range(nchunk))
            last = L - 1 if d == 0 else 0
            for c in order:
                c0 = c * L
                i2 = ci % 2; ci += 1
                it = c0 // NT
                kb.dma("sp", qT[i2][:, :, :], qv[:, :, c0:c0 + L], reads=[("qkd", it, b_) for b_ in range(8)], writes=[("qT", i2)])
                kb.dma("sp", kT[i2][:, :, :], kv[:, :, c0:c0 + L], reads=[("qkd", it, b_) for b_ in range(8, 16)], writes=[("kT", i2)])
                kb.dma("sp", ktm[i2][:, :, :], scr["ktm"][c0:c0 + L, :].rearrange("t (h k) -> t h k", h=ML_H), reads=[("ktmd", c, 0), ("ktmd", c, 1)], writes=[("ktm", i2)])
                kb.dma("sp", Vx[i2][:, :, 0:ML_DV], scr["vtm"][c0:c0 + L, :].rearrange("t (h k) -> t h k", h=ML_H),
                       reads=[("vtmd", c, j_) for j_ in range(4)], writes=[("Vx", i2)])
                kb.dma("sp", cdt[i2][:, 0, :], scr["cumT"][d * 8:(d + 1) * 8, c0:c0 + L], reads=[("cumd", it, d)], writes=[("cdt", i2)])
                kb.dma("sp", cdt[i2][:, 1, :], scr["eiT"][d * 8:(d + 1) * 8, c0:c0 + L], reads=[("eid", it, d)], writes=[("cdt", i2)])
                kb.dma("sp", cb[i2][:, :, :], scr["cumT"][d * 8:(d + 1) * 8, c0:c0 + L].partition_broadcast(128), reads=[("cumd", it, d)], writes=[("cb", i2)])
                if d == 1:
                    kb.dma("sp", hfl[:, :, :], scr["hf"][c0:c0 + L, :].rearrange("t (h k) -> t h k", h=ML_H), reads=[("hfd", c)], writes=["hfl"])
                    kb.dma("sp", so[:, :], scr["so"][c0:c0 + L, :], reads=[("sod", c, j_) for j_ in range(4)], writes=["so"])
                cd_ = cdt[i2]
                kb.op("act", lambda: nc.scalar.activation(out=wT[:, :], in_=cd_[:, 0, :], func=AF.Exp, bias=cd_[:, 0, last:last + 1], scale=-1.0),
                      reads=[("cdt", i2)], writes=["wT"])
                kb.op("dve", lambda: nc.vector.tensor_tensor(out=wT[:, :], in0=wT[:, :], in1=cd_[:, 1, :], op=ALU.mult), reads=["wT", ("cdt", i2)], writes=["wT"])
                kb.op("act", lambda: nc.scalar.activation(out=eg[:, 0:1], in_=cd_[:, 0, last:last + 1], func=AF.Exp), reads=[("cdt", i2)], writes=["eg"])
                kb.op("dve", lambda: nc.vector.tensor_scalar(out=dg[:, :], in0=cx.identf[0:8, 0:8], scalar1=eg[:, 0:1], scalar2=None, op0=ALU.mult),
                      reads=["eg", "identf"], writes=["dg"])
                ps, pt = pp.next()
                kb.op("pe", lambda: nc.tensor.matmul(ps[:, 0:8], lhsT=cd_[:, 0, :], rhs=cx.identf[0:8, 0:8], start=True, stop=True),
                      reads=[("cdt", i2), "identf"], writes=[pt], inc=False)
                kb.op("pe", lambda: nc.tensor.matmul(ps[:, 8:16], lhsT=cd_[:, 1, :], rhs=cx.identf[0:8, 0:8], start=True, stop=True),
                      reads=[("cdt", i2)], writes=[pt], inc=False)
                kb.op("pe", lambda: nc.tensor.matmul(ps[:, 16:24], lhsT=wT[:, :], rhs=cx.identf[0:8, 0:8], start=True, stop=True),
                      reads=["wT"], writes=[pt], inc=False)
                kb.op("pe", lambda: nc.tensor.matmul(ps[:, 24:32], lhsT=cx.onesf[0:8, :], rhs=dg[:, :], start=True, stop=True),
                      reads=["dg", "onesf"], writes=[pt], inc=True)
                TM = tm[i2]
                kb.op("act", lambda: nc.scalar.copy(out=TM[:, :, :].rearrange("p a b -> p (a b)"), in_=ps[:, 0:32]), reads=[pt], writes=[("tm", i2)])
                qk = []
                for half in range(2):
                    psq, ptq = pp.next()
                    for hh_ in range(4):
                        h_ = half * 4 + hh_
                        kb.op("pe", lambda: nc.tensor.matmul(psq[:, hh_ * L:(hh_ + 1) * L], lhsT=kT[i2][:, h_, :], rhs=qT[i2][:, h_, :], start=True, stop=True),
                              reads=[("kT", i2), ("qT", i2)], writes=[ptq], inc=(hh_ == 3))
                    qk.append((psq, ptq))
                cum_b = TM[:, 0, :].unsqueeze(2).to_broadcast([128, ML_H, L])
                ei_b = TM[:, 1, :].unsqueeze(2).to_broadcast([128, ML_H, L])
                D_ = Dc[i2]
                kb.op("dve", lambda: nc.vector.tensor_tensor(out=D_[:, :, :], in0=cb[i2][:, :, :], in1=cum_b, op=ALU.subtract),
                      reads=[("cb", i2), ("tm", i2)], writes=[("Dc", i2)])
                kb.op("dve", lambda: nc.vector.tensor_scalar(out=D_[:, :, :], in0=D_[:, :, :], scalar1=0.0, scalar2=None, op0=ALU.min), reads=[("Dc", i2)], writes=[("Dc", i2)])
                kb.op("act", lambda: nc.scalar.activation(out=D_[:, :, :], in_=D_[:, :, :], func=AF.Exp), reads=[("Dc", i2)], writes=[("Dc", i2)])
                kb.op("dve", lambda: nc.vector.tensor_tensor(out=D_[:, :, :], in0=D_[:, :, :], in1=cx.mask[d][:, :].unsqueeze(1).to_broadcast([128, ML_H, L]), op=ALU.mult),
                      reads=[("Dc", i2), ("mask", d)], writes=[("Dc", i2)])
                kb.op("dve", lambda: nc.vector.tensor_tensor(out=D_[:, :, :], in0=D_[:, :, :], in1=ei_b, op=ALU.mult), reads=[("Dc", i2), ("tm", i2)], writes=[("Dc", i2)])
                for half in range(2):
                    psq, ptq = qk[half]
                    kb.op("dve", lambda: nc.vector.tensor_tensor(out=Wt[i2][:, half * 4:(half + 1) * 4, :], in0=D_[:, half * 4:(half + 1) * 4, :],
                                                                 in1=psq[:, :512].rearrange("p (h l) -> p h l", h=4), op=ALU.mult),
                          reads=[("Dc", i2), ptq], writes=[("Wt", i2)])
                kb.op("act", lambda: nc.scalar.activation(out=ecb[i2][:, :, :], in_=cb[i2][:, :, :], func=AF.Exp), reads=[("cb", i2)], writes=[("ecb", i2)])
                kb.op("dve", lambda: nc.vector.tensor_tensor(out=qsT[i2][:, :, :], in0=ecb[i2][:, :, :], in1=qT[i2][:, :, :], op=ALU.mult),
                      reads=[("ecb", i2), ("qT", i2)], writes=[("qsT", i2)])
                w_b = TM[:, 2, :].unsqueeze(2).to_broadcast([128, ML_H, ML_DK])
                kb.op("dve", lambda: nc.vector.tensor_tensor(out=kw[i2][:, :, :], in0=ktm[i2][:, :, :], in1=w_b, op=ALU.mult),
                      reads=[("ktm", i2), ("tm", i2)], writes=[("kw", i2)])
                for h_ in range(ML_H):
                    ps_o, pt_o = pp.next()
                    kb.op("pe", lambda: nc.tensor.matmul(ps_o[:, :ML_DV + 1], lhsT=Wt[i2][:, h_, :], rhs=Vx[i2][:, h_, :], start=True, stop=False),
                          reads=[("Wt", i2), ("Vx", i2), ("Vx1", i2)], writes=[pt_o], inc=False)
                    kb.op("pe", lambda: nc.tensor.matmul(ps_o[:, :ML_DV + 1], lhsT=qsT[i2][:, h_, :], rhs=Cb[:, h_, :], start=False, stop=True),
                          reads=[("qsT", i2), ("Cb", h_)], writes=[pt_o], inc=True)
                    kb.op("act", lambda: nc.scalar.copy(out=num[:, h_, :], in_=ps_o[:, 0:ML_DV]), reads=[pt_o], writes=[("num", h_)])
                    kb.op("act", lambda: nc.scalar.activation(out=den[:, 0, h_:h_ + 1], in_=ps_o[:, ML_DV:ML_DV + 1], func=AF.Abs), reads=[pt_o], writes=[("den", h_)])
                    ps_s, pt_s = pp.next()
                    kb.op("pe", lambda: nc.tensor.matmul(ps_s[:, :ML_DV + 1], lhsT=kw[i2][:, h_, :], rhs=Vx[i2][:, h_, :], start=True, stop=True),
                          reads=[("kw", i2), ("Vx", i2), ("Vx1", i2)], writes=[pt_s], inc=True)
                    kb.op("dve", lambda: nc.vector.scalar_tensor_tensor(out=Cf[:, h_, :], in0=Cf[:, h_, :], scalar=TM[:, 3, h_:h_ + 1], in1=ps_s[:, :ML_DV + 1],
                                                                        op0=ALU.mult, op1=ALU.add),
                          reads=[("Cf", h_), ("tm", i2), pt_s], writes=[("Cf", h_)])
                    kb.op("act", lambda: nc.scalar.copy(out=Cb[:, h_, :], in_=Cf[:, h_, :]), reads=[("Cf", h_)], writes=[("Cb", h_)])
                allnum = [("num", h_) for h_ in range(ML_H)]
                allden = [("den", h_) for h_ in range(ML_H)]
                kb.op("dve", lambda: nc.vector.tensor_scalar(out=den[:, 1, :], in0=den[:, 0, :], scalar1=1.0, scalar2=None, op0=ALU.max), reads=allden, writes=["den1"])
                kb.op("dve", lambda: nc.vector.reciprocal(out=den[:, 1, :], in_=den[:, 1, :]), reads=["den1"], writes=["den1"])
                r_b = den[:, 1, :].unsqueeze(2).to_broadcast([128, ML_H, ML_DV])
                kb.op("dve", lambda: nc.vector.tensor_tensor(out=num[:, :, :], in0=num[:, :, :], in1=r_b, op=ALU.mult), reads=allnum + ["den1"], writes=allnum)
                if d == 0:
                    kb.dma("pool", scr["hf"][c0:c0 + L, :].rearrange("t (h k) -> t h k", h=ML_H), num[:, :, :], reads=allnum, writes=[("hfd", c)])
                else:
                    kb.op("dve", lambda: nc.vector.tensor_tensor(out=num[:, :, :], in0=num[:, :, :], in1=hfl[:, :, :], op=ALU.add), reads=allnum + ["hfl"], writes=allnum)
                    kb.op("pool", lambda: nc.gpsimd.tensor_tensor(out=sq[:, :, :], in0=num[:, :, :], in1=num[:, :, :], op=ALU.mult), reads=allnum, writes=["sq2"])
                    kb.op("dve", lambda: nc.vector.tensor_reduce(out=ss[:, 0, :], in_=sq[:, :, :], axis=AX.X, op=ALU.add), reads=["sq2"], writes=["ss"])
                    kb.op("dve", lambda: nc.vector.tensor_scalar(out=ss[:, 1, :], in0=ss[:, 0, :], scalar1=1.0 / ML_DV, scalar2=EPS, op0=ALU.mult, op1=ALU.add),
                          reads=["ss"], writes=["ss"])
                    emit_rsqrt_inplace(kb, ss[:, 1, :], "ss")
                    rs_b = ss[:, 1, :].unsqueeze(2).to_broadcast([128, ML_H, ML_DV])
                    kb.op("dve", lambda: nc.vector.tensor_tensor(out=num[:, :, :], in0=num[:, :, :], in1=rs_b, op=ALU.mult), reads=allnum + ["ss"], writes=allnum)
                    numf = num[:, :, :].rearrange("p h k -> p (h k)")
                    kb.op("pool", lambda: nc.gpsimd.tensor_tensor(out=numf, in0=numf, in1=gN[:, :], op=ALU.mult), reads=allnum + ["gN"], writes=allnum)
                    kb.op("dve", lambda: nc.vector.tensor_tensor(out=hhb[:, :], in0=numf, in1=so[:, :], op=ALU.mult), reads=allnum + ["so"], writes=["hhb"])
                    for grp in range(2):
                        pb = cx.psb[cx.psbi[0] % 2]; pbtok = ("psb", cx.psbi[0] % 2); cx.psbi[0] += 1
                        for j in range(8):
                            blk = grp * 8 + j
                            kb.op("pe", lambda: nc.tensor.transpose(out=pb[:, j * 128:(j + 1) * 128], in_=hhb[:, blk * 128:(blk + 1) * 128], identity=cx.ident[:, :]),
                                  reads=["hhb", "ident"], writes=[pbtok], inc=(j == 7))
                        yo = hhT_sb[grp % 2]
                        kb.op("act", lambda: nc.scalar.copy(out=yo[:, :, :].rearrange("p a b -> p (a b)"), in_=pb[:, :]), reads=[pbtok], writes=[("hhT", grp % 2)])
                        kb.dma("pool", hhv[:, grp * 8:(grp + 1) * 8, c0:c0 + L], yo[:, :, :], reads=[("hhT", grp % 2)], writes=[("hhd", c, grp)])
    with kb.phase():
        tile_ctx(kb, cx)
        hT = cx.hT
        yT = sb(kb, "m3yT", [128, NCH, NT], BF16)
        for it in range(nt):
            t0 = it * NT
            for c in range(NCH):
                kb.dma("sp", hT[:, c, :NT], src_v[:, c, src_pad + t0:src_pad + t0 + NT], writes=["hT"])
            kb.dma("sp", yT[:, :, :], hhv[:, :, t0:t0 + NT], writes=[("yT", k_) for k_ in range(NCH)])
            emit_outproj_residual(kb, pp, yT, lambda k_: ("yT", k_), NCH, w["wout"], cx.wO, cx.wOi, hT, "hT", 0, NT,
                                  dst_v, dst_pad + t0, cx.ho_bufs, out_toks, "ml")


def prep_mlstm_weights(w_in, gate_bias, head_norm, w_out, g):
    f = lambda a: np.ascontiguousarray(a, dtype=np.float32)
    def fm_blocks(wm):
        n = wm.shape[1] // 128
        return f(wm.reshape(-1, 128, n, 128).transpose(2, 1, 0, 3))
    def tm_groups(wm):
        n = wm.shape[1] // 512
        return wm.reshape(NCH, 128, n, 512).transpose(2, 1, 0, 3)
    wqk = fm_blocks(w_in[:, 0:2048])
    wtm = np.concatenate([tm_groups(w_in[:, 1024:2048]), tm_groups(w_in[:, 2048:4096]), tm_groups(w_in[:, 4096:6144])], 0)
    wg = w_in[:, 6144:6176].reshape(NCH, 128, 32).transpose(1, 0, 2)
    return dict(wqk=wqk, wtm=f(wtm), wg=f(wg), gbias=f(gate_bias.T), hn=f(head_norm.reshape(1, D_MODEL)),
                wout=fm_blocks(w_out), g=f(g.reshape(NCH, 128).T))


ML_W_SHAPES = dict(wqk=[16, 128, NCH, 128], wtm=[10, 128, NCH, 512], wg=[128, NCH, 32], gbias=[8, 4], hn=[1, D_MODEL],
                   wout=[NCH, 128, NCH, 128], g=[128, NCH])


def mlstm_scratch(nc, T, pfx=""):
    dtn = lambda n, s, dt: nc.dram_tensor(pfx + n, s, dt).ap()
    return dict(qT=dtn("qT", [1024, T], BF16), kT=dtn("kT", [1024, T], BF16), ktm=dtn("ktm", [T, 1024], BF16),
                vtm=dtn("vtm", [T, D_MODEL], BF16), so=dtn("so", [T, D_MODEL], BF16), cumT=dtn("mcumT", [16, T], F32),
                eiT=dtn("meiT", [16, T], F32), hf=dtn("mhf", [T, D_MODEL], F32), hhT=dtn("hhT", [D_MODEL, T], BF16))


def build_mlstm_test(T, NT=512):
    nc = bass.Bass("TRN2", target_bir_lowering=False)
    hin = nc.dram_tensor("hin", [D_MODEL, T], F32, kind="ExternalInput").ap()
    w = {k: nc.dram_tensor(k, shp, F32, kind="ExternalInput").ap() for k, shp in ML_W_SHAPES.items()}
    hout = nc.dram_tensor("hout", [D_MODEL, T], F32, kind="ExternalOutput").ap()
    scr = mlstm_scratch(nc, T)
    with contextlib.ExitStack() as stack:
        kb = KB(nc, stack)
        pp = PsumPool(kb, nbanks=6)
        cx = make_ctx(kb, NT)
        outs = []
        emit_mlstm(kb, pp, cx, T, hin.rearrange("(c p) t -> p c t", p=128), 0, hout.rearrange("(c p) t -> p c t", p=128), 0, w, scr, outs, NT)
        kb.finish(outs)
    return nc


def emit_ffn(kb, pp, cx, T, src_v, src_pad, dst_v, dst_pad, w, out_toks, gfin=None, NT=512):
    nc = kb.nc
    nt = T // NT
    W = NT + 2
    with kb.phase():
        tile_ctx(kb, cx)
        hT, uT, g_sb = cx.hT, cx.uT, cx.g_sb
        actT = sb(kb, "f_actT", [128, FFN_BLK, NT], BF16)
        cw_sb = sb(kb, "f_cw", [128, FFN_BLK, 4], F32)
        gate_sb = [sb(kb, f"f_gate{i}", [128, W], F32) for i in range(2)]
        acc_sb = [sb(kb, f"f_acc{i}", [128, NT], F32) for i in range(2)]
        sil_sb = [sb(kb, f"f_sil{i}", [128, NT], F32) for i in range(2)]
        kb.dma("sp", g_sb[:, :], w["g"][:, :], writes=["g"])
        kb.dma("sp", cw_sb[:, :, :], w["cw"], writes=["cw"])
        if gfin is not None:
            gf_sb = sb(kb, "f_gf", [128, NCH], F32)
            hn = sb(kb, "f_hn", [128, NCH, NT], F32)
            kb.dma("sp", gf_sb[:, :], gfin[:, :], writes=["gf"])
        for it in range(nt):
            t0 = it * NT
            for c in range(NCH):
                kb.dma("sp", hT[:, c, :W], src_v[:, c, src_pad + t0 - 1:src_pad + t0 - 1 + W], writes=["hT"])
            emit_rmsnorm_fm(kb, pp, hT, "hT", g_sb, uT, "uT", cx.ones_bf, W, cx.sq_bufs, (cx.rstd_t, "rstd"))
            for j in range(FFN_BLK):
                wb = cx.wA[cx.wAi[0] % 2]; wtok = ("wA", cx.wAi[0] % 2); cx.wAi[0] += 1
                kb.dma("pool", wb[:, :, :], w["wup"][j], writes=[wtok])
                (ps_g, tg), (ps_x, tx) = emit_proj_block(kb, pp, wb, wtok, 0, uT, "uT", [(0, NT), (NT, 2)])
                (ps_v, tv), = emit_proj_block(kb, pp, wb, wtok, 128, uT, "uT", [(1, NT)])
                gs = gate_sb[j % 2]; gtok = ("gate", j % 2)
                kb.op("act", lambda: nc.scalar.copy(out=gs[:, 0:NT], in_=ps_g[:, :NT]), reads=[tg], writes=[gtok])
                kb.op("act", lambda: nc.scalar.copy(out=gs[:, NT:NT + 2], in_=ps_x[:, :2]), reads=[tx], writes=[gtok])
                ac = acc_sb[j % 2]; atok = ("acc", j % 2)
                kb.op("dve", lambda: nc.vector.tensor_scalar(out=ac[:, :], in0=gs[:, 0:NT], scalar1=cw_sb[:, j, 0:1], scalar2=cw_sb[:, j, 3:4],
                                                             op0=ALU.mult, op1=ALU.add), reads=[gtok, "cw"], writes=[atok])
                for tap in (1, 2):
                    kb.op("dve", lambda: nc.vector.scalar_tensor_tensor(out=ac[:, :], in0=gs[:, tap:tap + NT], scalar=cw_sb[:, j, tap:tap + 1],
                                                                        in1=ac[:, :], op0=ALU.mult, op1=ALU.add), reads=[gtok, atok], writes=[atok])
                sl = sil_sb[j % 2]; stok = ("sil", j % 2)
                kb.op("act", lambda: nc.scalar.activation(out=sl[:, :], in_=ac[:, :], func=AF.Silu), reads=[atok], writes=[stok])
                kb.op("dve", lambda: nc.vector.tensor_tensor(out=actT[:, j, :], in0=sl[:, :], in1=ps_v[:, :NT], op=ALU.mult),
                      reads=[stok, tv], writes=[("actT", j)])
            if gfin is None:
                emit_outproj_residual(kb, pp, actT, lambda k_: ("actT", k_), FFN_BLK, w["wdn"], cx.wO, cx.wOi, hT, "hT", 1, NT,
                                      dst_v, dst_pad + t0, cx.ho_bufs, out_toks, "ffn")
            else:
                for mb in range(NCH):
                    wd = cx.wO[cx.wOi[0] % 2]; dtok = ("ffnw", cx.wOi[0] % 2); cx.wOi[0] += 1
                    kb.dma("pool", wd[:, :, :], w["wdn"][mb], writes=[dtok])
                    ps_o, to = pp.next()
                    for k_ in range(FFN_BLK):
                        kb.op("pe", lambda: nc.tensor.matmul(ps_o[:, :NT], lhsT=wd[:, k_, :], rhs=actT[:, k_, :], start=(k_ == 0), stop=(k_ == FFN_BLK - 1)),
                              reads=[dtok, ("actT", k_)], writes=[to], inc=(k_ == FFN_BLK - 1))
                    kb.op("dve", lambda: nc.vector.tensor_tensor(out=hn[:, mb, :], in0=ps_o[:, :NT], in1=hT[:, mb, 1:NT + 1], op=ALU.add),
                          reads=[to, "hT"], writes=["hn"])
                ps, ps_tok = pp.next()
                for c in range(NCH):
                    sq, sq_tok = cx.sq_bufs[c % len(cx.sq_bufs)]
                    kb.op("act", lambda: nc.scalar.activation(out=sq[:, :NT], in_=hn[:, c, :], func=AF.Square), reads=["hn"], writes=[sq_tok])
                    kb.op("pe", lambda: nc.tensor.matmul(ps[:, :NT], lhsT=cx.ones_bf[:, :], rhs=sq[:, :NT], start=(c == 0), stop=(c == NCH - 1)),
                          reads=[sq_tok, "ones"], writes=[ps_tok], inc=True)
                rstd_t = cx.rstd_t
                kb.op("dve", lambda: nc.vector.tensor_scalar(out=rstd_t[:, :NT], in0=ps[:, :NT], scalar1=1.0 / D_MODEL, scalar2=EPS, op0=ALU.mult, op1=ALU.add),
                      reads=[ps_tok], writes=["rstd"])
                emit_rsqrt_inplace(kb, rstd_t[:, :NT], "rstd")
                for c in range(NCH):
                    ho = cx.ho_bufs[c % 3]; htok = ("ho", c % 3)
                    kb.op("dve", lambda: nc.vector.scalar_tensor_tensor(out=ho[:, :], in0=hn[:, c, :], scalar=gf_sb[:, c:c + 1], in1=rstd_t[:, :NT],
                                                                        op0=ALU.mult, op1=ALU.mult), reads=["hn", "rstd", "gf"], writes=[htok])
                    otok = ("fout", it, c)
                    kb.dma("sp", dst_v[:, c, dst_pad + t0:dst_pad + t0 + NT], ho[:, :], reads=[htok], writes=[otok])
                    out_toks.append(otok)


FFN_W_SHAPES = dict(wup=[FFN_BLK, 128, NCH, 256], wdn=[NCH, 128, FFN_BLK, 128], cw=[128, FFN_BLK, 4], g=[128, NCH])
XA_W_SHAPES = dict(wq=[NCH, 128, NCH, 128], wk=[NCH, 128, NCH, 128], wv=[4, 128, NCH, 512], wo=[NCH, 128, NCH, 128], g=[128, NCH])
RG_W_SHAPES = dict(win=[32, 128, NCH, 128], gw=[128, 2, RG_BLOCKS, 2, 512], cw=[128, NCH, 5], gb=[128, 2, NCH, 2], lam=[128, 2, NCH],
                   wout=[NCH, 128, NCH, 128], g=[128, NCH])
DEPTH = 4
PAD = 2
PRECAST = True
PRECAST_KEYS = ('wup', 'wdn', 'wq', 'wk', 'wv', 'wo', 'win', 'wz', 'wx', 'wdt', 'wout', 'wqk', 'wtm')


def rglru_scratch(nc, T, pfx=""):
    return dict(ggT=nc.dram_tensor(pfx + "ggT", [D_MODEL, T], BF16).ap(), hfT=nc.dram_tensor(pfx + "hfT", [D_MODEL, T], F32).ap(),
                abT=nc.dram_tensor(pfx + "abT", [D_MODEL, T], F32).ap(), bxbT=nc.dram_tensor(pfx + "bxbT", [D_MODEL, T], F32).ap())


def build_prog(T, plist, NT=512, depth=DEPTH):
    nc = bass.Bass("TRN2", target_bir_lowering=False)
    ext = lambda n, s: nc.dram_tensor(n, s, F32, kind="ExternalInput").ap()
    xT = ext("xT", [D_MODEL, T + 2 * PAD])
    has_xa = any(k == "xa" for _, k in plist)
    has_fin = any(k == "ffn" and i == depth - 1 for i, k in plist)
    if has_xa:
        memT = ext("memT", [D_MODEL, N_MEM])
        gmem = ext("gmem", [128, NCH])
    gfin = ext("gfin", [128, NCH]) if has_fin else None
    lw = {}
    for (i, kind) in plist:
        if kind == "mix":
            shapes = [SSD_W_SHAPES, ML_W_SHAPES, RG_W_SHAPES][i % 3]
        elif kind == "xa":
            shapes = XA_W_SHAPES
        else:
            shapes = FFN_W_SHAPES
        lw[(i, kind)] = {k: ext(f"l{i}_{kind}_{k}", s) for k, s in shapes.items()}
    outT = nc.dram_tensor("outT", [D_MODEL, T], F32, kind="ExternalOutput").ap()
    nint = max(0, len(plist) - 1)
    hbufs = [nc.dram_tensor(f"hI{j}", [D_MODEL, T + 2 * PAD], F32).ap() for j in range(min(2, nint))]
    kinds = set(i % 3 for i, k in plist if k == "mix")
    scr_ssd = ssd_scratch(nc, T, "s_") if 0 in kinds else None
    scr_ml = mlstm_scratch(nc, T, "m_") if 1 in kinds else None
    scr_rg = rglru_scratch(nc, T, "r_") if 2 in kinds else None
    view = lambda ap: ap.rearrange("(c p) t -> p c t", p=128)
    with contextlib.ExitStack() as stack:
        kb = KB(nc, stack)
        pp = PsumPool(kb, nbanks=6)
        cx = make_ctx(kb, NT)
        outs = []
        if hbufs:
            with kb.phase():
                z = sb(kb, "zpad", [128, NCH, PAD], F32)
                kb.op("dve", lambda: nc.vector.memset(z[:, :, :], 0.0), writes=["z"])
                for hb in hbufs:
                    kb.dma("sp", view(hb)[:, :, 0:PAD], z[:, :, :], reads=["z"], writes=[("zp", id(hb), 0)])
                    kb.dma("sp", view(hb)[:, :, T + PAD:T + 2 * PAD], z[:, :, :], reads=["z"], writes=[("zp", id(hb), 1)])
        if PRECAST:
            ci_ = 0
            for key_, wd_ in lw.items():
                for nm_ in list(wd_):
                    if nm_ not in PRECAST_KEYS:
                        continue
                    src_ = wd_[nm_]
                    shp_ = list(src_.shape)
                    dstt_ = nc.dram_tensor(f"bf_l{key_[0]}_{key_[1]}_{nm_}", shp_, BF16).ap()
                    for j_ in range(shp_[0]):
                        kb.dma("pool", dstt_[j_], src_[j_], writes=[("precast", ci_)])
                        ci_ += 1
                    wd_[nm_] = dstt_
            kb.barrier()
        if has_xa:
            emit_mem_norm(kb, pp, cx, memT, gmem)
        cur, cur_pad = xT, PAD
        for n, (i, kind) in enumerate(plist):
            lastp = (n == len(plist) - 1)
            dst, dpad = (outT, 0) if lastp else (hbufs[n % 2], PAD)
            w = lw[(i, kind)]
            if kind == "mix":
                if i % 3 == 0:
                    emit_ssd(kb, pp, cx, T, view(cur), cur_pad, view(dst), dpad, w, scr_ssd, outs, NT)
                elif i % 3 == 1:
                    emit_mlstm(kb, pp, cx, T, view(cur), cur_pad, view(dst), dpad, w, scr_ml, outs, NT)
                else:
                    emit_rglru(kb, pp, cx, T, view(cur), cur_pad, view(dst), dpad, w, scr_rg, outs, NT)
            elif kind == "xa":
                emit_xattn(kb, pp, cx, T, view(cur), cur_pad, view(dst), dpad, w, outs, NT)
            else:
                emit_ffn(kb, pp, cx, T, view(cur), cur_pad, view(dst), dpad, w, outs, gfin if i == depth - 1 else None, NT)
            cur, cur_pad = dst, dpad
        kb.finish(outs)
    return nc


def all_phases(depth=DEPTH):
    return [(i, k) for i in range(depth) for k in ("mix", "xa", "ffn")]


def build_full(T, NT=512, depth=DEPTH):
    return build_prog(T, all_phases(depth), NT, depth)


def prep_ffn_w(w_up, conv_w, conv_b, w_down, g):
    d = prep_ffn_weights(w_up, conv_w, conv_b, w_down, g)
    return dict(wup=d["wup"], wdn=d["wdn"], cw=d["cw"], g=d["gnorm"])


def prep_all(inp, depth=DEPTH):
    f = lambda a: np.ascontiguousarray(a, dtype=np.float32)
    col = lambda v: f(np.asarray(v).reshape(NCH, 128).T)
    out = dict(gmem=col(inp["mem_norm"]), gfin=col(inp["final_norm"]))
    for i in range(depth):
        kind, j = i % 3, i // 3
        g = np.asarray(inp["mix_norm"][i])
        if kind == 0:
            mix = prep_ssd_weights(*[np.asarray(inp[k][j]) for k in ["ssd_w_in", "ssd_conv_w", "ssd_conv_b", "ssd_dt_bias", "ssd_a_log",
                                                                      "ssd_d_skip", "ssd_norm", "ssd_w_out"]], g)
        elif kind == 1:
            mix = prep_mlstm_weights(*[np.asarray(inp[k][j]) for k in ["mlstm_w_in", "mlstm_gate_bias", "mlstm_head_norm", "mlstm_w_out"]], g)
        else:
            mix = prep_rglru_weights(*[np.asarray(inp[k][j]) for k in ["rglru_w_in", "rglru_conv_w", "rglru_conv_b", "rglru_gate_w",
                                                                        "rglru_gate_b", "rglru_lambda", "rglru_w_out"]], g)
        xa = prep_xattn_weights(np.asarray(inp["xattn_wq"][i]), np.asarray(inp["xattn_wkv"][i]), np.asarray(inp["xattn_wo"][i]),
                                np.asarray(inp["xattn_norm"][i]))
        ff = prep_ffn_w(np.asarray(inp["ffn_w_up"][i]), np.asarray(inp["ffn_conv_w"][i]), np.asarray(inp["ffn_conv_b"][i]),
                        np.asarray(inp["ffn_w_down"][i]), np.asarray(inp["ffn_norm"][i]))
        for k, v in mix.items():
            out[f"l{i}_mix_{k}"] = v
        for k, v in xa.items():
            out[f"l{i}_xa_{k}"] = v
        for k, v in ff.items():
            out[f"l{i}_ffn_{k}"] = v
    return out


FUSED_GROUPS = None


def run_full(inputs, T, depth=DEPTH, NT=512, groups=None):
    x = np.asarray(inputs["x"])
    mem = np.asarray(inputs["mem"])
    B = x.shape[0]
    shared = prep_all(inputs, depth)
    if groups is None:
        groups = [all_phases(depth)]
    cur = [np.ascontiguousarray(x[b, :T].T, dtype=np.float32) for b in range(B)]
    for plist in groups:
        nc = build_prog(T, plist, NT, depth)
        names = set()
        for alloc_name in shared:
            names.add(alloc_name)
        need = lambda k: any(k.startswith(f"l{i}_{kind}_") for (i, kind) in plist)
        in_maps = []
        for b in range(B):
            xp = np.zeros((D_MODEL, T + 2 * PAD), np.float32)
            xp[:, PAD:PAD + T] = cur[b]
            m = {k: v for k, v in shared.items() if need(k)}
            if any(k == "xa" for _, k in plist):
                m["memT"] = np.ascontiguousarray(mem[b].T, dtype=np.float32)
                m["gmem"] = shared["gmem"]
            if any(k == "ffn" and i == depth - 1 for i, k in plist):
                m["gfin"] = shared["gfin"]
            m["xT"] = xp
            in_maps.append(m)
        res = run_bass_kernel_spmd(nc, in_maps, core_ids=list(range(B)))
        cur = [res.results[b]["outT"] for b in range(B)]
        if VERBOSE:
            print("launch done", plist, float(np.abs(cur[0]).mean()), flush=True)
    out = np.stack([cur[b].T for b in range(B)], 0)
    return np.ascontiguousarray(out, dtype=np.float32)


VERBOSE = False
SPLIT = False


def kernel(**inputs):
    groups = [[p] for p in all_phases()] if SPLIT else None
    return run_full(inputs, T=16384, groups=groups)
```
